# Optimizing a Trainium2 kernel written in Bass

```python
import functools
import jax, jax.numpy as jnp
from jax import lax
import numpy as np

D_MODEL = 1024
BATCH = 32
SEQ = 2048
DEPTH = 2
DEC_BATCH = 32
DEC_SEQ = 64
PAST_LEN = 2048

CHUNK = 64
Q_BLOCK = 128
HEAD_DIM = 64
H_A = 6
H_B = 6
H_C = 4
W_A = H_A * HEAD_DIM
W_B = H_B * HEAD_DIM
W_C = H_C * HEAD_DIM
W_MIX = W_A + W_B + W_C
D_DECAY_LORA = 64
D_AAA_LORA = 64
SHIFT_W = 3 * W_A + D_DECAY_LORA + D_AAA_LORA
IN_COLS = SHIFT_W + W_A + 4 * W_B + H_B + 4 * W_C
ALPHA = (2 * DEPTH) ** 0.25
BETA = (8 * DEPTH) ** -0.25
GN_EPS = 64e-5
LN_EPS = 1e-5
NEG_INF = -1e30

kernel_name = 'hybrid_rwkv7_fox_stickbreak_stream_step'


def _layernorm(x, g, b):
    xf = x.astype(jnp.float32)
    mu = jnp.mean(xf, axis=-1, keepdims=True)
    var = jnp.mean(jnp.square(xf - mu), axis=-1, keepdims=True)
    return ((xf - mu) * lax.rsqrt(var + LN_EPS) * g + b).astype(x.dtype)


def _wkv7_scan(s0, r, decay, k, v, kk, a):
    def step(s, inp):
        r_t, w_t, k_t, v_t, kk_t, a_t = inp
        sa = jnp.einsum('bhvk,bhk->bhv', s, kk_t)
        s = (s * w_t[:, :, None, :]
             - sa[..., None] * (kk_t * a_t)[:, :, None, :]
             + v_t[..., None] * k_t[:, :, None, :])
        return s, jnp.einsum('bhvk,bhk->bhv', s, r_t)
    xs = tuple(jnp.moveaxis(z, 1, 0) for z in (r, decay, k, v, kk, a))
    s_fin, y = lax.scan(step, s0, xs)
    return s_fin, jnp.moveaxis(y, 0, 1)


def _sweep_queries(block_fn, q_inputs, q_pos):
    tq = q_pos.shape[0]
    qb = min(Q_BLOCK, tq)
    nb = -(-tq // qb)
    pad = nb * qb - tq

    def to_blocks(arr):
        arr = jnp.pad(arr, [(0, 0), (0, pad)] + [(0, 0)] * (arr.ndim - 2), mode='edge')
        return jnp.moveaxis(arr.reshape(arr.shape[0], nb, qb, *arr.shape[2:]), 1, 0)

    qs = tuple(to_blocks(arr) for arr in q_inputs)
    qp = jnp.pad(q_pos, (0, pad), mode='edge').reshape(nb, qb)
    out = lax.map(lambda xs: block_fn(*xs[0], xs[1]), (qs, qp))
    out = jnp.moveaxis(out, 0, 1)
    return out.reshape(out.shape[0], nb * qb, *out.shape[3:])[:, :tq]


def _fox_block(q, fq, q_pos, k, v, fk, k_pos):
    s = jnp.einsum('bqhd,bkhd->bhqk', q, k).astype(jnp.float32) * (HEAD_DIM ** -0.5)
    s = s + jnp.swapaxes(fq, 1, 2)[..., :, None] - jnp.swapaxes(fk, 1, 2)[..., None, :]
    s = jnp.where(k_pos[None, :] <= q_pos[:, None], s, NEG_INF)
    p = jax.nn.softmax(s, axis=-1)
    return jnp.einsum('bhqk,bkhd->bqhd', p.astype(v.dtype), v)


def _sb_block(q, q_pos, k, v, k_pos):
    z = jnp.einsum('bqhd,bkhd->bhqk', q, k).astype(jnp.float32) * (HEAD_DIM ** -0.5)
    mask = k_pos[None, :] < q_pos[:, None]
    log_1m = jnp.where(mask, jax.nn.log_sigmoid(-z), 0.0)
    after = lax.cumsum(log_1m, axis=3, reverse=True) - log_1m
    w = jnp.where(mask, jnp.exp(jax.nn.log_sigmoid(z) + after), 0.0)
    return jnp.einsum('bhqk,bkhd->bqhd', w.astype(v.dtype), v)


def _layer(x, hist, w_in, mu_shift, w0_decay, w_decay, a0, w_aaa, k_k, k_a, r_k,
           lnx_g, lnx_b, fox_fb, w_out, ln_g, ln_b):
    f32 = jnp.float32
    bsz, t, _ = x.shape
    past = 0 if hist is None else hist[0].shape[1]
    q_pos = past + jnp.arange(t, dtype=jnp.int32)
    k_pos = jnp.arange(past + t, dtype=jnp.int32)

    u = jnp.einsum('btd,dc->btc', x, w_in)
    sizes = (SHIFT_W, W_A, W_B, W_B, W_B, H_B, W_B, W_C, W_C, W_C, W_C)
    (u_shift, g_a, q_b, k_b, v_b, f_b, g_b, q_c, k_c, v_c, g_c) = jnp.split(
        u, np.cumsum(sizes)[:-1].tolist(), axis=-1)

    prev = (jnp.zeros((bsz, 1, SHIFT_W), u.dtype) if hist is None else hist[6].astype(u.dtype))
    z_prev = jnp.concatenate([prev, u_shift[:, :-1]], axis=1)
    zs = u_shift + (z_prev - u_shift) * mu_shift
    r, k, v, w_lo, a_lo = jnp.split(
        zs, [W_A, 2 * W_A, 3 * W_A, 3 * W_A + D_DECAY_LORA], axis=-1)
    w_log = -jax.nn.softplus(-(w0_decay + jnp.tanh(w_lo) @ w_decay).astype(f32)) - 0.5
    decay = jnp.exp(-jnp.exp(w_log))
    a = jax.nn.sigmoid((a0 + a_lo @ w_aaa).astype(f32))

    def heads_a(z):
        return z.reshape(bsz, t, H_A, HEAD_DIM).astype(f32)

    r, k, v, decay, a = (heads_a(z) for z in (r, k, v, decay, a))
    kk = k * k_k.reshape(H_A, HEAD_DIM)
    kk = kk * lax.rsqrt(jnp.sum(kk * kk, axis=-1, keepdims=True) + 1e-12)
    k = k * (1.0 + (a - 1.0) * k_a.reshape(H_A, HEAD_DIM))
    s0 = (jnp.zeros((bsz, H_A, HEAD_DIM, HEAD_DIM), f32) if hist is None else hist[5].astype(f32))
    wkv_new, y = _wkv7_scan(s0, r, decay, k, v, kk, a)
    mean = jnp.mean(y, axis=-1, keepdims=True)
    var = jnp.mean(jnp.square(y - mean), axis=-1, keepdims=True)
    yn = ((y - mean) * lax.rsqrt(var + GN_EPS) * lnx_g.reshape(H_A, HEAD_DIM)
          + lnx_b.reshape(H_A, HEAD_DIM))
    bonus = jnp.sum(r * k * r_k, axis=-1, keepdims=True) * v
    o_a = (yn + bonus).reshape(bsz, t, W_A).astype(x.dtype) * jax.nn.silu(g_a)

    qh_b = q_b.reshape(bsz, t, H_B, HEAD_DIM)
    kh_b = k_b.reshape(bsz, t, H_B, HEAD_DIM)
    vh_b = v_b.reshape(bsz, t, H_B, HEAD_DIM)
    logf = jax.nn.log_sigmoid((f_b + fox_fb).astype(f32))
    if hist is None:
        kb_all, vb_all, lf_all = kh_b, vh_b, logf
    else:
        kb_all = jnp.concatenate([hist[0].astype(kh_b.dtype), kh_b], axis=1)
        vb_all = jnp.concatenate([hist[1].astype(vh_b.dtype), vh_b], axis=1)
        lf_all = jnp.concatenate([hist[2].astype(f32), logf], axis=1)
    cum_f = jnp.cumsum(lf_all, axis=1)
    o = _sweep_queries(
        functools.partial(_fox_block, k=kb_all, v=vb_all, fk=cum_f, k_pos=k_pos),
        (qh_b, cum_f[:, past:]), q_pos)
    o_b = o.reshape(bsz, t, W_B).astype(x.dtype) * jax.nn.silu(g_b)

    qh_c = q_c.reshape(bsz, t, H_C, HEAD_DIM)
    kh_c = k_c.reshape(bsz, t, H_C, HEAD_DIM)
    vh_c = v_c.reshape(bsz, t, H_C, HEAD_DIM)
    if hist is None:
        kc_all, vc_all = kh_c, vh_c
    else:
        kc_all = jnp.concatenate([hist[3].astype(kh_c.dtype), kh_c], axis=1)
        vc_all = jnp.concatenate([hist[4].astype(vh_c.dtype), vh_c], axis=1)
    o = _sweep_queries(
        functools.partial(_sb_block, k=kc_all, v=vc_all, k_pos=k_pos), (qh_c,), q_pos)
    o_c = o.reshape(bsz, t, W_C).astype(x.dtype) * jax.nn.silu(g_c)

    out = jnp.einsum('btc,cd->btd', jnp.concatenate([o_a, o_b, o_c], axis=-1), w_out)
    y_out = _layernorm(ALPHA * x + out, ln_g, ln_b)
    return y_out, (kh_b, vh_b, logf, kh_c, vh_c, wkv_new, u_shift[:, -1:])


def setup_inputs(seed: int = 0) -> dict:
    key = jax.random.key(seed)
    ks = jax.random.split(key, 24)
    f32 = jnp.float32
    L = DEPTH

    def nrm(k, shape, s=1.0):
        return s * jax.random.normal(k, shape, f32)

    def uni(k, shape, lo, hi):
        return jax.random.uniform(k, shape, f32, lo, hi)

    return {
        'x_prompt': nrm(ks[0], (BATCH, SEQ, D_MODEL)),
        'x_sample': nrm(ks[1], (DEC_BATCH, DEC_SEQ, D_MODEL)),
        'cache_fox_k': nrm(ks[2], (L, DEC_BATCH, PAST_LEN, H_B, HEAD_DIM)),
        'cache_fox_v': nrm(ks[3], (L, DEC_BATCH, PAST_LEN, H_B, HEAD_DIM)),
        'cache_fox_logf': jax.nn.log_sigmoid(3.0 + nrm(ks[4], (L, DEC_BATCH, PAST_LEN, H_B))),
        'cache_sb_k': nrm(ks[5], (L, DEC_BATCH, PAST_LEN, H_C, HEAD_DIM)),
        'cache_sb_v': nrm(ks[6], (L, DEC_BATCH, PAST_LEN, H_C, HEAD_DIM)),
        'state_wkv': nrm(ks[7], (L, DEC_BATCH, H_A, HEAD_DIM, HEAD_DIM), 0.1),
        'state_shift': nrm(ks[8], (L, DEC_BATCH, 1, SHIFT_W)),
        'w_in': nrm(ks[9], (L, D_MODEL, IN_COLS), D_MODEL ** -0.5),
        'mu_shift': uni(ks[10], (L, SHIFT_W), 0.0, 1.0),
        'w0_decay': uni(ks[11], (L, W_A), -5.0, 1.0),
        'w_decay': nrm(ks[12], (L, D_DECAY_LORA, W_A), 0.5 * D_DECAY_LORA ** -0.5),
        'a0': nrm(ks[13], (L, W_A), 0.1),
        'w_aaa': nrm(ks[14], (L, D_AAA_LORA, W_A), 0.5 * D_AAA_LORA ** -0.5),
        'k_k': 0.85 + nrm(ks[15], (L, W_A), 0.05),
        'k_a': 1.0 + nrm(ks[16], (L, W_A), 0.05),
        'r_k': nrm(ks[17], (L, H_A, HEAD_DIM), 0.1),
        'lnx_g': 1.0 + nrm(ks[18], (L, W_A), 0.05),
        'lnx_b': nrm(ks[19], (L, W_A), 0.01),
        'fox_fb': uni(ks[20], (L, H_B), 1.0, 5.0),
        'w_out': nrm(ks[21], (L, W_MIX, D_MODEL), BETA * W_MIX ** -0.5),
        'ln_g': 1.0 + nrm(ks[22], (L, D_MODEL), 0.05),
        'ln_b': nrm(ks[23], (L, D_MODEL), 0.01),
    }


def reference(x_prompt, x_sample, cache_fox_k, cache_fox_v, cache_fox_logf, cache_sb_k,
              cache_sb_v, state_wkv, state_shift, w_in, mu_shift, w0_decay, w_decay, a0,
              w_aaa, k_k, k_a, r_k, lnx_g, lnx_b, fox_fb, w_out, ln_g, ln_b):
    yp, ys = x_prompt, x_sample
    new_p, new_s = [], []
    for l in range(DEPTH):
        prm = (w_in[l], mu_shift[l], w0_decay[l], w_decay[l], a0[l], w_aaa[l], k_k[l],
               k_a[l], r_k[l], lnx_g[l], lnx_b[l], fox_fb[l], w_out[l], ln_g[l], ln_b[l])
        yp, st_p = _layer(yp, None, *prm)
        hist = (cache_fox_k[l], cache_fox_v[l], cache_fox_logf[l], cache_sb_k[l],
                cache_sb_v[l], state_wkv[l], state_shift[l])
        ys, st_s = _layer(ys, hist, *prm)
        new_p.append(st_p)
        new_s.append(st_s)

    def stack(sts, i):
        return jnp.stack([st[i] for st in sts])

    p_fox_k, p_fox_v, p_fox_logf = stack(new_p, 0), stack(new_p, 1), stack(new_p, 2)
    p_sb_k, p_sb_v = stack(new_p, 3), stack(new_p, 4)
    p_wkv, p_shift = stack(new_p, 5), stack(new_p, 6)
    s_fox_k, s_fox_v, s_fox_logf = stack(new_s, 0), stack(new_s, 1), stack(new_s, 2)
    s_sb_k, s_sb_v = stack(new_s, 3), stack(new_s, 4)
    s_wkv, s_shift = stack(new_s, 5), stack(new_s, 6)
    return (yp, ys, p_fox_k, p_fox_v, p_fox_logf, p_sb_k, p_sb_v, p_wkv, p_shift,
            s_fox_k, s_fox_v, s_fox_logf, s_sb_k, s_sb_v, s_wkv, s_shift)
```

```python
import contextlib
import os
import numpy as np
import ml_dtypes
import concourse.bass as bass
import concourse.mybir as mybir
from concourse.bass_utils import run_bass_kernel_spmd

F32 = mybir.dt.float32
BF16 = mybir.dt.bfloat16
ALU = mybir.AluOpType
AF = mybir.ActivationFunctionType

PE, ACT, DVE, POOL, SP = "tensor", "scalar", "vector", "gpsimd", "sync"
ENGS = (PE, ACT, DVE, POOL, SP)
SEM_WRAP = 30000
ANNOTATE = bool(os.environ.get("ANNOTATE"))
HEAT_S = int(os.environ.get("HEAT_S", "1"))
HEAT_F = int(os.environ.get("HEAT_F", "0"))
HEAT_R = int(os.environ.get("HEAT_R", "0"))
ADV2 = int(os.environ.get("ADV2", "0"))

D = 1024
T_P = 2048
T_S = 64
PAST = 2048
NL = 2
INC = 4230
SHIFT_W = 1280
ALPHA = (2 * NL) ** 0.25
GN_EPS = 64e-5
LN_EPS = 1e-5
NEG = -30000.0
DEC_SCALE = -float(np.exp(-0.5))

FT = ([(128 * i, 128) for i in range(10)] +
      [(1280 + 128 * i, 128) for i in range(3)] +
      [(1664 + 128 * i, 128) for i in range(3)] +
      [(2048 + 128 * i, 128) for i in range(3)] +
      [(2822 + 128 * i, 128) for i in range(3)] +
      [(3206 + 128 * i, 128) for i in range(2)] +
      [(3462 + 128 * i, 128) for i in range(2)] +
      [(3974 + 128 * i, 128) for i in range(2)] +
      [(2816, 6)])
FT_GA, FT_QB, FT_KB, FT_GB, FT_QC, FT_KC, FT_GC, FT_FB = 10, 13, 16, 19, 22, 24, 26, 28
NFT = len(FT)
TG = [(2048, 384), (2432, 390), (3462, 512)]


class Buf:
    __slots__ = ("name", "w", "r", "excl")

    def __init__(self, name, excl=False):
        self.name = name
        self.w = None
        self.r = []
        self.excl = excl


class Prog:
    def __init__(self, nc, n_dma_sems=64):
        self.nc = nc
        self.q = {e: [] for e in ENGS}
        self.nsem = 0
        self.cur = {}
        self.cnt = {}
        self.allsems = {e: [] for e in ENGS}
        for e in ENGS:
            if e != SP:
                self.cur[e] = self._newsem()
                self.cnt[e] = 0
                self.allsems[e].append(self.cur[e])
        self.dma_pool = {SP: [self._newsem() for _ in range(20)], POOL: [self._newsem() for _ in range(12)]}
        self.dma_cnt = {s: 0 for e in self.dma_pool for s in self.dma_pool[e]}
        self.dma_next = {e: 0 for e in self.dma_pool}
        self.waited = {e: {} for e in ENGS}
        self.bar = {e: [] for e in ENGS}
        self.final = {}
        self.tag = None

    def _newsem(self):
        k = self.nsem
        self.nsem += 1
        return k

    def _need(self, eng, ev, waits):
        if ev is None:
            return
        k, v = ev[0], ev[1]
        if self.waited[eng].get(k, 0) >= v:
            return
        self.waited[eng][k] = v
        waits.append((k, v))

    def barrier(self):
        evs = []
        for e in ENGS:
            if e != SP and self.cnt[e]:
                evs.append((self.cur[e], self.cnt[e], e, False))
        for s, c in self.dma_cnt.items():
            if c:
                evs.append((s, 16 * c, None, True))
        for e in ENGS:
            self.bar[e] = self.bar[e] + evs

    def op(self, eng, fn, reads=(), writes=(), is_dma=False):
        waits = []
        if self.bar[eng]:
            for ev in self.bar[eng]:
                self._need(eng, ev, waits)
            self.bar[eng] = []
        for b in reads:
            self._need(eng, b.w, waits)
            if b.excl:
                for ev in b.r:
                    if ev[2] != eng:
                        self._need(eng, ev, waits)
        for b in writes:
            w = b.w
            if w is not None and not (w[2] == eng and not w[3] and not is_dma and eng != POOL):
                self._need(eng, w, waits)
            for ev in b.r:
                if ev[2] == eng and not is_dma and not ev[3] and eng != POOL:
                    continue
                self._need(eng, ev, waits)
        if is_dma:
            pool = self.dma_pool[eng]
            s = pool[self.dma_next[eng]]
            self.dma_next[eng] = (self.dma_next[eng] + 1) % len(pool)
            prev = self.dma_cnt[s]
            if prev:
                self._need(eng, (s, 16 * prev, None, True), waits)
            self.dma_cnt[s] = prev + 1
            ev = (s, 16 * (prev + 1), eng, True)
            inc = (s, 16)
        else:
            if self.cnt[eng] >= SEM_WRAP:
                self.final[self.cur[eng]] = self.cnt[eng]
                self.cur[eng] = self._newsem()
                self.cnt[eng] = 0
            self.cnt[eng] += 1
            ev = (self.cur[eng], self.cnt[eng], eng, False)
            inc = (self.cur[eng], 1)
        m = {}
        for k, v in waits:
            m[k] = max(m.get(k, 0), v)
        self.q[eng].append((list(m.items()), fn, inc, self.tag))
        for b in reads:
            b.r.append(ev)
            if len(b.r) > 64:
                b.r = b.r[-64:] if False else b.r
        for b in writes:
            b.w = ev
            b.r = []
        return ev

    def dma(self, eng, out, in_, reads=(), writes=(), **kw):
        def fn(e, out=out, in_=in_, kw=kw):
            return e.dma_start(out=out, in_=in_, **kw)
        return self.op(eng, fn, reads, writes, is_dma=True)

    def finish(self):
        nc = self.nc
        fw = dict(self.final)
        for e in ENGS:
            if e != SP and self.cnt[e]:
                fw[self.cur[e]] = self.cnt[e]
        for s, c in self.dma_cnt.items():
            if c:
                fw[s] = 16 * c
        with contextlib.ExitStack() as es:
            sems = [es.enter_context(nc.semaphore(f"s{i}")) for i in range(self.nsem)]
            es.enter_context(nc.allow_non_contiguous_dma("small strided parameter / state transfers"))
            block = es.enter_context(nc.Block())

            def emit(engname):
                def body(e):
                    for waits, fn, inc, tag in self.q[engname]:
                        for k, v in waits:
                            e.wait_ge(sems[k], v)
                        ins = fn(e)
                        ins.then_inc(sems[inc[0]], inc[1])
                        if tag and ANNOTATE:
                            ins.annotate(tag)
                    if engname == SP:
                        for k, v in fw.items():
                            e.wait_ge(sems[k], v)
                return body
            block.tensor(emit(PE))
            block.scalar(emit(ACT))
            block.vector(emit(DVE))
            block.gpsimd(emit(POOL))
            block.sync(emit(SP))


class Tl:
    def __init__(self, h, name):
        self.h = h
        self.b = Buf(name)

    def __getitem__(self, k):
        return self.h[k]


class Mem:
    def __init__(self, nc, base, limit):
        self.nc, self.top, self.limit, self.n = nc, base, limit, 0

    def alloc(self, name, shape, dt):
        sz = int(np.prod(shape[1:])) * (4 if dt == F32 else 2)
        sz = (sz + 31) // 32 * 32
        self.n += 1
        h = self.nc.alloc_sbuf_tensor_at(f"{name}_{self.n}", list(shape), dt, offset=self.top)
        self.top += sz
        assert self.top <= self.limit, f"SBUF overflow at {name}: {self.top}"
        return Tl(h, name)


def host_consts():
    c = {}
    c["identf"] = np.eye(128, dtype=np.float32)
    bo = np.zeros((128, 128), np.float32)
    bo[:64, :64] = 1
    bo[64:, 64:] = 1
    c["bones"] = bo
    cm = np.ones((128, 512), np.float32)
    cm[:, ::64] = 0
    c["chunkmask"] = cm
    i = np.arange(128)[:, None]
    j = np.arange(128)[None, :]
    same = (i // 64) == (j // 64)
    sl = (same & (j < i)).astype(np.float32)
    su = (same & (i < j)).astype(np.float32)
    iu = (same & (i <= j)).astype(np.float32)
    c["rwmask"] = np.concatenate([-sl, -su, su, iu], axis=1)
    c["iumask"] = iu
    sg = np.ones((128, 128), np.float32)
    sg[:, :64] = -1
    c["signs"] = sg
    kl = np.arange(128)[:, None]
    cc = np.arange(896)[None, :]
    c["negle"] = np.where(kl <= cc - 384, 0.0, NEG).astype(np.float32)
    c["neglt"] = np.where(kl < cc - 384, 0.0, NEG).astype(np.float32)
    c["trineg"] = np.where(i >= j, -1.0, 0.0).astype(np.float32)
    c["iumask4"] = np.tile(iu, (1, 4))
    c["signs4"] = np.tile(sg, (1, 4))
    c["ident2"] = np.concatenate([np.eye(64, dtype=np.float32)] * 2, axis=0)
    ci = np.zeros((128, 2), np.float32)
    ci[:64, 0] = 1
    ci[64:, 1] = 1
    c["cind"] = ci
    return c


CONST_SHAPES = dict(identf=(128, 128), bones=(128, 128), chunkmask=(128, 512), rwmask=(128, 512),
                    iumask=(128, 128), signs=(128, 128), negle=(128, 896), neglt=(128, 896),
                    trineg=(128, 128), cind=(128, 2), iumask4=(128, 512), signs4=(128, 512),
                    ident2=(128, 64))


def build_program(NP, NS, dbg=None):
    nc = bass.Bass("TRN2", target_bir_lowering=False)
    P = Prog(nc)

    def din(name, shape):
        return nc.dram_tensor(name, list(shape), F32, kind="ExternalInput").ap()

    def dout(name, shape):
        return nc.dram_tensor(name, list(shape), F32, kind="ExternalOutput").ap()

    NPm, NSm = max(NP, 1), max(NS, 1)
    xp = din("x_prompt", (NPm, T_P, D))
    xs_ = din("x_sample", (NSm, T_S, D))
    cfk = din("cache_fox_k", (NL, NSm, PAST, 384))
    cfv = din("cache_fox_v", (NL, NSm, PAST, 384))
    cfl = din("cache_fox_logf", (NL, NSm, PAST, 6))
    csk = din("cache_sb_k", (NL, NSm, PAST, 256))
    csv = din("cache_sb_v", (NL, NSm, PAST, 256))
    swkv = din("state_wkv", (NL, NSm, 6, 64, 64))
    sshift = din("state_shift", (NL, NSm, SHIFT_W))
    w_in = din("w_in", (NL, D, INC))
    mu_shift = din("mu_shift", (NL, SHIFT_W))
    w0_decay = din("w0_decay", (NL, 384))
    w_decay = din("w_decay", (NL, 64, 384))
    a0 = din("a0", (NL, 384))
    w_aaa = din("w_aaa", (NL, 64, 384))
    k_k = din("k_k", (NL, 384))
    k_a = din("k_a", (NL, 384))
    r_k = din("r_k", (NL, 384))
    lnx_g = din("lnx_g", (NL, 384))
    lnx_b = din("lnx_b", (NL, 384))
    fox_fb = din("fox_fb", (NL, 6))
    w_out = din("w_out", (NL, D, D))
    ln_g = din("ln_g", (NL, D))
    ln_b = din("ln_b", (NL, D))
    cdr = {k: din("c_" + k, s) for k, s in CONST_SHAPES.items()}

    outs = {}
    for pre, n, t in (("p", NPm, T_P), ("s", NSm, T_S)):
        outs[pre + "y"] = dout(pre + "_y", (n, t, D))
        outs[pre + "fk"] = dout(pre + "_fox_k", (NL, n, t, 384))
        outs[pre + "fv"] = dout(pre + "_fox_v", (NL, n, t, 384))
        outs[pre + "fl"] = dout(pre + "_fox_logf", (NL, n, t, 6))
        outs[pre + "sk"] = dout(pre + "_sb_k", (NL, n, t, 256))
        outs[pre + "sv"] = dout(pre + "_sb_v", (NL, n, t, 256))
        outs[pre + "wkv"] = dout(pre + "_wkv", (NL, n, 6, 64, 64))
        outs[pre + "sh"] = dout(pre + "_shift", (NL, n, SHIFT_W))
    dbg_out = {}
    if dbg:
        for k, s in dbg.items():
            if not k.startswith("_"):
                dbg_out[k] = dout("dbg_" + k, s)

    WF = nc.dram_tensor("WF", [NL, NFT, 128, 8, 128], BF16).ap()
    WT = nc.dram_tensor("WT", [NL, 3, 128, 8, 512], BF16).ap()
    WO = nc.dram_tensor("WO", [NL, 128, 8, 1024], BF16).ap()
    Y0 = nc.dram_tensor("Y0", [NPm + NSm, T_P, D], F32).ap()
    bWF = [[Buf(f"WF{l}_{t}") for t in range(NFT)] for l in range(NL)]
    bWT = [[Buf(f"WT{l}_{g}") for g in range(3)] for l in range(NL)]
    bWO = [Buf(f"WO{l}") for l in range(NL)]
    bY0 = [[Buf(f"Y0_{i}_{t}") for t in range(16)] for i in range(NPm + NSm)]

    for l in range(NL):
        wv = w_in[l].rearrange("(c p) n -> p c n", p=128)
        for t, (s0, n) in enumerate(FT):
            P.dma(POOL, WF[l, t, :, :, 0:n], wv[:, :, s0:s0 + n], writes=[bWF[l][t]])
        for g, (s0, n) in enumerate(TG):
            P.dma(POOL, WT[l, g, :, :, 0:n], wv[:, :, s0:s0 + n], writes=[bWT[l][g]])
        wov = w_out[l].rearrange("(c p) n -> p c n", p=128)
        for hfi in range(2):
            P.dma(POOL, WO[l, :, :, hfi * 512:(hfi + 1) * 512], wov[:, :, hfi * 512:(hfi + 1) * 512],
                  writes=[bWO[l]])

    mem = Mem(nc, 16640, 228000)
    psum = [Tl(nc.alloc_psum_tensor(f"pb{i}", [128, 512], F32), f"pb{i}") for i in range(8)]
    for p_ in psum:
        p_.b.excl = True

    def pbf(i):
        return psum[i].h.ap().bitcast(BF16)

    C = {}
    for k, s in CONST_SHAPES.items():
        C[k] = mem.alloc("c_" + k, s, F32)
        P.dma(SP, C[k][:], cdr[k], writes=[C[k].b])
    identb = mem.alloc("identb", (128, 128), BF16)
    bonesb = mem.alloc("bonesb", (128, 128), BF16)
    bones64 = mem.alloc("bones64", (128, 128), BF16)
    ones64 = mem.alloc("ones64", (128, 64), BF16)
    onesneg = mem.alloc("onesneg", (128, 128), BF16)
    trinegb = mem.alloc("trinegb", (128, 128), BF16)
    negleb = mem.alloc("negleb", (128, 896), BF16)
    negltb = mem.alloc("negltb", (128, 896), BF16)
    onecol = mem.alloc("onecol", (128, 1), F32)
    ident8 = mem.alloc("ident8", (128, 8, 128), BF16)
    for u_ in range(8):
        P.op(DVE, lambda e, u_=u_: e.tensor_copy(out=ident8[:, u_, :], in_=C["identf"][:]), reads=[C["identf"].b], writes=[ident8.b])
    P.op(DVE, lambda e: e.tensor_copy(out=identb[:], in_=C["identf"][:]), reads=[C["identf"].b], writes=[identb.b])
    P.op(DVE, lambda e: e.tensor_copy(out=bonesb[:], in_=C["bones"][:]), reads=[C["bones"].b], writes=[bonesb.b])
    P.op(DVE, lambda e: e.tensor_scalar(out=bones64[:], in0=C["bones"][:], scalar1=1.0 / 64, scalar2=None, op0=ALU.mult),
         reads=[C["bones"].b], writes=[bones64.b])
    P.op(DVE, lambda e: e.memset(ones64[:], 1.0), writes=[ones64.b])
    P.op(DVE, lambda e: e.memset(onesneg[:], -1.0), writes=[onesneg.b])
    P.op(DVE, lambda e: e.memset(onecol[:], 1.0), writes=[onecol.b])
    P.op(DVE, lambda e: e.tensor_copy(out=trinegb[:], in_=C["trineg"][:]), reads=[C["trineg"].b], writes=[trinegb.b])
    P.op(DVE, lambda e: e.tensor_copy(out=negleb[:], in_=C["negle"][:]), reads=[C["negle"].b], writes=[negleb.b])
    P.op(DVE, lambda e: e.tensor_copy(out=negltb[:], in_=C["neglt"][:]), reads=[C["neglt"].b], writes=[negltb.b])

    def load_cols(name, src, ntile):
        t = mem.alloc(name, (128, NL, ntile), F32)
        with nc.allow_non_contiguous_dma("small param transpose load"):
            for l in range(NL):
                P.dma(SP, t[:, l, :], src[l].rearrange("(t p) -> p t", p=128), writes=[t.b])
        return t
    mu_t = load_cols("mu", mu_shift, 10)
    omm_t = mem.alloc("omm", (128, NL, 10), F32)
    P.op(DVE, lambda e: e.tensor_scalar(out=omm_t[:], in0=mu_t[:], scalar1=-1.0, scalar2=1.0, op0=ALU.mult, op1=ALU.add),
         reads=[mu_t.b], writes=[omm_t.b])
    w0_t = load_cols("w0", w0_decay, 3)
    a0_t = load_cols("a0", a0, 3)
    kk_t = load_cols("kk", k_k, 3)
    ka_t = load_cols("ka", k_a, 3)
    rk_t = load_cols("rk", r_k, 3)
    lg_t = load_cols("lxg", lnx_g, 3)
    lb_t = load_cols("lxb", lnx_b, 3)
    omka_t = mem.alloc("omka", (128, NL, 3), F32)
    P.op(DVE, lambda e: e.tensor_scalar(out=omka_t[:], in0=ka_t[:], scalar1=-1.0, scalar2=1.0, op0=ALU.mult, op1=ALU.add),
         reads=[ka_t.b], writes=[omka_t.b])
    nfb_t = mem.alloc("nfb", (128, NL), F32)
    with nc.allow_non_contiguous_dma("small param transpose load"):
        P.dma(SP, nfb_t[0:6, :], fox_fb.rearrange("l h -> h l"), writes=[nfb_t.b])
    P.op(DVE, lambda e: e.tensor_scalar(out=nfb_t[0:6, :], in0=nfb_t[0:6, :], scalar1=-1.0, scalar2=None, op0=ALU.mult),
         reads=[nfb_t.b], writes=[nfb_t.b])
    fbb = mem.alloc("fbb", (128, NL, 6), F32)
    P.dma(SP, fbb[:].rearrange("p l h -> p (l h)"), fox_fb.rearrange("l h -> (l h)").partition_broadcast(128), writes=[fbb.b])
    lwf = mem.alloc("lwf", (128, NL, 384), F32)
    lw = mem.alloc("lw", (128, NL, 384), BF16)
    for l in range(NL):
        P.dma(SP, lwf[0:64, l, :], w_decay[l], writes=[lwf.b])
        P.dma(SP, lwf[64:128, l, :], w_aaa[l], writes=[lwf.b])
    P.op(DVE, lambda e: e.tensor_copy(out=lw[:], in_=lwf[:]), reads=[lwf.b], writes=[lw.b])

    xT = mem.alloc("xT", (128, 8, T_P), BF16)
    oT = mem.alloc("oT", (128, 8, T_P), BF16)
    oTb = [Buf(f"oT{c}") for c in range(8)]
    Vb = mem.alloc("Vb", (128, 17, 384), BF16)
    Vc = mem.alloc("Vc", (128, 17, 256), BF16)
    Gst = [mem.alloc(f"G{i}", (128, 3, 64), F32) for i in range(2)]
    Gb = [[[Buf(f"G{i}_{c3}_{hh}") for hh in range(2)] for c3 in range(3)] for i in range(2)]
    ucarry = mem.alloc("ucarry", (128, 10), F32)
    ucb = [Buf(f"uc{ct}") for ct in range(10)]
    phase_base = mem.top
    if dbg:
        for c in range(8):
            P.op(POOL, lambda e, c=c: e.memset(oT[:, c, :], 0.0), writes=[oTb[c]])

    bank_rr = [0]

    def next_bank(lo=0, hi=2):
        i = lo + bank_rr[0] % (hi - lo)
        bank_rr[0] += 1
        return psum[i]

    wslot_rr = [0]

    from types import SimpleNamespace
    PHASES = set(dbg["_phases"]) if (dbg and "_phases" in dbg) else set("RFS")

    def dump(name, ap, reads):
        if name in dbg_out:
            P.dma(POOL, dbg_out[name], ap, reads=reads)

    def act(out, in_, func, R, W, bias=None, scale=None):
        kw = {}
        if bias is not None:
            kw["bias"] = bias
        if scale is not None:
            kw["scale"] = scale
        P.op(ACT, lambda e: e.activation(out=out, in_=in_, func=func, **kw), reads=R, writes=W)

    def tt(eng, out, in0, in1, op, R, W):
        P.op(eng, lambda e: e.tensor_tensor(out=out, in0=in0, in1=in1, op=op), reads=R, writes=W)

    def ts(eng, out, in0, s1, s2, op0, op1, R, W):
        if op1 is None and eng == POOL and op0 == ALU.mult:
            op1, s2 = ALU.mult, 1.0
        if op1 is None:
            P.op(eng, lambda e: e.tensor_scalar(out=out, in0=in0, scalar1=s1, scalar2=None, op0=op0), reads=R, writes=W)
        else:
            P.op(eng, lambda e: e.tensor_scalar(out=out, in0=in0, scalar1=s1, scalar2=s2, op0=op0, op1=op1), reads=R, writes=W)

    def stt(out, in0, scalar, in1, op0, op1, R, W):
        P.op(DVE, lambda e: e.scalar_tensor_tensor(out=out, in0=in0, scalar=scalar, in1=in1, op0=op0, op1=op1), reads=R, writes=W)

    def cp(eng, out, in_, R, W):
        if eng == ACT:
            P.op(ACT, lambda e: e.activation(out=out, in_=in_, func=AF.Copy), reads=R, writes=W)
        else:
            P.op(eng, lambda e: e.tensor_copy(out=out, in_=in_), reads=R, writes=W)

    heat = {"n": 0, "bank": None}

    def mm(out, lhsT, rhs, start, stop, R, W):
        P.op(PE, lambda e: e.matmul(out, lhsT=lhsT, rhs=rhs, start=start, stop=stop), reads=R, writes=W)
        if heat["n"] and stop:
            hb_ = heat["bank"]
            for _ in range(heat["n"]):
                P.op(PE, lambda e: e.matmul(hb_[:, 0:512], lhsT=identb[:, :], rhs=negltb[:, 0:512], start=True, stop=True),
                     reads=[identb.b, negltb.b], writes=[hb_.b])

    def tr(out, in_, ident, R, W):
        P.op(PE, lambda e: e.transpose(out=out, in_=in_, identity=ident), reads=R, writes=W)

    def memset(eng, ap, val, W):
        P.op(eng, lambda e: e.memset(ap, val), writes=W)

    def load_w(cx, slots, t):
        ws = slots[wslot_rr[0] % len(slots)]
        wslot_rr[0] += 1
        n = FT[t][1]
        P.dma(SP, ws[:, :, 0:n], WF[cx.l, t, :, :, 0:n], reads=[bWF[cx.l][t]], writes=[ws.b])
        return ws

    def project(cx, ws, ncol, blk, bank):
        cols = slice(blk * cx.BW, (blk + 1) * cx.BW)
        for c in range(8):
            mm(bank[0:ncol, 0:cx.BW], ws[:, c, 0:ncol], xT[:, c, cols], c == 0, c == 7, [ws.b, xT.b], [bank.b])

    def to_xT(cx, src, tt_, banks):
        TT = cx.TT
        for c in range(8):
            tr(banks[c // 4][:, (c % 4) * TT:(c % 4 + 1) * TT], src[:TT, c * 128:(c + 1) * 128],
               C["identf"][:TT, :TT], [src.b, C["identf"].b], [banks[c // 4].b])
        for j in range(2):
            s_ = banks[j][:, 0:4 * TT].rearrange("p (c t) -> p c t", t=TT)
            d_ = xT[:, 4 * j:4 * j + 4, tt_ * TT:(tt_ + 1) * TT]
            cp(ACT if j == 0 else DVE, d_, s_, [banks[j].b], [xT.b])

    def phase_X(cx):
        P.tag = "X"
        mem.top = phase_base
        P.barrier()
        xin = [mem.alloc(f"xin{i}", (128, D), F32) for i in range(2)]
        for tt_ in range(cx.NTT):
            sl = xin[tt_ % 2]
            P.dma(SP, sl[:cx.TT, :], cx.xsrc[cx.si, tt_ * cx.TT:(tt_ + 1) * cx.TT, :], writes=[sl.b])
            bk = (psum[0], psum[1]) if tt_ % 2 == 0 else (psum[2], psum[3])
            to_xT(cx, sl, tt_, bk)

    def phase_T(cx):
        P.tag = "T"
        l, si, TT, O = cx.l, cx.si, cx.TT, cx.O
        mem.top = phase_base
        P.barrier()
        wt = [mem.alloc(f"wt{g}", (128, 8, 512), BF16) for g in range(3)]
        stage = [mem.alloc(f"stg{i}", (128, 1288), F32) for i in range(2)]
        for g in range(3):
            P.dma(SP, wt[g][:, :, 0:TG[g][1]], WT[l, g, :, :, 0:TG[g][1]], reads=[bWT[l][g]], writes=[wt[g].b])
        if cx.kind == "s":
            P.dma(POOL, Vb[:, 0:16, :], cfv[l, si].rearrange("(t p) c -> p t c", p=128), writes=[Vb.b])
            P.dma(POOL, Vc[:, 0:16, :], csv[l, si].rearrange("(t p) c -> p t c", p=128), writes=[Vc.b])
        for tt_ in range(min(cx.NTT, int(os.environ.get("DBG_NTT", "99")))):
            st = stage[tt_ % 2]
            rows = slice(tt_ * TT, (tt_ + 1) * TT)
            bks = [psum[3 * (tt_ % 2) + g] for g in range(3)]
            for g, (s0, n) in enumerate(TG):
                for c in range(8):
                    mm(bks[g][:TT, 0:n], xT[:, c, rows], wt[g][:, c, 0:n], c == 0, c == 7, [xT.b, wt[g].b], [bks[g].b])
            cp(ACT, st[:TT, 0:384], bks[0][:TT, 0:384], [bks[0].b], [st.b])
            cp(DVE, st[:TT, 384:768], bks[1][:TT, 0:384], [bks[1].b], [st.b])
            cp(ACT, st[:TT, 768:1280], bks[2][:TT, 0:512], [bks[2].b], [st.b])
            tt(DVE, st[:TT, 1280:1286], bks[1][:TT, 384:390], fbb[:TT, l, :], ALU.add, [bks[1].b, fbb.b], [st.b])
            act(st[:TT, 1280:1286], st[:TT, 1280:1286], AF.Exp, [st.b], [st.b], scale=-1.0)
            act(st[:TT, 1280:1286], st[:TT, 1280:1286], AF.Ln, [st.b], [st.b], bias=1.0)
            ts(DVE, st[:TT, 1280:1286], st[:TT, 1280:1286], -1.0, None, ALU.mult, None, [st.b], [st.b])
            P.dma(SP, O["fl"][l, si, rows, :], st[:TT, 1280:1286], reads=[st.b])
            DBGT = int(os.environ.get("DBG_T", "9"))
            if DBGT in (2, 4, 9):
                cp(DVE, Vb[:TT, cx.PKT + tt_, :], bks[1][:TT, 0:384], [bks[1].b], [Vb.b])
            if DBGT in (2, 5, 9):
                cp(DVE, Vc[:TT, cx.PKT + tt_, :], st[:TT, 1024:1280], [st.b], [Vc.b])
            if DBGT != 9:
                continue
            P.dma(SP, O["fk"][l, si, rows, :], st[:TT, 0:384], reads=[st.b])
            P.dma(SP, O["fv"][l, si, rows, :], st[:TT, 384:768], reads=[st.b])
            P.dma(SP, O["sk"][l, si, rows, :], st[:TT, 768:1024], reads=[st.b])
            P.dma(SP, O["sv"][l, si, rows, :], st[:TT, 1024:1280], reads=[st.b])

    def phase_F(cx):
        P.tag = "F.pre"
        l, si, T, NK, BW, QW, past, O = cx.l, cx.si, cx.T, cx.NK, cx.BW, cx.BW, cx.past, cx.O
        mem.top = phase_base
        P.barrier()
        wsl = [mem.alloc(f"wf{i}", (128, 8, 128), BF16) for i in range(4)]
        Qa = [mem.alloc(f"Qa{i}", (128, T), BF16) for i in range(2)]
        Ka = [mem.alloc(f"Ka{i}", (128, NK), BF16) for i in range(2)]
        gate = mem.alloc("gate", (128, T), BF16)
        Aa = mem.alloc("Aa", (128, NK), F32)
        Sa = mem.alloc("Sa", (128, NK), F32)
        SPL = mem.alloc("SPL", (128, 3, NK), BF16)
        pts = [mem.alloc(f"pt{i}", (128, QW), BF16) for i in range(4)]
        rD = [mem.alloc(f"rD{i}", (128, QW), F32) for i in range(2)]
        o1 = [mem.alloc(f"o1{i}", (128, QW), F32) for i in range(2)]
        kst = mem.alloc("kst", (128, 16, 128), F32) if cx.kind == "s" else None
        Vaug = mem.alloc("Vaug", (128, 17, 2, 128), BF16)
        memset(POOL, Vaug[:, :, :, :], 1.0, [Vaug.b])

        wfb = load_w(cx, wsl, FT_FB)
        if cx.kind == "s":
            with nc.allow_non_contiguous_dma("logf cache transpose load"):
                P.dma(SP, Aa[0:6, 0:past], cfl[l, si].rearrange("t h -> h t"), writes=[Aa.b])
        for blk in range(cx.NBLK):
            bank = next_bank()
            project(cx, wfb, 6, blk, bank)
            cols = slice(past + blk * BW, past + (blk + 1) * BW)
            act(Aa[0:6, cols], bank[0:6, 0:BW], AF.Exp, [bank.b, nfb_t.b], [Aa.b], bias=nfb_t[0:6, l:l + 1], scale=-1.0)
            act(Aa[0:6, cols], Aa[0:6, cols], AF.Ln, [Aa.b], [Aa.b], bias=1.0)
            ts(DVE, Aa[0:6, cols], Aa[0:6, cols], -1.0, None, ALU.mult, None, [Aa.b], [Aa.b])
        P.op(DVE, lambda e: e.tensor_tensor_scan(out=Sa[0:6, 0:NK], data0=onecol[0:6, 0:1].to_broadcast([6, NK]),
                                                 data1=Aa[0:6, 0:NK], initial=0.0, op0=ALU.mult, op1=ALU.subtract),
             reads=[Aa.b, onecol.b], writes=[Sa.b])
        cp(DVE, SPL[0:6, 0, :], Sa[0:6, 0:NK], [Sa.b], [SPL.b])
        tt(DVE, Aa[0:6, 0:NK], Sa[0:6, 0:NK], SPL[0:6, 0, :], ALU.subtract, [Sa.b, SPL.b], [Aa.b])
        cp(DVE, SPL[0:6, 1, :], Aa[0:6, 0:NK], [Aa.b], [SPL.b])
        tt(DVE, Aa[0:6, 0:NK], Aa[0:6, 0:NK], SPL[0:6, 1, :], ALU.subtract, [Aa.b, SPL.b], [Aa.b])
        cp(DVE, SPL[0:6, 2, :], Aa[0:6, 0:NK], [Aa.b], [SPL.b])

        DBGF = int(os.environ.get("DBG_F", "9"))
        if DBGF < 1:
            return
        for hp in range(3):
            P.tag = "F.proj"
            wq = load_w(cx, wsl, FT_QB + hp)
            wk = load_w(cx, wsl, FT_KB + hp)
            wg = load_w(cx, wsl, FT_GB + hp)
            for hh in range(2):
                h = 2 * hp + hh
                memset(POOL, Qa[hh][64:70, 0:T], 1.0, [Qa[hh].b])
                memset(POOL, Ka[hh][64:70, 0:NK], -1.0, [Ka[hh].b])
                for r in range(3):
                    P.dma(SP, Qa[hh][64 + r:65 + r, 0:T], SPL[h:h + 1, r, past:NK], reads=[SPL.b], writes=[Qa[hh].b])
                    P.dma(SP, Ka[hh][67 + r:68 + r, 0:NK], SPL[h:h + 1, r, 0:NK], reads=[SPL.b], writes=[Ka[hh].b])
            if cx.kind == "s":
                P.dma(SP, kst[:], cfk[l, si].rearrange("(t p) c -> p t c", p=128)[:, :, hp * 128:(hp + 1) * 128], writes=[kst.b])
                for hh in range(2):
                    for t4 in range(4):
                        bank = next_bank()
                        for t in range(4):
                            tr(bank[0:64, t * 128:(t + 1) * 128], kst[:, 4 * t4 + t, hh * 64:(hh + 1) * 64], C["identf"][:, :],
                               [kst.b, C["identf"].b], [bank.b])
                        cp(DVE if t4 % 2 else ACT, Ka[hh][0:64, t4 * 512:(t4 + 1) * 512], bank[0:64, 0:512], [bank.b], [Ka[hh].b])
            for blk in range(cx.NBLK):
                cols = slice(blk * BW, (blk + 1) * BW)
                kcols = slice(past + blk * BW, past + (blk + 1) * BW)
                bq = next_bank()
                project(cx, wq, 128, blk, bq)
                act(Qa[0][0:64, cols], bq[0:64, 0:BW], AF.Copy, [bq.b], [Qa[0].b], scale=0.125)
                ts(DVE, Qa[1][0:64, cols], bq[64:128, 0:BW], 0.125, None, ALU.mult, None, [bq.b], [Qa[1].b])
                bk_ = next_bank()
                project(cx, wk, 128, blk, bk_)
                cp(ACT, Ka[0][0:64, kcols], bk_[0:64, 0:BW], [bk_.b], [Ka[0].b])
                cp(DVE, Ka[1][0:64, kcols], bk_[64:128, 0:BW], [bk_.b], [Ka[1].b])
                bg = next_bank()
                project(cx, wg, 128, blk, bg)
                act(gate[:, cols], bg[:, 0:BW], AF.Silu, [bg.b], [gate.b])
            P.tag = "F.attn"
            nfull = NK // 128
            for hh_ in range(2):
                vcols = slice((2 * hp + hh_) * 64, (2 * hp + hh_ + 1) * 64)
                acols = slice(64 * hh_, 64 * hh_ + 64)
                cp(POOL, Vaug[:, 0:nfull, hh_, acols], Vb[:, 0:nfull, vcols], [Vb.b], [Vaug.b])
                if NK % 128:
                    cp(POOL, Vaug[0:NK % 128, nfull, hh_, acols], Vb[0:NK % 128, nfull, vcols], [Vb.b], [Vaug.b])
            heat["n"], heat["bank"] = HEAT_F, psum[0]
            for hh in range(2):
                if DBGF < 2:
                    break
                h = 2 * hp + hh
                ob = 64 * hh
                db = 64 - ob
                for qt in range(T // QW):
                    q0 = qt * QW
                    qlo = past + q0
                    qhi = qlo + QW - 1
                    kts = [kt for kt in range(cx.NKT) if kt * 128 <= qhi]
                    Ob = psum[4 + (2 * hh + qt) % 4]

                    def stage2(kt, kn, pt, first, last):
                        mm(Ob[:, 0:QW], Vaug[0:kn, kt, hh, :], pt[0:kn, 0:QW], first, last, [Vaug.b, pt.b], [Ob.b])
                    pend = None
                    for i, kt in enumerate(kts):
                        kn = min(128, NK - kt * 128)
                        Sb = psum[i % 4]
                        partial = kt * 128 + kn - 1 > qlo
                        mm(Sb[0:kn, 0:QW], Ka[hh][0:70, kt * 128:kt * 128 + kn], Qa[hh][0:70, q0:q0 + QW], True, not partial,
                           [Ka[hh].b, Qa[hh].b], [Sb.b])
                        if partial:
                            d = kt * 128 - qlo
                            mm(Sb[0:kn, 0:QW], identb[0:kn, 0:kn], negleb[0:kn, 384 - d:384 - d + QW], False, True,
                               [identb.b, negleb.b], [Sb.b])
                        pt = pts[i % 4]
                        act(pt[0:kn, 0:QW], Sb[0:kn, 0:QW], AF.Exp, [Sb.b], [pt.b])
                        if pend:
                            stage2(*pend)
                        pend = (kt, kn, pt, i == 0, i == len(kts) - 1)
                    stage2(*pend)
                    rd, oo = rD[qt % 2], o1[qt % 2]
                    P.op(DVE, lambda e, o=rd[ob:ob + 64, 0:QW], i_=Ob[db:db + 64, 0:QW]: e.reciprocal(out=o, in_=i_),
                         reads=[Ob.b], writes=[rd.b])
                    tt(DVE, oo[ob:ob + 64, 0:QW], Ob[ob:ob + 64, 0:QW], rd[ob:ob + 64, 0:QW], ALU.mult, [Ob.b, rd.b], [oo.b])
                    tt(POOL, oT[ob:ob + 64, 3 + h // 2, q0:q0 + QW], oo[ob:ob + 64, 0:QW], gate[ob:ob + 64, q0:q0 + QW], ALU.mult,
                       [oo.b, gate.b], [oTb[3 + h // 2]])
            heat["n"] = 0

    def phase_S(cx):
        P.tag = "S.pre"
        l, si, T, NK, BW, QW, past, O = cx.l, cx.si, cx.T, cx.NK, cx.BW, cx.BW, cx.past, cx.O
        mem.top = phase_base
        P.barrier()
        wsl = [mem.alloc(f"wf{i}", (128, 8, 128), BF16) for i in range(4)]
        Qc = [mem.alloc(f"Qc{i}", (128, T), BF16) for i in range(2)]
        Kc = [mem.alloc(f"Kc{i}", (128, NK), BF16) for i in range(2)]
        gate = mem.alloc("gate", (128, T), BF16)
        Ef = [mem.alloc(f"Ef{i}", (128, QW), F32) for i in range(3)]
        Xe = [mem.alloc(f"Xe{i}", (128, QW), BF16) for i in range(3)]
        spb = [mem.alloc(f"spb{i}", (128, QW), BF16) for i in range(3)]
        Lsb = [mem.alloc(f"Lsb{i}", (128, QW), BF16) for i in range(3)]
        Wt = [mem.alloc(f"Wt{i}", (128, QW), BF16) for i in range(3)]
        Lsum = [mem.alloc(f"Lsum{i}", (128, QW), F32) for i in range(2)]
        kst = mem.alloc("kst", (128, 16, 128), F32) if cx.kind == "s" else None
        for sp in range(2):
            P.tag = "S.proj"
            wq = load_w(cx, wsl, FT_QC + sp)
            wk = load_w(cx, wsl, FT_KC + sp)
            wg = load_w(cx, wsl, FT_GC + sp)
            if cx.kind == "s":
                P.dma(SP, kst[:], csk[l, si].rearrange("(t p) c -> p t c", p=128)[:, :, sp * 128:(sp + 1) * 128], writes=[kst.b])
                for hh in range(2):
                    for t4 in range(4):
                        bank = next_bank()
                        for t in range(4):
                            tr(bank[0:64, t * 128:(t + 1) * 128], kst[:, 4 * t4 + t, hh * 64:(hh + 1) * 64], C["identf"][:, :],
                               [kst.b, C["identf"].b], [bank.b])
                        cp(DVE if t4 % 2 else ACT, Kc[hh][0:64, t4 * 512:(t4 + 1) * 512], bank[0:64, 0:512], [bank.b], [Kc[hh].b])
            for blk in range(cx.NBLK):
                cols = slice(blk * BW, (blk + 1) * BW)
                kcols = slice(past + blk * BW, past + (blk + 1) * BW)
                bq = next_bank()
                project(cx, wq, 128, blk, bq)
                act(Qc[0][0:64, cols], bq[0:64, 0:BW], AF.Copy, [bq.b], [Qc[0].b], scale=0.125)
                ts(DVE, Qc[1][0:64, cols], bq[64:128, 0:BW], 0.125, None, ALU.mult, None, [bq.b], [Qc[1].b])
                bk_ = next_bank()
                project(cx, wk, 128, blk, bk_)
                cp(ACT, Kc[0][0:64, kcols], bk_[0:64, 0:BW], [bk_.b], [Kc[0].b])
                cp(DVE, Kc[1][0:64, kcols], bk_[64:128, 0:BW], [bk_.b], [Kc[1].b])
                bg = next_bank()
                project(cx, wg, 128, blk, bg)
                act(gate[:, cols], bg[:, 0:BW], AF.Silu, [bg.b], [gate.b])
            P.tag = "S.attn"
            heat["n"], heat["bank"] = HEAT_S, psum[0]
            for hh in range(2):
                hc = 2 * sp + hh
                ob = 64 * hh
                for qt in range(T // QW):
                    q0 = qt * QW
                    qlo = past + q0
                    qhi = qlo + QW - 1
                    kts = list(reversed([kt for kt in range(cx.NKT) if kt * 128 <= qhi]))
                    n = len(kts)
                    Ob = psum[6 + qt % 2]
                    memset(POOL, Lsum[0][:, :], 0.0, [Lsum[0].b])
                    memset(POOL, Lsum[1][:, :], 0.0, [Lsum[1].b])

                    def info(i):
                        kt = kts[i]
                        return kt, min(128, NK - kt * 128), psum[2 + i % 2]

                    def stage1(i):
                        kt, kn, Zb = info(i)
                        partial = kt * 128 + kn - 1 >= qlo
                        mm(Zb[0:kn, 0:QW], Kc[hh][0:64, kt * 128:kt * 128 + kn], Qc[hh][0:64, q0:q0 + QW], True, not partial,
                           [Kc[hh].b, Qc[hh].b], [Zb.b])
                        if partial:
                            d = kt * 128 - qlo
                            mm(Zb[0:kn, 0:QW], identb[0:kn, 0:kn], negltb[0:kn, 384 - d:384 - d + QW], False, True,
                               [identb.b, negltb.b], [Zb.b])
                        j = i % 3
                        act(Ef[j][0:kn, :], Zb[0:kn, 0:QW], AF.Exp, [Zb.b], [Ef[j].b])
                        act(spb[j][0:kn, :], Ef[j][0:kn, :], AF.Ln, [Ef[j].b], [spb[j].b], bias=1.0)
                        La, Lb = Lsum[i % 2], Lsum[(i + 1) % 2]
                        if i + 1 < n:
                            jn = (i + 1) % 3
                            if kn < 128:
                                memset(POOL, Lsb[jn][kn:128, :], 0.0, [Lsb[jn].b])
                            tt(DVE, Lsb[jn][0:kn, :], La[0:kn, :], spb[j][0:kn, :], ALU.add, [La.b, spb[j].b], [Lsb[jn].b])
                            tt(DVE, Lb[0:kn, :], La[0:kn, :], spb[j][0:kn, :], ALU.add, [La.b, spb[j].b], [Lb.b])

                    def stage2(i):
                        kt, kn, _ = info(i)
                        Zb = psum[4 + i % 2]
                        j = i % 3
                        mm(Zb[0:kn, 0:QW], trinegb[0:kn, 0:kn], spb[j][0:kn, :], True, i == 0, [trinegb.b, spb[j].b], [Zb.b])
                        if i > 0:
                            mm(Zb[0:kn, 0:QW], onesneg[:, 0:kn], Lsb[j][:, :], False, True, [onesneg.b, Lsb[j].b], [Zb.b])
                        act(Xe[j][0:kn, :], Zb[0:kn, 0:QW], AF.Exp, [Zb.b], [Xe[j].b])
                        tt(DVE, Wt[j][0:kn, :], Ef[j][0:kn, :], Xe[j][0:kn, :], ALU.mult, [Ef[j].b, Xe[j].b], [Wt[j].b])

                    def stage3(i):
                        kt, kn, Zb = info(i)
                        j = i % 3
                        mm(Ob[ob:ob + 64, 0:QW], Vc[0:kn, kt, hc * 64:(hc + 1) * 64], Wt[j][0:kn, :], i == 0, i == n - 1,
                           [Vc.b, Wt[j].b], [Ob.b])
                    for s_ in range(n + 2):
                        if s_ < n:
                            stage1(s_)
                        if 0 <= s_ - 1 < n:
                            stage2(s_ - 1)
                        if 0 <= s_ - 2 < n:
                            stage3(s_ - 2)
                    tt(DVE, oT[ob:ob + 64, 6 + hc // 2, q0:q0 + QW], Ob[ob:ob + 64, 0:QW], gate[ob:ob + 64, q0:q0 + QW], ALU.mult,
                       [Ob.b, gate.b], [oTb[6 + hc // 2]])
            heat["n"] = 0

    def phase_E(cx):
        P.tag = "E"
        l, si, TT, O = cx.l, cx.si, cx.TT, cx.O
        mem.top = phase_base
        P.barrier()
        wo = mem.alloc("wo", (128, 8, 1024), BF16)
        lng = mem.alloc("lng", (128, D), F32)
        lnb = mem.alloc("lnb", (128, D), F32)
        xres = [mem.alloc(f"xres{i}", (128, D), F32) for i in range(2)]
        Rr = [mem.alloc(f"Rr{i}", (128, D), F32) for i in range(2)]
        yv = [mem.alloc(f"yv{i}", (128, D), F32) for i in range(2)]
        st = mem.alloc("bnst", (128, 12), F32)
        mv = mem.alloc("bnmv", (128, 2), F32)
        rs = mem.alloc("bnrs", (128, 1), F32)
        nb = mem.alloc("bnnb", (128, 1), F32)
        P.dma(SP, wo[:], WO[l], reads=[bWO[l]], writes=[wo.b])
        P.dma(SP, lng[:], ln_g[l].partition_broadcast(128), writes=[lng.b])
        P.dma(SP, lnb[:], ln_b[l].partition_broadcast(128), writes=[lnb.b])
        for tt_ in range(cx.NTT):
            rows = slice(tt_ * TT, (tt_ + 1) * TT)
            xr, R_, y_ = xres[tt_ % 2], Rr[tt_ % 2], yv[tt_ % 2]
            if l == 0:
                P.dma(SP, xr[:TT, :], cx.xsrc[si, rows, :], writes=[xr.b])
            else:
                P.dma(SP, xr[:TT, :], Y0[cx.yidx, rows, :], reads=[bY0[cx.yidx][tt_]], writes=[xr.b])
            bA, bB = psum[2 * (tt_ % 2)], psum[2 * (tt_ % 2) + 1]
            for c in range(8):
                mm(bA[:TT, 0:512], oT[:, c, rows], wo[:, c, 0:512], c == 0, c == 7, [oTb[c], wo.b], [bA.b])
            for c in range(8):
                mm(bB[:TT, 0:512], oT[:, c, rows], wo[:, c, 512:1024], c == 0, c == 7, [oTb[c], wo.b], [bB.b])
            stt(R_[:TT, 0:512], xr[:TT, 0:512], ALPHA, bA[:TT, 0:512], ALU.mult, ALU.add, [xr.b, bA.b], [R_.b])
            stt(R_[:TT, 512:1024], xr[:TT, 512:1024], ALPHA, bB[:TT, 0:512], ALU.mult, ALU.add, [xr.b, bB.b], [R_.b])
            P.op(DVE, lambda e, o=st[:TT, 0:6], i_=R_[:TT, 0:512]: e.bn_stats(out=o, in_=i_), reads=[R_.b], writes=[st.b])
            P.op(DVE, lambda e, o=st[:TT, 6:12], i_=R_[:TT, 512:1024]: e.bn_stats(out=o, in_=i_), reads=[R_.b], writes=[st.b])
            P.op(DVE, lambda e, o=mv[:TT, 0:2], i_=st[:TT, 0:12]: e.bn_aggr(out=o, in_=i_), reads=[st.b], writes=[mv.b])
            act(rs[:TT, :], mv[:TT, 1:2], AF.Ln, [mv.b], [rs.b], bias=LN_EPS)
            act(rs[:TT, :], rs[:TT, :], AF.Exp, [rs.b], [rs.b], scale=-0.5)
            stt(nb[:TT, :], mv[:TT, 0:1], -1.0, rs[:TT, 0:1], ALU.mult, ALU.mult, [mv.b, rs.b], [nb.b])
            act(y_[:TT, :], R_[:TT, :], AF.Identity, [R_.b, rs.b, nb.b], [y_.b], bias=nb[:TT, 0:1], scale=rs[:TT, 0:1])
            tt(DVE, y_[:TT, :], y_[:TT, :], lng[:TT, :], ALU.mult, [y_.b, lng.b], [y_.b])
            tt(POOL, y_[:TT, :], y_[:TT, :], lnb[:TT, :], ALU.add, [y_.b, lnb.b], [y_.b])
            if l == 0:
                P.dma(POOL, Y0[cx.yidx, rows, :], y_[:TT, :], reads=[y_.b], writes=[bY0[cx.yidx][tt_]])
                bk = (psum[4], psum[5]) if tt_ % 2 == 0 else (psum[6], psum[7])
                to_xT(cx, y_, tt_, bk)
            else:
                P.dma(POOL, O["y"][si, rows, :], y_[:TT, :], reads=[y_.b])

    def phase_R(cx):
        P.tag = "R.init"
        l, si, T, BW, TT, past, O = cx.l, cx.si, cx.T, cx.BW, cx.TT, cx.past, cx.O
        NCH = TT // 64
        mem.top = phase_base
        P.barrier()
        NT = BW // TT
        NU = 2 * NT
        NCHK = NT * NCH
        wsl = [mem.alloc(f"wf{i}", (128, 8, 128), BF16) for i in range(3)]
        U = [mem.alloc(f"U{i}", (128, BW + 1), F32) for i in range(3)]
        Ul = mem.alloc("Ul", (128, BW + 1), F32)
        Dt = mem.alloc("Dt", (128, BW), F32)
        tw = mem.alloc("tw", (128, BW), BF16)
        gaT = mem.alloc("gaT", (128, BW), BF16)
        fnames = "lgc lgx ex eneg ld av tmp esfx kk kkn k2 bb epos Rt bonus Y".split()
        f = {}
        foff = {}
        for n_ in fnames:
            foff[n_] = mem.top
            f[n_] = mem.alloc(n_, (128, BW), F32)
        b = {n: mem.alloc(n, (128, BW), BF16) for n in "kk2 Rtb KKt Kh Bh Kg Bg Vbf rkb Ybf Ysq".split()}
        def alias(name, shape, dt, off):
            mem.n += 1
            return nc.alloc_sbuf_tensor_at(f"{name}_{mem.n}", list(shape), dt, offset=off)
        if BW == 512:
            MK = alias("MK", (128, 8, 512), BF16, foff["lgc"])
            MKb = [f[("lgc", "lgx", "ex", "eneg")[u // 2]].b for u in range(8)]
            MM = [alias("MM0", (128, 8, 256), BF16, foff["ld"]), alias("MM1", (128, 8, 256), BF16, foff["tmp"])]
            MMb = [[f[("ld", "av")[g // 2]].b for g in range(4)], [f[("tmp", "esfx")[g // 2]].b for g in range(4)]]
        else:
            MKt = mem.alloc("MK", (128, 8, 512), BF16)
            MK, MKb = MKt.h, [MKt.b] * 8
            MMt = [mem.alloc(f"MM{i}", (128, 8, 256), BF16) for i in range(2)]
            MM, MMb = [t_.h for t_ in MMt], [[t_.b] * 4 for t_ in MMt]
        QcT, McTt, D1sb, Y0sb = f["kk"], f["kkn"], f["k2"], f["bb"]
        TOK = mem.alloc("TOK", (128, 4, 4, 128), BF16)
        Pt = [mem.alloc(f"Pt{i}", (128, 8, 128), BF16) for i in range(2)]
        Ptb = [[Buf(f"Ptb{i}{g}") for g in range(2)] for i in range(2)]
        ArbT = mem.alloc("ArbT", (128, 2, 4, 128), BF16)
        MKraw = [mem.alloc(f"MKraw{i}", (128, 512), BF16) for i in range(2)]
        W1b = mem.alloc("W1b", (128, 8, 64), BF16)
        UW = mem.alloc("UW", (128, 8, 128), BF16)
        UWm = [mem.alloc(f"UWm{i}", (128, 8, 128), BF16) for i in range(2)]
        Vm = [mem.alloc(f"Vm{i}", (128, 4, 128), BF16) for i in range(2)]
        wst = mem.alloc("wst", (128, 6, 64), F32)
        gidx = [[0, 0] for _ in range(3)]

        if cx.kind == "p":
            memset(POOL, ucarry[:, :], 0.0, ucb)
            memset(POOL, Gst[0][:, :, :], 0.0, [Gb[0][c3][hh] for c3 in range(3) for hh in range(2)])
            memset(POOL, Gst[1][:, :, :], 0.0, [Gb[1][c3][hh] for c3 in range(3) for hh in range(2)])
        else:
            with nc.allow_non_contiguous_dma("state_shift transpose load"):
                P.dma(SP, ucarry[:, :], sshift[l, si].rearrange("(t p) -> p t", p=128), writes=ucb)
            memset(POOL, Gst[1][:, :, :], 0.0, [Gb[1][c3][hh] for c3 in range(3) for hh in range(2)])
            P.dma(SP, wst[0:64, :, :], swkv[l, si].rearrange("h v k -> v h k"), writes=[wst.b])
            for c3 in range(3):
                for hh in range(2):
                    hb = 64 * hh
                    mm(psum[2][hb:hb + 64, 256 + hh * 64:256 + (hh + 1) * 64], wst[0:64, 2 * c3 + hh, :], C["identf"][0:64, 0:64],
                       True, True, [wst.b, C["identf"].b], [psum[2].b])
                    cp(DVE, Gst[0][hb:hb + 64, c3, :], psum[2][hb:hb + 64, 256 + hh * 64:256 + (hh + 1) * 64], [psum[2].b], [Gb[0][c3][hh]])

        def uproc(Ut, bank, ct, last_blk):
            cp(ACT, Ut[:, 1:BW + 1], bank[:, 0:BW], [bank.b], [Ut.b])
            cp(ACT, Ut[:, 0:1], ucarry[:, ct:ct + 1], [ucb[ct]], [Ut.b])
            cp(ACT, ucarry[:, ct:ct + 1], Ut[:, BW:BW + 1], [Ut.b], [ucb[ct]])
            if last_blk:
                with nc.allow_non_contiguous_dma("shift state store"):
                    P.dma(POOL, O["sh"][l, si, ct * 128:(ct + 1) * 128].rearrange("(p o) -> p o", o=1), Ut[:, BW:BW + 1], reads=[Ut.b])
            act(Dt[:, :], Ut[:, 0:BW], AF.Copy, [Ut.b, mu_t.b], [Dt.b], scale=mu_t[:, l, ct:ct + 1])
            stt(Ut[:, 1:BW + 1], Ut[:, 1:BW + 1], omm_t[:, l, ct:ct + 1], Dt[:, :], ALU.mult, ALU.add, [Dt.b, omm_t.b, Ut.b], [Ut.b])

        ABANKS = (2, 4)

        def prep_A(blk, c3):
            last_blk = blk == cx.NBLK - 1
            if c3 == 0:
                P.tag = "R.lora"
                w9 = load_w(cx, wsl, 9)
                bank = next_bank(*ABANKS)
                project(cx, w9, 128, blk, bank)
                uproc(Ul, bank, 9, last_blk)
                act(tw[0:64, :], Ul[0:64, 1:BW + 1], AF.Tanh, [Ul.b], [tw.b])
                cp(DVE, tw[64:128, :], Ul[64:128, 1:BW + 1], [Ul.b], [tw.b])
                yield
            P.tag = "R.prepA"
            for j, ct in enumerate((c3, 3 + c3, 6 + c3)):
                ws = load_w(cx, wsl, ct)
                bank = next_bank(*ABANKS)
                project(cx, ws, 128, blk, bank)
                uproc(U[j], bank, ct, last_blk)
                yield
            cs = slice(c3 * 128, (c3 + 1) * 128)
            bank = next_bank(*ABANKS)
            mm(bank[:, 0:BW], lw[0:64, l, cs], tw[0:64, :], True, True, [lw.b, tw.b], [bank.b])
            act(f["ld"][:, :], bank[:, 0:BW], AF.Sigmoid, [bank.b, w0_t.b], [f["ld"].b], bias=w0_t[:, l, c3:c3 + 1])
            yield
            bank = next_bank(*ABANKS)
            mm(bank[:, 0:BW], lw[64:128, l, cs], tw[64:128, :], True, True, [lw.b, tw.b], [bank.b])
            act(f["av"][:, :], bank[:, 0:BW], AF.Sigmoid, [bank.b, a0_t.b], [f["av"].b], bias=a0_t[:, l, c3:c3 + 1])
            act(f["ld"][:, :], f["ld"][:, :], AF.Copy, [f["ld"].b], [f["ld"].b], scale=DEC_SCALE)
            yield
            P.op(DVE, lambda e: e.tensor_tensor_scan(out=f["lgc"][:, :], data0=C["chunkmask"][:, 0:BW], data1=f["ld"][:, :],
                                                     initial=0.0, op0=ALU.mult, op1=ALU.add),
                 reads=[C["chunkmask"].b, f["ld"].b], writes=[f["lgc"].b])
            tt(DVE, f["lgx"][:, :], f["lgc"][:, :], f["ld"][:, :], ALU.subtract, [f["lgc"].b, f["ld"].b], [f["lgx"].b])
            yield
            act(f["epos"][:, :], f["lgc"][:, :], AF.Exp, [f["lgc"].b], [f["epos"].b])
            act(f["ex"][:, :], f["lgx"][:, :], AF.Exp, [f["lgx"].b], [f["ex"].b])
            act(f["eneg"][:, :], f["lgc"][:, :], AF.Exp, [f["lgc"].b], [f["eneg"].b], scale=-1.0)
            yield
            lg3 = f["lgc"][:, :].rearrange("p (c n) -> p c n", n=64)
            tt(DVE, f["esfx"][:, :].rearrange("p (c n) -> p c n", n=64), lg3[:, :, 63:64].to_broadcast([128, BW // 64, 64]), lg3,
               ALU.subtract, [f["lgc"].b], [f["esfx"].b])
            act(f["esfx"][:, :], f["esfx"][:, :], AF.Exp, [f["esfx"].b], [f["esfx"].b])
            yield

        def advance(g, n):
            if g is None:
                return
            tag0 = P.tag
            for _ in range(n):
                try:
                    next(g)
                except StopIteration:
                    break
            P.tag = tag0

        def drain(g):
            advance(g, 10 ** 6)

        its = [(blk_, c3_) for blk_ in range(cx.NBLK) for c3_ in range(3)]
        drain(prep_A(0, 0))
        for blk in range(cx.NBLK):
            for c3 in range(3):
                idx_it = blk * 3 + c3
                nxtA = prep_A(*its[idx_it + 1]) if idx_it + 1 < len(its) else None
                P.tag = "R.prep"
                ws = load_w(cx, wsl, FT_GA + c3)
                bank = next_bank()
                project(cx, ws, 128, blk, bank)
                act(gaT[:, :], bank[:, 0:BW], AF.Silu, [bank.b], [gaT.b])
                r_, k_, v_ = U[0][:, 1:BW + 1], U[1][:, 1:BW + 1], U[2][:, 1:BW + 1]
                rb, kb_, vb_ = U[0].b, U[1].b, U[2].b
                ts(DVE, f["kk"][:, :], k_, kk_t[:, l, c3:c3 + 1], None, ALU.mult, None, [kb_, kk_t.b], [f["kk"].b])
                act(b["kk2"][:, :], f["kk"][:, :], AF.Square, [f["kk"].b], [b["kk2"].b])
                bank = next_bank()
                mm(bank[:, 0:BW], bonesb[:, :], b["kk2"][:, :], True, True, [bonesb.b, b["kk2"].b], [bank.b])
                act(f["tmp"][:, :], bank[:, 0:BW], AF.Ln, [bank.b], [f["tmp"].b], bias=1e-12)
                act(f["tmp"][:, :], f["tmp"][:, :], AF.Exp, [f["tmp"].b], [f["tmp"].b], scale=-0.5)
                tt(DVE, f["kkn"][:, :], f["kk"][:, :], f["tmp"][:, :], ALU.mult, [f["kk"].b, f["tmp"].b], [f["kkn"].b])
                ts(DVE, f["k2"][:, :], f["av"][:, :], ka_t[:, l, c3:c3 + 1], omka_t[:, l, c3:c3 + 1], ALU.mult, ALU.add,
                   [f["av"].b, ka_t.b, omka_t.b], [f["k2"].b])
                tt(DVE, f["k2"][:, :], f["k2"][:, :], k_, ALU.mult, [f["k2"].b, kb_], [f["k2"].b])
                tt(DVE, f["bb"][:, :], f["kkn"][:, :], f["av"][:, :], ALU.mult, [f["kkn"].b, f["av"].b], [f["bb"].b])
                tt(DVE, f["tmp"][:, :], r_, f["k2"][:, :], ALU.mult, [rb, f["k2"].b], [f["tmp"].b])
                ts(DVE, b["rkb"][:, :], f["tmp"][:, :], rk_t[:, l, c3:c3 + 1], None, ALU.mult, None, [f["tmp"].b, rk_t.b], [b["rkb"].b])
                bank = next_bank()
                mm(bank[:, 0:BW], bonesb[:, :], b["rkb"][:, :], True, True, [bonesb.b, b["rkb"].b], [bank.b])
                tt(DVE, f["bonus"][:, :], bank[:, 0:BW], v_, ALU.mult, [bank.b, vb_], [f["bonus"].b])
                tt(DVE, f["Rt"][:, :], r_, f["epos"][:, :], ALU.mult, [rb, f["epos"].b], [f["Rt"].b])
                cp(ACT, b["Rtb"][:, :], f["Rt"][:, :], [f["Rt"].b], [b["Rtb"].b])
                tt(DVE, b["KKt"][:, :], f["kkn"][:, :], f["ex"][:, :], ALU.mult, [f["kkn"].b, f["ex"].b], [b["KKt"].b])
                tt(DVE, b["Kh"][:, :], f["k2"][:, :], f["eneg"][:, :], ALU.mult, [f["k2"].b, f["eneg"].b], [b["Kh"].b])
                tt(DVE, b["Bh"][:, :], f["bb"][:, :], f["eneg"][:, :], ALU.mult, [f["bb"].b, f["eneg"].b], [b["Bh"].b])
                tt(DVE, b["Kg"][:, :], f["k2"][:, :], f["esfx"][:, :], ALU.mult, [f["k2"].b, f["esfx"].b], [b["Kg"].b])
                tt(POOL, b["Bg"][:, :], f["bb"][:, :], f["esfx"][:, :], ALU.mult, [f["bb"].b, f["esfx"].b], [b["Bg"].b])
                cp(ACT, b["Vbf"][:, :], v_, [vb_], [b["Vbf"].b])

                v3 = lambda ap, c=128, n=TT: ap.rearrange("p (a c) -> p a c", c=c)[:, :, 0:n]
                P.tag = "R.S0"
                for tl in range(NT):
                    tc = slice(tl * TT, (tl + 1) * TT)
                    bk = psum[6 + tl % 2]
                    tb = pbf(6 + tl % 2)
                    for j, nm in enumerate(("KKt", "Kg", "Bg", "Vbf")):
                        tr(tb[0:TT, j * 128:(j + 1) * 128], b[nm][:, tc], identb[:, :], [b[nm].b, identb.b], [bk.b])
                    cp(DVE if tl % 2 else ACT, TOK[0:TT, tl, :, :], tb[0:TT, 0:512].rearrange("p (a c) -> p a c", c=128), [bk.b], [TOK.b])
                P.tag = "R.S1"
                for tl in range(NT):
                    tc = slice(tl * TT, (tl + 1) * TT)
                    for hh in range(2):
                        hs = slice(64 * hh, 64 * hh + 64)
                        bA = psum[2 * (tl % 2) + hh]
                        mm(bA[0:TT, 0:TT], b["KKt"][hs, tc], b["Bh"][hs, tc], True, True, [b["KKt"].b, b["Bh"].b], [bA.b])
                        mm(bA[0:TT, 128:128 + TT], b["Bh"][hs, tc], b["KKt"][hs, tc], True, True, [b["KKt"].b, b["Bh"].b], [bA.b])
                        mm(bA[0:TT, 256:256 + TT], b["Kh"][hs, tc], b["KKt"][hs, tc], True, True, [b["KKt"].b, b["Kh"].b], [bA.b])
                        mm(bA[0:TT, 384:384 + TT], b["Kh"][hs, tc], b["Rtb"][hs, tc], True, True, [b["Rtb"].b, b["Kh"].b], [bA.b])
                        mm(psum[4 + hh][0:TT, tl * 128:tl * 128 + TT], b["Bh"][hs, tc], b["Rtb"][hs, tc], True, True,
                           [b["Rtb"].b, b["Bh"].b], [psum[4 + hh].b])
                    for hh in range(2):
                        u = 2 * tl + hh
                        bA = psum[2 * (tl % 2) + hh]
                        if hh == 0:
                            tt(DVE, v3(MK[0:TT, u, :]), v3(bA[0:TT, :]), v3(C["rwmask"][0:TT, :]), ALU.mult, [bA.b, C["rwmask"].b], [MKb[u]])
                        else:
                            raw = MKraw[tl % 2]
                            cp(ACT, v3(raw[0:TT, :]), v3(bA[0:TT, :]), [bA.b], [raw.b])
                            tt(POOL, v3(MK[0:TT, u, :]), v3(raw[0:TT, :]), v3(C["rwmask"][0:TT, :]), ALU.mult, [raw.b, C["rwmask"].b], [MKb[u]])
                for hh in range(2):
                    tt(DVE, ArbT[0:TT, hh, 0:NT, 0:TT], v3(psum[4 + hh][0:TT, :])[:, 0:NT, :], v3(C["iumask4"][0:TT, :])[:, 0:NT, :], ALU.mult,
                       [psum[4 + hh].b, C["iumask4"].b], [ArbT.b])
                mkall = sorted(set(MKb), key=id)
                tt(POOL, Pt[0][0:TT, 0:NU, 0:TT], MK[0:TT, 0:NU, 128:128 + TT], ident8[0:TT, 0:NU, 0:TT], ALU.add,
                   mkall + [ident8.b], [Ptb[0][0], Ptb[0][1]])
                P.tag = "R.S2"
                Mcur = [(MK[0:TT, u, 0:TT], MK[0:TT, u, 128:128 + TT], MKb[u]) for u in range(NU)]
                for m in range(1, 6):
                    par = m % 2
                    for u in range(NU):
                        bk = psum[u // 2]
                        off = 256 * (u % 2)
                        Mp, Mtp, mb = Mcur[u]
                        mm(bk[0:TT, off:off + TT], Mtp, Mp, True, True, [mb], [bk.b])
                        mm(bk[0:TT, off + 128:off + 128 + TT], Mp, Mtp, True, True, [mb], [bk.b])
                    for g in range(NU // 2):
                        dst = MM[par][0:TT, 2 * g:2 * g + 2, :].rearrange("p u (a c) -> p (u a) c", c=128)[:, :, 0:TT]
                        cp(DVE if g == 3 else ACT, dst, v3(psum[g][0:TT, :]), [psum[g].b], [MMb[par][g]])
                        for u in (2 * g, 2 * g + 1):
                            Mcur[u] = (MM[par][0:TT, u, 0:TT], MM[par][0:TT, u, 128:128 + TT], MMb[par][g])
                    for u in range(NU):
                        pbk = psum[4 + u // 4]
                        po = (u % 4) * 128
                        Pp = Pt[1 - par]
                        ppb = Ptb[1 - par][u // 4]
                        mm(pbk[0:TT, po:po + TT], Mcur[u][0], Pp[0:TT, u, 0:TT], True, True, [Mcur[u][2], ppb], [pbk.b])
                    for g in range((NU + 3) // 4):
                        nu_ = min(4, NU - 4 * g)
                        tt(DVE, Pt[par][0:TT, 4 * g:4 * g + nu_, 0:TT], v3(psum[4 + g][0:TT, :])[:, 0:nu_, :],
                           Pt[1 - par][0:TT, 4 * g:4 * g + nu_, 0:TT], ALU.add, [psum[4 + g].b, Ptb[1 - par][g]], [Ptb[par][g]])
                    advance(nxtA, ADV2)
                PtF, PtFb = Pt[1], Ptb[1]
                P.tag = "R.S3-6"
                for u in range(NU):
                    tl, hh = u // 2, u % 2
                    hs = slice(64 * hh, 64 * hh + 64)
                    mm(psum[6][0:TT, u * 64:(u + 1) * 64], MK[0:TT, u, 256:256 + TT], TOK[0:TT, tl, 3, hs], True, True, [MKb[u], TOK.b], [psum[6].b])
                cp(ACT, W1b[0:TT, 0:NU, :], psum[6][0:TT, 0:NU * 64].rearrange("p (u c) -> p u c", c=64), [psum[6].b], [W1b.b])
                for u in range(NU):
                    tl, hh = u // 2, u % 2
                    hs = slice(64 * hh, 64 * hh + 64)
                    bk = psum[u // 4]
                    uo = (u % 4) * 128
                    mm(bk[0:TT, uo:uo + 64], PtF[0:TT, u, 0:TT], W1b[0:TT, u, :], True, True, [PtFb[u // 4], W1b.b], [bk.b])
                    mm(bk[0:TT, uo + 64:uo + 128], PtF[0:TT, u, 0:TT], TOK[0:TT, tl, 0, hs], True, True, [PtFb[u // 4], TOK.b], [bk.b])
                for g in range((NU + 3) // 4):
                    nu_ = min(4, NU - 4 * g)
                    tt(DVE, UW[0:TT, 4 * g:4 * g + nu_, :], v3(psum[g][0:TT, :], n=128)[:, 0:nu_, :], v3(C["signs4"][0:TT, :], n=128)[:, 0:nu_, :],
                       ALU.mult, [psum[g].b, C["signs4"].b], [UW.b])
                for cc in range(NCH):
                    ts(POOL, UWm[cc][0:TT, 0:NU, :], UW[0:TT, 0:NU, :], C["cind"][0:TT, cc:cc + 1], None, ALU.mult, None,
                       [UW.b, C["cind"].b], [UWm[cc].b])
                    ts(POOL, Vm[cc][0:TT, 0:NT, :], TOK[0:TT, 0:NT, 3, :], C["cind"][0:TT, cc:cc + 1], None, ALU.mult, None,
                       [TOK.b, C["cind"].b], [Vm[cc].b])
                for u in range(NU):
                    tl, hh = u // 2, u % 2
                    hs = slice(64 * hh, 64 * hh + 64)
                    mm(psum[2][hs, tl * 128:tl * 128 + TT], UW[0:TT, u, 64:128], ArbT[0:TT, hh, tl, 0:TT], True, True, [UW.b, ArbT.b], [psum[2].b])
                vb = lambda ap: ap.rearrange("p (t c) -> p t c", c=TT)
                tt(DVE, vb(QcT[:, 0:BW]), vb(f["Rt"][:, 0:BW]), v3(psum[2][:, :])[:, 0:NT, :], ALU.subtract, [f["Rt"].b, psum[2].b], [QcT.b])
                for u in range(NU):
                    tl, hh = u // 2, u % 2
                    hs = slice(64 * hh, 64 * hh + 64)
                    mm(psum[3][hs, tl * 128:tl * 128 + TT], TOK[0:TT, tl, 3, hs], MK[0:TT, u, 384:384 + TT], True, False, [TOK.b, MKb[u]], [psum[3].b])
                    mm(psum[3][hs, tl * 128:tl * 128 + TT], UW[0:TT, u, 0:64], ArbT[0:TT, hh, tl, 0:TT], False, True, [UW.b, ArbT.b], [psum[3].b])
                cp(ACT, vb(Y0sb[:, 0:BW]), v3(psum[3][:, :])[:, 0:NT, :], [psum[3].b], [Y0sb.b])
                P.tag = "R.S7ab"
                McT = McTt[:, :].rearrange("p (c k) -> p c k", k=64) if BW == 512 else McTt[:, 0:64].rearrange("p (c k) -> p c k", k=64)
                for u in range(NU):
                    tl, hh = u // 2, u % 2
                    hs = slice(64 * hh, 64 * hh + 64)
                    for cc in range(NCH):
                        ch = tl * NCH + cc
                        mm(psum[6][hs, ch * 64:(ch + 1) * 64], UWm[cc][0:TT, u, 64:128], TOK[0:TT, tl, 2, hs], True, True,
                           [UWm[cc].b, TOK.b], [psum[6].b])
                for ch in range(NCHK):
                    gcol = ch * 64 + 63
                    stt(McT[:, ch, :], C["ident2"][:, :], f["epos"][:, gcol:gcol + 1], psum[6][:, ch * 64:(ch + 1) * 64],
                        ALU.mult, ALU.subtract, [C["ident2"].b, f["epos"].b, psum[6].b], [McTt.b])
                for u in range(NU):
                    tl, hh = u // 2, u % 2
                    hs = slice(64 * hh, 64 * hh + 64)
                    for cc in range(NCH):
                        ch = tl * NCH + cc
                        mm(psum[7][hs, ch * 64:(ch + 1) * 64], TOK[0:TT, tl, 1, hs], Vm[cc][0:TT, tl, hs], True, False, [TOK.b, Vm[cc].b], [psum[7].b])
                        mm(psum[7][hs, ch * 64:(ch + 1) * 64], TOK[0:TT, tl, 2, hs], UWm[cc][0:TT, u, 0:64], False, True, [TOK.b, UWm[cc].b], [psum[7].b])
                cp(DVE, D1sb[:, 0:NCHK * 64], psum[7][:, 0:NCHK * 64], [psum[7].b], [D1sb.b])
                P.tag = "R.S7c"
                for ch in range(NCHK):
                    cs_ = slice(ch * 64, (ch + 1) * 64)
                    for hh in range(2):
                        hs = slice(64 * hh, 64 * hh + 64)
                        gi = gidx[c3][hh]
                        Gc, Gn = Gst[gi], Gst[1 - gi]
                        Gcb, Gnb = Gb[gi][c3][hh], Gb[1 - gi][c3][hh]
                        mm(psum[4 + hh][hs, cs_], Gc[hs, c3, :], QcT[hs, cs_], True, True, [Gcb, QcT.b], [psum[4 + hh].b])
                        mm(psum[hh][hs, cs_], McT[hs, ch, :], Gc[hs, c3, :], True, True, [McTt.b, Gcb], [psum[hh].b])
                        tt(DVE, Gn[hs, c3, :], psum[hh][hs, cs_], D1sb[hs, cs_], ALU.add, [psum[hh].b, D1sb.b], [Gnb])
                        gidx[c3][hh] = 1 - gi
                    advance(nxtA, int(os.environ.get("ADV", "0")))
                for hh in range(2):
                    hs = slice(64 * hh, 64 * hh + 64)
                    tt(DVE, f["Y"][hs, 0:BW], psum[4 + hh][hs, 0:BW], Y0sb[hs, 0:BW], ALU.add, [psum[4 + hh].b, Y0sb.b], [f["Y"].b])
                P.tag = "R.post"
                act(b["Ybf"][:, :], f["Y"][:, :], AF.Copy, [f["Y"].b], [b["Ybf"].b])
                act(b["Ysq"][:, :], f["Y"][:, :], AF.Square, [f["Y"].b], [b["Ysq"].b])
                bm = next_bank()
                mm(bm[:, 0:BW], bones64[:, :], b["Ybf"][:, :], True, True, [bones64.b, b["Ybf"].b], [bm.b])
                cp(ACT, f["tmp"][:, :], bm[:, 0:BW], [bm.b], [f["tmp"].b])
                bq = next_bank()
                mm(bq[:, 0:BW], bones64[:, :], b["Ysq"][:, :], True, True, [bones64.b, b["Ysq"].b], [bq.b])
                act(f["lgx"][:, :], bm[:, 0:BW], AF.Square, [bm.b], [f["lgx"].b])
                tt(DVE, f["lgx"][:, :], bq[:, 0:BW], f["lgx"][:, :], ALU.subtract, [bq.b, f["lgx"].b], [f["lgx"].b])
                ts(DVE, f["lgx"][:, :], f["lgx"][:, :], 0.0, None, ALU.max, None, [f["lgx"].b], [f["lgx"].b])
                act(f["lgx"][:, :], f["lgx"][:, :], AF.Ln, [f["lgx"].b], [f["lgx"].b], bias=GN_EPS)
                act(f["lgx"][:, :], f["lgx"][:, :], AF.Exp, [f["lgx"].b], [f["lgx"].b], scale=-0.5)
                tt(DVE, f["Y"][:, :], f["Y"][:, :], f["tmp"][:, :], ALU.subtract, [f["Y"].b, f["tmp"].b], [f["Y"].b])
                tt(DVE, f["Y"][:, :], f["Y"][:, :], f["lgx"][:, :], ALU.mult, [f["Y"].b, f["lgx"].b], [f["Y"].b])
                ts(DVE, f["Y"][:, :], f["Y"][:, :], lg_t[:, l, c3:c3 + 1], lb_t[:, l, c3:c3 + 1], ALU.mult, ALU.add,
                   [f["Y"].b, lg_t.b, lb_t.b], [f["Y"].b])
                tt(DVE, f["Y"][:, :], f["Y"][:, :], f["bonus"][:, :], ALU.add, [f["Y"].b, f["bonus"].b], [f["Y"].b])
                tt(POOL, oT[:, c3, blk * BW:(blk + 1) * BW], f["Y"][:, :], gaT[:, :], ALU.mult, [f["Y"].b, gaT.b], [oTb[c3]])
                drain(nxtA)

        for c3 in range(3):
            for hh in range(2):
                hb = 64 * hh
                hs = slice(hb, hb + 64)
                gi = gidx[c3][hh]
                h = 2 * c3 + hh
                fb_ = psum[3] if hh == 0 else psum[2]
                tr(fb_[0:64, 128:192], Gst[gi][hs, c3, :], C["identf"][hs, hs], [Gb[gi][c3][hh], C["identf"].b], [fb_.b])
                cp(DVE, wst[0:64, h, :], fb_[0:64, 128:192], [fb_.b], [wst.b])
        P.dma(POOL, O["wkv"][l, si].rearrange("h v k -> v h k"), wst[0:64, :, :], reads=[wst.b])

    for kind, n in (("p", NP), ("s", NS)):
        for si in range(n):
            T = T_P if kind == "p" else T_S
            past = 0 if kind == "p" else PAST
            cx = SimpleNamespace(kind=kind, si=si, T=T, past=past, NK=past + T, TT=min(128, T), NTT=T // min(128, T),
                                 BW=min(512, T), NBLK=T // min(512, T), NKT=(past + T + 127) // 128, PKT=past // 128,
                                 xsrc=(xp if kind == "p" else xs_), yidx=(si if kind == "p" else NPm + si),
                                 O={k[1:]: v for k, v in outs.items() if k[0] == kind})
            for l in range(NL):
                cx.l = l
                if l == 0:
                    phase_X(cx)
                if "t" not in PHASES:
                    phase_T(cx)
                if "R" in PHASES:
                    phase_R(cx)
                if "F" in PHASES:
                    phase_F(cx)
                if "S" in PHASES:
                    phase_S(cx)
                if dbg and "oT" in dbg_out and l == dbg.get("_layer", 0) and T == dbg["oT"][2] and si == 0:
                    P.dma(POOL, dbg_out["oT"], oT[:, :, 0:T], reads=oTb)
                if "e" not in PHASES:
                    phase_E(cx)
    P.finish()
    global LASTP
    LASTP = P
    return nc


_NC_CACHE = {}


def kernel(**inputs):
    n = 8
    NP, NS = 32 // n, 32 // n
    key = (NP, NS)
    consts = host_consts()
    in_maps = []
    f32 = lambda a: np.ascontiguousarray(np.asarray(a, dtype=np.float32))
    for c in range(n):
        ps, ss = slice(c * NP, (c + 1) * NP), slice(c * NS, (c + 1) * NS)
        m = {
            "x_prompt": f32(inputs["x_prompt"][ps]),
            "x_sample": f32(inputs["x_sample"][ss]),
            "cache_fox_k": f32(np.asarray(inputs["cache_fox_k"])[:, ss].reshape(NL, NS, PAST, 384)),
            "cache_fox_v": f32(np.asarray(inputs["cache_fox_v"])[:, ss].reshape(NL, NS, PAST, 384)),
            "cache_fox_logf": f32(np.asarray(inputs["cache_fox_logf"])[:, ss]),
            "cache_sb_k": f32(np.asarray(inputs["cache_sb_k"])[:, ss].reshape(NL, NS, PAST, 256)),
            "cache_sb_v": f32(np.asarray(inputs["cache_sb_v"])[:, ss].reshape(NL, NS, PAST, 256)),
            "state_wkv": f32(np.asarray(inputs["state_wkv"])[:, ss]),
            "state_shift": f32(np.asarray(inputs["state_shift"])[:, ss].reshape(NL, NS, SHIFT_W)),
            "r_k": f32(np.asarray(inputs["r_k"]).reshape(NL, 384)),
        }
        for k in ("w_in", "mu_shift", "w0_decay", "w_decay", "a0", "w_aaa", "k_k", "k_a", "lnx_g", "lnx_b",
                  "fox_fb", "w_out", "ln_g", "ln_b"):
            m[k] = f32(inputs[k])
        for k, v in consts.items():
            m["c_" + k] = v
        in_maps.append(m)
    nc = build_program(NP, NS)
    res = run_bass_kernel_spmd(nc, in_maps, core_ids=list(range(n)))
    R = res.results

    def cat(name, axis, shape=None):
        a = np.concatenate([np.asarray(r[name], dtype=np.float32) for r in R], axis=axis)
        return a.reshape(shape) if shape is not None else a
    B = 32
    out = (
        cat("p_y", 0), cat("s_y", 0),
        cat("p_fox_k", 1, (NL, B, T_P, 6, 64)), cat("p_fox_v", 1, (NL, B, T_P, 6, 64)), cat("p_fox_logf", 1),
        cat("p_sb_k", 1, (NL, B, T_P, 4, 64)), cat("p_sb_v", 1, (NL, B, T_P, 4, 64)),
        cat("p_wkv", 1), cat("p_shift", 1, (NL, B, 1, SHIFT_W)),
        cat("s_fox_k", 1, (NL, B, T_S, 6, 64)), cat("s_fox_v", 1, (NL, B, T_S, 6, 64)), cat("s_fox_logf", 1),
        cat("s_sb_k", 1, (NL, B, T_S, 4, 64)), cat("s_sb_v", 1, (NL, B, T_S, 4, 64)),
        cat("s_wkv", 1), cat("s_shift", 1, (NL, B, 1, SHIFT_W)),
    )
    return out
```

```python
import contextlib
import os
import numpy as np
import concourse.bass as bass
import concourse.mybir as mybir
from concourse.bass_utils import run_bass_kernel_spmd

F32 = mybir.dt.float32
BF16 = mybir.dt.bfloat16
ALU = mybir.AluOpType
AF = mybir.ActivationFunctionType

PE, ACT, DVE, POOL, SP = "tensor", "scalar", "vector", "gpsimd", "sync"
ENGS = (PE, ACT, DVE, POOL, SP)
SEM_WRAP = 30000
ANNOTATE = bool(os.environ.get("ANNOTATE"))
HEAT_S = int(os.environ.get("HEAT_S", "1"))
HEAT_F = int(os.environ.get("HEAT_F", "0"))
HEAT_R = int(os.environ.get("HEAT_R", "0"))
ADV2 = int(os.environ.get("ADV2", "0"))

D = 1024
T_P = 2048
T_S = 64
PAST = 2048
NL = 2
INC = 4230
SHIFT_W = 1280
ALPHA = (2 * NL) ** 0.25
GN_EPS = 64e-5
LN_EPS = 1e-5
NEG = -30000.0
DEC_SCALE = -float(np.exp(-0.5))

FT = ([(128 * i, 128) for i in range(10)] +
      [(1280 + 128 * i, 128) for i in range(3)] +
      [(1664 + 128 * i, 128) for i in range(3)] +
      [(2048 + 128 * i, 128) for i in range(3)] +
      [(2822 + 128 * i, 128) for i in range(3)] +
      [(3206 + 128 * i, 128) for i in range(2)] +
      [(3462 + 128 * i, 128) for i in range(2)] +
      [(3974 + 128 * i, 128) for i in range(2)] +
      [(2816, 6)])
FT_GA, FT_QB, FT_KB, FT_GB, FT_QC, FT_KC, FT_GC, FT_FB = 10, 13, 16, 19, 22, 24, 26, 28
NFT = len(FT)
TG = [(2048, 384), (2432, 390), (3462, 512)]


class Buf:
    __slots__ = ("name", "w", "r", "excl")

    def __init__(self, name, excl=False):
        self.name = name
        self.w = None
        self.r = []
        self.excl = excl


class Prog:
    def __init__(self, nc, n_dma_sems=64):
        self.nc = nc
        self.q = {e: [] for e in ENGS}
        self.nsem = 0
        self.cur = {}
        self.cnt = {}
        self.allsems = {e: [] for e in ENGS}
        for e in ENGS:
            if e != SP:
                self.cur[e] = self._newsem()
                self.cnt[e] = 0
                self.allsems[e].append(self.cur[e])
        self.dma_pool = {SP: [self._newsem() for _ in range(20)], POOL: [self._newsem() for _ in range(12)]}
        self.dma_cnt = {s: 0 for e in self.dma_pool for s in self.dma_pool[e]}
        self.dma_next = {e: 0 for e in self.dma_pool}
        self.waited = {e: {} for e in ENGS}
        self.bar = {e: [] for e in ENGS}
        self.final = {}
        self.tag = None

    def _newsem(self):
        k = self.nsem
        self.nsem += 1
        return k

    def _need(self, eng, ev, waits):
        if ev is None:
            return
        k, v = ev[0], ev[1]
        if self.waited[eng].get(k, 0) >= v:
            return
        self.waited[eng][k] = v
        waits.append((k, v))

    def barrier(self):
        evs = []
        for e in ENGS:
            if e != SP and self.cnt[e]:
                evs.append((self.cur[e], self.cnt[e], e, False))
        for s, c in self.dma_cnt.items():
            if c:
                evs.append((s, 16 * c, None, True))
        for e in ENGS:
            self.bar[e] = self.bar[e] + evs

    def op(self, eng, fn, reads=(), writes=(), is_dma=False):
        waits = []
        if self.bar[eng]:
            for ev in self.bar[eng]:
                self._need(eng, ev, waits)
            self.bar[eng] = []
        for b in reads:
            self._need(eng, b.w, waits)
            if b.excl:
                for ev in b.r:
                    if ev[2] != eng:
                        self._need(eng, ev, waits)
        for b in writes:
            w = b.w
            if w is not None and not (w[2] == eng and not w[3] and not is_dma and eng != POOL):
                self._need(eng, w, waits)
            for ev in b.r:
                if ev[2] == eng and not is_dma and not ev[3] and eng != POOL:
                    continue
                self._need(eng, ev, waits)
        if is_dma:
            pool = self.dma_pool[eng]
            s = pool[self.dma_next[eng]]
            self.dma_next[eng] = (self.dma_next[eng] + 1) % len(pool)
            prev = self.dma_cnt[s]
            if prev:
                self._need(eng, (s, 16 * prev, None, True), waits)
            self.dma_cnt[s] = prev + 1
            ev = (s, 16 * (prev + 1), eng, True)
            inc = (s, 16)
        else:
            if self.cnt[eng] >= SEM_WRAP:
                self.final[self.cur[eng]] = self.cnt[eng]
                self.cur[eng] = self._newsem()
                self.cnt[eng] = 0
            self.cnt[eng] += 1
            ev = (self.cur[eng], self.cnt[eng], eng, False)
            inc = (self.cur[eng], 1)
        m = {}
        for k, v in waits:
            m[k] = max(m.get(k, 0), v)
        self.q[eng].append((list(m.items()), fn, inc, self.tag))
        for b in reads:
            b.r.append(ev)
            if len(b.r) > 64:
                b.r = b.r[-64:] if False else b.r
        for b in writes:
            b.w = ev
            b.r = []
        return ev

    def dma(self, eng, out, in_, reads=(), writes=(), **kw):
        def fn(e, out=out, in_=in_, kw=kw):
            return e.dma_start(out=out, in_=in_, **kw)
        return self.op(eng, fn, reads, writes, is_dma=True)

    def finish(self):
        nc = self.nc
        fw = dict(self.final)
        for e in ENGS:
            if e != SP and self.cnt[e]:
                fw[self.cur[e]] = self.cnt[e]
        for s, c in self.dma_cnt.items():
            if c:
                fw[s] = 16 * c
        with contextlib.ExitStack() as es:
            sems = [es.enter_context(nc.semaphore(f"s{i}")) for i in range(self.nsem)]
            es.enter_context(nc.allow_non_contiguous_dma("small strided parameter / state transfers"))
            block = es.enter_context(nc.Block())

            def emit(engname):
                def body(e):
                    for waits, fn, inc, tag in self.q[engname]:
                        for k, v in waits:
                            e.wait_ge(sems[k], v)
                        ins = fn(e)
                        ins.then_inc(sems[inc[0]], inc[1])
                        if tag and ANNOTATE:
                            ins.annotate(tag)
                    if engname == SP:
                        for k, v in fw.items():
                            e.wait_ge(sems[k], v)
                return body
            block.tensor(emit(PE))
            block.scalar(emit(ACT))
            block.vector(emit(DVE))
            block.gpsimd(emit(POOL))
            block.sync(emit(SP))


class Tl:
    def __init__(self, h, name):
        self.h = h
        self.b = Buf(name)

    def __getitem__(self, k):
        return self.h[k]


class Mem:
    def __init__(self, nc, base, limit):
        self.nc, self.top, self.limit, self.n = nc, base, limit, 0

    def alloc(self, name, shape, dt):
        sz = int(np.prod(shape[1:])) * (4 if dt == F32 else 2)
        sz = (sz + 31) // 32 * 32
        self.n += 1
        h = self.nc.alloc_sbuf_tensor_at(f"{name}_{self.n}", list(shape), dt, offset=self.top)
        self.top += sz
        assert self.top <= self.limit, f"SBUF overflow at {name}: {self.top}"
        return Tl(h, name)


def host_consts():
    c = {}
    c["identf"] = np.eye(128, dtype=np.float32)
    bo = np.zeros((128, 128), np.float32)
    bo[:64, :64] = 1
    bo[64:, 64:] = 1
    c["bones"] = bo
    cm = np.ones((128, 512), np.float32)
    cm[:, ::64] = 0
    c["chunkmask"] = cm
    i = np.arange(128)[:, None]
    j = np.arange(128)[None, :]
    same = (i // 64) == (j // 64)
    sl = (same & (j < i)).astype(np.float32)
    su = (same & (i < j)).astype(np.float32)
    iu = (same & (i <= j)).astype(np.float32)
    c["rwmask"] = np.concatenate([-sl, -su, su, iu], axis=1)
    c["iumask"] = iu
    sg = np.ones((128, 128), np.float32)
    sg[:, :64] = -1
    c["signs"] = sg
    kl = np.arange(128)[:, None]
    cc = np.arange(896)[None, :]
    c["negle"] = np.where(kl <= cc - 384, 0.0, NEG).astype(np.float32)
    c["neglt"] = np.where(kl < cc - 384, 0.0, NEG).astype(np.float32)
    c["trineg"] = np.where(i >= j, -1.0, 0.0).astype(np.float32)
    c["iumask4"] = np.tile(iu, (1, 4))
    c["signs4"] = np.tile(sg, (1, 4))
    c["ident2"] = np.concatenate([np.eye(64, dtype=np.float32)] * 2, axis=0)
    ci = np.zeros((128, 2), np.float32)
    ci[:64, 0] = 1
    ci[64:, 1] = 1
    c["cind"] = ci
    return c


CONST_SHAPES = dict(identf=(128, 128), bones=(128, 128), chunkmask=(128, 512), rwmask=(128, 512),
                    iumask=(128, 128), signs=(128, 128), negle=(128, 896), neglt=(128, 896),
                    trineg=(128, 128), cind=(128, 2), iumask4=(128, 512), signs4=(128, 512),
                    ident2=(128, 64))


def build_program(NP, NS, dbg=None):
    nc = bass.Bass("TRN2", target_bir_lowering=False)
    P = Prog(nc)

    def din(name, shape):
        return nc.dram_tensor(name, list(shape), F32, kind="ExternalInput").ap()

    def dout(name, shape):
        return nc.dram_tensor(name, list(shape), F32, kind="ExternalOutput").ap()

    NPm, NSm = max(NP, 1), max(NS, 1)
    xp = din("x_prompt", (NPm, T_P, D))
    xs_ = din("x_sample", (NSm, T_S, D))
    cfk = din("cache_fox_k", (NL, NSm, PAST, 384))
    cfv = din("cache_fox_v", (NL, NSm, PAST, 384))
    cfl = din("cache_fox_logf", (NL, NSm, PAST, 6))
    csk = din("cache_sb_k", (NL, NSm, PAST, 256))
    csv = din("cache_sb_v", (NL, NSm, PAST, 256))
    swkv = din("state_wkv", (NL, NSm, 6, 64, 64))
    sshift = din("state_shift", (NL, NSm, SHIFT_W))
    w_in = din("w_in", (NL, D, INC))
    mu_shift = din("mu_shift", (NL, SHIFT_W))
    w0_decay = din("w0_decay", (NL, 384))
    w_decay = din("w_decay", (NL, 64, 384))
    a0 = din("a0", (NL, 384))
    w_aaa = din("w_aaa", (NL, 64, 384))
    k_k = din("k_k", (NL, 384))
    k_a = din("k_a", (NL, 384))
    r_k = din("r_k", (NL, 384))
    lnx_g = din("lnx_g", (NL, 384))
    lnx_b = din("lnx_b", (NL, 384))
    fox_fb = din("fox_fb", (NL, 6))
    w_out = din("w_out", (NL, D, D))
    ln_g = din("ln_g", (NL, D))
    ln_b = din("ln_b", (NL, D))
    cdr = {k: din("c_" + k, s) for k, s in CONST_SHAPES.items()}

    outs = {}
    for pre, n, t in (("p", NPm, T_P), ("s", NSm, T_S)):
        outs[pre + "y"] = dout(pre + "_y", (n, t, D))
        outs[pre + "fk"] = dout(pre + "_fox_k", (NL, n, t, 384))
        outs[pre + "fv"] = dout(pre + "_fox_v", (NL, n, t, 384))
        outs[pre + "fl"] = dout(pre + "_fox_logf", (NL, n, t, 6))
        outs[pre + "sk"] = dout(pre + "_sb_k", (NL, n, t, 256))
        outs[pre + "sv"] = dout(pre + "_sb_v", (NL, n, t, 256))
        outs[pre + "wkv"] = dout(pre + "_wkv", (NL, n, 6, 64, 64))
        outs[pre + "sh"] = dout(pre + "_shift", (NL, n, SHIFT_W))
    dbg_out = {}
    if dbg:
        for k, s in dbg.items():
            if not k.startswith("_"):
                dbg_out[k] = dout("dbg_" + k, s)

    WF = nc.dram_tensor("WF", [NL, NFT, 128, 8, 128], BF16).ap()
    WT = nc.dram_tensor("WT", [NL, 3, 128, 8, 512], BF16).ap()
    WO = nc.dram_tensor("WO", [NL, 128, 8, 1024], BF16).ap()
    Y0 = nc.dram_tensor("Y0", [NPm + NSm, T_P, D], F32).ap()
    bWF = [[Buf(f"WF{l}_{t}") for t in range(NFT)] for l in range(NL)]
    bWT = [[Buf(f"WT{l}_{g}") for g in range(3)] for l in range(NL)]
    bWO = [Buf(f"WO{l}") for l in range(NL)]
    bY0 = [[Buf(f"Y0_{i}_{t}") for t in range(16)] for i in range(NPm + NSm)]

    for l in range(NL):
        wv = w_in[l].rearrange("(c p) n -> p c n", p=128)
        for g, (s0, n) in enumerate(TG):
            P.dma(POOL, WT[l, g, :, :, 0:n], wv[:, :, s0:s0 + n], writes=[bWT[l][g]])
        use_order = [9, 0, 3, 6, 10, 1, 4, 7, 11, 2, 5, 8, 12, FT_FB] + list(range(13, 28))
        for t in use_order:
            s0, n = FT[t]
            P.dma(POOL, WF[l, t, :, :, 0:n], wv[:, :, s0:s0 + n], writes=[bWF[l][t]])
        wov = w_out[l].rearrange("(c p) n -> p c n", p=128)
        for hfi in range(2):
            P.dma(POOL, WO[l, :, :, hfi * 512:(hfi + 1) * 512], wov[:, :, hfi * 512:(hfi + 1) * 512],
                  writes=[bWO[l]])

    mem = Mem(nc, 16640, 228000)
    psum = [Tl(nc.alloc_psum_tensor(f"pb{i}", [128, 512], F32), f"pb{i}") for i in range(8)]
    for p_ in psum:
        p_.b.excl = True

    def pbf(i):
        return psum[i].h.ap().bitcast(BF16)

    C = {}
    for k, s in CONST_SHAPES.items():
        C[k] = mem.alloc("c_" + k, s, F32)
        P.dma(SP, C[k][:], cdr[k], writes=[C[k].b])
    identb = mem.alloc("identb", (128, 128), BF16)
    bonesb = mem.alloc("bonesb", (128, 128), BF16)
    bones64 = mem.alloc("bones64", (128, 128), BF16)
    ones64 = mem.alloc("ones64", (128, 64), BF16)
    onesneg = mem.alloc("onesneg", (128, 128), BF16)
    trinegb = mem.alloc("trinegb", (128, 128), BF16)
    negleb = mem.alloc("negleb", (128, 896), BF16)
    negltb = mem.alloc("negltb", (128, 896), BF16)
    onecol = mem.alloc("onecol", (128, 1), F32)
    ident8 = mem.alloc("ident8", (128, 8, 128), BF16)
    for u_ in range(8):
        P.op(DVE, lambda e, u_=u_: e.tensor_copy(out=ident8[:, u_, :], in_=C["identf"][:]), reads=[C["identf"].b], writes=[ident8.b])
    P.op(DVE, lambda e: e.tensor_copy(out=identb[:], in_=C["identf"][:]), reads=[C["identf"].b], writes=[identb.b])
    P.op(DVE, lambda e: e.tensor_copy(out=bonesb[:], in_=C["bones"][:]), reads=[C["bones"].b], writes=[bonesb.b])
    P.op(DVE, lambda e: e.tensor_scalar(out=bones64[:], in0=C["bones"][:], scalar1=1.0 / 64, scalar2=None, op0=ALU.mult),
         reads=[C["bones"].b], writes=[bones64.b])
    P.op(DVE, lambda e: e.memset(ones64[:], 1.0), writes=[ones64.b])
    P.op(DVE, lambda e: e.memset(onesneg[:], -1.0), writes=[onesneg.b])
    P.op(DVE, lambda e: e.memset(onecol[:], 1.0), writes=[onecol.b])
    P.op(DVE, lambda e: e.tensor_copy(out=trinegb[:], in_=C["trineg"][:]), reads=[C["trineg"].b], writes=[trinegb.b])
    P.op(DVE, lambda e: e.tensor_copy(out=negleb[:], in_=C["negle"][:]), reads=[C["negle"].b], writes=[negleb.b])
    P.op(DVE, lambda e: e.tensor_copy(out=negltb[:], in_=C["neglt"][:]), reads=[C["neglt"].b], writes=[negltb.b])

    def load_cols(name, src, ntile):
        t = mem.alloc(name, (128, NL, ntile), F32)
        with nc.allow_non_contiguous_dma("small param transpose load"):
            for l in range(NL):
                P.dma(SP, t[:, l, :], src[l].rearrange("(t p) -> p t", p=128), writes=[t.b])
        return t
    mu_t = load_cols("mu", mu_shift, 10)
    omm_t = mem.alloc("omm", (128, NL, 10), F32)
    P.op(DVE, lambda e: e.tensor_scalar(out=omm_t[:], in0=mu_t[:], scalar1=-1.0, scalar2=1.0, op0=ALU.mult, op1=ALU.add),
         reads=[mu_t.b], writes=[omm_t.b])
    w0_t = load_cols("w0", w0_decay, 3)
    a0_t = load_cols("a0", a0, 3)
    kk_t = load_cols("kk", k_k, 3)
    ka_t = load_cols("ka", k_a, 3)
    rk_t = load_cols("rk", r_k, 3)
    lg_t = load_cols("lxg", lnx_g, 3)
    lb_t = load_cols("lxb", lnx_b, 3)
    omka_t = mem.alloc("omka", (128, NL, 3), F32)
    P.op(DVE, lambda e: e.tensor_scalar(out=omka_t[:], in0=ka_t[:], scalar1=-1.0, scalar2=1.0, op0=ALU.mult, op1=ALU.add),
         reads=[ka_t.b], writes=[omka_t.b])
    nfb_t = mem.alloc("nfb", (128, NL), F32)
    with nc.allow_non_contiguous_dma("small param transpose load"):
        P.dma(SP, nfb_t[0:6, :], fox_fb.rearrange("l h -> h l"), writes=[nfb_t.b])
    P.op(DVE, lambda e: e.tensor_scalar(out=nfb_t[0:6, :], in0=nfb_t[0:6, :], scalar1=-1.0, scalar2=None, op0=ALU.mult),
         reads=[nfb_t.b], writes=[nfb_t.b])
    fbb = mem.alloc("fbb", (128, NL, 6), F32)
    P.dma(SP, fbb[:].rearrange("p l h -> p (l h)"), fox_fb.rearrange("l h -> (l h)").partition_broadcast(128), writes=[fbb.b])
    lwf = mem.alloc("lwf", (128, NL, 384), F32)
    lw = mem.alloc("lw", (128, NL, 384), BF16)
    for l in range(NL):
        P.dma(SP, lwf[0:64, l, :], w_decay[l], writes=[lwf.b])
        P.dma(SP, lwf[64:128, l, :], w_aaa[l], writes=[lwf.b])
    P.op(DVE, lambda e: e.tensor_copy(out=lw[:], in_=lwf[:]), reads=[lwf.b], writes=[lw.b])

    xT = mem.alloc("xT", (128, 8, T_P), BF16)
    oT = mem.alloc("oT", (128, 8, T_P), BF16)
    oTb = [Buf(f"oT{c}") for c in range(8)]
    Vb = mem.alloc("Vb", (128, 17, 384), BF16)
    Vc = mem.alloc("Vc", (128, 17, 256), BF16)
    Gst = [mem.alloc(f"G{i}", (128, 3, 64), F32) for i in range(2)]
    Gb = [[[Buf(f"G{i}_{c3}_{hh}") for hh in range(2)] for c3 in range(3)] for i in range(2)]
    ucarry = mem.alloc("ucarry", (128, 10), F32)
    ucb = [Buf(f"uc{ct}") for ct in range(10)]
    phase_base = mem.top
    if dbg:
        for c in range(8):
            P.op(POOL, lambda e, c=c: e.memset(oT[:, c, :], 0.0), writes=[oTb[c]])

    bank_rr = [0]

    def next_bank(lo=0, hi=2):
        i = lo + bank_rr[0] % (hi - lo)
        bank_rr[0] += 1
        return psum[i]

    wslot_rr = [0]

    from types import SimpleNamespace
    PHASES = set(dbg["_phases"]) if (dbg and "_phases" in dbg) else set("RFS")

    def dump(name, ap, reads):
        if name in dbg_out:
            P.dma(POOL, dbg_out[name], ap, reads=reads)

    def act(out, in_, func, R, W, bias=None, scale=None):
        kw = {}
        if bias is not None:
            kw["bias"] = bias
        if scale is not None:
            kw["scale"] = scale
        P.op(ACT, lambda e: e.activation(out=out, in_=in_, func=func, **kw), reads=R, writes=W)

    def tt(eng, out, in0, in1, op, R, W):
        P.op(eng, lambda e: e.tensor_tensor(out=out, in0=in0, in1=in1, op=op), reads=R, writes=W)

    def ts(eng, out, in0, s1, s2, op0, op1, R, W):
        if op1 is None and eng == POOL and op0 == ALU.mult:
            op1, s2 = ALU.mult, 1.0
        if op1 is None:
            P.op(eng, lambda e: e.tensor_scalar(out=out, in0=in0, scalar1=s1, scalar2=None, op0=op0), reads=R, writes=W)
        else:
            P.op(eng, lambda e: e.tensor_scalar(out=out, in0=in0, scalar1=s1, scalar2=s2, op0=op0, op1=op1), reads=R, writes=W)

    def stt(out, in0, scalar, in1, op0, op1, R, W):
        P.op(DVE, lambda e: e.scalar_tensor_tensor(out=out, in0=in0, scalar=scalar, in1=in1, op0=op0, op1=op1), reads=R, writes=W)

    def cp(eng, out, in_, R, W):
        if eng == ACT:
            P.op(ACT, lambda e: e.activation(out=out, in_=in_, func=AF.Copy), reads=R, writes=W)
        else:
            P.op(eng, lambda e: e.tensor_copy(out=out, in_=in_), reads=R, writes=W)

    heat = {"n": 0, "bank": None}

    def mm(out, lhsT, rhs, start, stop, R, W):
        P.op(PE, lambda e: e.matmul(out, lhsT=lhsT, rhs=rhs, start=start, stop=stop), reads=R, writes=W)
        if heat["n"] and stop:
            hb_ = heat["bank"]
            for _ in range(heat["n"]):
                P.op(PE, lambda e: e.matmul(hb_[:, 0:512], lhsT=identb[:, :], rhs=negltb[:, 0:512], start=True, stop=True),
                     reads=[identb.b, negltb.b], writes=[hb_.b])

    def tr(out, in_, ident, R, W):
        P.op(PE, lambda e: e.transpose(out=out, in_=in_, identity=ident), reads=R, writes=W)

    def memset(eng, ap, val, W):
        P.op(eng, lambda e: e.memset(ap, val), writes=W)

    def load_w(cx, slots, t):
        ws = slots[wslot_rr[0] % len(slots)]
        wslot_rr[0] += 1
        n = FT[t][1]
        P.dma(SP, ws[:, :, 0:n], WF[cx.l, t, :, :, 0:n], reads=[bWF[cx.l][t]], writes=[ws.b])
        return ws

    def project(cx, ws, ncol, blk, bank):
        cols = slice(blk * cx.BW, (blk + 1) * cx.BW)
        for c in range(8):
            mm(bank[0:ncol, 0:cx.BW], ws[:, c, 0:ncol], xT[:, c, cols], c == 0, c == 7, [ws.b, xT.b], [bank.b])

    def to_xT(cx, src, tt_, banks):
        TT = cx.TT
        for c in range(8):
            tr(banks[c // 4][:, (c % 4) * TT:(c % 4 + 1) * TT], src[:TT, c * 128:(c + 1) * 128],
               C["identf"][:TT, :TT], [src.b, C["identf"].b], [banks[c // 4].b])
        for j in range(2):
            s_ = banks[j][:, 0:4 * TT].rearrange("p (c t) -> p c t", t=TT)
            d_ = xT[:, 4 * j:4 * j + 4, tt_ * TT:(tt_ + 1) * TT]
            cp(ACT if j == 0 else DVE, d_, s_, [banks[j].b], [xT.b])

    def phase_X(cx):
        P.tag = "X"
        mem.top = phase_base
        P.barrier()
        xin = [mem.alloc(f"xin{i}", (128, D), F32) for i in range(2)]
        for tt_ in range(cx.NTT):
            sl = xin[tt_ % 2]
            P.dma(SP, sl[:cx.TT, :], cx.xsrc[cx.si, tt_ * cx.TT:(tt_ + 1) * cx.TT, :], writes=[sl.b])
            bk = (psum[0], psum[1]) if tt_ % 2 == 0 else (psum[2], psum[3])
            to_xT(cx, sl, tt_, bk)

    def phase_T(cx):
        P.tag = "T"
        l, si, TT, O = cx.l, cx.si, cx.TT, cx.O
        mem.top = phase_base
        P.barrier()
        wt = [mem.alloc(f"wt{g}", (128, 8, 512), BF16) for g in range(3)]
        stage = [mem.alloc(f"stg{i}", (128, 1288), F32) for i in range(2)]
        for g in range(3):
            P.dma(SP, wt[g][:, :, 0:TG[g][1]], WT[l, g, :, :, 0:TG[g][1]], reads=[bWT[l][g]], writes=[wt[g].b])
        if cx.kind == "s":
            P.dma(POOL, Vb[:, 0:16, :], cfv[l, si].rearrange("(t p) c -> p t c", p=128), writes=[Vb.b])
            P.dma(POOL, Vc[:, 0:16, :], csv[l, si].rearrange("(t p) c -> p t c", p=128), writes=[Vc.b])
        for tt_ in range(min(cx.NTT, int(os.environ.get("DBG_NTT", "99")))):
            st = stage[tt_ % 2]
            rows = slice(tt_ * TT, (tt_ + 1) * TT)
            bks = [psum[3 * (tt_ % 2) + g] for g in range(3)]
            for g, (s0, n) in enumerate(TG):
                for c in range(8):
                    mm(bks[g][:TT, 0:n], xT[:, c, rows], wt[g][:, c, 0:n], c == 0, c == 7, [xT.b, wt[g].b], [bks[g].b])
            cp(ACT, st[:TT, 0:384], bks[0][:TT, 0:384], [bks[0].b], [st.b])
            cp(DVE, st[:TT, 384:768], bks[1][:TT, 0:384], [bks[1].b], [st.b])
            cp(ACT, st[:TT, 768:1280], bks[2][:TT, 0:512], [bks[2].b], [st.b])
            tt(DVE, st[:TT, 1280:1286], bks[1][:TT, 384:390], fbb[:TT, l, :], ALU.add, [bks[1].b, fbb.b], [st.b])
            act(st[:TT, 1280:1286], st[:TT, 1280:1286], AF.Exp, [st.b], [st.b], scale=-1.0)
            act(st[:TT, 1280:1286], st[:TT, 1280:1286], AF.Ln, [st.b], [st.b], bias=1.0)
            ts(DVE, st[:TT, 1280:1286], st[:TT, 1280:1286], -1.0, None, ALU.mult, None, [st.b], [st.b])
            P.dma(SP, O["fl"][l, si, rows, :], st[:TT, 1280:1286], reads=[st.b])
            DBGT = int(os.environ.get("DBG_T", "9"))
            if DBGT in (2, 4, 9):
                cp(DVE, Vb[:TT, cx.PKT + tt_, :], bks[1][:TT, 0:384], [bks[1].b], [Vb.b])
            if DBGT in (2, 5, 9):
                cp(DVE, Vc[:TT, cx.PKT + tt_, :], st[:TT, 1024:1280], [st.b], [Vc.b])
            if DBGT != 9:
                continue
            P.dma(SP, O["fk"][l, si, rows, :], st[:TT, 0:384], reads=[st.b])
            P.dma(SP, O["fv"][l, si, rows, :], st[:TT, 384:768], reads=[st.b])
            P.dma(SP, O["sk"][l, si, rows, :], st[:TT, 768:1024], reads=[st.b])
            P.dma(SP, O["sv"][l, si, rows, :], st[:TT, 1024:1280], reads=[st.b])

    def phase_F(cx):
        P.tag = "F.pre"
        l, si, T, NK, BW, QW, past, O = cx.l, cx.si, cx.T, cx.NK, cx.BW, cx.BW, cx.past, cx.O
        mem.top = phase_base
        P.barrier()
        wsl = [mem.alloc(f"wf{i}", (128, 8, 128), BF16) for i in range(4)]
        Qa = [mem.alloc(f"Qa{i}", (128, T), BF16) for i in range(2)]
        Ka = [mem.alloc(f"Ka{i}", (128, NK), BF16) for i in range(2)]
        gate = mem.alloc("gate", (128, T), BF16)
        Aa = mem.alloc("Aa", (128, NK), F32)
        Sa = mem.alloc("Sa", (128, NK), F32)
        SPL = mem.alloc("SPL", (128, 3, NK), BF16)
        pts = [mem.alloc(f"pt{i}", (128, QW), BF16) for i in range(4)]
        rD = [mem.alloc(f"rD{i}", (128, QW), F32) for i in range(2)]
        o1 = [mem.alloc(f"o1{i}", (128, QW), F32) for i in range(2)]
        kst = [mem.alloc(f"kst{i}", (128, 16, 128), F32) for i in range(2)] if cx.kind == "s" else None

        def load_kcache(hp_):
            k_ = kst[hp_ % 2]
            P.dma(SP, k_[:], cfk[l, si].rearrange("(t p) c -> p t c", p=128)[:, :, hp_ * 128:(hp_ + 1) * 128], writes=[k_.b])
        if cx.kind == "s":
            load_kcache(0)
        Vaug = mem.alloc("Vaug", (128, 17, 2, 128), BF16)
        memset(POOL, Vaug[:, :, :, :], 1.0, [Vaug.b])

        wfb = load_w(cx, wsl, FT_FB)
        if cx.kind == "s":
            lst = mem.alloc("lst", (128, 16, 6), F32)
            P.dma(SP, lst[:, :, :], cfl[l, si].rearrange("(t p) h -> p t h", p=128), writes=[lst.b])
            for t4 in range(4):
                bank = next_bank()
                for t in range(4):
                    tr(bank[0:6, t * 128:(t + 1) * 128], lst[:, 4 * t4 + t, :], C["identf"][:, :], [lst.b, C["identf"].b], [bank.b])
                cp(DVE if t4 % 2 else ACT, Aa[0:6, t4 * 512:(t4 + 1) * 512], bank[0:6, 0:512], [bank.b], [Aa.b])
        for blk in range(cx.NBLK):
            bank = next_bank()
            project(cx, wfb, 6, blk, bank)
            cols = slice(past + blk * BW, past + (blk + 1) * BW)
            act(Aa[0:6, cols], bank[0:6, 0:BW], AF.Exp, [bank.b, nfb_t.b], [Aa.b], bias=nfb_t[0:6, l:l + 1], scale=-1.0)
            act(Aa[0:6, cols], Aa[0:6, cols], AF.Ln, [Aa.b], [Aa.b], bias=1.0)
            ts(DVE, Aa[0:6, cols], Aa[0:6, cols], -1.0, None, ALU.mult, None, [Aa.b], [Aa.b])
        P.op(DVE, lambda e: e.tensor_tensor_scan(out=Sa[0:6, 0:NK], data0=onecol[0:6, 0:1].to_broadcast([6, NK]),
                                                 data1=Aa[0:6, 0:NK], initial=0.0, op0=ALU.mult, op1=ALU.subtract),
             reads=[Aa.b, onecol.b], writes=[Sa.b])
        cp(DVE, SPL[0:6, 0, :], Sa[0:6, 0:NK], [Sa.b], [SPL.b])
        tt(DVE, Aa[0:6, 0:NK], Sa[0:6, 0:NK], SPL[0:6, 0, :], ALU.subtract, [Sa.b, SPL.b], [Aa.b])
        cp(DVE, SPL[0:6, 1, :], Aa[0:6, 0:NK], [Aa.b], [SPL.b])
        tt(DVE, Aa[0:6, 0:NK], Aa[0:6, 0:NK], SPL[0:6, 1, :], ALU.subtract, [Aa.b, SPL.b], [Aa.b])
        cp(DVE, SPL[0:6, 2, :], Aa[0:6, 0:NK], [Aa.b], [SPL.b])

        DBGF = int(os.environ.get("DBG_F", "9"))
        if DBGF < 1:
            return
        for hp in range(3):
            P.tag = "F.proj"
            wq = load_w(cx, wsl, FT_QB + hp)
            wk = load_w(cx, wsl, FT_KB + hp)
            wg = load_w(cx, wsl, FT_GB + hp)
            for hh in range(2):
                h = 2 * hp + hh
                memset(POOL, Qa[hh][64:70, 0:T], 1.0, [Qa[hh].b])
                memset(POOL, Ka[hh][64:70, 0:NK], -1.0, [Ka[hh].b])
                for r in range(3):
                    P.dma(SP, Qa[hh][64 + r:65 + r, 0:T], SPL[h:h + 1, r, past:NK], reads=[SPL.b], writes=[Qa[hh].b])
                    P.dma(SP, Ka[hh][67 + r:68 + r, 0:NK], SPL[h:h + 1, r, 0:NK], reads=[SPL.b], writes=[Ka[hh].b])
            if cx.kind == "s":
                kcur = kst[hp % 2]
                for hh in range(2):
                    for t4 in range(4):
                        bank = next_bank()
                        for t in range(4):
                            tr(bank[0:64, t * 128:(t + 1) * 128], kcur[:, 4 * t4 + t, hh * 64:(hh + 1) * 64], C["identf"][:, :],
                               [kcur.b, C["identf"].b], [bank.b])
                        cp(DVE if t4 % 2 else ACT, Ka[hh][0:64, t4 * 512:(t4 + 1) * 512], bank[0:64, 0:512], [bank.b], [Ka[hh].b])
                if hp + 1 < 3:
                    load_kcache(hp + 1)
            for blk in range(cx.NBLK):
                cols = slice(blk * BW, (blk + 1) * BW)
                kcols = slice(past + blk * BW, past + (blk + 1) * BW)
                bq = next_bank()
                project(cx, wq, 128, blk, bq)
                act(Qa[0][0:64, cols], bq[0:64, 0:BW], AF.Copy, [bq.b], [Qa[0].b], scale=0.125)
                ts(DVE, Qa[1][0:64, cols], bq[64:128, 0:BW], 0.125, None, ALU.mult, None, [bq.b], [Qa[1].b])
                bk_ = next_bank()
                project(cx, wk, 128, blk, bk_)
                cp(ACT, Ka[0][0:64, kcols], bk_[0:64, 0:BW], [bk_.b], [Ka[0].b])
                cp(DVE, Ka[1][0:64, kcols], bk_[64:128, 0:BW], [bk_.b], [Ka[1].b])
                bg = next_bank()
                project(cx, wg, 128, blk, bg)
                act(gate[:, cols], bg[:, 0:BW], AF.Silu, [bg.b], [gate.b])
            P.tag = "F.attn"
            nfull = NK // 128
            for hh_ in range(2):
                vcols = slice((2 * hp + hh_) * 64, (2 * hp + hh_ + 1) * 64)
                acols = slice(64 * hh_, 64 * hh_ + 64)
                cp(POOL, Vaug[:, 0:nfull, hh_, acols], Vb[:, 0:nfull, vcols], [Vb.b], [Vaug.b])
                if NK % 128:
                    cp(POOL, Vaug[0:NK % 128, nfull, hh_, acols], Vb[0:NK % 128, nfull, vcols], [Vb.b], [Vaug.b])
            heat["n"], heat["bank"] = HEAT_F, psum[0]
            for hh in range(2):
                if DBGF < 2:
                    break
                h = 2 * hp + hh
                ob = 64 * hh
                db = 64 - ob
                for qt in range(T // QW):
                    q0 = qt * QW
                    qlo = past + q0
                    qhi = qlo + QW - 1
                    kts = [kt for kt in range(cx.NKT) if kt * 128 <= qhi]
                    Ob = psum[4 + (2 * hh + qt) % 4]

                    def stage2(kt, kn, pt, first, last):
                        mm(Ob[:, 0:QW], Vaug[0:kn, kt, hh, :], pt[0:kn, 0:QW], first, last, [Vaug.b, pt.b], [Ob.b])
                    pend = None
                    for i, kt in enumerate(kts):
                        kn = min(128, NK - kt * 128)
                        Sb = psum[i % 4]
                        partial = kt * 128 + kn - 1 > qlo
                        mm(Sb[0:kn, 0:QW], Ka[hh][0:70, kt * 128:kt * 128 + kn], Qa[hh][0:70, q0:q0 + QW], True, not partial,
                           [Ka[hh].b, Qa[hh].b], [Sb.b])
                        if partial:
                            d = kt * 128 - qlo
                            mm(Sb[0:kn, 0:QW], identb[0:kn, 0:kn], negleb[0:kn, 384 - d:384 - d + QW], False, True,
                               [identb.b, negleb.b], [Sb.b])
                        pt = pts[i % 4]
                        act(pt[0:kn, 0:QW], Sb[0:kn, 0:QW], AF.Exp, [Sb.b], [pt.b])
                        if pend:
                            stage2(*pend)
                        pend = (kt, kn, pt, i == 0, i == len(kts) - 1)
                    stage2(*pend)
                    rd, oo = rD[qt % 2], o1[qt % 2]
                    P.op(DVE, lambda e, o=rd[ob:ob + 64, 0:QW], i_=Ob[db:db + 64, 0:QW]: e.reciprocal(out=o, in_=i_),
                         reads=[Ob.b], writes=[rd.b])
                    tt(DVE, oo[ob:ob + 64, 0:QW], Ob[ob:ob + 64, 0:QW], rd[ob:ob + 64, 0:QW], ALU.mult, [Ob.b, rd.b], [oo.b])
                    tt(POOL, oT[ob:ob + 64, 3 + h // 2, q0:q0 + QW], oo[ob:ob + 64, 0:QW], gate[ob:ob + 64, q0:q0 + QW], ALU.mult,
                       [oo.b, gate.b], [oTb[3 + h // 2]])
            heat["n"] = 0

    def phase_S(cx):
        P.tag = "S.pre"
        l, si, T, NK, BW, QW, past, O = cx.l, cx.si, cx.T, cx.NK, cx.BW, cx.BW, cx.past, cx.O
        mem.top = phase_base
        P.barrier()
        wsl = [mem.alloc(f"wf{i}", (128, 8, 128), BF16) for i in range(4)]
        Qc = [mem.alloc(f"Qc{i}", (128, T), BF16) for i in range(2)]
        Kc = [mem.alloc(f"Kc{i}", (128, NK), BF16) for i in range(2)]
        gate = mem.alloc("gate", (128, T), BF16)
        Ef = [mem.alloc(f"Ef{i}", (128, QW), F32) for i in range(3)]
        Xe = [mem.alloc(f"Xe{i}", (128, QW), BF16) for i in range(3)]
        spb = [mem.alloc(f"spb{i}", (128, QW), BF16) for i in range(3)]
        Lsb = [mem.alloc(f"Lsb{i}", (128, QW), BF16) for i in range(3)]
        Wt = [mem.alloc(f"Wt{i}", (128, QW), BF16) for i in range(3)]
        Lsum = [mem.alloc(f"Lsum{i}", (128, QW), F32) for i in range(2)]
        kst = [mem.alloc(f"kst{i}", (128, 16, 128), F32) for i in range(2)] if cx.kind == "s" else None

        def load_kcache(sp_):
            k_ = kst[sp_ % 2]
            P.dma(SP, k_[:], csk[l, si].rearrange("(t p) c -> p t c", p=128)[:, :, sp_ * 128:(sp_ + 1) * 128], writes=[k_.b])
        if cx.kind == "s":
            load_kcache(0)
            load_kcache(1)
        for sp in range(2):
            P.tag = "S.proj"
            wq = load_w(cx, wsl, FT_QC + sp)
            wk = load_w(cx, wsl, FT_KC + sp)
            wg = load_w(cx, wsl, FT_GC + sp)
            if cx.kind == "s":
                kcur = kst[sp % 2]
                for hh in range(2):
                    for t4 in range(4):
                        bank = next_bank()
                        for t in range(4):
                            tr(bank[0:64, t * 128:(t + 1) * 128], kcur[:, 4 * t4 + t, hh * 64:(hh + 1) * 64], C["identf"][:, :],
                               [kcur.b, C["identf"].b], [bank.b])
                        cp(DVE if t4 % 2 else ACT, Kc[hh][0:64, t4 * 512:(t4 + 1) * 512], bank[0:64, 0:512], [bank.b], [Kc[hh].b])
            for blk in range(cx.NBLK):
                cols = slice(blk * BW, (blk + 1) * BW)
                kcols = slice(past + blk * BW, past + (blk + 1) * BW)
                bq = next_bank()
                project(cx, wq, 128, blk, bq)
                act(Qc[0][0:64, cols], bq[0:64, 0:BW], AF.Copy, [bq.b], [Qc[0].b], scale=0.125)
                ts(DVE, Qc[1][0:64, cols], bq[64:128, 0:BW], 0.125, None, ALU.mult, None, [bq.b], [Qc[1].b])
                bk_ = next_bank()
                project(cx, wk, 128, blk, bk_)
                cp(ACT, Kc[0][0:64, kcols], bk_[0:64, 0:BW], [bk_.b], [Kc[0].b])
                cp(DVE, Kc[1][0:64, kcols], bk_[64:128, 0:BW], [bk_.b], [Kc[1].b])
                bg = next_bank()
                project(cx, wg, 128, blk, bg)
                act(gate[:, cols], bg[:, 0:BW], AF.Silu, [bg.b], [gate.b])
            P.tag = "S.attn"
            heat["n"], heat["bank"] = HEAT_S, psum[0]
            for hh in range(2):
                hc = 2 * sp + hh
                ob = 64 * hh
                for qt in range(T // QW):
                    q0 = qt * QW
                    qlo = past + q0
                    qhi = qlo + QW - 1
                    kts = list(reversed([kt for kt in range(cx.NKT) if kt * 128 <= qhi]))
                    n = len(kts)
                    Ob = psum[6 + qt % 2]
                    memset(POOL, Lsum[0][:, :], 0.0, [Lsum[0].b])
                    memset(POOL, Lsum[1][:, :], 0.0, [Lsum[1].b])

                    def info(i):
                        kt = kts[i]
                        return kt, min(128, NK - kt * 128), psum[2 + i % 2]

                    def stage1(i):
                        kt, kn, Zb = info(i)
                        partial = kt * 128 + kn - 1 >= qlo
                        mm(Zb[0:kn, 0:QW], Kc[hh][0:64, kt * 128:kt * 128 + kn], Qc[hh][0:64, q0:q0 + QW], True, not partial,
                           [Kc[hh].b, Qc[hh].b], [Zb.b])
                        if partial:
                            d = kt * 128 - qlo
                            mm(Zb[0:kn, 0:QW], identb[0:kn, 0:kn], negltb[0:kn, 384 - d:384 - d + QW], False, True,
                               [identb.b, negltb.b], [Zb.b])
                        j = i % 3
                        act(Ef[j][0:kn, :], Zb[0:kn, 0:QW], AF.Exp, [Zb.b], [Ef[j].b])
                        act(spb[j][0:kn, :], Ef[j][0:kn, :], AF.Ln, [Ef[j].b], [spb[j].b], bias=1.0)
                        La, Lb = Lsum[i % 2], Lsum[(i + 1) % 2]
                        if i + 1 < n:
                            jn = (i + 1) % 3
                            if kn < 128:
                                memset(POOL, Lsb[jn][kn:128, :], 0.0, [Lsb[jn].b])
                            tt(DVE, Lsb[jn][0:kn, :], La[0:kn, :], spb[j][0:kn, :], ALU.add, [La.b, spb[j].b], [Lsb[jn].b])
                            tt(DVE, Lb[0:kn, :], La[0:kn, :], spb[j][0:kn, :], ALU.add, [La.b, spb[j].b], [Lb.b])

                    def stage2(i):
                        kt, kn, _ = info(i)
                        Zb = psum[4 + i % 2]
                        j = i % 3
                        mm(Zb[0:kn, 0:QW], trinegb[0:kn, 0:kn], spb[j][0:kn, :], True, i == 0, [trinegb.b, spb[j].b], [Zb.b])
                        if i > 0:
                            mm(Zb[0:kn, 0:QW], onesneg[:, 0:kn], Lsb[j][:, :], False, True, [onesneg.b, Lsb[j].b], [Zb.b])
                        act(Xe[j][0:kn, :], Zb[0:kn, 0:QW], AF.Exp, [Zb.b], [Xe[j].b])
                        tt(DVE, Wt[j][0:kn, :], Ef[j][0:kn, :], Xe[j][0:kn, :], ALU.mult, [Ef[j].b, Xe[j].b], [Wt[j].b])

                    def stage3(i):
                        kt, kn, Zb = info(i)
                        j = i % 3
                        mm(Ob[ob:ob + 64, 0:QW], Vc[0:kn, kt, hc * 64:(hc + 1) * 64], Wt[j][0:kn, :], i == 0, i == n - 1,
                           [Vc.b, Wt[j].b], [Ob.b])
                    for s_ in range(n + 2):
                        if s_ < n:
                            stage1(s_)
                        if 0 <= s_ - 1 < n:
                            stage2(s_ - 1)
                        if 0 <= s_ - 2 < n:
                            stage3(s_ - 2)
                    tt(DVE, oT[ob:ob + 64, 6 + hc // 2, q0:q0 + QW], Ob[ob:ob + 64, 0:QW], gate[ob:ob + 64, q0:q0 + QW], ALU.mult,
                       [Ob.b, gate.b], [oTb[6 + hc // 2]])
            heat["n"] = 0

    def phase_E(cx):
        P.tag = "E"
        l, si, TT, O = cx.l, cx.si, cx.TT, cx.O
        mem.top = phase_base
        P.barrier()
        wo = mem.alloc("wo", (128, 8, 1024), BF16)
        lng = mem.alloc("lng", (128, D), F32)
        lnb = mem.alloc("lnb", (128, D), F32)
        xres = [mem.alloc(f"xres{i}", (128, D), F32) for i in range(3)]
        Rr = [mem.alloc(f"Rr{i}", (128, D), F32) for i in range(2)]
        yv = [mem.alloc(f"yv{i}", (128, D), F32) for i in range(3)]
        st = mem.alloc("bnst", (128, 12), F32)
        mv = mem.alloc("bnmv", (128, 2), F32)
        rs = mem.alloc("bnrs", (128, 1), F32)
        nb = mem.alloc("bnnb", (128, 1), F32)
        P.dma(SP, wo[:], WO[l], reads=[bWO[l]], writes=[wo.b])
        P.dma(SP, lng[:], ln_g[l].partition_broadcast(128), writes=[lng.b])
        P.dma(SP, lnb[:], ln_b[l].partition_broadcast(128), writes=[lnb.b])
        st2 = [st, mem.alloc("bnst2", (128, 12), F32)]

        def front(tt_):
            rows = slice(tt_ * TT, (tt_ + 1) * TT)
            xr, R_, st_ = xres[tt_ % 3], Rr[tt_ % 2], st2[tt_ % 2]
            if l == 0:
                P.dma(SP, xr[:TT, :], cx.xsrc[si, rows, :], writes=[xr.b])
            else:
                P.dma(SP, xr[:TT, :], Y0[cx.yidx, rows, :], reads=[bY0[cx.yidx][tt_]], writes=[xr.b])
            bA, bB = psum[2 * (tt_ % 2)], psum[2 * (tt_ % 2) + 1]
            for c in range(8):
                mm(bA[:TT, 0:512], oT[:, c, rows], wo[:, c, 0:512], c == 0, c == 7, [oTb[c], wo.b], [bA.b])
            for c in range(8):
                mm(bB[:TT, 0:512], oT[:, c, rows], wo[:, c, 512:1024], c == 0, c == 7, [oTb[c], wo.b], [bB.b])

        def frontB(tt_):
            xr, R_, st_ = xres[tt_ % 3], Rr[tt_ % 2], st2[tt_ % 2]
            bA, bB = psum[2 * (tt_ % 2)], psum[2 * (tt_ % 2) + 1]
            stt(R_[:TT, 0:512], xr[:TT, 0:512], ALPHA, bA[:TT, 0:512], ALU.mult, ALU.add, [xr.b, bA.b], [R_.b])
            stt(R_[:TT, 512:1024], xr[:TT, 512:1024], ALPHA, bB[:TT, 0:512], ALU.mult, ALU.add, [xr.b, bB.b], [R_.b])
            P.op(DVE, lambda e, o=st_[:TT, 0:6], i_=R_[:TT, 0:512]: e.bn_stats(out=o, in_=i_), reads=[R_.b], writes=[st_.b])
            P.op(DVE, lambda e, o=st_[:TT, 6:12], i_=R_[:TT, 512:1024]: e.bn_stats(out=o, in_=i_), reads=[R_.b], writes=[st_.b])

        def back(tt_):
            rows = slice(tt_ * TT, (tt_ + 1) * TT)
            R_, y_, st_ = Rr[tt_ % 2], yv[tt_ % 3], st2[tt_ % 2]
            P.op(DVE, lambda e, o=mv[:TT, 0:2], i_=st_[:TT, 0:12]: e.bn_aggr(out=o, in_=i_), reads=[st_.b], writes=[mv.b])
            act(rs[:TT, :], mv[:TT, 1:2], AF.Ln, [mv.b], [rs.b], bias=LN_EPS)
            act(rs[:TT, :], rs[:TT, :], AF.Exp, [rs.b], [rs.b], scale=-0.5)
            stt(nb[:TT, :], mv[:TT, 0:1], -1.0, rs[:TT, 0:1], ALU.mult, ALU.mult, [mv.b, rs.b], [nb.b])
            act(y_[:TT, :], R_[:TT, :], AF.Identity, [R_.b, rs.b, nb.b], [y_.b], bias=nb[:TT, 0:1], scale=rs[:TT, 0:1])
            tt(DVE, y_[:TT, :], y_[:TT, :], lng[:TT, :], ALU.mult, [y_.b, lng.b], [y_.b])
            tt(POOL, y_[:TT, :], y_[:TT, :], lnb[:TT, :], ALU.add, [y_.b, lnb.b], [y_.b])
            if l == 0:
                P.dma(POOL, Y0[cx.yidx, rows, :], y_[:TT, :], reads=[y_.b], writes=[bY0[cx.yidx][tt_]])
                bk = (psum[4], psum[5]) if tt_ % 2 == 0 else (psum[6], psum[7])
                to_xT(cx, y_, tt_, bk)
            else:
                P.dma(POOL, O["y"][si, rows, :], y_[:TT, :], reads=[y_.b])

        front(0)
        frontB(0)
        if cx.NTT > 1:
            front(1)
        for tt_ in range(cx.NTT):
            back(tt_)
            if tt_ + 1 < cx.NTT:
                frontB(tt_ + 1)
            if tt_ + 2 < cx.NTT:
                front(tt_ + 2)

    def phase_R(cx):
        P.tag = "R.init"
        l, si, T, BW, TT, past, O = cx.l, cx.si, cx.T, cx.BW, cx.TT, cx.past, cx.O
        NCH = TT // 64
        mem.top = phase_base
        P.barrier()
        NT = BW // TT
        NU = 2 * NT
        NCHK = NT * NCH
        wsl = [mem.alloc(f"wf{i}", (128, 8, 128), BF16) for i in range(3)]
        U = [mem.alloc(f"U{i}", (128, BW + 1), F32) for i in range(3)]
        Ul = mem.alloc("Ul", (128, BW + 1), F32)
        Dt = mem.alloc("Dt", (128, BW), F32)
        tw = mem.alloc("tw", (128, BW), BF16)
        gaT = mem.alloc("gaT", (128, BW), BF16)
        fnames = "lgc lgx ex eneg ld av tmp esfx kk kkn k2 bb epos Rt bonus Y".split()
        f = {}
        foff = {}
        for n_ in fnames:
            foff[n_] = mem.top
            f[n_] = mem.alloc(n_, (128, BW), F32)
        b = {n: mem.alloc(n, (128, BW), BF16) for n in "kk2 Rtb KKt Kh Bh Kg Bg Vbf rkb Ybf Ysq".split()}
        def alias(name, shape, dt, off):
            mem.n += 1
            return nc.alloc_sbuf_tensor_at(f"{name}_{mem.n}", list(shape), dt, offset=off)
        if BW == 512:
            MK = alias("MK", (128, 8, 512), BF16, foff["lgc"])
            MKb = [f[("lgc", "lgx", "ex", "eneg")[u // 2]].b for u in range(8)]
            MM = [alias("MM0", (128, 8, 256), BF16, foff["ld"]), alias("MM1", (128, 8, 256), BF16, foff["tmp"])]
            MMb = [[f[("ld", "av")[g // 2]].b for g in range(4)], [f[("tmp", "esfx")[g // 2]].b for g in range(4)]]
        else:
            MKt = mem.alloc("MK", (128, 8, 512), BF16)
            MK, MKb = MKt.h, [MKt.b] * 8
            MMt = [mem.alloc(f"MM{i}", (128, 8, 256), BF16) for i in range(2)]
            MM, MMb = [t_.h for t_ in MMt], [[t_.b] * 4 for t_ in MMt]
        QcT, McTt, D1sb, Y0sb = f["kk"], f["kkn"], f["k2"], f["bb"]
        TOK = mem.alloc("TOK", (128, 4, 4, 128), BF16)
        Pt = [mem.alloc(f"Pt{i}", (128, 8, 128), BF16) for i in range(2)]
        Ptb = [[Buf(f"Ptb{i}{g}") for g in range(2)] for i in range(2)]
        ArbT = mem.alloc("ArbT", (128, 2, 4, 128), BF16)
        MKraw = [mem.alloc(f"MKraw{i}", (128, 512), BF16) for i in range(2)]
        W1b = mem.alloc("W1b", (128, 8, 64), BF16)
        UW = mem.alloc("UW", (128, 8, 128), BF16)
        UWm = [mem.alloc(f"UWm{i}", (128, 8, 128), BF16) for i in range(2)]
        Vm = [mem.alloc(f"Vm{i}", (128, 4, 128), BF16) for i in range(2)]
        wst = mem.alloc("wst", (128, 6, 64), F32)
        gidx = [[0, 0] for _ in range(3)]

        if cx.kind == "p":
            memset(POOL, ucarry[:, :], 0.0, ucb)
            memset(POOL, Gst[0][:, :, :], 0.0, [Gb[0][c3][hh] for c3 in range(3) for hh in range(2)])
            memset(POOL, Gst[1][:, :, :], 0.0, [Gb[1][c3][hh] for c3 in range(3) for hh in range(2)])
        else:
            with nc.allow_non_contiguous_dma("state_shift transpose load"):
                P.dma(SP, ucarry[:, :], sshift[l, si].rearrange("(t p) -> p t", p=128), writes=ucb)
            memset(POOL, Gst[1][:, :, :], 0.0, [Gb[1][c3][hh] for c3 in range(3) for hh in range(2)])
            P.dma(SP, wst[0:64, :, :], swkv[l, si].rearrange("h v k -> v h k"), writes=[wst.b])
            for c3 in range(3):
                for hh in range(2):
                    hb = 64 * hh
                    mm(psum[2][hb:hb + 64, 256 + hh * 64:256 + (hh + 1) * 64], wst[0:64, 2 * c3 + hh, :], C["identf"][0:64, 0:64],
                       True, True, [wst.b, C["identf"].b], [psum[2].b])
                    cp(DVE, Gst[0][hb:hb + 64, c3, :], psum[2][hb:hb + 64, 256 + hh * 64:256 + (hh + 1) * 64], [psum[2].b], [Gb[0][c3][hh]])

        def uproc(Ut, bank, ct, last_blk):
            cp(ACT, Ut[:, 1:BW + 1], bank[:, 0:BW], [bank.b], [Ut.b])
            cp(ACT, Ut[:, 0:1], ucarry[:, ct:ct + 1], [ucb[ct]], [Ut.b])
            cp(ACT, ucarry[:, ct:ct + 1], Ut[:, BW:BW + 1], [Ut.b], [ucb[ct]])
            if last_blk:
                with nc.allow_non_contiguous_dma("shift state store"):
                    P.dma(POOL, O["sh"][l, si, ct * 128:(ct + 1) * 128].rearrange("(p o) -> p o", o=1), Ut[:, BW:BW + 1], reads=[Ut.b])
            act(Dt[:, :], Ut[:, 0:BW], AF.Copy, [Ut.b, mu_t.b], [Dt.b], scale=mu_t[:, l, ct:ct + 1])
            stt(Ut[:, 1:BW + 1], Ut[:, 1:BW + 1], omm_t[:, l, ct:ct + 1], Dt[:, :], ALU.mult, ALU.add, [Dt.b, omm_t.b, Ut.b], [Ut.b])

        ABANKS = (2, 4)

        def prep_A(blk, c3):
            last_blk = blk == cx.NBLK - 1
            if c3 == 0:
                P.tag = "R.lora"
                w9 = load_w(cx, wsl, 9)
                bank = next_bank(*ABANKS)
                project(cx, w9, 128, blk, bank)
                uproc(Ul, bank, 9, last_blk)
                act(tw[0:64, :], Ul[0:64, 1:BW + 1], AF.Tanh, [Ul.b], [tw.b])
                cp(DVE, tw[64:128, :], Ul[64:128, 1:BW + 1], [Ul.b], [tw.b])
                yield
            P.tag = "R.prepA"
            for j, ct in enumerate((c3, 3 + c3, 6 + c3)):
                ws = load_w(cx, wsl, ct)
                bank = next_bank(*ABANKS)
                project(cx, ws, 128, blk, bank)
                uproc(U[j], bank, ct, last_blk)
                yield
            cs = slice(c3 * 128, (c3 + 1) * 128)
            bank = next_bank(*ABANKS)
            mm(bank[:, 0:BW], lw[0:64, l, cs], tw[0:64, :], True, True, [lw.b, tw.b], [bank.b])
            act(f["ld"][:, :], bank[:, 0:BW], AF.Sigmoid, [bank.b, w0_t.b], [f["ld"].b], bias=w0_t[:, l, c3:c3 + 1])
            yield
            bank = next_bank(*ABANKS)
            mm(bank[:, 0:BW], lw[64:128, l, cs], tw[64:128, :], True, True, [lw.b, tw.b], [bank.b])
            act(f["av"][:, :], bank[:, 0:BW], AF.Sigmoid, [bank.b, a0_t.b], [f["av"].b], bias=a0_t[:, l, c3:c3 + 1])
            act(f["ld"][:, :], f["ld"][:, :], AF.Copy, [f["ld"].b], [f["ld"].b], scale=DEC_SCALE)
            yield
            P.op(DVE, lambda e: e.tensor_tensor_scan(out=f["lgc"][:, :], data0=C["chunkmask"][:, 0:BW], data1=f["ld"][:, :],
                                                     initial=0.0, op0=ALU.mult, op1=ALU.add),
                 reads=[C["chunkmask"].b, f["ld"].b], writes=[f["lgc"].b])
            tt(DVE, f["lgx"][:, :], f["lgc"][:, :], f["ld"][:, :], ALU.subtract, [f["lgc"].b, f["ld"].b], [f["lgx"].b])
            yield
            act(f["epos"][:, :], f["lgc"][:, :], AF.Exp, [f["lgc"].b], [f["epos"].b])
            act(f["ex"][:, :], f["lgx"][:, :], AF.Exp, [f["lgx"].b], [f["ex"].b])
            act(f["eneg"][:, :], f["lgc"][:, :], AF.Exp, [f["lgc"].b], [f["eneg"].b], scale=-1.0)
            yield
            lg3 = f["lgc"][:, :].rearrange("p (c n) -> p c n", n=64)
            tt(DVE, f["esfx"][:, :].rearrange("p (c n) -> p c n", n=64), lg3[:, :, 63:64].to_broadcast([128, BW // 64, 64]), lg3,
               ALU.subtract, [f["lgc"].b], [f["esfx"].b])
            act(f["esfx"][:, :], f["esfx"][:, :], AF.Exp, [f["esfx"].b], [f["esfx"].b])
            yield

        def advance(g, n):
            if g is None:
                return
            tag0 = P.tag
            for _ in range(n):
                try:
                    next(g)
                except StopIteration:
                    break
            P.tag = tag0

        def drain(g):
            advance(g, 10 ** 6)

        its = [(blk_, c3_) for blk_ in range(cx.NBLK) for c3_ in range(3)]
        drain(prep_A(0, 0))
        for blk in range(cx.NBLK):
            for c3 in range(3):
                idx_it = blk * 3 + c3
                nxtA = prep_A(*its[idx_it + 1]) if idx_it + 1 < len(its) else None
                P.tag = "R.prep"
                ws = load_w(cx, wsl, FT_GA + c3)
                bank = next_bank()
                project(cx, ws, 128, blk, bank)
                act(gaT[:, :], bank[:, 0:BW], AF.Silu, [bank.b], [gaT.b])
                r_, k_, v_ = U[0][:, 1:BW + 1], U[1][:, 1:BW + 1], U[2][:, 1:BW + 1]
                rb, kb_, vb_ = U[0].b, U[1].b, U[2].b
                ts(DVE, f["kk"][:, :], k_, kk_t[:, l, c3:c3 + 1], None, ALU.mult, None, [kb_, kk_t.b], [f["kk"].b])
                act(b["kk2"][:, :], f["kk"][:, :], AF.Square, [f["kk"].b], [b["kk2"].b])
                bank = next_bank()
                mm(bank[:, 0:BW], bonesb[:, :], b["kk2"][:, :], True, True, [bonesb.b, b["kk2"].b], [bank.b])
                act(f["tmp"][:, :], bank[:, 0:BW], AF.Ln, [bank.b], [f["tmp"].b], bias=1e-12)
                act(f["tmp"][:, :], f["tmp"][:, :], AF.Exp, [f["tmp"].b], [f["tmp"].b], scale=-0.5)
                tt(DVE, f["kkn"][:, :], f["kk"][:, :], f["tmp"][:, :], ALU.mult, [f["kk"].b, f["tmp"].b], [f["kkn"].b])
                ts(DVE, f["k2"][:, :], f["av"][:, :], ka_t[:, l, c3:c3 + 1], omka_t[:, l, c3:c3 + 1], ALU.mult, ALU.add,
                   [f["av"].b, ka_t.b, omka_t.b], [f["k2"].b])
                tt(DVE, f["k2"][:, :], f["k2"][:, :], k_, ALU.mult, [f["k2"].b, kb_], [f["k2"].b])
                tt(DVE, f["bb"][:, :], f["kkn"][:, :], f["av"][:, :], ALU.mult, [f["kkn"].b, f["av"].b], [f["bb"].b])
                tt(DVE, f["tmp"][:, :], r_, f["k2"][:, :], ALU.mult, [rb, f["k2"].b], [f["tmp"].b])
                ts(DVE, b["rkb"][:, :], f["tmp"][:, :], rk_t[:, l, c3:c3 + 1], None, ALU.mult, None, [f["tmp"].b, rk_t.b], [b["rkb"].b])
                bank = next_bank()
                mm(bank[:, 0:BW], bonesb[:, :], b["rkb"][:, :], True, True, [bonesb.b, b["rkb"].b], [bank.b])
                tt(DVE, f["bonus"][:, :], bank[:, 0:BW], v_, ALU.mult, [bank.b, vb_], [f["bonus"].b])
                tt(DVE, f["Rt"][:, :], r_, f["epos"][:, :], ALU.mult, [rb, f["epos"].b], [f["Rt"].b])
                cp(ACT, b["Rtb"][:, :], f["Rt"][:, :], [f["Rt"].b], [b["Rtb"].b])
                tt(DVE, b["KKt"][:, :], f["kkn"][:, :], f["ex"][:, :], ALU.mult, [f["kkn"].b, f["ex"].b], [b["KKt"].b])
                tt(DVE, b["Kh"][:, :], f["k2"][:, :], f["eneg"][:, :], ALU.mult, [f["k2"].b, f["eneg"].b], [b["Kh"].b])
                tt(DVE, b["Bh"][:, :], f["bb"][:, :], f["eneg"][:, :], ALU.mult, [f["bb"].b, f["eneg"].b], [b["Bh"].b])
                tt(DVE, b["Kg"][:, :], f["k2"][:, :], f["esfx"][:, :], ALU.mult, [f["k2"].b, f["esfx"].b], [b["Kg"].b])
                tt(POOL, b["Bg"][:, :], f["bb"][:, :], f["esfx"][:, :], ALU.mult, [f["bb"].b, f["esfx"].b], [b["Bg"].b])
                cp(ACT, b["Vbf"][:, :], v_, [vb_], [b["Vbf"].b])

                v3 = lambda ap, c=128, n=TT: ap.rearrange("p (a c) -> p a c", c=c)[:, :, 0:n]
                P.tag = "R.S0"
                for tl in range(NT):
                    tc = slice(tl * TT, (tl + 1) * TT)
                    bk = psum[6 + tl % 2]
                    tb = pbf(6 + tl % 2)
                    for j, nm in enumerate(("KKt", "Kg", "Bg", "Vbf")):
                        tr(tb[0:TT, j * 128:(j + 1) * 128], b[nm][:, tc], identb[:, :], [b[nm].b, identb.b], [bk.b])
                    cp(DVE if tl % 2 else ACT, TOK[0:TT, tl, :, :], tb[0:TT, 0:512].rearrange("p (a c) -> p a c", c=128), [bk.b], [TOK.b])
                P.tag = "R.S1"
                for tl in range(NT):
                    tc = slice(tl * TT, (tl + 1) * TT)
                    for hh in range(2):
                        hs = slice(64 * hh, 64 * hh + 64)
                        bA = psum[2 * (tl % 2) + hh]
                        mm(bA[0:TT, 0:TT], b["KKt"][hs, tc], b["Bh"][hs, tc], True, True, [b["KKt"].b, b["Bh"].b], [bA.b])
                        mm(bA[0:TT, 128:128 + TT], b["Bh"][hs, tc], b["KKt"][hs, tc], True, True, [b["KKt"].b, b["Bh"].b], [bA.b])
                        mm(bA[0:TT, 256:256 + TT], b["Kh"][hs, tc], b["KKt"][hs, tc], True, True, [b["KKt"].b, b["Kh"].b], [bA.b])
                        mm(bA[0:TT, 384:384 + TT], b["Kh"][hs, tc], b["Rtb"][hs, tc], True, True, [b["Rtb"].b, b["Kh"].b], [bA.b])
                        mm(psum[4 + hh][0:TT, tl * 128:tl * 128 + TT], b["Bh"][hs, tc], b["Rtb"][hs, tc], True, True,
                           [b["Rtb"].b, b["Bh"].b], [psum[4 + hh].b])
                    for hh in range(2):
                        u = 2 * tl + hh
                        bA = psum[2 * (tl % 2) + hh]
                        if hh == 0 or tl % 2 == 1:
                            tt(DVE, v3(MK[0:TT, u, :]), v3(bA[0:TT, :]), v3(C["rwmask"][0:TT, :]), ALU.mult, [bA.b, C["rwmask"].b], [MKb[u]])
                        else:
                            raw = MKraw[tl % 2]
                            cp(ACT, v3(raw[0:TT, :]), v3(bA[0:TT, :]), [bA.b], [raw.b])
                            tt(POOL, v3(MK[0:TT, u, :]), v3(raw[0:TT, :]), v3(C["rwmask"][0:TT, :]), ALU.mult, [raw.b, C["rwmask"].b], [MKb[u]])
                for hh in range(2):
                    tt(DVE, ArbT[0:TT, hh, 0:NT, 0:TT], v3(psum[4 + hh][0:TT, :])[:, 0:NT, :], v3(C["iumask4"][0:TT, :])[:, 0:NT, :], ALU.mult,
                       [psum[4 + hh].b, C["iumask4"].b], [ArbT.b])
                mkall = sorted(set(MKb), key=id)
                tt(POOL, Pt[0][0:TT, 0:NU, 0:TT], MK[0:TT, 0:NU, 128:128 + TT], ident8[0:TT, 0:NU, 0:TT], ALU.add,
                   mkall + [ident8.b], [Ptb[0][0], Ptb[0][1]])
                P.tag = "R.S2"
                Mcur = [(MK[0:TT, u, 0:TT], MK[0:TT, u, 128:128 + TT], MKb[u]) for u in range(NU)]
                for m in range(1, 6):
                    par = m % 2
                    for u in range(NU):
                        bk = psum[u // 2]
                        off = 256 * (u % 2)
                        Mp, Mtp, mb = Mcur[u]
                        mm(bk[0:TT, off:off + TT], Mtp, Mp, True, True, [mb], [bk.b])
                        mm(bk[0:TT, off + 128:off + 128 + TT], Mp, Mtp, True, True, [mb], [bk.b])
                    for g in range(NU // 2):
                        dst = MM[par][0:TT, 2 * g:2 * g + 2, :].rearrange("p u (a c) -> p (u a) c", c=128)[:, :, 0:TT]
                        cp(DVE if g == 3 else ACT, dst, v3(psum[g][0:TT, :]), [psum[g].b], [MMb[par][g]])
                        for u in (2 * g, 2 * g + 1):
                            Mcur[u] = (MM[par][0:TT, u, 0:TT], MM[par][0:TT, u, 128:128 + TT], MMb[par][g])
                    for u in range(NU):
                        pbk = psum[4 + u // 4]
                        po = (u % 4) * 128
                        Pp = Pt[1 - par]
                        ppb = Ptb[1 - par][u // 4]
                        mm(pbk[0:TT, po:po + TT], Mcur[u][0], Pp[0:TT, u, 0:TT], True, True, [Mcur[u][2], ppb], [pbk.b])
                    for g in range((NU + 3) // 4):
                        nu_ = min(4, NU - 4 * g)
                        tt(DVE, Pt[par][0:TT, 4 * g:4 * g + nu_, 0:TT], v3(psum[4 + g][0:TT, :])[:, 0:nu_, :],
                           Pt[1 - par][0:TT, 4 * g:4 * g + nu_, 0:TT], ALU.add, [psum[4 + g].b, Ptb[1 - par][g]], [Ptb[par][g]])
                    advance(nxtA, ADV2)
                PtF, PtFb = Pt[1], Ptb[1]
                P.tag = "R.S3-6"
                for u in range(NU):
                    tl, hh = u // 2, u % 2
                    hs = slice(64 * hh, 64 * hh + 64)
                    mm(psum[6][0:TT, u * 64:(u + 1) * 64], MK[0:TT, u, 256:256 + TT], TOK[0:TT, tl, 3, hs], True, True, [MKb[u], TOK.b], [psum[6].b])
                cp(ACT, W1b[0:TT, 0:NU, :], psum[6][0:TT, 0:NU * 64].rearrange("p (u c) -> p u c", c=64), [psum[6].b], [W1b.b])
                for u in range(NU):
                    tl, hh = u // 2, u % 2
                    hs = slice(64 * hh, 64 * hh + 64)
                    bk = psum[u // 4]
                    uo = (u % 4) * 128
                    mm(bk[0:TT, uo:uo + 64], PtF[0:TT, u, 0:TT], W1b[0:TT, u, :], True, True, [PtFb[u // 4], W1b.b], [bk.b])
                    mm(bk[0:TT, uo + 64:uo + 128], PtF[0:TT, u, 0:TT], TOK[0:TT, tl, 0, hs], True, True, [PtFb[u // 4], TOK.b], [bk.b])
                for g in range((NU + 3) // 4):
                    nu_ = min(4, NU - 4 * g)
                    tt(DVE, UW[0:TT, 4 * g:4 * g + nu_, :], v3(psum[g][0:TT, :], n=128)[:, 0:nu_, :], v3(C["signs4"][0:TT, :], n=128)[:, 0:nu_, :],
                       ALU.mult, [psum[g].b, C["signs4"].b], [UW.b])
                for cc in range(NCH):
                    ts(POOL, UWm[cc][0:TT, 0:NU, :], UW[0:TT, 0:NU, :], C["cind"][0:TT, cc:cc + 1], None, ALU.mult, None,
                       [UW.b, C["cind"].b], [UWm[cc].b])
                    ts(POOL, Vm[cc][0:TT, 0:NT, :], TOK[0:TT, 0:NT, 3, :], C["cind"][0:TT, cc:cc + 1], None, ALU.mult, None,
                       [TOK.b, C["cind"].b], [Vm[cc].b])
                for u in range(NU):
                    tl, hh = u // 2, u % 2
                    hs = slice(64 * hh, 64 * hh + 64)
                    mm(psum[2][hs, tl * 128:tl * 128 + TT], UW[0:TT, u, 64:128], ArbT[0:TT, hh, tl, 0:TT], True, True, [UW.b, ArbT.b], [psum[2].b])
                vb = lambda ap: ap.rearrange("p (t c) -> p t c", c=TT)
                tt(DVE, vb(QcT[:, 0:BW]), vb(f["Rt"][:, 0:BW]), v3(psum[2][:, :])[:, 0:NT, :], ALU.subtract, [f["Rt"].b, psum[2].b], [QcT.b])
                for u in range(NU):
                    tl, hh = u // 2, u % 2
                    hs = slice(64 * hh, 64 * hh + 64)
                    mm(psum[3][hs, tl * 128:tl * 128 + TT], TOK[0:TT, tl, 3, hs], MK[0:TT, u, 384:384 + TT], True, False, [TOK.b, MKb[u]], [psum[3].b])
                    mm(psum[3][hs, tl * 128:tl * 128 + TT], UW[0:TT, u, 0:64], ArbT[0:TT, hh, tl, 0:TT], False, True, [UW.b, ArbT.b], [psum[3].b])
                cp(ACT, vb(Y0sb[:, 0:BW]), v3(psum[3][:, :])[:, 0:NT, :], [psum[3].b], [Y0sb.b])
                P.tag = "R.S7ab"
                McT = McTt[:, :].rearrange("p (c k) -> p c k", k=64) if BW == 512 else McTt[:, 0:64].rearrange("p (c k) -> p c k", k=64)
                for u in range(NU):
                    tl, hh = u // 2, u % 2
                    hs = slice(64 * hh, 64 * hh + 64)
                    for cc in range(NCH):
                        ch = tl * NCH + cc
                        mm(psum[6][hs, ch * 64:(ch + 1) * 64], UWm[cc][0:TT, u, 64:128], TOK[0:TT, tl, 2, hs], True, True,
                           [UWm[cc].b, TOK.b], [psum[6].b])
                for ch in range(NCHK):
                    gcol = ch * 64 + 63
                    stt(McT[:, ch, :], C["ident2"][:, :], f["epos"][:, gcol:gcol + 1], psum[6][:, ch * 64:(ch + 1) * 64],
                        ALU.mult, ALU.subtract, [C["ident2"].b, f["epos"].b, psum[6].b], [McTt.b])
                for u in range(NU):
                    tl, hh = u // 2, u % 2
                    hs = slice(64 * hh, 64 * hh + 64)
                    for cc in range(NCH):
                        ch = tl * NCH + cc
                        mm(psum[7][hs, ch * 64:(ch + 1) * 64], TOK[0:TT, tl, 1, hs], Vm[cc][0:TT, tl, hs], True, False, [TOK.b, Vm[cc].b], [psum[7].b])
                        mm(psum[7][hs, ch * 64:(ch + 1) * 64], TOK[0:TT, tl, 2, hs], UWm[cc][0:TT, u, 0:64], False, True, [TOK.b, UWm[cc].b], [psum[7].b])
                cp(DVE, D1sb[:, 0:NCHK * 64], psum[7][:, 0:NCHK * 64], [psum[7].b], [D1sb.b])
                P.tag = "R.S7c"
                for ch in range(NCHK):
                    cs_ = slice(ch * 64, (ch + 1) * 64)
                    for hh in range(2):
                        hs = slice(64 * hh, 64 * hh + 64)
                        gi = gidx[c3][hh]
                        Gc, Gn = Gst[gi], Gst[1 - gi]
                        Gcb, Gnb = Gb[gi][c3][hh], Gb[1 - gi][c3][hh]
                        mm(psum[4 + hh][hs, cs_], Gc[hs, c3, :], QcT[hs, cs_], True, True, [Gcb, QcT.b], [psum[4 + hh].b])
                        mm(psum[hh][hs, cs_], McT[hs, ch, :], Gc[hs, c3, :], True, True, [McTt.b, Gcb], [psum[hh].b])
                        tt(DVE, Gn[hs, c3, :], psum[hh][hs, cs_], D1sb[hs, cs_], ALU.add, [psum[hh].b, D1sb.b], [Gnb])
                        gidx[c3][hh] = 1 - gi
                    advance(nxtA, int(os.environ.get("ADV", "0")))
                for hh in range(2):
                    hs = slice(64 * hh, 64 * hh + 64)
                    tt(DVE, f["Y"][hs, 0:BW], psum[4 + hh][hs, 0:BW], Y0sb[hs, 0:BW], ALU.add, [psum[4 + hh].b, Y0sb.b], [f["Y"].b])
                P.tag = "R.post"
                act(b["Ybf"][:, :], f["Y"][:, :], AF.Copy, [f["Y"].b], [b["Ybf"].b])
                act(b["Ysq"][:, :], f["Y"][:, :], AF.Square, [f["Y"].b], [b["Ysq"].b])
                bm = next_bank()
                mm(bm[:, 0:BW], bones64[:, :], b["Ybf"][:, :], True, True, [bones64.b, b["Ybf"].b], [bm.b])
                cp(ACT, f["tmp"][:, :], bm[:, 0:BW], [bm.b], [f["tmp"].b])
                bq = next_bank()
                mm(bq[:, 0:BW], bones64[:, :], b["Ysq"][:, :], True, True, [bones64.b, b["Ysq"].b], [bq.b])
                act(f["lgx"][:, :], bm[:, 0:BW], AF.Square, [bm.b], [f["lgx"].b])
                tt(DVE, f["lgx"][:, :], bq[:, 0:BW], f["lgx"][:, :], ALU.subtract, [bq.b, f["lgx"].b], [f["lgx"].b])
                ts(DVE, f["lgx"][:, :], f["lgx"][:, :], 0.0, None, ALU.max, None, [f["lgx"].b], [f["lgx"].b])
                act(f["lgx"][:, :], f["lgx"][:, :], AF.Ln, [f["lgx"].b], [f["lgx"].b], bias=GN_EPS)
                act(f["lgx"][:, :], f["lgx"][:, :], AF.Exp, [f["lgx"].b], [f["lgx"].b], scale=-0.5)
                tt(DVE, f["Y"][:, :], f["Y"][:, :], f["tmp"][:, :], ALU.subtract, [f["Y"].b, f["tmp"].b], [f["Y"].b])
                tt(DVE, f["Y"][:, :], f["Y"][:, :], f["lgx"][:, :], ALU.mult, [f["Y"].b, f["lgx"].b], [f["Y"].b])
                ts(DVE, f["Y"][:, :], f["Y"][:, :], lg_t[:, l, c3:c3 + 1], lb_t[:, l, c3:c3 + 1], ALU.mult, ALU.add,
                   [f["Y"].b, lg_t.b, lb_t.b], [f["Y"].b])
                tt(DVE, f["Y"][:, :], f["Y"][:, :], f["bonus"][:, :], ALU.add, [f["Y"].b, f["bonus"].b], [f["Y"].b])
                tt(POOL, oT[:, c3, blk * BW:(blk + 1) * BW], f["Y"][:, :], gaT[:, :], ALU.mult, [f["Y"].b, gaT.b], [oTb[c3]])
                drain(nxtA)

        for c3 in range(3):
            for hh in range(2):
                hb = 64 * hh
                hs = slice(hb, hb + 64)
                gi = gidx[c3][hh]
                h = 2 * c3 + hh
                fb_ = psum[3] if hh == 0 else psum[2]
                tr(fb_[0:64, 128:192], Gst[gi][hs, c3, :], C["identf"][hs, hs], [Gb[gi][c3][hh], C["identf"].b], [fb_.b])
                cp(DVE, wst[0:64, h, :], fb_[0:64, 128:192], [fb_.b], [wst.b])
        P.dma(POOL, O["wkv"][l, si].rearrange("h v k -> v h k"), wst[0:64, :, :], reads=[wst.b])

    for kind, n in (("p", NP), ("s", NS)):
        for si in range(n):
            T = T_P if kind == "p" else T_S
            past = 0 if kind == "p" else PAST
            cx = SimpleNamespace(kind=kind, si=si, T=T, past=past, NK=past + T, TT=min(128, T), NTT=T // min(128, T),
                                 BW=min(512, T), NBLK=T // min(512, T), NKT=(past + T + 127) // 128, PKT=past // 128,
                                 xsrc=(xp if kind == "p" else xs_), yidx=(si if kind == "p" else NPm + si),
                                 O={k[1:]: v for k, v in outs.items() if k[0] == kind})
            for l in range(NL):
                cx.l = l
                if l == 0:
                    phase_X(cx)
                if "t" not in PHASES:
                    phase_T(cx)
                if "R" in PHASES:
                    phase_R(cx)
                if "F" in PHASES:
                    phase_F(cx)
                if "S" in PHASES:
                    phase_S(cx)
                if dbg and "oT" in dbg_out and l == dbg.get("_layer", 0) and T == dbg["oT"][2] and si == 0:
                    P.dma(POOL, dbg_out["oT"], oT[:, :, 0:T], reads=oTb)
                if "e" not in PHASES:
                    phase_E(cx)
    P.finish()
    global LASTP
    LASTP = P
    return nc


_NC_CACHE = {}


def kernel(**inputs):
    n = 8
    NP, NS = 32 // n, 32 // n
    key = (NP, NS)
    consts = host_consts()
    in_maps = []
    f32 = lambda a: np.ascontiguousarray(np.asarray(a, dtype=np.float32))
    for c in range(n):
        ps, ss = slice(c * NP, (c + 1) * NP), slice(c * NS, (c + 1) * NS)
        m = {
            "x_prompt": f32(inputs["x_prompt"][ps]),
            "x_sample": f32(inputs["x_sample"][ss]),
            "cache_fox_k": f32(np.asarray(inputs["cache_fox_k"])[:, ss].reshape(NL, NS, PAST, 384)),
            "cache_fox_v": f32(np.asarray(inputs["cache_fox_v"])[:, ss].reshape(NL, NS, PAST, 384)),
            "cache_fox_logf": f32(np.asarray(inputs["cache_fox_logf"])[:, ss]),
            "cache_sb_k": f32(np.asarray(inputs["cache_sb_k"])[:, ss].reshape(NL, NS, PAST, 256)),
            "cache_sb_v": f32(np.asarray(inputs["cache_sb_v"])[:, ss].reshape(NL, NS, PAST, 256)),
            "state_wkv": f32(np.asarray(inputs["state_wkv"])[:, ss]),
            "state_shift": f32(np.asarray(inputs["state_shift"])[:, ss].reshape(NL, NS, SHIFT_W)),
            "r_k": f32(np.asarray(inputs["r_k"]).reshape(NL, 384)),
        }
        for k in ("w_in", "mu_shift", "w0_decay", "w_decay", "a0", "w_aaa", "k_k", "k_a", "lnx_g", "lnx_b",
                  "fox_fb", "w_out", "ln_g", "ln_b"):
            m[k] = f32(inputs[k])
        for k, v in consts.items():
            m["c_" + k] = v
        in_maps.append(m)
    nc = build_program(NP, NS)
    res = run_bass_kernel_spmd(nc, in_maps, core_ids=list(range(n)))
    R = res.results

    def cat(name, axis, shape=None):
        a = np.concatenate([np.asarray(r[name], dtype=np.float32) for r in R], axis=axis)
        return a.reshape(shape) if shape is not None else a
    B = 32
    out = (
        cat("p_y", 0), cat("s_y", 0),
        cat("p_fox_k", 1, (NL, B, T_P, 6, 64)), cat("p_fox_v", 1, (NL, B, T_P, 6, 64)), cat("p_fox_logf", 1),
        cat("p_sb_k", 1, (NL, B, T_P, 4, 64)), cat("p_sb_v", 1, (NL, B, T_P, 4, 64)),
        cat("p_wkv", 1), cat("p_shift", 1, (NL, B, 1, SHIFT_W)),
        cat("s_fox_k", 1, (NL, B, T_S, 6, 64)), cat("s_fox_v", 1, (NL, B, T_S, 6, 64)), cat("s_fox_logf", 1),
        cat("s_sb_k", 1, (NL, B, T_S, 4, 64)), cat("s_sb_v", 1, (NL, B, T_S, 4, 64)),
        cat("s_wkv", 1), cat("s_shift", 1, (NL, B, 1, SHIFT_W)),
    )
    return out
```

```python
import contextlib
import os
import numpy as np
import concourse.bass as bass
import concourse.mybir as mybir
from concourse.bass_utils import run_bass_kernel_spmd

F32 = mybir.dt.float32
BF16 = mybir.dt.bfloat16
ALU = mybir.AluOpType
AF = mybir.ActivationFunctionType

PE, ACT, DVE, POOL, SP = "tensor", "scalar", "vector", "gpsimd", "sync"
ENGS = (PE, ACT, DVE, POOL, SP)
SEM_WRAP = 30000
ANNOTATE = bool(os.environ.get("ANNOTATE"))
HEAT_S = int(os.environ.get("HEAT_S", "1"))
HEAT_F = int(os.environ.get("HEAT_F", "0"))
HEAT_R = int(os.environ.get("HEAT_R", "0"))
ADV2 = int(os.environ.get("ADV2", "0"))

D = 1024
T_P = 2048
T_S = 64
PAST = 2048
NL = 2
INC = 4230
SHIFT_W = 1280
ALPHA = (2 * NL) ** 0.25
GN_EPS = 64e-5
LN_EPS = 1e-5
NEG = -30000.0
DEC_SCALE = -float(np.exp(-0.5))

FT = ([(128 * i, 128) for i in range(10)] +
      [(1280 + 128 * i, 128) for i in range(3)] +
      [(1664 + 128 * i, 128) for i in range(3)] +
      [(2048 + 128 * i, 128) for i in range(3)] +
      [(2822 + 128 * i, 128) for i in range(3)] +
      [(3206 + 128 * i, 128) for i in range(2)] +
      [(3462 + 128 * i, 128) for i in range(2)] +
      [(3974 + 128 * i, 128) for i in range(2)] +
      [(2816, 6)])
FT_GA, FT_QB, FT_KB, FT_GB, FT_QC, FT_KC, FT_GC, FT_FB = 10, 13, 16, 19, 22, 24, 26, 28
NFT = len(FT)
TG = [(2048, 384), (2432, 390), (3462, 512)]


class Buf:
    __slots__ = ("name", "w", "r", "excl")

    def __init__(self, name, excl=False):
        self.name = name
        self.w = None
        self.r = []
        self.excl = excl


class Prog:
    def __init__(self, nc, n_dma_sems=64):
        self.nc = nc
        self.q = {e: [] for e in ENGS}
        self.nsem = 0
        self.cur = {}
        self.cnt = {}
        self.allsems = {e: [] for e in ENGS}
        for e in ENGS:
            if e != SP:
                self.cur[e] = self._newsem()
                self.cnt[e] = 0
                self.allsems[e].append(self.cur[e])
        self.dma_pool = {SP: [self._newsem() for _ in range(20)], POOL: [self._newsem() for _ in range(12)]}
        self.dma_cnt = {s: 0 for e in self.dma_pool for s in self.dma_pool[e]}
        self.dma_next = {e: 0 for e in self.dma_pool}
        self.waited = {e: {} for e in ENGS}
        self.bar = {e: [] for e in ENGS}
        self.final = {}
        self.tag = None

    def _newsem(self):
        k = self.nsem
        self.nsem += 1
        return k

    def _need(self, eng, ev, waits):
        if ev is None:
            return
        k, v = ev[0], ev[1]
        if self.waited[eng].get(k, 0) >= v:
            return
        self.waited[eng][k] = v
        waits.append((k, v))

    def barrier(self):
        evs = []
        for e in ENGS:
            if e != SP and self.cnt[e]:
                evs.append((self.cur[e], self.cnt[e], e, False))
        for s, c in self.dma_cnt.items():
            if c:
                evs.append((s, 16 * c, None, True))
        for e in ENGS:
            self.bar[e] = self.bar[e] + evs

    def op(self, eng, fn, reads=(), writes=(), is_dma=False):
        waits = []
        if self.bar[eng]:
            for ev in self.bar[eng]:
                self._need(eng, ev, waits)
            self.bar[eng] = []
        for b in reads:
            self._need(eng, b.w, waits)
            if b.excl:
                for ev in b.r:
                    if ev[2] != eng:
                        self._need(eng, ev, waits)
        for b in writes:
            w = b.w
            if w is not None and not (w[2] == eng and not w[3] and not is_dma and eng != POOL):
                self._need(eng, w, waits)
            for ev in b.r:
                if ev[2] == eng and not is_dma and not ev[3] and eng != POOL:
                    continue
                self._need(eng, ev, waits)
        if is_dma:
            pool = self.dma_pool[eng]
            s = pool[self.dma_next[eng]]
            self.dma_next[eng] = (self.dma_next[eng] + 1) % len(pool)
            prev = self.dma_cnt[s]
            if prev:
                self._need(eng, (s, 16 * prev, None, True), waits)
            self.dma_cnt[s] = prev + 1
            ev = (s, 16 * (prev + 1), eng, True)
            inc = (s, 16)
        else:
            if self.cnt[eng] >= SEM_WRAP:
                self.final[self.cur[eng]] = self.cnt[eng]
                self.cur[eng] = self._newsem()
                self.cnt[eng] = 0
            self.cnt[eng] += 1
            ev = (self.cur[eng], self.cnt[eng], eng, False)
            inc = (self.cur[eng], 1)
        m = {}
        for k, v in waits:
            m[k] = max(m.get(k, 0), v)
        self.q[eng].append((list(m.items()), fn, inc, self.tag))
        for b in reads:
            b.r.append(ev)
            if len(b.r) > 64:
                b.r = b.r[-64:] if False else b.r
        for b in writes:
            b.w = ev
            b.r = []
        return ev

    def dma(self, eng, out, in_, reads=(), writes=(), **kw):
        def fn(e, out=out, in_=in_, kw=kw):
            return e.dma_start(out=out, in_=in_, **kw)
        return self.op(eng, fn, reads, writes, is_dma=True)

    def finish(self):
        nc = self.nc
        fw = dict(self.final)
        for e in ENGS:
            if e != SP and self.cnt[e]:
                fw[self.cur[e]] = self.cnt[e]
        for s, c in self.dma_cnt.items():
            if c:
                fw[s] = 16 * c
        with contextlib.ExitStack() as es:
            sems = [es.enter_context(nc.semaphore(f"s{i}")) for i in range(self.nsem)]
            es.enter_context(nc.allow_non_contiguous_dma("small strided parameter / state transfers"))
            block = es.enter_context(nc.Block())

            def emit(engname):
                def body(e):
                    for waits, fn, inc, tag in self.q[engname]:
                        for k, v in waits:
                            e.wait_ge(sems[k], v)
                        ins = fn(e)
                        ins.then_inc(sems[inc[0]], inc[1])
                        if tag and ANNOTATE:
                            ins.annotate(tag)
                    if engname == SP:
                        for k, v in fw.items():
                            e.wait_ge(sems[k], v)
                return body
            block.tensor(emit(PE))
            block.scalar(emit(ACT))
            block.vector(emit(DVE))
            block.gpsimd(emit(POOL))
            block.sync(emit(SP))


class Tl:
    def __init__(self, h, name):
        self.h = h
        self.b = Buf(name)

    def __getitem__(self, k):
        return self.h[k]


class Mem:
    def __init__(self, nc, base, limit):
        self.nc, self.top, self.limit, self.n = nc, base, limit, 0

    def alloc(self, name, shape, dt):
        sz = int(np.prod(shape[1:])) * (4 if dt == F32 else 2)
        sz = (sz + 31) // 32 * 32
        self.n += 1
        h = self.nc.alloc_sbuf_tensor_at(f"{name}_{self.n}", list(shape), dt, offset=self.top)
        self.top += sz
        assert self.top <= self.limit, f"SBUF overflow at {name}: {self.top}"
        return Tl(h, name)


def host_consts():
    c = {}
    c["identf"] = np.eye(128, dtype=np.float32)
    bo = np.zeros((128, 128), np.float32)
    bo[:64, :64] = 1
    bo[64:, 64:] = 1
    c["bones"] = bo
    cm = np.ones((128, 512), np.float32)
    cm[:, ::64] = 0
    c["chunkmask"] = cm
    i = np.arange(128)[:, None]
    j = np.arange(128)[None, :]
    same = (i // 64) == (j // 64)
    sl = (same & (j < i)).astype(np.float32)
    su = (same & (i < j)).astype(np.float32)
    iu = (same & (i <= j)).astype(np.float32)
    c["rwmask"] = np.concatenate([-sl, -su, su, iu], axis=1)
    c["iumask"] = iu
    sg = np.ones((128, 128), np.float32)
    sg[:, :64] = -1
    c["signs"] = sg
    kl = np.arange(128)[:, None]
    cc = np.arange(896)[None, :]
    c["negle"] = np.where(kl <= cc - 384, 0.0, NEG).astype(np.float32)
    c["neglt"] = np.where(kl < cc - 384, 0.0, NEG).astype(np.float32)
    c["trineg"] = np.where(i >= j, -1.0, 0.0).astype(np.float32)
    c["iumask4"] = np.tile(iu, (1, 4))
    c["signs4"] = np.tile(sg, (1, 4))
    c["ident2"] = np.concatenate([np.eye(64, dtype=np.float32)] * 2, axis=0)
    ci = np.zeros((128, 2), np.float32)
    ci[:64, 0] = 1
    ci[64:, 1] = 1
    c["cind"] = ci
    return c


CONST_SHAPES = dict(identf=(128, 128), bones=(128, 128), chunkmask=(128, 512), rwmask=(128, 512),
                    iumask=(128, 128), signs=(128, 128), negle=(128, 896), neglt=(128, 896),
                    trineg=(128, 128), cind=(128, 2), iumask4=(128, 512), signs4=(128, 512),
                    ident2=(128, 64))


def build_program(NP, NS, dbg=None):
    nc = bass.Bass("TRN2", target_bir_lowering=False)
    P = Prog(nc)

    def din(name, shape):
        return nc.dram_tensor(name, list(shape), F32, kind="ExternalInput").ap()

    def dout(name, shape):
        return nc.dram_tensor(name, list(shape), F32, kind="ExternalOutput").ap()

    NPm, NSm = max(NP, 1), max(NS, 1)
    xp = din("x_prompt", (NPm, T_P, D))
    xs_ = din("x_sample", (NSm, T_S, D))
    cfk = din("cache_fox_k", (NL, NSm, PAST, 384))
    cfv = din("cache_fox_v", (NL, NSm, PAST, 384))
    cfl = din("cache_fox_logf", (NL, NSm, PAST, 6))
    csk = din("cache_sb_k", (NL, NSm, PAST, 256))
    csv = din("cache_sb_v", (NL, NSm, PAST, 256))
    swkv = din("state_wkv", (NL, NSm, 6, 64, 64))
    sshift = din("state_shift", (NL, NSm, SHIFT_W))
    w_in = din("w_in", (NL, D, INC))
    mu_shift = din("mu_shift", (NL, SHIFT_W))
    w0_decay = din("w0_decay", (NL, 384))
    w_decay = din("w_decay", (NL, 64, 384))
    a0 = din("a0", (NL, 384))
    w_aaa = din("w_aaa", (NL, 64, 384))
    k_k = din("k_k", (NL, 384))
    k_a = din("k_a", (NL, 384))
    r_k = din("r_k", (NL, 384))
    lnx_g = din("lnx_g", (NL, 384))
    lnx_b = din("lnx_b", (NL, 384))
    fox_fb = din("fox_fb", (NL, 6))
    w_out = din("w_out", (NL, D, D))
    ln_g = din("ln_g", (NL, D))
    ln_b = din("ln_b", (NL, D))
    cdr = {k: din("c_" + k, s) for k, s in CONST_SHAPES.items()}

    outs = {}
    for pre, n, t in (("p", NPm, T_P), ("s", NSm, T_S)):
        outs[pre + "y"] = dout(pre + "_y", (n, t, D))
        outs[pre + "fk"] = dout(pre + "_fox_k", (NL, n, t, 384))
        outs[pre + "fv"] = dout(pre + "_fox_v", (NL, n, t, 384))
        outs[pre + "fl"] = dout(pre + "_fox_logf", (NL, n, t, 6))
        outs[pre + "sk"] = dout(pre + "_sb_k", (NL, n, t, 256))
        outs[pre + "sv"] = dout(pre + "_sb_v", (NL, n, t, 256))
        outs[pre + "wkv"] = dout(pre + "_wkv", (NL, n, 6, 64, 64))
        outs[pre + "sh"] = dout(pre + "_shift", (NL, n, SHIFT_W))
    dbg_out = {}
    if dbg:
        for k, s in dbg.items():
            if not k.startswith("_"):
                dbg_out[k] = dout("dbg_" + k, s)

    WF = nc.dram_tensor("WF", [NL, NFT, 128, 8, 128], BF16).ap()
    WT = nc.dram_tensor("WT", [NL, 3, 128, 8, 512], BF16).ap()
    WO = nc.dram_tensor("WO", [NL, 128, 8, 1024], BF16).ap()
    Y0 = nc.dram_tensor("Y0", [NPm + NSm, T_P, D], F32).ap()
    bWF = [[Buf(f"WF{l}_{t}") for t in range(NFT)] for l in range(NL)]
    bWT = [[Buf(f"WT{l}_{g}") for g in range(3)] for l in range(NL)]
    bWO = [Buf(f"WO{l}") for l in range(NL)]
    bY0 = [[Buf(f"Y0_{i}_{t}") for t in range(16)] for i in range(NPm + NSm)]

    for l in range(NL):
        wv = w_in[l].rearrange("(c p) n -> p c n", p=128)
        for g, (s0, n) in enumerate(TG):
            P.dma(POOL, WT[l, g, :, :, 0:n], wv[:, :, s0:s0 + n], writes=[bWT[l][g]])
        use_order = [9, 0, 3, 6, 10, 1, 4, 7, 11, 2, 5, 8, 12, FT_FB] + list(range(13, 28))
        for t in use_order:
            s0, n = FT[t]
            P.dma(POOL, WF[l, t, :, :, 0:n], wv[:, :, s0:s0 + n], writes=[bWF[l][t]])
        wov = w_out[l].rearrange("(c p) n -> p c n", p=128)
        for hfi in range(2):
            P.dma(POOL, WO[l, :, :, hfi * 512:(hfi + 1) * 512], wov[:, :, hfi * 512:(hfi + 1) * 512],
                  writes=[bWO[l]])

    mem = Mem(nc, 16640, 228000)
    psum = [Tl(nc.alloc_psum_tensor(f"pb{i}", [128, 512], F32), f"pb{i}") for i in range(8)]
    for p_ in psum:
        p_.b.excl = True

    def pbf(i):
        return psum[i].h.ap().bitcast(BF16)

    C = {}
    for k, s in CONST_SHAPES.items():
        C[k] = mem.alloc("c_" + k, s, F32)
        P.dma(SP, C[k][:], cdr[k], writes=[C[k].b])
    identb = mem.alloc("identb", (128, 128), BF16)
    bonesb = mem.alloc("bonesb", (128, 128), BF16)
    bones64 = mem.alloc("bones64", (128, 128), BF16)
    ones64 = mem.alloc("ones64", (128, 64), BF16)
    onesneg = mem.alloc("onesneg", (128, 128), BF16)
    trinegb = mem.alloc("trinegb", (128, 128), BF16)
    negleb = mem.alloc("negleb", (128, 896), BF16)
    negltb = mem.alloc("negltb", (128, 896), BF16)
    onecol = mem.alloc("onecol", (128, 1), F32)
    ident8 = mem.alloc("ident8", (128, 8, 128), BF16)
    for u_ in range(8):
        P.op(DVE, lambda e, u_=u_: e.tensor_copy(out=ident8[:, u_, :], in_=C["identf"][:]), reads=[C["identf"].b], writes=[ident8.b])
    P.op(DVE, lambda e: e.tensor_copy(out=identb[:], in_=C["identf"][:]), reads=[C["identf"].b], writes=[identb.b])
    P.op(DVE, lambda e: e.tensor_copy(out=bonesb[:], in_=C["bones"][:]), reads=[C["bones"].b], writes=[bonesb.b])
    P.op(DVE, lambda e: e.tensor_scalar(out=bones64[:], in0=C["bones"][:], scalar1=1.0 / 64, scalar2=None, op0=ALU.mult),
         reads=[C["bones"].b], writes=[bones64.b])
    P.op(DVE, lambda e: e.memset(ones64[:], 1.0), writes=[ones64.b])
    P.op(DVE, lambda e: e.memset(onesneg[:], -1.0), writes=[onesneg.b])
    P.op(DVE, lambda e: e.memset(onecol[:], 1.0), writes=[onecol.b])
    P.op(DVE, lambda e: e.tensor_copy(out=trinegb[:], in_=C["trineg"][:]), reads=[C["trineg"].b], writes=[trinegb.b])
    P.op(DVE, lambda e: e.tensor_copy(out=negleb[:], in_=C["negle"][:]), reads=[C["negle"].b], writes=[negleb.b])
    P.op(DVE, lambda e: e.tensor_copy(out=negltb[:], in_=C["neglt"][:]), reads=[C["neglt"].b], writes=[negltb.b])

    def load_cols(name, src, ntile):
        t = mem.alloc(name, (128, NL, ntile), F32)
        with nc.allow_non_contiguous_dma("small param transpose load"):
            for l in range(NL):
                P.dma(SP, t[:, l, :], src[l].rearrange("(t p) -> p t", p=128), writes=[t.b])
        return t
    mu_t = load_cols("mu", mu_shift, 10)
    omm_t = mem.alloc("omm", (128, NL, 10), F32)
    P.op(DVE, lambda e: e.tensor_scalar(out=omm_t[:], in0=mu_t[:], scalar1=-1.0, scalar2=1.0, op0=ALU.mult, op1=ALU.add),
         reads=[mu_t.b], writes=[omm_t.b])
    w0_t = load_cols("w0", w0_decay, 3)
    a0_t = load_cols("a0", a0, 3)
    kk_t = load_cols("kk", k_k, 3)
    ka_t = load_cols("ka", k_a, 3)
    rk_t = load_cols("rk", r_k, 3)
    lg_t = load_cols("lxg", lnx_g, 3)
    lb_t = load_cols("lxb", lnx_b, 3)
    omka_t = mem.alloc("omka", (128, NL, 3), F32)
    P.op(DVE, lambda e: e.tensor_scalar(out=omka_t[:], in0=ka_t[:], scalar1=-1.0, scalar2=1.0, op0=ALU.mult, op1=ALU.add),
         reads=[ka_t.b], writes=[omka_t.b])
    nfb_t = mem.alloc("nfb", (128, NL), F32)
    with nc.allow_non_contiguous_dma("small param transpose load"):
        P.dma(SP, nfb_t[0:6, :], fox_fb.rearrange("l h -> h l"), writes=[nfb_t.b])
    P.op(DVE, lambda e: e.tensor_scalar(out=nfb_t[0:6, :], in0=nfb_t[0:6, :], scalar1=-1.0, scalar2=None, op0=ALU.mult),
         reads=[nfb_t.b], writes=[nfb_t.b])
    fbb = mem.alloc("fbb", (128, NL, 6), F32)
    P.dma(SP, fbb[:].rearrange("p l h -> p (l h)"), fox_fb.rearrange("l h -> (l h)").partition_broadcast(128), writes=[fbb.b])
    lwf = mem.alloc("lwf", (128, NL, 384), F32)
    lw = mem.alloc("lw", (128, NL, 384), BF16)
    for l in range(NL):
        P.dma(SP, lwf[0:64, l, :], w_decay[l], writes=[lwf.b])
        P.dma(SP, lwf[64:128, l, :], w_aaa[l], writes=[lwf.b])
    P.op(DVE, lambda e: e.tensor_copy(out=lw[:], in_=lwf[:]), reads=[lwf.b], writes=[lw.b])

    xT = mem.alloc("xT", (128, 8, T_P), BF16)
    oT = mem.alloc("oT", (128, 8, T_P), BF16)
    oTb = [Buf(f"oT{c}") for c in range(8)]
    Vb = mem.alloc("Vb", (128, 17, 384), BF16)
    Vc = mem.alloc("Vc", (128, 17, 256), BF16)
    Gst = [mem.alloc(f"G{i}", (128, 3, 64), F32) for i in range(2)]
    Gb = [[[Buf(f"G{i}_{c3}_{hh}") for hh in range(2)] for c3 in range(3)] for i in range(2)]
    ucarry = mem.alloc("ucarry", (128, 10), F32)
    ucb = [Buf(f"uc{ct}") for ct in range(10)]
    phase_base = mem.top
    if dbg:
        for c in range(8):
            P.op(POOL, lambda e, c=c: e.memset(oT[:, c, :], 0.0), writes=[oTb[c]])

    bank_rr = [0]

    def next_bank(lo=0, hi=2):
        i = lo + bank_rr[0] % (hi - lo)
        bank_rr[0] += 1
        return psum[i]

    wslot_rr = [0]

    from types import SimpleNamespace
    PHASES = set(dbg["_phases"]) if (dbg and "_phases" in dbg) else set("RFS")

    def dump(name, ap, reads):
        if name in dbg_out:
            P.dma(POOL, dbg_out[name], ap, reads=reads)

    def act(out, in_, func, R, W, bias=None, scale=None):
        kw = {}
        if bias is not None:
            kw["bias"] = bias
        if scale is not None:
            kw["scale"] = scale
        P.op(ACT, lambda e: e.activation(out=out, in_=in_, func=func, **kw), reads=R, writes=W)

    def tt(eng, out, in0, in1, op, R, W):
        P.op(eng, lambda e: e.tensor_tensor(out=out, in0=in0, in1=in1, op=op), reads=R, writes=W)

    def ts(eng, out, in0, s1, s2, op0, op1, R, W):
        if op1 is None and eng == POOL and op0 == ALU.mult:
            op1, s2 = ALU.mult, 1.0
        if op1 is None:
            P.op(eng, lambda e: e.tensor_scalar(out=out, in0=in0, scalar1=s1, scalar2=None, op0=op0), reads=R, writes=W)
        else:
            P.op(eng, lambda e: e.tensor_scalar(out=out, in0=in0, scalar1=s1, scalar2=s2, op0=op0, op1=op1), reads=R, writes=W)

    def stt(out, in0, scalar, in1, op0, op1, R, W):
        P.op(DVE, lambda e: e.scalar_tensor_tensor(out=out, in0=in0, scalar=scalar, in1=in1, op0=op0, op1=op1), reads=R, writes=W)

    def cp(eng, out, in_, R, W):
        if eng == ACT:
            P.op(ACT, lambda e: e.activation(out=out, in_=in_, func=AF.Copy), reads=R, writes=W)
        else:
            P.op(eng, lambda e: e.tensor_copy(out=out, in_=in_), reads=R, writes=W)

    heat = {"n": 0, "bank": None}

    def mm(out, lhsT, rhs, start, stop, R, W):
        P.op(PE, lambda e: e.matmul(out, lhsT=lhsT, rhs=rhs, start=start, stop=stop), reads=R, writes=W)
        if heat["n"] and stop:
            hb_ = heat["bank"]
            for _ in range(heat["n"]):
                P.op(PE, lambda e: e.matmul(hb_[:, 0:512], lhsT=identb[:, :], rhs=negltb[:, 0:512], start=True, stop=True),
                     reads=[identb.b, negltb.b], writes=[hb_.b])

    def tr(out, in_, ident, R, W):
        P.op(PE, lambda e: e.transpose(out=out, in_=in_, identity=ident), reads=R, writes=W)

    def memset(eng, ap, val, W):
        P.op(eng, lambda e: e.memset(ap, val), writes=W)

    def load_w(cx, slots, t):
        ws = slots[wslot_rr[0] % len(slots)]
        wslot_rr[0] += 1
        n = FT[t][1]
        P.dma(SP, ws[:, :, 0:n], WF[cx.l, t, :, :, 0:n], reads=[bWF[cx.l][t]], writes=[ws.b])
        return ws

    def project(cx, ws, ncol, blk, bank):
        cols = slice(blk * cx.BW, (blk + 1) * cx.BW)
        for c in range(8):
            mm(bank[0:ncol, 0:cx.BW], ws[:, c, 0:ncol], xT[:, c, cols], c == 0, c == 7, [ws.b, xT.b], [bank.b])

    def to_xT(cx, src, tt_, banks):
        TT = cx.TT
        for c in range(8):
            tr(banks[c // 4][:, (c % 4) * TT:(c % 4 + 1) * TT], src[:TT, c * 128:(c + 1) * 128],
               C["identf"][:TT, :TT], [src.b, C["identf"].b], [banks[c // 4].b])
        for j in range(2):
            s_ = banks[j][:, 0:4 * TT].rearrange("p (c t) -> p c t", t=TT)
            d_ = xT[:, 4 * j:4 * j + 4, tt_ * TT:(tt_ + 1) * TT]
            cp(ACT if j == 0 else DVE, d_, s_, [banks[j].b], [xT.b])

    def phase_X(cx):
        P.tag = "X"
        mem.top = phase_base
        P.barrier()
        xin = [mem.alloc(f"xin{i}", (128, D), F32) for i in range(2)]
        for tt_ in range(cx.NTT):
            sl = xin[tt_ % 2]
            P.dma(SP, sl[:cx.TT, :], cx.xsrc[cx.si, tt_ * cx.TT:(tt_ + 1) * cx.TT, :], writes=[sl.b])
            bk = (psum[0], psum[1]) if tt_ % 2 == 0 else (psum[2], psum[3])
            to_xT(cx, sl, tt_, bk)

    def phase_T(cx):
        P.tag = "T"
        l, si, TT, O = cx.l, cx.si, cx.TT, cx.O
        mem.top = phase_base
        P.barrier()
        wt = [mem.alloc(f"wt{g}", (128, 8, 512), BF16) for g in range(3)]
        stage = [mem.alloc(f"stg{i}", (128, 1288), F32) for i in range(2)]
        for g in range(3):
            P.dma(SP, wt[g][:, :, 0:TG[g][1]], WT[l, g, :, :, 0:TG[g][1]], reads=[bWT[l][g]], writes=[wt[g].b])
        if cx.kind == "s":
            P.dma(POOL, Vb[:, 0:16, :], cfv[l, si].rearrange("(t p) c -> p t c", p=128), writes=[Vb.b])
            P.dma(POOL, Vc[:, 0:16, :], csv[l, si].rearrange("(t p) c -> p t c", p=128), writes=[Vc.b])
        for tt_ in range(min(cx.NTT, int(os.environ.get("DBG_NTT", "99")))):
            st = stage[tt_ % 2]
            rows = slice(tt_ * TT, (tt_ + 1) * TT)
            bks = [psum[3 * (tt_ % 2) + g] for g in range(3)]
            for g, (s0, n) in enumerate(TG):
                for c in range(8):
                    mm(bks[g][:TT, 0:n], xT[:, c, rows], wt[g][:, c, 0:n], c == 0, c == 7, [xT.b, wt[g].b], [bks[g].b])
            cp(ACT, st[:TT, 0:384], bks[0][:TT, 0:384], [bks[0].b], [st.b])
            cp(DVE, st[:TT, 384:768], bks[1][:TT, 0:384], [bks[1].b], [st.b])
            cp(ACT, st[:TT, 768:1280], bks[2][:TT, 0:512], [bks[2].b], [st.b])
            tt(DVE, st[:TT, 1280:1286], bks[1][:TT, 384:390], fbb[:TT, l, :], ALU.add, [bks[1].b, fbb.b], [st.b])
            act(st[:TT, 1280:1286], st[:TT, 1280:1286], AF.Exp, [st.b], [st.b], scale=-1.0)
            act(st[:TT, 1280:1286], st[:TT, 1280:1286], AF.Ln, [st.b], [st.b], bias=1.0)
            ts(DVE, st[:TT, 1280:1286], st[:TT, 1280:1286], -1.0, None, ALU.mult, None, [st.b], [st.b])
            P.dma(SP, O["fl"][l, si, rows, :], st[:TT, 1280:1286], reads=[st.b])
            DBGT = int(os.environ.get("DBG_T", "9"))
            if DBGT in (2, 4, 9):
                cp(DVE, Vb[:TT, cx.PKT + tt_, :], bks[1][:TT, 0:384], [bks[1].b], [Vb.b])
            if DBGT in (2, 5, 9):
                cp(DVE, Vc[:TT, cx.PKT + tt_, :], st[:TT, 1024:1280], [st.b], [Vc.b])
            if DBGT != 9:
                continue
            P.dma(SP, O["fk"][l, si, rows, :], st[:TT, 0:384], reads=[st.b])
            P.dma(SP, O["fv"][l, si, rows, :], st[:TT, 384:768], reads=[st.b])
            P.dma(SP, O["sk"][l, si, rows, :], st[:TT, 768:1024], reads=[st.b])
            P.dma(SP, O["sv"][l, si, rows, :], st[:TT, 1024:1280], reads=[st.b])

    def phase_F(cx):
        P.tag = "F.pre"
        l, si, T, NK, BW, QW, past, O = cx.l, cx.si, cx.T, cx.NK, cx.BW, cx.BW, cx.past, cx.O
        mem.top = phase_base
        P.barrier()
        wsl = [mem.alloc(f"wf{i}", (128, 8, 128), BF16) for i in range(4)]
        Qa = [mem.alloc(f"Qa{i}", (128, T), BF16) for i in range(2)]
        Ka = [mem.alloc(f"Ka{i}", (128, NK), BF16) for i in range(2)]
        gate = mem.alloc("gate", (128, T), BF16)
        Aa = mem.alloc("Aa", (128, NK), F32)
        Sa = mem.alloc("Sa", (128, NK), F32)
        SPL = mem.alloc("SPL", (128, 3, NK), BF16)
        pts = [mem.alloc(f"pt{i}", (128, QW), BF16) for i in range(4)]
        rD = [mem.alloc(f"rD{i}", (128, QW), F32) for i in range(2)]
        o1 = [mem.alloc(f"o1{i}", (128, QW), F32) for i in range(2)]
        kst = [mem.alloc(f"kst{i}", (128, 16, 128), F32) for i in range(2)] if cx.kind == "s" else None

        def load_kcache(hp_):
            k_ = kst[hp_ % 2]
            P.dma(SP, k_[:], cfk[l, si].rearrange("(t p) c -> p t c", p=128)[:, :, hp_ * 128:(hp_ + 1) * 128], writes=[k_.b])
        if cx.kind == "s":
            load_kcache(0)
        Vaug = mem.alloc("Vaug", (128, 17, 2, 128), BF16)
        memset(POOL, Vaug[:, :, :, :], 1.0, [Vaug.b])

        wfb = load_w(cx, wsl, FT_FB)
        if cx.kind == "s":
            lst = mem.alloc("lst", (128, 16, 6), F32)
            P.dma(SP, lst[:, :, :], cfl[l, si].rearrange("(t p) h -> p t h", p=128), writes=[lst.b])
            for t4 in range(4):
                bank = next_bank()
                for t in range(4):
                    tr(bank[0:6, t * 128:(t + 1) * 128], lst[:, 4 * t4 + t, :], C["identf"][:, :], [lst.b, C["identf"].b], [bank.b])
                cp(DVE if t4 % 2 else ACT, Aa[0:6, t4 * 512:(t4 + 1) * 512], bank[0:6, 0:512], [bank.b], [Aa.b])
        for blk in range(cx.NBLK):
            bank = next_bank()
            project(cx, wfb, 6, blk, bank)
            cols = slice(past + blk * BW, past + (blk + 1) * BW)
            act(Aa[0:6, cols], bank[0:6, 0:BW], AF.Exp, [bank.b, nfb_t.b], [Aa.b], bias=nfb_t[0:6, l:l + 1], scale=-1.0)
            act(Aa[0:6, cols], Aa[0:6, cols], AF.Ln, [Aa.b], [Aa.b], bias=1.0)
            ts(DVE, Aa[0:6, cols], Aa[0:6, cols], -1.0, None, ALU.mult, None, [Aa.b], [Aa.b])
        P.op(DVE, lambda e: e.tensor_tensor_scan(out=Sa[0:6, 0:NK], data0=onecol[0:6, 0:1].to_broadcast([6, NK]),
                                                 data1=Aa[0:6, 0:NK], initial=0.0, op0=ALU.mult, op1=ALU.subtract),
             reads=[Aa.b, onecol.b], writes=[Sa.b])
        cp(DVE, SPL[0:6, 0, :], Sa[0:6, 0:NK], [Sa.b], [SPL.b])
        tt(DVE, Aa[0:6, 0:NK], Sa[0:6, 0:NK], SPL[0:6, 0, :], ALU.subtract, [Sa.b, SPL.b], [Aa.b])
        cp(DVE, SPL[0:6, 1, :], Aa[0:6, 0:NK], [Aa.b], [SPL.b])
        tt(DVE, Aa[0:6, 0:NK], Aa[0:6, 0:NK], SPL[0:6, 1, :], ALU.subtract, [Aa.b, SPL.b], [Aa.b])
        cp(DVE, SPL[0:6, 2, :], Aa[0:6, 0:NK], [Aa.b], [SPL.b])

        DBGF = int(os.environ.get("DBG_F", "9"))
        if DBGF < 1:
            return
        for hp in range(3):
            P.tag = "F.proj"
            wq = load_w(cx, wsl, FT_QB + hp)
            wk = load_w(cx, wsl, FT_KB + hp)
            wg = load_w(cx, wsl, FT_GB + hp)
            for hh in range(2):
                h = 2 * hp + hh
                memset(POOL, Qa[hh][64:70, 0:T], 1.0, [Qa[hh].b])
                memset(POOL, Ka[hh][64:70, 0:NK], -1.0, [Ka[hh].b])
                for r in range(3):
                    P.dma(SP, Qa[hh][64 + r:65 + r, 0:T], SPL[h:h + 1, r, past:NK], reads=[SPL.b], writes=[Qa[hh].b])
                    P.dma(SP, Ka[hh][67 + r:68 + r, 0:NK], SPL[h:h + 1, r, 0:NK], reads=[SPL.b], writes=[Ka[hh].b])
            if cx.kind == "s":
                kcur = kst[hp % 2]
                for hh in range(2):
                    for t4 in range(4):
                        bank = next_bank()
                        for t in range(4):
                            tr(bank[0:64, t * 128:(t + 1) * 128], kcur[:, 4 * t4 + t, hh * 64:(hh + 1) * 64], C["identf"][:, :],
                               [kcur.b, C["identf"].b], [bank.b])
                        cp(DVE if t4 % 2 else ACT, Ka[hh][0:64, t4 * 512:(t4 + 1) * 512], bank[0:64, 0:512], [bank.b], [Ka[hh].b])
                if hp + 1 < 3:
                    load_kcache(hp + 1)
            for blk in range(cx.NBLK):
                cols = slice(blk * BW, (blk + 1) * BW)
                kcols = slice(past + blk * BW, past + (blk + 1) * BW)
                bq = next_bank()
                project(cx, wq, 128, blk, bq)
                act(Qa[0][0:64, cols], bq[0:64, 0:BW], AF.Copy, [bq.b], [Qa[0].b], scale=0.125)
                ts(DVE, Qa[1][0:64, cols], bq[64:128, 0:BW], 0.125, None, ALU.mult, None, [bq.b], [Qa[1].b])
                bk_ = next_bank()
                project(cx, wk, 128, blk, bk_)
                cp(ACT, Ka[0][0:64, kcols], bk_[0:64, 0:BW], [bk_.b], [Ka[0].b])
                cp(DVE, Ka[1][0:64, kcols], bk_[64:128, 0:BW], [bk_.b], [Ka[1].b])
                bg = next_bank()
                project(cx, wg, 128, blk, bg)
                act(gate[:, cols], bg[:, 0:BW], AF.Silu, [bg.b], [gate.b])
            P.tag = "F.attn"
            nfull = NK // 128
            for hh_ in range(2):
                vcols = slice((2 * hp + hh_) * 64, (2 * hp + hh_ + 1) * 64)
                acols = slice(64 * hh_, 64 * hh_ + 64)
                cp(POOL, Vaug[:, 0:nfull, hh_, acols], Vb[:, 0:nfull, vcols], [Vb.b], [Vaug.b])
                if NK % 128:
                    cp(POOL, Vaug[0:NK % 128, nfull, hh_, acols], Vb[0:NK % 128, nfull, vcols], [Vb.b], [Vaug.b])
            heat["n"], heat["bank"] = HEAT_F, psum[0]
            for hh in range(2):
                if DBGF < 2:
                    break
                h = 2 * hp + hh
                ob = 64 * hh
                db = 64 - ob
                for qt in range(T // QW):
                    q0 = qt * QW
                    qlo = past + q0
                    qhi = qlo + QW - 1
                    kts = [kt for kt in range(cx.NKT) if kt * 128 <= qhi]
                    Ob = psum[4 + (2 * hh + qt) % 4]

                    def stage2(kt, kn, pt, first, last):
                        mm(Ob[:, 0:QW], Vaug[0:kn, kt, hh, :], pt[0:kn, 0:QW], first, last, [Vaug.b, pt.b], [Ob.b])
                    pend = None
                    for i, kt in enumerate(kts):
                        kn = min(128, NK - kt * 128)
                        Sb = psum[i % 4]
                        partial = kt * 128 + kn - 1 > qlo
                        mm(Sb[0:kn, 0:QW], Ka[hh][0:70, kt * 128:kt * 128 + kn], Qa[hh][0:70, q0:q0 + QW], True, not partial,
                           [Ka[hh].b, Qa[hh].b], [Sb.b])
                        if partial:
                            d = kt * 128 - qlo
                            mm(Sb[0:kn, 0:QW], identb[0:kn, 0:kn], negleb[0:kn, 384 - d:384 - d + QW], False, True,
                               [identb.b, negleb.b], [Sb.b])
                        pt = pts[i % 4]
                        act(pt[0:kn, 0:QW], Sb[0:kn, 0:QW], AF.Exp, [Sb.b], [pt.b])
                        if pend:
                            stage2(*pend)
                        pend = (kt, kn, pt, i == 0, i == len(kts) - 1)
                    stage2(*pend)
                    rd, oo = rD[qt % 2], o1[qt % 2]
                    P.op(DVE, lambda e, o=rd[ob:ob + 64, 0:QW], i_=Ob[db:db + 64, 0:QW]: e.reciprocal(out=o, in_=i_),
                         reads=[Ob.b], writes=[rd.b])
                    tt(DVE, oo[ob:ob + 64, 0:QW], Ob[ob:ob + 64, 0:QW], rd[ob:ob + 64, 0:QW], ALU.mult, [Ob.b, rd.b], [oo.b])
                    tt(POOL, oT[ob:ob + 64, 3 + h // 2, q0:q0 + QW], oo[ob:ob + 64, 0:QW], gate[ob:ob + 64, q0:q0 + QW], ALU.mult,
                       [oo.b, gate.b], [oTb[3 + h // 2]])
            heat["n"] = 0

    def phase_S(cx):
        P.tag = "S.pre"
        l, si, T, NK, BW, QW, past, O = cx.l, cx.si, cx.T, cx.NK, cx.BW, cx.BW, cx.past, cx.O
        mem.top = phase_base
        P.barrier()
        wsl = [mem.alloc(f"wf{i}", (128, 8, 128), BF16) for i in range(4)]
        Qc = [mem.alloc(f"Qc{i}", (128, T), BF16) for i in range(2)]
        Kc = [mem.alloc(f"Kc{i}", (128, NK), BF16) for i in range(2)]
        gate = mem.alloc("gate", (128, T), BF16)
        Ef = [mem.alloc(f"Ef{i}", (128, QW), F32) for i in range(3)]
        Xe = [mem.alloc(f"Xe{i}", (128, QW), BF16) for i in range(3)]
        spb = [mem.alloc(f"spb{i}", (128, QW), BF16) for i in range(3)]
        Lsb = [mem.alloc(f"Lsb{i}", (128, QW), BF16) for i in range(3)]
        Wt = [mem.alloc(f"Wt{i}", (128, QW), BF16) for i in range(3)]
        Lsum = [mem.alloc(f"Lsum{i}", (128, QW), F32) for i in range(2)]
        kst = [mem.alloc(f"kst{i}", (128, 16, 128), F32) for i in range(2)] if cx.kind == "s" else None

        def load_kcache(sp_):
            k_ = kst[sp_ % 2]
            P.dma(SP, k_[:], csk[l, si].rearrange("(t p) c -> p t c", p=128)[:, :, sp_ * 128:(sp_ + 1) * 128], writes=[k_.b])
        if cx.kind == "s":
            load_kcache(0)
            load_kcache(1)
        for sp in range(2):
            P.tag = "S.proj"
            wq = load_w(cx, wsl, FT_QC + sp)
            wk = load_w(cx, wsl, FT_KC + sp)
            wg = load_w(cx, wsl, FT_GC + sp)
            if cx.kind == "s":
                kcur = kst[sp % 2]
                for hh in range(2):
                    for t4 in range(4):
                        bank = next_bank()
                        for t in range(4):
                            tr(bank[0:64, t * 128:(t + 1) * 128], kcur[:, 4 * t4 + t, hh * 64:(hh + 1) * 64], C["identf"][:, :],
                               [kcur.b, C["identf"].b], [bank.b])
                        cp(DVE if t4 % 2 else ACT, Kc[hh][0:64, t4 * 512:(t4 + 1) * 512], bank[0:64, 0:512], [bank.b], [Kc[hh].b])
            for blk in range(cx.NBLK):
                cols = slice(blk * BW, (blk + 1) * BW)
                kcols = slice(past + blk * BW, past + (blk + 1) * BW)
                bq = next_bank()
                project(cx, wq, 128, blk, bq)
                act(Qc[0][0:64, cols], bq[0:64, 0:BW], AF.Copy, [bq.b], [Qc[0].b], scale=0.125)
                ts(DVE, Qc[1][0:64, cols], bq[64:128, 0:BW], 0.125, None, ALU.mult, None, [bq.b], [Qc[1].b])
                bk_ = next_bank()
                project(cx, wk, 128, blk, bk_)
                cp(ACT, Kc[0][0:64, kcols], bk_[0:64, 0:BW], [bk_.b], [Kc[0].b])
                cp(DVE, Kc[1][0:64, kcols], bk_[64:128, 0:BW], [bk_.b], [Kc[1].b])
                bg = next_bank()
                project(cx, wg, 128, blk, bg)
                act(gate[:, cols], bg[:, 0:BW], AF.Silu, [bg.b], [gate.b])
            P.tag = "S.attn"
            heat["n"], heat["bank"] = HEAT_S, psum[0]
            for hh in range(2):
                hc = 2 * sp + hh
                ob = 64 * hh
                for qt in range(T // QW):
                    q0 = qt * QW
                    qlo = past + q0
                    qhi = qlo + QW - 1
                    kts = list(reversed([kt for kt in range(cx.NKT) if kt * 128 <= qhi]))
                    n = len(kts)
                    Ob = psum[6 + qt % 2]
                    memset(POOL, Lsum[0][:, :], 0.0, [Lsum[0].b])
                    memset(POOL, Lsum[1][:, :], 0.0, [Lsum[1].b])

                    def info(i):
                        kt = kts[i]
                        return kt, min(128, NK - kt * 128), psum[2 + i % 2]

                    def stage1(i):
                        kt, kn, Zb = info(i)
                        partial = kt * 128 + kn - 1 >= qlo
                        mm(Zb[0:kn, 0:QW], Kc[hh][0:64, kt * 128:kt * 128 + kn], Qc[hh][0:64, q0:q0 + QW], True, not partial,
                           [Kc[hh].b, Qc[hh].b], [Zb.b])
                        if partial:
                            d = kt * 128 - qlo
                            mm(Zb[0:kn, 0:QW], identb[0:kn, 0:kn], negltb[0:kn, 384 - d:384 - d + QW], False, True,
                               [identb.b, negltb.b], [Zb.b])
                        j = i % 3
                        act(Ef[j][0:kn, :], Zb[0:kn, 0:QW], AF.Exp, [Zb.b], [Ef[j].b])
                        act(spb[j][0:kn, :], Ef[j][0:kn, :], AF.Ln, [Ef[j].b], [spb[j].b], bias=1.0)
                        La, Lb = Lsum[i % 2], Lsum[(i + 1) % 2]
                        if i + 1 < n:
                            jn = (i + 1) % 3
                            if kn < 128:
                                memset(POOL, Lsb[jn][kn:128, :], 0.0, [Lsb[jn].b])
                            tt(DVE, Lsb[jn][0:kn, :], La[0:kn, :], spb[j][0:kn, :], ALU.add, [La.b, spb[j].b], [Lsb[jn].b])
                            tt(DVE, Lb[0:kn, :], La[0:kn, :], spb[j][0:kn, :], ALU.add, [La.b, spb[j].b], [Lb.b])

                    def stage2(i):
                        kt, kn, _ = info(i)
                        Zb = psum[4 + i % 2]
                        j = i % 3
                        mm(Zb[0:kn, 0:QW], trinegb[0:kn, 0:kn], spb[j][0:kn, :], True, i == 0, [trinegb.b, spb[j].b], [Zb.b])
                        if i > 0:
                            mm(Zb[0:kn, 0:QW], onesneg[:, 0:kn], Lsb[j][:, :], False, True, [onesneg.b, Lsb[j].b], [Zb.b])
                        act(Xe[j][0:kn, :], Zb[0:kn, 0:QW], AF.Exp, [Zb.b], [Xe[j].b])
                        tt(DVE, Wt[j][0:kn, :], Ef[j][0:kn, :], Xe[j][0:kn, :], ALU.mult, [Ef[j].b, Xe[j].b], [Wt[j].b])

                    def stage3(i):
                        kt, kn, Zb = info(i)
                        j = i % 3
                        mm(Ob[ob:ob + 64, 0:QW], Vc[0:kn, kt, hc * 64:(hc + 1) * 64], Wt[j][0:kn, :], i == 0, i == n - 1,
                           [Vc.b, Wt[j].b], [Ob.b])
                    for s_ in range(n + 2):
                        if s_ < n:
                            stage1(s_)
                        if 0 <= s_ - 1 < n:
                            stage2(s_ - 1)
                        if 0 <= s_ - 2 < n:
                            stage3(s_ - 2)
                    tt(DVE, oT[ob:ob + 64, 6 + hc // 2, q0:q0 + QW], Ob[ob:ob + 64, 0:QW], gate[ob:ob + 64, q0:q0 + QW], ALU.mult,
                       [Ob.b, gate.b], [oTb[6 + hc // 2]])
            heat["n"] = 0

    def phase_E(cx):
        P.tag = "E"
        l, si, TT, O = cx.l, cx.si, cx.TT, cx.O
        mem.top = phase_base
        P.barrier()
        wo = mem.alloc("wo", (128, 8, 1024), BF16)
        lng = mem.alloc("lng", (128, D), F32)
        lnb = mem.alloc("lnb", (128, D), F32)
        xres = [mem.alloc(f"xres{i}", (128, D), F32) for i in range(3)]
        Rr = [mem.alloc(f"Rr{i}", (128, D), F32) for i in range(2)]
        yv = [mem.alloc(f"yv{i}", (128, D), F32) for i in range(3)]
        st = mem.alloc("bnst", (128, 12), F32)
        mv = mem.alloc("bnmv", (128, 2), F32)
        rs = mem.alloc("bnrs", (128, 1), F32)
        nb = mem.alloc("bnnb", (128, 1), F32)
        P.dma(SP, wo[:], WO[l], reads=[bWO[l]], writes=[wo.b])
        P.dma(SP, lng[:], ln_g[l].partition_broadcast(128), writes=[lng.b])
        P.dma(SP, lnb[:], ln_b[l].partition_broadcast(128), writes=[lnb.b])
        st2 = [st, mem.alloc("bnst2", (128, 12), F32)]

        def front(tt_):
            rows = slice(tt_ * TT, (tt_ + 1) * TT)
            xr, R_, st_ = xres[tt_ % 3], Rr[tt_ % 2], st2[tt_ % 2]
            if l == 0:
                P.dma(SP, xr[:TT, :], cx.xsrc[si, rows, :], writes=[xr.b])
            else:
                P.dma(SP, xr[:TT, :], Y0[cx.yidx, rows, :], reads=[bY0[cx.yidx][tt_]], writes=[xr.b])
            bA, bB = psum[2 * (tt_ % 2)], psum[2 * (tt_ % 2) + 1]
            for c in range(8):
                mm(bA[:TT, 0:512], oT[:, c, rows], wo[:, c, 0:512], c == 0, c == 7, [oTb[c], wo.b], [bA.b])
            for c in range(8):
                mm(bB[:TT, 0:512], oT[:, c, rows], wo[:, c, 512:1024], c == 0, c == 7, [oTb[c], wo.b], [bB.b])

        def frontB(tt_):
            xr, R_, st_ = xres[tt_ % 3], Rr[tt_ % 2], st2[tt_ % 2]
            bA, bB = psum[2 * (tt_ % 2)], psum[2 * (tt_ % 2) + 1]
            stt(R_[:TT, 0:512], xr[:TT, 0:512], ALPHA, bA[:TT, 0:512], ALU.mult, ALU.add, [xr.b, bA.b], [R_.b])
            stt(R_[:TT, 512:1024], xr[:TT, 512:1024], ALPHA, bB[:TT, 0:512], ALU.mult, ALU.add, [xr.b, bB.b], [R_.b])
            P.op(DVE, lambda e, o=st_[:TT, 0:6], i_=R_[:TT, 0:512]: e.bn_stats(out=o, in_=i_), reads=[R_.b], writes=[st_.b])
            P.op(DVE, lambda e, o=st_[:TT, 6:12], i_=R_[:TT, 512:1024]: e.bn_stats(out=o, in_=i_), reads=[R_.b], writes=[st_.b])

        def back(tt_):
            rows = slice(tt_ * TT, (tt_ + 1) * TT)
            R_, y_, st_ = Rr[tt_ % 2], yv[tt_ % 3], st2[tt_ % 2]
            P.op(DVE, lambda e, o=mv[:TT, 0:2], i_=st_[:TT, 0:12]: e.bn_aggr(out=o, in_=i_), reads=[st_.b], writes=[mv.b])
            act(rs[:TT, :], mv[:TT, 1:2], AF.Ln, [mv.b], [rs.b], bias=LN_EPS)
            act(rs[:TT, :], rs[:TT, :], AF.Exp, [rs.b], [rs.b], scale=-0.5)
            stt(nb[:TT, :], mv[:TT, 0:1], -1.0, rs[:TT, 0:1], ALU.mult, ALU.mult, [mv.b, rs.b], [nb.b])
            act(y_[:TT, :], R_[:TT, :], AF.Identity, [R_.b, rs.b, nb.b], [y_.b], bias=nb[:TT, 0:1], scale=rs[:TT, 0:1])
            tt(DVE, y_[:TT, :], y_[:TT, :], lng[:TT, :], ALU.mult, [y_.b, lng.b], [y_.b])
            tt(POOL, y_[:TT, :], y_[:TT, :], lnb[:TT, :], ALU.add, [y_.b, lnb.b], [y_.b])
            if l == 0:
                P.dma(POOL, Y0[cx.yidx, rows, :], y_[:TT, :], reads=[y_.b], writes=[bY0[cx.yidx][tt_]])
                bk = (psum[4], psum[5]) if tt_ % 2 == 0 else (psum[6], psum[7])
                to_xT(cx, y_, tt_, bk)
            else:
                P.dma(POOL, O["y"][si, rows, :], y_[:TT, :], reads=[y_.b])

        front(0)
        frontB(0)
        if cx.NTT > 1:
            front(1)
        for tt_ in range(cx.NTT):
            back(tt_)
            if tt_ + 1 < cx.NTT:
                frontB(tt_ + 1)
            if tt_ + 2 < cx.NTT:
                front(tt_ + 2)

    def phase_R(cx):
        P.tag = "R.init"
        l, si, T, BW, TT, past, O = cx.l, cx.si, cx.T, cx.BW, cx.TT, cx.past, cx.O
        NCH = TT // 64
        mem.top = phase_base
        P.barrier()
        NT = BW // TT
        NU = 2 * NT
        NCHK = NT * NCH
        wsl = [mem.alloc(f"wf{i}", (128, 8, 128), BF16) for i in range(3)]
        U = [mem.alloc(f"U{i}", (128, BW + 1), F32) for i in range(3)]
        Ul = mem.alloc("Ul", (128, BW + 1), F32)
        Dt = mem.alloc("Dt", (128, BW), F32)
        tw = mem.alloc("tw", (128, BW), BF16)
        gaT = mem.alloc("gaT", (128, BW), BF16)
        fnames = "lgc lgx ex eneg ld av tmp esfx kk kkn k2 bb epos Rt bonus Y".split()
        f = {}
        foff = {}
        for n_ in fnames:
            foff[n_] = mem.top
            f[n_] = mem.alloc(n_, (128, BW), F32)
        b = {n: mem.alloc(n, (128, BW), BF16) for n in "kk2 Rtb KKt Kh Bh Kg Bg Vbf rkb Ybf Ysq".split()}
        def alias(name, shape, dt, off):
            mem.n += 1
            return nc.alloc_sbuf_tensor_at(f"{name}_{mem.n}", list(shape), dt, offset=off)
        if BW == 512:
            MK = alias("MK", (128, 8, 512), BF16, foff["lgc"])
            MKb = [f[("lgc", "lgx", "ex", "eneg")[u // 2]].b for u in range(8)]
            MM = [alias("MM0", (128, 8, 256), BF16, foff["ld"]), alias("MM1", (128, 8, 256), BF16, foff["tmp"])]
            MMb = [[f[("ld", "av")[g // 2]].b for g in range(4)], [f[("tmp", "esfx")[g // 2]].b for g in range(4)]]
        else:
            MKt = mem.alloc("MK", (128, 8, 512), BF16)
            MK, MKb = MKt.h, [MKt.b] * 8
            MMt = [mem.alloc(f"MM{i}", (128, 8, 256), BF16) for i in range(2)]
            MM, MMb = [t_.h for t_ in MMt], [[t_.b] * 4 for t_ in MMt]
        QcT, McTt, D1sb, Y0sb = f["kk"], f["kkn"], f["k2"], f["bb"]
        TOK = mem.alloc("TOK", (128, 4, 4, 128), BF16)
        Pt = [mem.alloc(f"Pt{i}", (128, 8, 128), BF16) for i in range(2)]
        Ptb = [[Buf(f"Ptb{i}{g}") for g in range(2)] for i in range(2)]
        ArbT = mem.alloc("ArbT", (128, 2, 4, 128), BF16)
        MKraw = [mem.alloc(f"MKraw{i}", (128, 512), BF16) for i in range(2)]
        W1b = mem.alloc("W1b", (128, 8, 64), BF16)
        UW = mem.alloc("UW", (128, 8, 128), BF16)
        UWm = [mem.alloc(f"UWm{i}", (128, 8, 128), BF16) for i in range(2)]
        Vm = [mem.alloc(f"Vm{i}", (128, 4, 128), BF16) for i in range(2)]
        wst = mem.alloc("wst", (128, 6, 64), F32)
        gidx = [[0, 0] for _ in range(3)]

        if cx.kind == "p":
            memset(POOL, ucarry[:, :], 0.0, ucb)
            memset(POOL, Gst[0][:, :, :], 0.0, [Gb[0][c3][hh] for c3 in range(3) for hh in range(2)])
            memset(POOL, Gst[1][:, :, :], 0.0, [Gb[1][c3][hh] for c3 in range(3) for hh in range(2)])
        else:
            with nc.allow_non_contiguous_dma("state_shift transpose load"):
                P.dma(SP, ucarry[:, :], sshift[l, si].rearrange("(t p) -> p t", p=128), writes=ucb)
            memset(POOL, Gst[1][:, :, :], 0.0, [Gb[1][c3][hh] for c3 in range(3) for hh in range(2)])
            P.dma(SP, wst[0:64, :, :], swkv[l, si].rearrange("h v k -> v h k"), writes=[wst.b])
            for c3 in range(3):
                for hh in range(2):
                    hb = 64 * hh
                    mm(psum[2][hb:hb + 64, 256 + hh * 64:256 + (hh + 1) * 64], wst[0:64, 2 * c3 + hh, :], C["identf"][0:64, 0:64],
                       True, True, [wst.b, C["identf"].b], [psum[2].b])
                    cp(DVE, Gst[0][hb:hb + 64, c3, :], psum[2][hb:hb + 64, 256 + hh * 64:256 + (hh + 1) * 64], [psum[2].b], [Gb[0][c3][hh]])

        def uproc(Ut, bank, ct, last_blk):
            cp(ACT, Ut[:, 1:BW + 1], bank[:, 0:BW], [bank.b], [Ut.b])
            cp(ACT, Ut[:, 0:1], ucarry[:, ct:ct + 1], [ucb[ct]], [Ut.b])
            cp(ACT, ucarry[:, ct:ct + 1], Ut[:, BW:BW + 1], [Ut.b], [ucb[ct]])
            if last_blk:
                with nc.allow_non_contiguous_dma("shift state store"):
                    P.dma(POOL, O["sh"][l, si, ct * 128:(ct + 1) * 128].rearrange("(p o) -> p o", o=1), Ut[:, BW:BW + 1], reads=[Ut.b])
            act(Dt[:, :], Ut[:, 0:BW], AF.Copy, [Ut.b, mu_t.b], [Dt.b], scale=mu_t[:, l, ct:ct + 1])
            stt(Ut[:, 1:BW + 1], Ut[:, 1:BW + 1], omm_t[:, l, ct:ct + 1], Dt[:, :], ALU.mult, ALU.add, [Dt.b, omm_t.b, Ut.b], [Ut.b])

        ABANKS = (2, 4)

        def prep_A(blk, c3):
            last_blk = blk == cx.NBLK - 1
            if c3 == 0:
                P.tag = "R.lora"
                w9 = load_w(cx, wsl, 9)
                bank = next_bank(*ABANKS)
                project(cx, w9, 128, blk, bank)
                uproc(Ul, bank, 9, last_blk)
                act(tw[0:64, :], Ul[0:64, 1:BW + 1], AF.Tanh, [Ul.b], [tw.b])
                cp(DVE, tw[64:128, :], Ul[64:128, 1:BW + 1], [Ul.b], [tw.b])
                yield
            P.tag = "R.prepA"
            for j, ct in enumerate((c3, 3 + c3, 6 + c3)):
                ws = load_w(cx, wsl, ct)
                bank = next_bank(*ABANKS)
                project(cx, ws, 128, blk, bank)
                uproc(U[j], bank, ct, last_blk)
                yield
            cs = slice(c3 * 128, (c3 + 1) * 128)
            bank = next_bank(*ABANKS)
            mm(bank[:, 0:BW], lw[0:64, l, cs], tw[0:64, :], True, True, [lw.b, tw.b], [bank.b])
            act(f["ld"][:, :], bank[:, 0:BW], AF.Sigmoid, [bank.b, w0_t.b], [f["ld"].b], bias=w0_t[:, l, c3:c3 + 1])
            yield
            bank = next_bank(*ABANKS)
            mm(bank[:, 0:BW], lw[64:128, l, cs], tw[64:128, :], True, True, [lw.b, tw.b], [bank.b])
            act(f["av"][:, :], bank[:, 0:BW], AF.Sigmoid, [bank.b, a0_t.b], [f["av"].b], bias=a0_t[:, l, c3:c3 + 1])
            act(f["ld"][:, :], f["ld"][:, :], AF.Copy, [f["ld"].b], [f["ld"].b], scale=DEC_SCALE)
            yield
            P.op(DVE, lambda e: e.tensor_tensor_scan(out=f["lgc"][:, :], data0=C["chunkmask"][:, 0:BW], data1=f["ld"][:, :],
                                                     initial=0.0, op0=ALU.mult, op1=ALU.add),
                 reads=[C["chunkmask"].b, f["ld"].b], writes=[f["lgc"].b])
            tt(DVE, f["lgx"][:, :], f["lgc"][:, :], f["ld"][:, :], ALU.subtract, [f["lgc"].b, f["ld"].b], [f["lgx"].b])
            yield
            act(f["epos"][:, :], f["lgc"][:, :], AF.Exp, [f["lgc"].b], [f["epos"].b])
            act(f["ex"][:, :], f["lgx"][:, :], AF.Exp, [f["lgx"].b], [f["ex"].b])
            act(f["eneg"][:, :], f["lgc"][:, :], AF.Exp, [f["lgc"].b], [f["eneg"].b], scale=-1.0)
            yield
            lg3 = f["lgc"][:, :].rearrange("p (c n) -> p c n", n=64)
            tt(DVE, f["esfx"][:, :].rearrange("p (c n) -> p c n", n=64), lg3[:, :, 63:64].to_broadcast([128, BW // 64, 64]), lg3,
               ALU.subtract, [f["lgc"].b], [f["esfx"].b])
            act(f["esfx"][:, :], f["esfx"][:, :], AF.Exp, [f["esfx"].b], [f["esfx"].b])
            yield

        def advance(g, n):
            if g is None:
                return
            tag0 = P.tag
            for _ in range(n):
                try:
                    next(g)
                except StopIteration:
                    break
            P.tag = tag0

        def drain(g):
            advance(g, 10 ** 6)

        its = [(blk_, c3_) for blk_ in range(cx.NBLK) for c3_ in range(3)]
        drain(prep_A(0, 0))
        for blk in range(cx.NBLK):
            for c3 in range(3):
                idx_it = blk * 3 + c3
                nxtA = prep_A(*its[idx_it + 1]) if idx_it + 1 < len(its) else None
                P.tag = "R.prep"
                ws = load_w(cx, wsl, FT_GA + c3)
                bank = next_bank()
                project(cx, ws, 128, blk, bank)
                act(gaT[:, :], bank[:, 0:BW], AF.Silu, [bank.b], [gaT.b])
                r_, k_, v_ = U[0][:, 1:BW + 1], U[1][:, 1:BW + 1], U[2][:, 1:BW + 1]
                rb, kb_, vb_ = U[0].b, U[1].b, U[2].b
                act(b["kk2"][:, :], k_, AF.Square, [kb_, kk_t.b], [b["kk2"].b], scale=kk_t[:, l, c3:c3 + 1])
                bank = next_bank()
                mm(bank[:, 0:BW], bonesb[:, :], b["kk2"][:, :], True, True, [bonesb.b, b["kk2"].b], [bank.b])
                act(f["tmp"][:, :], bank[:, 0:BW], AF.Ln, [bank.b], [f["tmp"].b], bias=1e-12)
                act(f["tmp"][:, :], f["tmp"][:, :], AF.Exp, [f["tmp"].b], [f["tmp"].b], scale=-0.5)
                stt(f["kkn"][:, :], k_, kk_t[:, l, c3:c3 + 1], f["tmp"][:, :], ALU.mult, ALU.mult, [kb_, kk_t.b, f["tmp"].b], [f["kkn"].b])
                ts(DVE, f["k2"][:, :], f["av"][:, :], ka_t[:, l, c3:c3 + 1], omka_t[:, l, c3:c3 + 1], ALU.mult, ALU.add,
                   [f["av"].b, ka_t.b, omka_t.b], [f["k2"].b])
                tt(DVE, f["k2"][:, :], f["k2"][:, :], k_, ALU.mult, [f["k2"].b, kb_], [f["k2"].b])
                tt(DVE, f["bb"][:, :], f["kkn"][:, :], f["av"][:, :], ALU.mult, [f["kkn"].b, f["av"].b], [f["bb"].b])
                stt(b["rkb"][:, :], r_, rk_t[:, l, c3:c3 + 1], f["k2"][:, :], ALU.mult, ALU.mult, [rb, rk_t.b, f["k2"].b], [b["rkb"].b])
                bank = next_bank()
                mm(bank[:, 0:BW], bonesb[:, :], b["rkb"][:, :], True, True, [bonesb.b, b["rkb"].b], [bank.b])
                tt(DVE, f["bonus"][:, :], bank[:, 0:BW], v_, ALU.mult, [bank.b, vb_], [f["bonus"].b])
                tt(DVE, f["Rt"][:, :], r_, f["epos"][:, :], ALU.mult, [rb, f["epos"].b], [f["Rt"].b])
                cp(ACT, b["Rtb"][:, :], f["Rt"][:, :], [f["Rt"].b], [b["Rtb"].b])
                tt(DVE, b["KKt"][:, :], f["kkn"][:, :], f["ex"][:, :], ALU.mult, [f["kkn"].b, f["ex"].b], [b["KKt"].b])
                tt(DVE, b["Kh"][:, :], f["k2"][:, :], f["eneg"][:, :], ALU.mult, [f["k2"].b, f["eneg"].b], [b["Kh"].b])
                tt(DVE, b["Bh"][:, :], f["bb"][:, :], f["eneg"][:, :], ALU.mult, [f["bb"].b, f["eneg"].b], [b["Bh"].b])
                tt(DVE, b["Kg"][:, :], f["k2"][:, :], f["esfx"][:, :], ALU.mult, [f["k2"].b, f["esfx"].b], [b["Kg"].b])
                tt(POOL, b["Bg"][:, :], f["bb"][:, :], f["esfx"][:, :], ALU.mult, [f["bb"].b, f["esfx"].b], [b["Bg"].b])
                cp(ACT, b["Vbf"][:, :], v_, [vb_], [b["Vbf"].b])

                v3 = lambda ap, c=128, n=TT: ap.rearrange("p (a c) -> p a c", c=c)[:, :, 0:n]
                P.tag = "R.S0"
                for tl in range(NT):
                    tc = slice(tl * TT, (tl + 1) * TT)
                    bk = psum[6 + tl % 2]
                    tb = pbf(6 + tl % 2)
                    for j, nm in enumerate(("KKt", "Kg", "Bg", "Vbf")):
                        tr(tb[0:TT, j * 128:(j + 1) * 128], b[nm][:, tc], identb[:, :], [b[nm].b, identb.b], [bk.b])
                    cp(DVE if tl % 2 else ACT, TOK[0:TT, tl, :, :], tb[0:TT, 0:512].rearrange("p (a c) -> p a c", c=128), [bk.b], [TOK.b])
                P.tag = "R.S1"
                for tl in range(NT):
                    tc = slice(tl * TT, (tl + 1) * TT)
                    for hh in range(2):
                        hs = slice(64 * hh, 64 * hh + 64)
                        bA = psum[2 * (tl % 2) + hh]
                        mm(bA[0:TT, 0:TT], b["KKt"][hs, tc], b["Bh"][hs, tc], True, True, [b["KKt"].b, b["Bh"].b], [bA.b])
                        mm(bA[0:TT, 128:128 + TT], b["Bh"][hs, tc], b["KKt"][hs, tc], True, True, [b["KKt"].b, b["Bh"].b], [bA.b])
                        mm(bA[0:TT, 256:256 + TT], b["Kh"][hs, tc], b["KKt"][hs, tc], True, True, [b["KKt"].b, b["Kh"].b], [bA.b])
                        mm(bA[0:TT, 384:384 + TT], b["Kh"][hs, tc], b["Rtb"][hs, tc], True, True, [b["Rtb"].b, b["Kh"].b], [bA.b])
                        mm(psum[4 + hh][0:TT, tl * 128:tl * 128 + TT], b["Bh"][hs, tc], b["Rtb"][hs, tc], True, True,
                           [b["Rtb"].b, b["Bh"].b], [psum[4 + hh].b])
                    for hh in range(2):
                        u = 2 * tl + hh
                        bA = psum[2 * (tl % 2) + hh]
                        if hh == 0 or tl % 2 == 1:
                            tt(DVE, v3(MK[0:TT, u, :]), v3(bA[0:TT, :]), v3(C["rwmask"][0:TT, :]), ALU.mult, [bA.b, C["rwmask"].b], [MKb[u]])
                        else:
                            raw = MKraw[tl % 2]
                            cp(ACT, v3(raw[0:TT, :]), v3(bA[0:TT, :]), [bA.b], [raw.b])
                            tt(POOL, v3(MK[0:TT, u, :]), v3(raw[0:TT, :]), v3(C["rwmask"][0:TT, :]), ALU.mult, [raw.b, C["rwmask"].b], [MKb[u]])
                for hh in range(2):
                    tt(DVE, ArbT[0:TT, hh, 0:NT, 0:TT], v3(psum[4 + hh][0:TT, :])[:, 0:NT, :], v3(C["iumask4"][0:TT, :])[:, 0:NT, :], ALU.mult,
                       [psum[4 + hh].b, C["iumask4"].b], [ArbT.b])
                mkall = sorted(set(MKb), key=id)
                tt(POOL, Pt[0][0:TT, 0:NU, 0:TT], MK[0:TT, 0:NU, 128:128 + TT], ident8[0:TT, 0:NU, 0:TT], ALU.add,
                   mkall + [ident8.b], [Ptb[0][0], Ptb[0][1]])
                P.tag = "R.S2"
                Mcur = [(MK[0:TT, u, 0:TT], MK[0:TT, u, 128:128 + TT], MKb[u]) for u in range(NU)]
                for m in range(1, 6):
                    par = m % 2
                    for u in range(NU):
                        bk = psum[u // 2]
                        off = 256 * (u % 2)
                        Mp, Mtp, mb = Mcur[u]
                        mm(bk[0:TT, off:off + TT], Mtp, Mp, True, True, [mb], [bk.b])
                        mm(bk[0:TT, off + 128:off + 128 + TT], Mp, Mtp, True, True, [mb], [bk.b])
                    for g in range(NU // 2):
                        dst = MM[par][0:TT, 2 * g:2 * g + 2, :].rearrange("p u (a c) -> p (u a) c", c=128)[:, :, 0:TT]
                        cp(DVE if g == 3 else ACT, dst, v3(psum[g][0:TT, :]), [psum[g].b], [MMb[par][g]])
                        for u in (2 * g, 2 * g + 1):
                            Mcur[u] = (MM[par][0:TT, u, 0:TT], MM[par][0:TT, u, 128:128 + TT], MMb[par][g])
                    for u in range(NU):
                        pbk = psum[4 + u // 4]
                        po = (u % 4) * 128
                        Pp = Pt[1 - par]
                        ppb = Ptb[1 - par][u // 4]
                        mm(pbk[0:TT, po:po + TT], Mcur[u][0], Pp[0:TT, u, 0:TT], True, True, [Mcur[u][2], ppb], [pbk.b])
                    for g in range((NU + 3) // 4):
                        nu_ = min(4, NU - 4 * g)
                        tt(DVE, Pt[par][0:TT, 4 * g:4 * g + nu_, 0:TT], v3(psum[4 + g][0:TT, :])[:, 0:nu_, :],
                           Pt[1 - par][0:TT, 4 * g:4 * g + nu_, 0:TT], ALU.add, [psum[4 + g].b, Ptb[1 - par][g]], [Ptb[par][g]])
                    advance(nxtA, ADV2)
                PtF, PtFb = Pt[1], Ptb[1]
                P.tag = "R.S3-6"
                for u in range(NU):
                    tl, hh = u // 2, u % 2
                    hs = slice(64 * hh, 64 * hh + 64)
                    mm(psum[6][0:TT, u * 64:(u + 1) * 64], MK[0:TT, u, 256:256 + TT], TOK[0:TT, tl, 3, hs], True, True, [MKb[u], TOK.b], [psum[6].b])
                cp(ACT, W1b[0:TT, 0:NU, :], psum[6][0:TT, 0:NU * 64].rearrange("p (u c) -> p u c", c=64), [psum[6].b], [W1b.b])
                for u in range(NU):
                    tl, hh = u // 2, u % 2
                    hs = slice(64 * hh, 64 * hh + 64)
                    bk = psum[u // 4]
                    uo = (u % 4) * 128
                    mm(bk[0:TT, uo:uo + 64], PtF[0:TT, u, 0:TT], W1b[0:TT, u, :], True, True, [PtFb[u // 4], W1b.b], [bk.b])
                    mm(bk[0:TT, uo + 64:uo + 128], PtF[0:TT, u, 0:TT], TOK[0:TT, tl, 0, hs], True, True, [PtFb[u // 4], TOK.b], [bk.b])
                for g in range((NU + 3) // 4):
                    nu_ = min(4, NU - 4 * g)
                    tt(DVE, UW[0:TT, 4 * g:4 * g + nu_, :], v3(psum[g][0:TT, :], n=128)[:, 0:nu_, :], v3(C["signs4"][0:TT, :], n=128)[:, 0:nu_, :],
                       ALU.mult, [psum[g].b, C["signs4"].b], [UW.b])
                for cc in range(NCH):
                    ts(POOL, UWm[cc][0:TT, 0:NU, :], UW[0:TT, 0:NU, :], C["cind"][0:TT, cc:cc + 1], None, ALU.mult, None,
                       [UW.b, C["cind"].b], [UWm[cc].b])
                    ts(POOL, Vm[cc][0:TT, 0:NT, :], TOK[0:TT, 0:NT, 3, :], C["cind"][0:TT, cc:cc + 1], None, ALU.mult, None,
                       [TOK.b, C["cind"].b], [Vm[cc].b])
                for u in range(NU):
                    tl, hh = u // 2, u % 2
                    hs = slice(64 * hh, 64 * hh + 64)
                    mm(psum[2][hs, tl * 128:tl * 128 + TT], UW[0:TT, u, 64:128], ArbT[0:TT, hh, tl, 0:TT], True, True, [UW.b, ArbT.b], [psum[2].b])
                vb = lambda ap: ap.rearrange("p (t c) -> p t c", c=TT)
                tt(DVE, vb(QcT[:, 0:BW]), vb(f["Rt"][:, 0:BW]), v3(psum[2][:, :])[:, 0:NT, :], ALU.subtract, [f["Rt"].b, psum[2].b], [QcT.b])
                for u in range(NU):
                    tl, hh = u // 2, u % 2
                    hs = slice(64 * hh, 64 * hh + 64)
                    mm(psum[3][hs, tl * 128:tl * 128 + TT], TOK[0:TT, tl, 3, hs], MK[0:TT, u, 384:384 + TT], True, False, [TOK.b, MKb[u]], [psum[3].b])
                    mm(psum[3][hs, tl * 128:tl * 128 + TT], UW[0:TT, u, 0:64], ArbT[0:TT, hh, tl, 0:TT], False, True, [UW.b, ArbT.b], [psum[3].b])
                cp(ACT, vb(Y0sb[:, 0:BW]), v3(psum[3][:, :])[:, 0:NT, :], [psum[3].b], [Y0sb.b])
                P.tag = "R.S7ab"
                McT = McTt[:, :].rearrange("p (c k) -> p c k", k=64) if BW == 512 else McTt[:, 0:64].rearrange("p (c k) -> p c k", k=64)
                for u in range(NU):
                    tl, hh = u // 2, u % 2
                    hs = slice(64 * hh, 64 * hh + 64)
                    for cc in range(NCH):
                        ch = tl * NCH + cc
                        mm(psum[6][hs, ch * 64:(ch + 1) * 64], UWm[cc][0:TT, u, 64:128], TOK[0:TT, tl, 2, hs], True, True,
                           [UWm[cc].b, TOK.b], [psum[6].b])
                for ch in range(NCHK):
                    gcol = ch * 64 + 63
                    stt(McT[:, ch, :], C["ident2"][:, :], f["epos"][:, gcol:gcol + 1], psum[6][:, ch * 64:(ch + 1) * 64],
                        ALU.mult, ALU.subtract, [C["ident2"].b, f["epos"].b, psum[6].b], [McTt.b])
                for u in range(NU):
                    tl, hh = u // 2, u % 2
                    hs = slice(64 * hh, 64 * hh + 64)
                    for cc in range(NCH):
                        ch = tl * NCH + cc
                        mm(psum[7][hs, ch * 64:(ch + 1) * 64], TOK[0:TT, tl, 1, hs], Vm[cc][0:TT, tl, hs], True, False, [TOK.b, Vm[cc].b], [psum[7].b])
                        mm(psum[7][hs, ch * 64:(ch + 1) * 64], TOK[0:TT, tl, 2, hs], UWm[cc][0:TT, u, 0:64], False, True, [TOK.b, UWm[cc].b], [psum[7].b])
                cp(DVE, D1sb[:, 0:NCHK * 64], psum[7][:, 0:NCHK * 64], [psum[7].b], [D1sb.b])
                P.tag = "R.S7c"
                for ch in range(NCHK):
                    cs_ = slice(ch * 64, (ch + 1) * 64)
                    for hh in range(2):
                        hs = slice(64 * hh, 64 * hh + 64)
                        gi = gidx[c3][hh]
                        Gc, Gn = Gst[gi], Gst[1 - gi]
                        Gcb, Gnb = Gb[gi][c3][hh], Gb[1 - gi][c3][hh]
                        mm(psum[4 + hh][hs, cs_], Gc[hs, c3, :], QcT[hs, cs_], True, True, [Gcb, QcT.b], [psum[4 + hh].b])
                        mm(psum[hh][hs, cs_], McT[hs, ch, :], Gc[hs, c3, :], True, True, [McTt.b, Gcb], [psum[hh].b])
                        tt(DVE, Gn[hs, c3, :], psum[hh][hs, cs_], D1sb[hs, cs_], ALU.add, [psum[hh].b, D1sb.b], [Gnb])
                        gidx[c3][hh] = 1 - gi
                    advance(nxtA, int(os.environ.get("ADV", "0")))
                for hh in range(2):
                    hs = slice(64 * hh, 64 * hh + 64)
                    tt(DVE, f["Y"][hs, 0:BW], psum[4 + hh][hs, 0:BW], Y0sb[hs, 0:BW], ALU.add, [psum[4 + hh].b, Y0sb.b], [f["Y"].b])
                P.tag = "R.post"
                act(b["Ybf"][:, :], f["Y"][:, :], AF.Copy, [f["Y"].b], [b["Ybf"].b])
                act(b["Ysq"][:, :], f["Y"][:, :], AF.Square, [f["Y"].b], [b["Ysq"].b])
                bm = next_bank()
                mm(bm[:, 0:BW], bones64[:, :], b["Ybf"][:, :], True, True, [bones64.b, b["Ybf"].b], [bm.b])
                cp(ACT, f["tmp"][:, :], bm[:, 0:BW], [bm.b], [f["tmp"].b])
                bq = next_bank()
                mm(bq[:, 0:BW], bones64[:, :], b["Ysq"][:, :], True, True, [bones64.b, b["Ysq"].b], [bq.b])
                act(f["lgx"][:, :], bm[:, 0:BW], AF.Square, [bm.b], [f["lgx"].b])
                tt(DVE, f["lgx"][:, :], bq[:, 0:BW], f["lgx"][:, :], ALU.subtract, [bq.b, f["lgx"].b], [f["lgx"].b])
                ts(DVE, f["lgx"][:, :], f["lgx"][:, :], 0.0, None, ALU.max, None, [f["lgx"].b], [f["lgx"].b])
                act(f["lgx"][:, :], f["lgx"][:, :], AF.Ln, [f["lgx"].b], [f["lgx"].b], bias=GN_EPS)
                act(f["lgx"][:, :], f["lgx"][:, :], AF.Exp, [f["lgx"].b], [f["lgx"].b], scale=-0.5)
                tt(DVE, f["Y"][:, :], f["Y"][:, :], f["tmp"][:, :], ALU.subtract, [f["Y"].b, f["tmp"].b], [f["Y"].b])
                tt(DVE, f["Y"][:, :], f["Y"][:, :], f["lgx"][:, :], ALU.mult, [f["Y"].b, f["lgx"].b], [f["Y"].b])
                ts(DVE, f["Y"][:, :], f["Y"][:, :], lg_t[:, l, c3:c3 + 1], lb_t[:, l, c3:c3 + 1], ALU.mult, ALU.add,
                   [f["Y"].b, lg_t.b, lb_t.b], [f["Y"].b])
                tt(DVE, f["Y"][:, :], f["Y"][:, :], f["bonus"][:, :], ALU.add, [f["Y"].b, f["bonus"].b], [f["Y"].b])
                tt(POOL, oT[:, c3, blk * BW:(blk + 1) * BW], f["Y"][:, :], gaT[:, :], ALU.mult, [f["Y"].b, gaT.b], [oTb[c3]])
                drain(nxtA)

        for c3 in range(3):
            for hh in range(2):
                hb = 64 * hh
                hs = slice(hb, hb + 64)
                gi = gidx[c3][hh]
                h = 2 * c3 + hh
                fb_ = psum[3] if hh == 0 else psum[2]
                tr(fb_[0:64, 128:192], Gst[gi][hs, c3, :], C["identf"][hs, hs], [Gb[gi][c3][hh], C["identf"].b], [fb_.b])
                cp(DVE, wst[0:64, h, :], fb_[0:64, 128:192], [fb_.b], [wst.b])
        P.dma(POOL, O["wkv"][l, si].rearrange("h v k -> v h k"), wst[0:64, :, :], reads=[wst.b])

    for kind, n in (("p", NP), ("s", NS)):
        for si in range(n):
            T = T_P if kind == "p" else T_S
            past = 0 if kind == "p" else PAST
            cx = SimpleNamespace(kind=kind, si=si, T=T, past=past, NK=past + T, TT=min(128, T), NTT=T // min(128, T),
                                 BW=min(512, T), NBLK=T // min(512, T), NKT=(past + T + 127) // 128, PKT=past // 128,
                                 xsrc=(xp if kind == "p" else xs_), yidx=(si if kind == "p" else NPm + si),
                                 O={k[1:]: v for k, v in outs.items() if k[0] == kind})
            for l in range(NL):
                cx.l = l
                if l == 0:
                    phase_X(cx)
                if "t" not in PHASES:
                    phase_T(cx)
                if "R" in PHASES:
                    phase_R(cx)
                if "F" in PHASES:
                    phase_F(cx)
                if "S" in PHASES:
                    phase_S(cx)
                if dbg and "oT" in dbg_out and l == dbg.get("_layer", 0) and T == dbg["oT"][2] and si == 0:
                    P.dma(POOL, dbg_out["oT"], oT[:, :, 0:T], reads=oTb)
                if "e" not in PHASES:
                    phase_E(cx)
    P.finish()
    global LASTP
    LASTP = P
    return nc


_NC_CACHE = {}


def kernel(**inputs):
    n = 8
    NP, NS = 32 // n, 32 // n
    key = (NP, NS)
    consts = host_consts()
    in_maps = []
    f32 = lambda a: np.ascontiguousarray(np.asarray(a, dtype=np.float32))
    for c in range(n):
        ps, ss = slice(c * NP, (c + 1) * NP), slice(c * NS, (c + 1) * NS)
        m = {
            "x_prompt": f32(inputs["x_prompt"][ps]),
            "x_sample": f32(inputs["x_sample"][ss]),
            "cache_fox_k": f32(np.asarray(inputs["cache_fox_k"])[:, ss].reshape(NL, NS, PAST, 384)),
            "cache_fox_v": f32(np.asarray(inputs["cache_fox_v"])[:, ss].reshape(NL, NS, PAST, 384)),
            "cache_fox_logf": f32(np.asarray(inputs["cache_fox_logf"])[:, ss]),
            "cache_sb_k": f32(np.asarray(inputs["cache_sb_k"])[:, ss].reshape(NL, NS, PAST, 256)),
            "cache_sb_v": f32(np.asarray(inputs["cache_sb_v"])[:, ss].reshape(NL, NS, PAST, 256)),
            "state_wkv": f32(np.asarray(inputs["state_wkv"])[:, ss]),
            "state_shift": f32(np.asarray(inputs["state_shift"])[:, ss].reshape(NL, NS, SHIFT_W)),
            "r_k": f32(np.asarray(inputs["r_k"]).reshape(NL, 384)),
        }
        for k in ("w_in", "mu_shift", "w0_decay", "w_decay", "a0", "w_aaa", "k_k", "k_a", "lnx_g", "lnx_b",
                  "fox_fb", "w_out", "ln_g", "ln_b"):
            m[k] = f32(inputs[k])
        for k, v in consts.items():
            m["c_" + k] = v
        in_maps.append(m)
    nc = build_program(NP, NS)
    res = run_bass_kernel_spmd(nc, in_maps, core_ids=list(range(n)))
    R = res.results

    def cat(name, axis, shape=None):
        a = np.concatenate([np.asarray(r[name], dtype=np.float32) for r in R], axis=axis)
        return a.reshape(shape) if shape is not None else a
    B = 32
    out = (
        cat("p_y", 0), cat("s_y", 0),
        cat("p_fox_k", 1, (NL, B, T_P, 6, 64)), cat("p_fox_v", 1, (NL, B, T_P, 6, 64)), cat("p_fox_logf", 1),
        cat("p_sb_k", 1, (NL, B, T_P, 4, 64)), cat("p_sb_v", 1, (NL, B, T_P, 4, 64)),
        cat("p_wkv", 1), cat("p_shift", 1, (NL, B, 1, SHIFT_W)),
        cat("s_fox_k", 1, (NL, B, T_S, 6, 64)), cat("s_fox_v", 1, (NL, B, T_S, 6, 64)), cat("s_fox_logf", 1),
        cat("s_sb_k", 1, (NL, B, T_S, 4, 64)), cat("s_sb_v", 1, (NL, B, T_S, 4, 64)),
        cat("s_wkv", 1), cat("s_shift", 1, (NL, B, 1, SHIFT_W)),
    )
    return out
```

```python
import contextlib
import os
import numpy as np
import concourse.bass as bass
import concourse.mybir as mybir
from concourse.bass_utils import run_bass_kernel_spmd

F32 = mybir.dt.float32
BF16 = mybir.dt.bfloat16
ALU = mybir.AluOpType
AF = mybir.ActivationFunctionType

PE, ACT, DVE, POOL, SP = "tensor", "scalar", "vector", "gpsimd", "sync"
ENGS = (PE, ACT, DVE, POOL, SP)
SEM_WRAP = 30000
ANNOTATE = bool(os.environ.get("ANNOTATE"))
HEAT_S = int(os.environ.get("HEAT_S", "1"))
HEAT_F = int(os.environ.get("HEAT_F", "0"))
HEAT_R = int(os.environ.get("HEAT_R", "0"))
ADV2 = int(os.environ.get("ADV2", "0"))

D = 1024
T_P = 2048
T_S = 64
PAST = 2048
NL = 2
INC = 4230
SHIFT_W = 1280
ALPHA = (2 * NL) ** 0.25
GN_EPS = 64e-5
LN_EPS = 1e-5
NEG = -30000.0
DEC_SCALE = -float(np.exp(-0.5))

FT = ([(128 * i, 128) for i in range(10)] +
      [(1280 + 128 * i, 128) for i in range(3)] +
      [(1664 + 128 * i, 128) for i in range(3)] +
      [(2048 + 128 * i, 128) for i in range(3)] +
      [(2822 + 128 * i, 128) for i in range(3)] +
      [(3206 + 128 * i, 128) for i in range(2)] +
      [(3462 + 128 * i, 128) for i in range(2)] +
      [(3974 + 128 * i, 128) for i in range(2)] +
      [(2816, 6)])
FT_GA, FT_QB, FT_KB, FT_GB, FT_QC, FT_KC, FT_GC, FT_FB = 10, 13, 16, 19, 22, 24, 26, 28
NFT = len(FT)
TG = [(2048, 384), (2432, 390), (3462, 512)]


class Buf:
    __slots__ = ("name", "w", "r", "excl")

    def __init__(self, name, excl=False):
        self.name = name
        self.w = None
        self.r = []
        self.excl = excl


class Prog:
    def __init__(self, nc, n_dma_sems=64):
        self.nc = nc
        self.q = {e: [] for e in ENGS}
        self.nsem = 0
        self.cur = {}
        self.cnt = {}
        self.allsems = {e: [] for e in ENGS}
        for e in ENGS:
            if e != SP:
                self.cur[e] = self._newsem()
                self.cnt[e] = 0
                self.allsems[e].append(self.cur[e])
        self.dma_pool = {SP: [self._newsem() for _ in range(20)], POOL: [self._newsem() for _ in range(12)]}
        self.dma_cnt = {s: 0 for e in self.dma_pool for s in self.dma_pool[e]}
        self.dma_next = {e: 0 for e in self.dma_pool}
        self.waited = {e: {} for e in ENGS}
        self.bar = {e: [] for e in ENGS}
        self.final = {}
        self.tag = None

    def _newsem(self):
        k = self.nsem
        self.nsem += 1
        return k

    def _need(self, eng, ev, waits):
        if ev is None:
            return
        k, v = ev[0], ev[1]
        if self.waited[eng].get(k, 0) >= v:
            return
        self.waited[eng][k] = v
        waits.append((k, v))

    def barrier(self):
        evs = []
        for e in ENGS:
            if e != SP and self.cnt[e]:
                evs.append((self.cur[e], self.cnt[e], e, False))
        for s, c in self.dma_cnt.items():
            if c:
                evs.append((s, 16 * c, None, True))
        for e in ENGS:
            self.bar[e] = self.bar[e] + evs

    def op(self, eng, fn, reads=(), writes=(), is_dma=False):
        waits = []
        if self.bar[eng]:
            for ev in self.bar[eng]:
                self._need(eng, ev, waits)
            self.bar[eng] = []
        for b in reads:
            self._need(eng, b.w, waits)
            if b.excl:
                for ev in b.r:
                    if ev[2] != eng:
                        self._need(eng, ev, waits)
        for b in writes:
            w = b.w
            if w is not None and not (w[2] == eng and not w[3] and not is_dma and eng != POOL):
                self._need(eng, w, waits)
            for ev in b.r:
                if ev[2] == eng and not is_dma and not ev[3] and eng != POOL:
                    continue
                self._need(eng, ev, waits)
        if is_dma:
            pool = self.dma_pool[eng]
            s = pool[self.dma_next[eng]]
            self.dma_next[eng] = (self.dma_next[eng] + 1) % len(pool)
            prev = self.dma_cnt[s]
            if prev:
                self._need(eng, (s, 16 * prev, None, True), waits)
            self.dma_cnt[s] = prev + 1
            ev = (s, 16 * (prev + 1), eng, True)
            inc = (s, 16)
        else:
            if self.cnt[eng] >= SEM_WRAP:
                self.final[self.cur[eng]] = self.cnt[eng]
                self.cur[eng] = self._newsem()
                self.cnt[eng] = 0
            self.cnt[eng] += 1
            ev = (self.cur[eng], self.cnt[eng], eng, False)
            inc = (self.cur[eng], 1)
        m = {}
        for k, v in waits:
            m[k] = max(m.get(k, 0), v)
        self.q[eng].append((list(m.items()), fn, inc, self.tag, ev[1]))
        for b in reads:
            b.r.append(ev)
            if len(b.r) > 64:
                b.r = b.r[-64:] if False else b.r
        for b in writes:
            b.w = ev
            b.r = []
        return ev

    def dma(self, eng, out, in_, reads=(), writes=(), **kw):
        def fn(e, out=out, in_=in_, kw=kw):
            return e.dma_start(out=out, in_=in_, **kw)
        return self.op(eng, fn, reads, writes, is_dma=True)

    def finish(self):
        nc = self.nc
        fw = dict(self.final)
        for e in ENGS:
            if e != SP and self.cnt[e]:
                fw[self.cur[e]] = self.cnt[e]
        for s, c in self.dma_cnt.items():
            if c:
                fw[s] = 16 * c
        dma_keys = set(self.dma_cnt)
        marked = {}
        for e_ in ENGS:
            for item in self.q[e_]:
                for k, v in item[0]:
                    if k not in dma_keys:
                        marked.setdefault(k, set()).add(v)
        for k, v in fw.items():
            if k not in dma_keys:
                marked.setdefault(k, set()).add(v)
        remap = {k: {v: i + 1 for i, v in enumerate(sorted(vs))} for k, vs in marked.items()}
        fw = {k: (v if k in dma_keys else remap[k][v]) for k, v in fw.items()}
        with contextlib.ExitStack() as es:
            sems = [es.enter_context(nc.semaphore(f"s{i}")) for i in range(self.nsem)]
            es.enter_context(nc.allow_non_contiguous_dma("small strided parameter / state transfers"))
            block = es.enter_context(nc.Block())

            def emit(engname):
                def body(e):
                    for waits, fn, inc, tag, pv in self.q[engname]:
                        for k, v in waits:
                            e.wait_ge(sems[k], v if k in dma_keys else remap[k][v])
                        ins = fn(e)
                        if inc[0] in dma_keys or pv in marked.get(inc[0], ()):
                            ins.then_inc(sems[inc[0]], inc[1])
                        if tag and ANNOTATE:
                            ins.annotate(tag)
                    if engname == SP:
                        for k, v in fw.items():
                            e.wait_ge(sems[k], v)
                return body
            block.tensor(emit(PE))
            block.scalar(emit(ACT))
            block.vector(emit(DVE))
            block.gpsimd(emit(POOL))
            block.sync(emit(SP))


class Tl:
    def __init__(self, h, name):
        self.h = h
        self.b = Buf(name)

    def __getitem__(self, k):
        return self.h[k]


class Mem:
    def __init__(self, nc, base, limit):
        self.nc, self.top, self.limit, self.n = nc, base, limit, 0

    def alloc(self, name, shape, dt):
        sz = int(np.prod(shape[1:])) * (4 if dt == F32 else 2)
        sz = (sz + 31) // 32 * 32
        self.n += 1
        h = self.nc.alloc_sbuf_tensor_at(f"{name}_{self.n}", list(shape), dt, offset=self.top)
        self.top += sz
        assert self.top <= self.limit, f"SBUF overflow at {name}: {self.top}"
        return Tl(h, name)


def host_consts():
    c = {}
    c["identf"] = np.eye(128, dtype=np.float32)
    bo = np.zeros((128, 128), np.float32)
    bo[:64, :64] = 1
    bo[64:, 64:] = 1
    c["bones"] = bo
    cm = np.ones((128, 512), np.float32)
    cm[:, ::64] = 0
    c["chunkmask"] = cm
    i = np.arange(128)[:, None]
    j = np.arange(128)[None, :]
    same = (i // 64) == (j // 64)
    sl = (same & (j < i)).astype(np.float32)
    su = (same & (i < j)).astype(np.float32)
    iu = (same & (i <= j)).astype(np.float32)
    c["rwmask"] = np.concatenate([-sl, -su, su, iu], axis=1)
    c["iumask"] = iu
    sg = np.ones((128, 128), np.float32)
    sg[:, :64] = -1
    c["signs"] = sg
    kl = np.arange(128)[:, None]
    cc = np.arange(896)[None, :]
    c["negle"] = np.where(kl <= cc - 384, 0.0, NEG).astype(np.float32)
    c["neglt"] = np.where(kl < cc - 384, 0.0, NEG).astype(np.float32)
    c["trineg"] = np.where(i >= j, -1.0, 0.0).astype(np.float32)
    c["iumask4"] = np.tile(iu, (1, 4))
    c["signs4"] = np.tile(sg, (1, 4))
    c["ident2"] = np.concatenate([np.eye(64, dtype=np.float32)] * 2, axis=0)
    ci = np.zeros((128, 2), np.float32)
    ci[:64, 0] = 1
    ci[64:, 1] = 1
    c["cind"] = ci
    return c


CONST_SHAPES = dict(identf=(128, 128), bones=(128, 128), chunkmask=(128, 512), rwmask=(128, 512),
                    iumask=(128, 128), signs=(128, 128), negle=(128, 896), neglt=(128, 896),
                    trineg=(128, 128), cind=(128, 2), iumask4=(128, 512), signs4=(128, 512),
                    ident2=(128, 64))


def build_program(NP, NS, dbg=None):
    nc = bass.Bass("TRN2", target_bir_lowering=False)
    P = Prog(nc)

    def din(name, shape):
        return nc.dram_tensor(name, list(shape), F32, kind="ExternalInput").ap()

    def dout(name, shape):
        return nc.dram_tensor(name, list(shape), F32, kind="ExternalOutput").ap()

    NPm, NSm = max(NP, 1), max(NS, 1)
    xp = din("x_prompt", (NPm, T_P, D))
    xs_ = din("x_sample", (NSm, T_S, D))
    cfk = din("cache_fox_k", (NL, NSm, PAST, 384))
    cfv = din("cache_fox_v", (NL, NSm, PAST, 384))
    cfl = din("cache_fox_logf", (NL, NSm, PAST, 6))
    csk = din("cache_sb_k", (NL, NSm, PAST, 256))
    csv = din("cache_sb_v", (NL, NSm, PAST, 256))
    swkv = din("state_wkv", (NL, NSm, 6, 64, 64))
    sshift = din("state_shift", (NL, NSm, SHIFT_W))
    w_in = din("w_in", (NL, D, INC))
    mu_shift = din("mu_shift", (NL, SHIFT_W))
    w0_decay = din("w0_decay", (NL, 384))
    w_decay = din("w_decay", (NL, 64, 384))
    a0 = din("a0", (NL, 384))
    w_aaa = din("w_aaa", (NL, 64, 384))
    k_k = din("k_k", (NL, 384))
    k_a = din("k_a", (NL, 384))
    r_k = din("r_k", (NL, 384))
    lnx_g = din("lnx_g", (NL, 384))
    lnx_b = din("lnx_b", (NL, 384))
    fox_fb = din("fox_fb", (NL, 6))
    w_out = din("w_out", (NL, D, D))
    ln_g = din("ln_g", (NL, D))
    ln_b = din("ln_b", (NL, D))
    cdr = {k: din("c_" + k, s) for k, s in CONST_SHAPES.items()}

    outs = {}
    for pre, n, t in (("p", NPm, T_P), ("s", NSm, T_S)):
        outs[pre + "y"] = dout(pre + "_y", (n, t, D))
        outs[pre + "fk"] = dout(pre + "_fox_k", (NL, n, t, 384))
        outs[pre + "fv"] = dout(pre + "_fox_v", (NL, n, t, 384))
        outs[pre + "fl"] = dout(pre + "_fox_logf", (NL, n, t, 6))
        outs[pre + "sk"] = dout(pre + "_sb_k", (NL, n, t, 256))
        outs[pre + "sv"] = dout(pre + "_sb_v", (NL, n, t, 256))
        outs[pre + "wkv"] = dout(pre + "_wkv", (NL, n, 6, 64, 64))
        outs[pre + "sh"] = dout(pre + "_shift", (NL, n, SHIFT_W))
    dbg_out = {}
    if dbg:
        for k, s in dbg.items():
            if not k.startswith("_"):
                dbg_out[k] = dout("dbg_" + k, s)

    WF = nc.dram_tensor("WF", [NL, NFT, 128, 8, 128], BF16).ap()
    WT = nc.dram_tensor("WT", [NL, 3, 128, 8, 512], BF16).ap()
    WO = nc.dram_tensor("WO", [NL, 128, 8, 1024], BF16).ap()
    Y0 = nc.dram_tensor("Y0", [NPm + NSm, T_P, D], F32).ap()
    bWF = [[Buf(f"WF{l}_{t}") for t in range(NFT)] for l in range(NL)]
    bWT = [[Buf(f"WT{l}_{g}") for g in range(3)] for l in range(NL)]
    bWO = [Buf(f"WO{l}") for l in range(NL)]
    bY0 = [[Buf(f"Y0_{i}_{t}") for t in range(16)] for i in range(NPm + NSm)]

    for l in range(NL):
        wv = w_in[l].rearrange("(c p) n -> p c n", p=128)
        for g, (s0, n) in enumerate(TG):
            P.dma(POOL, WT[l, g, :, :, 0:n], wv[:, :, s0:s0 + n], writes=[bWT[l][g]])
        use_order = [9, 0, 3, 6, 10, 1, 4, 7, 11, 2, 5, 8, 12, FT_FB] + list(range(13, 28))
        for t in use_order:
            s0, n = FT[t]
            P.dma(POOL, WF[l, t, :, :, 0:n], wv[:, :, s0:s0 + n], writes=[bWF[l][t]])
        wov = w_out[l].rearrange("(c p) n -> p c n", p=128)
        for hfi in range(2):
            P.dma(POOL, WO[l, :, :, hfi * 512:(hfi + 1) * 512], wov[:, :, hfi * 512:(hfi + 1) * 512],
                  writes=[bWO[l]])

    mem = Mem(nc, 16640, 228000)
    psum = [Tl(nc.alloc_psum_tensor(f"pb{i}", [128, 512], F32), f"pb{i}") for i in range(8)]
    for p_ in psum:
        p_.b.excl = True

    def pbf(i):
        return psum[i].h.ap().bitcast(BF16)

    C = {}
    for k, s in CONST_SHAPES.items():
        C[k] = mem.alloc("c_" + k, s, F32)
        P.dma(SP, C[k][:], cdr[k], writes=[C[k].b])
    identb = mem.alloc("identb", (128, 128), BF16)
    bonesb = mem.alloc("bonesb", (128, 128), BF16)
    bones64 = mem.alloc("bones64", (128, 128), BF16)
    ones64 = mem.alloc("ones64", (128, 64), BF16)
    onesneg = mem.alloc("onesneg", (128, 128), BF16)
    trinegb = mem.alloc("trinegb", (128, 128), BF16)
    negleb = mem.alloc("negleb", (128, 896), BF16)
    negltb = mem.alloc("negltb", (128, 896), BF16)
    onecol = mem.alloc("onecol", (128, 1), F32)
    ident8 = mem.alloc("ident8", (128, 8, 128), BF16)
    for u_ in range(8):
        P.op(DVE, lambda e, u_=u_: e.tensor_copy(out=ident8[:, u_, :], in_=C["identf"][:]), reads=[C["identf"].b], writes=[ident8.b])
    P.op(DVE, lambda e: e.tensor_copy(out=identb[:], in_=C["identf"][:]), reads=[C["identf"].b], writes=[identb.b])
    P.op(DVE, lambda e: e.tensor_copy(out=bonesb[:], in_=C["bones"][:]), reads=[C["bones"].b], writes=[bonesb.b])
    P.op(DVE, lambda e: e.tensor_scalar(out=bones64[:], in0=C["bones"][:], scalar1=1.0 / 64, scalar2=None, op0=ALU.mult),
         reads=[C["bones"].b], writes=[bones64.b])
    P.op(DVE, lambda e: e.memset(ones64[:], 1.0), writes=[ones64.b])
    P.op(DVE, lambda e: e.memset(onesneg[:], -1.0), writes=[onesneg.b])
    P.op(DVE, lambda e: e.memset(onecol[:], 1.0), writes=[onecol.b])
    P.op(DVE, lambda e: e.tensor_copy(out=trinegb[:], in_=C["trineg"][:]), reads=[C["trineg"].b], writes=[trinegb.b])
    P.op(DVE, lambda e: e.tensor_copy(out=negleb[:], in_=C["negle"][:]), reads=[C["negle"].b], writes=[negleb.b])
    P.op(DVE, lambda e: e.tensor_copy(out=negltb[:], in_=C["neglt"][:]), reads=[C["neglt"].b], writes=[negltb.b])

    def load_cols(name, src, ntile):
        t = mem.alloc(name, (128, NL, ntile), F32)
        with nc.allow_non_contiguous_dma("small param transpose load"):
            for l in range(NL):
                P.dma(SP, t[:, l, :], src[l].rearrange("(t p) -> p t", p=128), writes=[t.b])
        return t
    mu_t = load_cols("mu", mu_shift, 10)
    omm_t = mem.alloc("omm", (128, NL, 10), F32)
    P.op(DVE, lambda e: e.tensor_scalar(out=omm_t[:], in0=mu_t[:], scalar1=-1.0, scalar2=1.0, op0=ALU.mult, op1=ALU.add),
         reads=[mu_t.b], writes=[omm_t.b])
    w0_t = load_cols("w0", w0_decay, 3)
    a0_t = load_cols("a0", a0, 3)
    kk_t = load_cols("kk", k_k, 3)
    ka_t = load_cols("ka", k_a, 3)
    rk_t = load_cols("rk", r_k, 3)
    lg_t = load_cols("lxg", lnx_g, 3)
    lb_t = load_cols("lxb", lnx_b, 3)
    omka_t = mem.alloc("omka", (128, NL, 3), F32)
    P.op(DVE, lambda e: e.tensor_scalar(out=omka_t[:], in0=ka_t[:], scalar1=-1.0, scalar2=1.0, op0=ALU.mult, op1=ALU.add),
         reads=[ka_t.b], writes=[omka_t.b])
    nfb_t = mem.alloc("nfb", (128, NL), F32)
    with nc.allow_non_contiguous_dma("small param transpose load"):
        P.dma(SP, nfb_t[0:6, :], fox_fb.rearrange("l h -> h l"), writes=[nfb_t.b])
    P.op(DVE, lambda e: e.tensor_scalar(out=nfb_t[0:6, :], in0=nfb_t[0:6, :], scalar1=-1.0, scalar2=None, op0=ALU.mult),
         reads=[nfb_t.b], writes=[nfb_t.b])
    fbb = mem.alloc("fbb", (128, NL, 6), F32)
    P.dma(SP, fbb[:].rearrange("p l h -> p (l h)"), fox_fb.rearrange("l h -> (l h)").partition_broadcast(128), writes=[fbb.b])
    lwf = mem.alloc("lwf", (128, NL, 384), F32)
    lw = mem.alloc("lw", (128, NL, 384), BF16)
    for l in range(NL):
        P.dma(SP, lwf[0:64, l, :], w_decay[l], writes=[lwf.b])
        P.dma(SP, lwf[64:128, l, :], w_aaa[l], writes=[lwf.b])
    P.op(DVE, lambda e: e.tensor_copy(out=lw[:], in_=lwf[:]), reads=[lwf.b], writes=[lw.b])

    xT = mem.alloc("xT", (128, 8, T_P), BF16)
    oT = mem.alloc("oT", (128, 8, T_P), BF16)
    oTb = [Buf(f"oT{c}") for c in range(8)]
    Vb = mem.alloc("Vb", (128, 17, 384), BF16)
    Vc = mem.alloc("Vc", (128, 17, 256), BF16)
    Gst = [mem.alloc(f"G{i}", (128, 3, 64), F32) for i in range(2)]
    Gb = [[[Buf(f"G{i}_{c3}_{hh}") for hh in range(2)] for c3 in range(3)] for i in range(2)]
    ucarry = mem.alloc("ucarry", (128, 10), F32)
    ucb = [Buf(f"uc{ct}") for ct in range(10)]
    phase_base = mem.top
    if dbg:
        for c in range(8):
            P.op(POOL, lambda e, c=c: e.memset(oT[:, c, :], 0.0), writes=[oTb[c]])

    bank_rr = [0]

    def next_bank(lo=0, hi=2):
        i = lo + bank_rr[0] % (hi - lo)
        bank_rr[0] += 1
        return psum[i]

    wslot_rr = [0]

    from types import SimpleNamespace
    PHASES = set(dbg["_phases"]) if (dbg and "_phases" in dbg) else set("RFS")

    def dump(name, ap, reads):
        if name in dbg_out:
            P.dma(POOL, dbg_out[name], ap, reads=reads)

    def act(out, in_, func, R, W, bias=None, scale=None):
        kw = {}
        if bias is not None:
            kw["bias"] = bias
        if scale is not None:
            kw["scale"] = scale
        P.op(ACT, lambda e: e.activation(out=out, in_=in_, func=func, **kw), reads=R, writes=W)

    def tt(eng, out, in0, in1, op, R, W):
        P.op(eng, lambda e: e.tensor_tensor(out=out, in0=in0, in1=in1, op=op), reads=R, writes=W)

    def ts(eng, out, in0, s1, s2, op0, op1, R, W):
        if op1 is None and eng == POOL and op0 == ALU.mult:
            op1, s2 = ALU.mult, 1.0
        if op1 is None:
            P.op(eng, lambda e: e.tensor_scalar(out=out, in0=in0, scalar1=s1, scalar2=None, op0=op0), reads=R, writes=W)
        else:
            P.op(eng, lambda e: e.tensor_scalar(out=out, in0=in0, scalar1=s1, scalar2=s2, op0=op0, op1=op1), reads=R, writes=W)

    def stt(out, in0, scalar, in1, op0, op1, R, W):
        P.op(DVE, lambda e: e.scalar_tensor_tensor(out=out, in0=in0, scalar=scalar, in1=in1, op0=op0, op1=op1), reads=R, writes=W)

    def cp(eng, out, in_, R, W):
        if eng == ACT:
            P.op(ACT, lambda e: e.activation(out=out, in_=in_, func=AF.Copy), reads=R, writes=W)
        else:
            P.op(eng, lambda e: e.tensor_copy(out=out, in_=in_), reads=R, writes=W)

    heat = {"n": 0, "bank": None}

    def mm(out, lhsT, rhs, start, stop, R, W):
        P.op(PE, lambda e: e.matmul(out, lhsT=lhsT, rhs=rhs, start=start, stop=stop), reads=R, writes=W)
        if heat["n"] and stop:
            hb_ = heat["bank"]
            for _ in range(heat["n"]):
                P.op(PE, lambda e: e.matmul(hb_[:, 0:512], lhsT=identb[:, :], rhs=negltb[:, 0:512], start=True, stop=True),
                     reads=[identb.b, negltb.b], writes=[hb_.b])

    def tr(out, in_, ident, R, W):
        P.op(PE, lambda e: e.transpose(out=out, in_=in_, identity=ident), reads=R, writes=W)

    def memset(eng, ap, val, W):
        P.op(eng, lambda e: e.memset(ap, val), writes=W)

    def load_w(cx, slots, t):
        ws = slots[wslot_rr[0] % len(slots)]
        wslot_rr[0] += 1
        n = FT[t][1]
        P.dma(SP, ws[:, :, 0:n], WF[cx.l, t, :, :, 0:n], reads=[bWF[cx.l][t]], writes=[ws.b])
        return ws

    def project(cx, ws, ncol, blk, bank):
        cols = slice(blk * cx.BW, (blk + 1) * cx.BW)
        for c in range(8):
            mm(bank[0:ncol, 0:cx.BW], ws[:, c, 0:ncol], xT[:, c, cols], c == 0, c == 7, [ws.b, xT.b], [bank.b])

    def to_xT(cx, src, tt_, banks):
        TT = cx.TT
        for c in range(8):
            tr(banks[c // 4][:, (c % 4) * TT:(c % 4 + 1) * TT], src[:TT, c * 128:(c + 1) * 128],
               C["identf"][:TT, :TT], [src.b, C["identf"].b], [banks[c // 4].b])
        for j in range(2):
            s_ = banks[j][:, 0:4 * TT].rearrange("p (c t) -> p c t", t=TT)
            d_ = xT[:, 4 * j:4 * j + 4, tt_ * TT:(tt_ + 1) * TT]
            cp(ACT if j == 0 else DVE, d_, s_, [banks[j].b], [xT.b])

    def phase_X(cx):
        P.tag = "X"
        mem.top = phase_base
        P.barrier()
        xin = [mem.alloc(f"xin{i}", (128, D), F32) for i in range(2)]
        for tt_ in range(cx.NTT):
            sl = xin[tt_ % 2]
            P.dma(SP, sl[:cx.TT, :], cx.xsrc[cx.si, tt_ * cx.TT:(tt_ + 1) * cx.TT, :], writes=[sl.b])
            bk = (psum[0], psum[1]) if tt_ % 2 == 0 else (psum[2], psum[3])
            to_xT(cx, sl, tt_, bk)

    def phase_T(cx):
        P.tag = "T"
        l, si, TT, O = cx.l, cx.si, cx.TT, cx.O
        mem.top = phase_base
        P.barrier()
        wt = [mem.alloc(f"wt{g}", (128, 8, 512), BF16) for g in range(3)]
        stage = [mem.alloc(f"stg{i}", (128, 1288), F32) for i in range(2)]
        for g in range(3):
            P.dma(SP, wt[g][:, :, 0:TG[g][1]], WT[l, g, :, :, 0:TG[g][1]], reads=[bWT[l][g]], writes=[wt[g].b])
        if cx.kind == "s":
            P.dma(POOL, Vb[:, 0:16, :], cfv[l, si].rearrange("(t p) c -> p t c", p=128), writes=[Vb.b])
            P.dma(POOL, Vc[:, 0:16, :], csv[l, si].rearrange("(t p) c -> p t c", p=128), writes=[Vc.b])
        for tt_ in range(min(cx.NTT, int(os.environ.get("DBG_NTT", "99")))):
            st = stage[tt_ % 2]
            rows = slice(tt_ * TT, (tt_ + 1) * TT)
            bks = [psum[3 * (tt_ % 2) + g] for g in range(3)]
            for g, (s0, n) in enumerate(TG):
                for c in range(8):
                    mm(bks[g][:TT, 0:n], xT[:, c, rows], wt[g][:, c, 0:n], c == 0, c == 7, [xT.b, wt[g].b], [bks[g].b])
            cp(ACT, st[:TT, 0:384], bks[0][:TT, 0:384], [bks[0].b], [st.b])
            cp(DVE, st[:TT, 384:768], bks[1][:TT, 0:384], [bks[1].b], [st.b])
            cp(ACT, st[:TT, 768:1280], bks[2][:TT, 0:512], [bks[2].b], [st.b])
            tt(DVE, st[:TT, 1280:1286], bks[1][:TT, 384:390], fbb[:TT, l, :], ALU.add, [bks[1].b, fbb.b], [st.b])
            act(st[:TT, 1280:1286], st[:TT, 1280:1286], AF.Exp, [st.b], [st.b], scale=-1.0)
            act(st[:TT, 1280:1286], st[:TT, 1280:1286], AF.Ln, [st.b], [st.b], bias=1.0)
            ts(DVE, st[:TT, 1280:1286], st[:TT, 1280:1286], -1.0, None, ALU.mult, None, [st.b], [st.b])
            P.dma(SP, O["fl"][l, si, rows, :], st[:TT, 1280:1286], reads=[st.b])
            DBGT = int(os.environ.get("DBG_T", "9"))
            if DBGT in (2, 4, 9):
                cp(DVE, Vb[:TT, cx.PKT + tt_, :], bks[1][:TT, 0:384], [bks[1].b], [Vb.b])
            if DBGT in (2, 5, 9):
                cp(DVE, Vc[:TT, cx.PKT + tt_, :], st[:TT, 1024:1280], [st.b], [Vc.b])
            if DBGT != 9:
                continue
            P.dma(SP, O["fk"][l, si, rows, :], st[:TT, 0:384], reads=[st.b])
            P.dma(SP, O["fv"][l, si, rows, :], st[:TT, 384:768], reads=[st.b])
            P.dma(SP, O["sk"][l, si, rows, :], st[:TT, 768:1024], reads=[st.b])
            P.dma(SP, O["sv"][l, si, rows, :], st[:TT, 1024:1280], reads=[st.b])

    def phase_F(cx):
        P.tag = "F.pre"
        l, si, T, NK, BW, QW, past, O = cx.l, cx.si, cx.T, cx.NK, cx.BW, cx.BW, cx.past, cx.O
        mem.top = phase_base
        P.barrier()
        wsl = [mem.alloc(f"wf{i}", (128, 8, 128), BF16) for i in range(4)]
        Qa = [mem.alloc(f"Qa{i}", (128, T), BF16) for i in range(2)]
        Ka = [mem.alloc(f"Ka{i}", (128, NK), BF16) for i in range(2)]
        gate = mem.alloc("gate", (128, T), BF16)
        Aa = mem.alloc("Aa", (128, NK), F32)
        Sa = mem.alloc("Sa", (128, NK), F32)
        SPL = mem.alloc("SPL", (128, 3, NK), BF16)
        pts = [mem.alloc(f"pt{i}", (128, QW), BF16) for i in range(4)]
        rD = [mem.alloc(f"rD{i}", (128, QW), F32) for i in range(2)]
        o1 = [mem.alloc(f"o1{i}", (128, QW), F32) for i in range(2)]
        kst = [mem.alloc(f"kst{i}", (128, 16, 128), F32) for i in range(2)] if cx.kind == "s" else None

        def load_kcache(hp_):
            k_ = kst[hp_ % 2]
            P.dma(SP, k_[:], cfk[l, si].rearrange("(t p) c -> p t c", p=128)[:, :, hp_ * 128:(hp_ + 1) * 128], writes=[k_.b])
        if cx.kind == "s":
            load_kcache(0)
        Vaug = mem.alloc("Vaug", (128, 17, 2, 128), BF16)
        memset(POOL, Vaug[:, :, :, :], 1.0, [Vaug.b])

        wfb = load_w(cx, wsl, FT_FB)
        if cx.kind == "s":
            lst = mem.alloc("lst", (128, 16, 6), F32)
            P.dma(SP, lst[:, :, :], cfl[l, si].rearrange("(t p) h -> p t h", p=128), writes=[lst.b])
            for t4 in range(4):
                bank = next_bank()
                for t in range(4):
                    tr(bank[0:6, t * 128:(t + 1) * 128], lst[:, 4 * t4 + t, :], C["identf"][:, :], [lst.b, C["identf"].b], [bank.b])
                cp(DVE if t4 % 2 else ACT, Aa[0:6, t4 * 512:(t4 + 1) * 512], bank[0:6, 0:512], [bank.b], [Aa.b])
        for blk in range(cx.NBLK):
            bank = next_bank()
            project(cx, wfb, 6, blk, bank)
            cols = slice(past + blk * BW, past + (blk + 1) * BW)
            act(Aa[0:6, cols], bank[0:6, 0:BW], AF.Exp, [bank.b, nfb_t.b], [Aa.b], bias=nfb_t[0:6, l:l + 1], scale=-1.0)
            act(Aa[0:6, cols], Aa[0:6, cols], AF.Ln, [Aa.b], [Aa.b], bias=1.0)
            ts(DVE, Aa[0:6, cols], Aa[0:6, cols], -1.0, None, ALU.mult, None, [Aa.b], [Aa.b])
        P.op(DVE, lambda e: e.tensor_tensor_scan(out=Sa[0:6, 0:NK], data0=onecol[0:6, 0:1].to_broadcast([6, NK]),
                                                 data1=Aa[0:6, 0:NK], initial=0.0, op0=ALU.mult, op1=ALU.subtract),
             reads=[Aa.b, onecol.b], writes=[Sa.b])
        cp(DVE, SPL[0:6, 0, :], Sa[0:6, 0:NK], [Sa.b], [SPL.b])
        tt(DVE, Aa[0:6, 0:NK], Sa[0:6, 0:NK], SPL[0:6, 0, :], ALU.subtract, [Sa.b, SPL.b], [Aa.b])
        cp(DVE, SPL[0:6, 1, :], Aa[0:6, 0:NK], [Aa.b], [SPL.b])
        tt(DVE, Aa[0:6, 0:NK], Aa[0:6, 0:NK], SPL[0:6, 1, :], ALU.subtract, [Aa.b, SPL.b], [Aa.b])
        cp(DVE, SPL[0:6, 2, :], Aa[0:6, 0:NK], [Aa.b], [SPL.b])

        DBGF = int(os.environ.get("DBG_F", "9"))
        if DBGF < 1:
            return
        for hp in range(3):
            P.tag = "F.proj"
            wq = load_w(cx, wsl, FT_QB + hp)
            wk = load_w(cx, wsl, FT_KB + hp)
            wg = load_w(cx, wsl, FT_GB + hp)
            for hh in range(2):
                h = 2 * hp + hh
                memset(POOL, Qa[hh][64:70, 0:T], 1.0, [Qa[hh].b])
                memset(POOL, Ka[hh][64:70, 0:NK], -1.0, [Ka[hh].b])
                for r in range(3):
                    P.dma(SP, Qa[hh][64 + r:65 + r, 0:T], SPL[h:h + 1, r, past:NK], reads=[SPL.b], writes=[Qa[hh].b])
                    P.dma(SP, Ka[hh][67 + r:68 + r, 0:NK], SPL[h:h + 1, r, 0:NK], reads=[SPL.b], writes=[Ka[hh].b])
            if cx.kind == "s":
                kcur = kst[hp % 2]
                for hh in range(2):
                    for t4 in range(4):
                        bank = next_bank()
                        for t in range(4):
                            tr(bank[0:64, t * 128:(t + 1) * 128], kcur[:, 4 * t4 + t, hh * 64:(hh + 1) * 64], C["identf"][:, :],
                               [kcur.b, C["identf"].b], [bank.b])
                        cp(DVE if t4 % 2 else ACT, Ka[hh][0:64, t4 * 512:(t4 + 1) * 512], bank[0:64, 0:512], [bank.b], [Ka[hh].b])
                if hp + 1 < 3:
                    load_kcache(hp + 1)
            for blk in range(cx.NBLK):
                cols = slice(blk * BW, (blk + 1) * BW)
                kcols = slice(past + blk * BW, past + (blk + 1) * BW)
                bq = next_bank()
                project(cx, wq, 128, blk, bq)
                act(Qa[0][0:64, cols], bq[0:64, 0:BW], AF.Copy, [bq.b], [Qa[0].b], scale=0.125)
                ts(DVE, Qa[1][0:64, cols], bq[64:128, 0:BW], 0.125, None, ALU.mult, None, [bq.b], [Qa[1].b])
                bk_ = next_bank()
                project(cx, wk, 128, blk, bk_)
                cp(ACT, Ka[0][0:64, kcols], bk_[0:64, 0:BW], [bk_.b], [Ka[0].b])
                cp(DVE, Ka[1][0:64, kcols], bk_[64:128, 0:BW], [bk_.b], [Ka[1].b])
                bg = next_bank()
                project(cx, wg, 128, blk, bg)
                act(gate[:, cols], bg[:, 0:BW], AF.Silu, [bg.b], [gate.b])
            P.tag = "F.attn"
            nfull = NK // 128
            for hh_ in range(2):
                vcols = slice((2 * hp + hh_) * 64, (2 * hp + hh_ + 1) * 64)
                acols = slice(64 * hh_, 64 * hh_ + 64)
                cp(POOL, Vaug[:, 0:nfull, hh_, acols], Vb[:, 0:nfull, vcols], [Vb.b], [Vaug.b])
                if NK % 128:
                    cp(POOL, Vaug[0:NK % 128, nfull, hh_, acols], Vb[0:NK % 128, nfull, vcols], [Vb.b], [Vaug.b])
            heat["n"], heat["bank"] = HEAT_F, psum[0]
            for hh in range(2):
                if DBGF < 2:
                    break
                h = 2 * hp + hh
                ob = 64 * hh
                db = 64 - ob
                for qt in range(T // QW):
                    q0 = qt * QW
                    qlo = past + q0
                    qhi = qlo + QW - 1
                    kts = [kt for kt in range(cx.NKT) if kt * 128 <= qhi]
                    Ob = psum[4 + (2 * hh + qt) % 4]

                    def stage2(kt, kn, pt, first, last):
                        mm(Ob[:, 0:QW], Vaug[0:kn, kt, hh, :], pt[0:kn, 0:QW], first, last, [Vaug.b, pt.b], [Ob.b])
                    pend = None
                    for i, kt in enumerate(kts):
                        kn = min(128, NK - kt * 128)
                        Sb = psum[i % 4]
                        partial = kt * 128 + kn - 1 > qlo
                        mm(Sb[0:kn, 0:QW], Ka[hh][0:70, kt * 128:kt * 128 + kn], Qa[hh][0:70, q0:q0 + QW], True, not partial,
                           [Ka[hh].b, Qa[hh].b], [Sb.b])
                        if partial:
                            d = kt * 128 - qlo
                            mm(Sb[0:kn, 0:QW], identb[0:kn, 0:kn], negleb[0:kn, 384 - d:384 - d + QW], False, True,
                               [identb.b, negleb.b], [Sb.b])
                        pt = pts[i % 4]
                        act(pt[0:kn, 0:QW], Sb[0:kn, 0:QW], AF.Exp, [Sb.b], [pt.b])
                        if pend:
                            stage2(*pend)
                        pend = (kt, kn, pt, i == 0, i == len(kts) - 1)
                    stage2(*pend)
                    rd, oo = rD[qt % 2], o1[qt % 2]
                    P.op(DVE, lambda e, o=rd[ob:ob + 64, 0:QW], i_=Ob[db:db + 64, 0:QW]: e.reciprocal(out=o, in_=i_),
                         reads=[Ob.b], writes=[rd.b])
                    tt(DVE, oo[ob:ob + 64, 0:QW], Ob[ob:ob + 64, 0:QW], rd[ob:ob + 64, 0:QW], ALU.mult, [Ob.b, rd.b], [oo.b])
                    tt(POOL, oT[ob:ob + 64, 3 + h // 2, q0:q0 + QW], oo[ob:ob + 64, 0:QW], gate[ob:ob + 64, q0:q0 + QW], ALU.mult,
                       [oo.b, gate.b], [oTb[3 + h // 2]])
            heat["n"] = 0

    def phase_S(cx):
        P.tag = "S.pre"
        l, si, T, NK, BW, QW, past, O = cx.l, cx.si, cx.T, cx.NK, cx.BW, cx.BW, cx.past, cx.O
        mem.top = phase_base
        P.barrier()
        wsl = [mem.alloc(f"wf{i}", (128, 8, 128), BF16) for i in range(4)]
        Qc = [mem.alloc(f"Qc{i}", (128, T), BF16) for i in range(2)]
        Kc = [mem.alloc(f"Kc{i}", (128, NK), BF16) for i in range(2)]
        gate = mem.alloc("gate", (128, T), BF16)
        Ef = [mem.alloc(f"Ef{i}", (128, QW), F32) for i in range(3)]
        Xe = [mem.alloc(f"Xe{i}", (128, QW), BF16) for i in range(3)]
        spb = [mem.alloc(f"spb{i}", (128, QW), BF16) for i in range(3)]
        Lsb = [mem.alloc(f"Lsb{i}", (128, QW), BF16) for i in range(3)]
        Wt = [mem.alloc(f"Wt{i}", (128, QW), BF16) for i in range(3)]
        Lsum = [mem.alloc(f"Lsum{i}", (128, QW), F32) for i in range(2)]
        kst = [mem.alloc(f"kst{i}", (128, 16, 128), F32) for i in range(2)] if cx.kind == "s" else None

        def load_kcache(sp_):
            k_ = kst[sp_ % 2]
            P.dma(SP, k_[:], csk[l, si].rearrange("(t p) c -> p t c", p=128)[:, :, sp_ * 128:(sp_ + 1) * 128], writes=[k_.b])
        if cx.kind == "s":
            load_kcache(0)
            load_kcache(1)
        for sp in range(2):
            P.tag = "S.proj"
            wq = load_w(cx, wsl, FT_QC + sp)
            wk = load_w(cx, wsl, FT_KC + sp)
            wg = load_w(cx, wsl, FT_GC + sp)
            if cx.kind == "s":
                kcur = kst[sp % 2]
                for hh in range(2):
                    for t4 in range(4):
                        bank = next_bank()
                        for t in range(4):
                            tr(bank[0:64, t * 128:(t + 1) * 128], kcur[:, 4 * t4 + t, hh * 64:(hh + 1) * 64], C["identf"][:, :],
                               [kcur.b, C["identf"].b], [bank.b])
                        cp(DVE if t4 % 2 else ACT, Kc[hh][0:64, t4 * 512:(t4 + 1) * 512], bank[0:64, 0:512], [bank.b], [Kc[hh].b])
            for blk in range(cx.NBLK):
                cols = slice(blk * BW, (blk + 1) * BW)
                kcols = slice(past + blk * BW, past + (blk + 1) * BW)
                bq = next_bank()
                project(cx, wq, 128, blk, bq)
                act(Qc[0][0:64, cols], bq[0:64, 0:BW], AF.Copy, [bq.b], [Qc[0].b], scale=0.125)
                ts(DVE, Qc[1][0:64, cols], bq[64:128, 0:BW], 0.125, None, ALU.mult, None, [bq.b], [Qc[1].b])
                bk_ = next_bank()
                project(cx, wk, 128, blk, bk_)
                cp(ACT, Kc[0][0:64, kcols], bk_[0:64, 0:BW], [bk_.b], [Kc[0].b])
                cp(DVE, Kc[1][0:64, kcols], bk_[64:128, 0:BW], [bk_.b], [Kc[1].b])
                bg = next_bank()
                project(cx, wg, 128, blk, bg)
                act(gate[:, cols], bg[:, 0:BW], AF.Silu, [bg.b], [gate.b])
            P.tag = "S.attn"
            heat["n"], heat["bank"] = HEAT_S, psum[0]
            for hh in range(2):
                hc = 2 * sp + hh
                ob = 64 * hh
                for qt in range(T // QW):
                    q0 = qt * QW
                    qlo = past + q0
                    qhi = qlo + QW - 1
                    kts = list(reversed([kt for kt in range(cx.NKT) if kt * 128 <= qhi]))
                    n = len(kts)
                    Ob = psum[6 + qt % 2]
                    memset(POOL, Lsum[0][:, :], 0.0, [Lsum[0].b])
                    memset(POOL, Lsum[1][:, :], 0.0, [Lsum[1].b])

                    def info(i):
                        kt = kts[i]
                        return kt, min(128, NK - kt * 128), psum[2 + i % 2]

                    def stage1(i):
                        kt, kn, Zb = info(i)
                        partial = kt * 128 + kn - 1 >= qlo
                        mm(Zb[0:kn, 0:QW], Kc[hh][0:64, kt * 128:kt * 128 + kn], Qc[hh][0:64, q0:q0 + QW], True, not partial,
                           [Kc[hh].b, Qc[hh].b], [Zb.b])
                        if partial:
                            d = kt * 128 - qlo
                            mm(Zb[0:kn, 0:QW], identb[0:kn, 0:kn], negltb[0:kn, 384 - d:384 - d + QW], False, True,
                               [identb.b, negltb.b], [Zb.b])
                        j = i % 3
                        act(Ef[j][0:kn, :], Zb[0:kn, 0:QW], AF.Exp, [Zb.b], [Ef[j].b])
                        act(spb[j][0:kn, :], Ef[j][0:kn, :], AF.Ln, [Ef[j].b], [spb[j].b], bias=1.0)
                        La, Lb = Lsum[i % 2], Lsum[(i + 1) % 2]
                        if i + 1 < n:
                            jn = (i + 1) % 3
                            if kn < 128:
                                memset(POOL, Lsb[jn][kn:128, :], 0.0, [Lsb[jn].b])
                            tt(DVE, Lsb[jn][0:kn, :], La[0:kn, :], spb[j][0:kn, :], ALU.add, [La.b, spb[j].b], [Lsb[jn].b])
                            tt(DVE, Lb[0:kn, :], La[0:kn, :], spb[j][0:kn, :], ALU.add, [La.b, spb[j].b], [Lb.b])

                    def stage2(i):
                        kt, kn, _ = info(i)
                        Zb = psum[4 + i % 2]
                        j = i % 3
                        mm(Zb[0:kn, 0:QW], trinegb[0:kn, 0:kn], spb[j][0:kn, :], True, i == 0, [trinegb.b, spb[j].b], [Zb.b])
                        if i > 0:
                            mm(Zb[0:kn, 0:QW], onesneg[:, 0:kn], Lsb[j][:, :], False, True, [onesneg.b, Lsb[j].b], [Zb.b])
                        act(Xe[j][0:kn, :], Zb[0:kn, 0:QW], AF.Exp, [Zb.b], [Xe[j].b])
                        tt(DVE, Wt[j][0:kn, :], Ef[j][0:kn, :], Xe[j][0:kn, :], ALU.mult, [Ef[j].b, Xe[j].b], [Wt[j].b])

                    def stage3(i):
                        kt, kn, Zb = info(i)
                        j = i % 3
                        mm(Ob[ob:ob + 64, 0:QW], Vc[0:kn, kt, hc * 64:(hc + 1) * 64], Wt[j][0:kn, :], i == 0, i == n - 1,
                           [Vc.b, Wt[j].b], [Ob.b])
                    for s_ in range(n + 2):
                        if s_ < n:
                            stage1(s_)
                        if 0 <= s_ - 1 < n:
                            stage2(s_ - 1)
                        if 0 <= s_ - 2 < n:
                            stage3(s_ - 2)
                    tt(DVE, oT[ob:ob + 64, 6 + hc // 2, q0:q0 + QW], Ob[ob:ob + 64, 0:QW], gate[ob:ob + 64, q0:q0 + QW], ALU.mult,
                       [Ob.b, gate.b], [oTb[6 + hc // 2]])
            heat["n"] = 0

    def phase_E(cx):
        P.tag = "E"
        l, si, TT, O = cx.l, cx.si, cx.TT, cx.O
        mem.top = phase_base
        P.barrier()
        wo = mem.alloc("wo", (128, 8, 1024), BF16)
        lng = mem.alloc("lng", (128, D), F32)
        lnb = mem.alloc("lnb", (128, D), F32)
        xres = [mem.alloc(f"xres{i}", (128, D), F32) for i in range(3)]
        Rr = [mem.alloc(f"Rr{i}", (128, D), F32) for i in range(2)]
        yv = [mem.alloc(f"yv{i}", (128, D), F32) for i in range(3)]
        st = mem.alloc("bnst", (128, 12), F32)
        mv = mem.alloc("bnmv", (128, 2), F32)
        rs = mem.alloc("bnrs", (128, 1), F32)
        nb = mem.alloc("bnnb", (128, 1), F32)
        P.dma(SP, wo[:], WO[l], reads=[bWO[l]], writes=[wo.b])
        P.dma(SP, lng[:], ln_g[l].partition_broadcast(128), writes=[lng.b])
        P.dma(SP, lnb[:], ln_b[l].partition_broadcast(128), writes=[lnb.b])
        st2 = [st, mem.alloc("bnst2", (128, 12), F32)]

        def front(tt_):
            rows = slice(tt_ * TT, (tt_ + 1) * TT)
            xr, R_, st_ = xres[tt_ % 3], Rr[tt_ % 2], st2[tt_ % 2]
            if l == 0:
                P.dma(SP, xr[:TT, :], cx.xsrc[si, rows, :], writes=[xr.b])
            else:
                P.dma(SP, xr[:TT, :], Y0[cx.yidx, rows, :], reads=[bY0[cx.yidx][tt_]], writes=[xr.b])
            bA, bB = psum[2 * (tt_ % 2)], psum[2 * (tt_ % 2) + 1]
            for c in range(8):
                mm(bA[:TT, 0:512], oT[:, c, rows], wo[:, c, 0:512], c == 0, c == 7, [oTb[c], wo.b], [bA.b])
            for c in range(8):
                mm(bB[:TT, 0:512], oT[:, c, rows], wo[:, c, 512:1024], c == 0, c == 7, [oTb[c], wo.b], [bB.b])

        def frontB(tt_):
            xr, R_, st_ = xres[tt_ % 3], Rr[tt_ % 2], st2[tt_ % 2]
            bA, bB = psum[2 * (tt_ % 2)], psum[2 * (tt_ % 2) + 1]
            stt(R_[:TT, 0:512], xr[:TT, 0:512], ALPHA, bA[:TT, 0:512], ALU.mult, ALU.add, [xr.b, bA.b], [R_.b])
            stt(R_[:TT, 512:1024], xr[:TT, 512:1024], ALPHA, bB[:TT, 0:512], ALU.mult, ALU.add, [xr.b, bB.b], [R_.b])
            P.op(DVE, lambda e, o=st_[:TT, 0:6], i_=R_[:TT, 0:512]: e.bn_stats(out=o, in_=i_), reads=[R_.b], writes=[st_.b])
            P.op(DVE, lambda e, o=st_[:TT, 6:12], i_=R_[:TT, 512:1024]: e.bn_stats(out=o, in_=i_), reads=[R_.b], writes=[st_.b])

        def back(tt_):
            rows = slice(tt_ * TT, (tt_ + 1) * TT)
            R_, y_, st_ = Rr[tt_ % 2], yv[tt_ % 3], st2[tt_ % 2]
            P.op(DVE, lambda e, o=mv[:TT, 0:2], i_=st_[:TT, 0:12]: e.bn_aggr(out=o, in_=i_), reads=[st_.b], writes=[mv.b])
            act(rs[:TT, :], mv[:TT, 1:2], AF.Ln, [mv.b], [rs.b], bias=LN_EPS)
            act(rs[:TT, :], rs[:TT, :], AF.Exp, [rs.b], [rs.b], scale=-0.5)
            stt(nb[:TT, :], mv[:TT, 0:1], -1.0, rs[:TT, 0:1], ALU.mult, ALU.mult, [mv.b, rs.b], [nb.b])
            act(y_[:TT, :], R_[:TT, :], AF.Identity, [R_.b, rs.b, nb.b], [y_.b], bias=nb[:TT, 0:1], scale=rs[:TT, 0:1])
            tt(DVE, y_[:TT, :], y_[:TT, :], lng[:TT, :], ALU.mult, [y_.b, lng.b], [y_.b])
            tt(POOL, y_[:TT, :], y_[:TT, :], lnb[:TT, :], ALU.add, [y_.b, lnb.b], [y_.b])
            if l == 0:
                P.dma(POOL, Y0[cx.yidx, rows, :], y_[:TT, :], reads=[y_.b], writes=[bY0[cx.yidx][tt_]])
                bk = (psum[4], psum[5]) if tt_ % 2 == 0 else (psum[6], psum[7])
                to_xT(cx, y_, tt_, bk)
            else:
                P.dma(POOL, O["y"][si, rows, :], y_[:TT, :], reads=[y_.b])

        front(0)
        frontB(0)
        if cx.NTT > 1:
            front(1)
        for tt_ in range(cx.NTT):
            back(tt_)
            if tt_ + 1 < cx.NTT:
                frontB(tt_ + 1)
            if tt_ + 2 < cx.NTT:
                front(tt_ + 2)

    def phase_R(cx):
        P.tag = "R.init"
        l, si, T, BW, TT, past, O = cx.l, cx.si, cx.T, cx.BW, cx.TT, cx.past, cx.O
        NCH = TT // 64
        mem.top = phase_base
        P.barrier()
        NT = BW // TT
        NU = 2 * NT
        NCHK = NT * NCH
        wsl = [mem.alloc(f"wf{i}", (128, 8, 128), BF16) for i in range(3)]
        U = [mem.alloc(f"U{i}", (128, BW + 1), F32) for i in range(3)]
        Ul = mem.alloc("Ul", (128, BW + 1), F32)
        Dt = mem.alloc("Dt", (128, BW), F32)
        tw = mem.alloc("tw", (128, BW), BF16)
        gaT = mem.alloc("gaT", (128, BW), BF16)
        fnames = "lgc lgx ex eneg ld av tmp esfx kk kkn k2 bb epos Rt bonus Y".split()
        f = {}
        foff = {}
        for n_ in fnames:
            foff[n_] = mem.top
            f[n_] = mem.alloc(n_, (128, BW), F32)
        b = {n: mem.alloc(n, (128, BW), BF16) for n in "kk2 Rtb KKt Kh Bh Kg Bg Vbf rkb Ybf Ysq".split()}
        def alias(name, shape, dt, off):
            mem.n += 1
            return nc.alloc_sbuf_tensor_at(f"{name}_{mem.n}", list(shape), dt, offset=off)
        if BW == 512:
            MK = alias("MK", (128, 8, 512), BF16, foff["lgc"])
            MKb = [f[("lgc", "lgx", "ex", "eneg")[u // 2]].b for u in range(8)]
            MM = [alias("MM0", (128, 8, 256), BF16, foff["ld"]), alias("MM1", (128, 8, 256), BF16, foff["tmp"])]
            MMb = [[f[("ld", "av")[g // 2]].b for g in range(4)], [f[("tmp", "esfx")[g // 2]].b for g in range(4)]]
        else:
            MKt = mem.alloc("MK", (128, 8, 512), BF16)
            MK, MKb = MKt.h, [MKt.b] * 8
            MMt = [mem.alloc(f"MM{i}", (128, 8, 256), BF16) for i in range(2)]
            MM, MMb = [t_.h for t_ in MMt], [[t_.b] * 4 for t_ in MMt]
        QcT, McTt, D1sb, Y0sb = f["kk"], f["kkn"], f["k2"], f["bb"]
        TOK = mem.alloc("TOK", (128, 4, 4, 128), BF16)
        Pt = [mem.alloc(f"Pt{i}", (128, 8, 128), BF16) for i in range(2)]
        Ptb = [[Buf(f"Ptb{i}{g}") for g in range(2)] for i in range(2)]
        ArbT = mem.alloc("ArbT", (128, 2, 4, 128), BF16)
        MKraw = [mem.alloc(f"MKraw{i}", (128, 512), BF16) for i in range(2)]
        W1b = mem.alloc("W1b", (128, 8, 64), BF16)
        UW = mem.alloc("UW", (128, 8, 128), BF16)
        UWm = [mem.alloc(f"UWm{i}", (128, 8, 128), BF16) for i in range(2)]
        Vm = [mem.alloc(f"Vm{i}", (128, 4, 128), BF16) for i in range(2)]
        wst = mem.alloc("wst", (128, 6, 64), F32)
        gidx = [[0, 0] for _ in range(3)]

        if cx.kind == "p":
            memset(POOL, ucarry[:, :], 0.0, ucb)
            memset(POOL, Gst[0][:, :, :], 0.0, [Gb[0][c3][hh] for c3 in range(3) for hh in range(2)])
            memset(POOL, Gst[1][:, :, :], 0.0, [Gb[1][c3][hh] for c3 in range(3) for hh in range(2)])
        else:
            with nc.allow_non_contiguous_dma("state_shift transpose load"):
                P.dma(SP, ucarry[:, :], sshift[l, si].rearrange("(t p) -> p t", p=128), writes=ucb)
            memset(POOL, Gst[1][:, :, :], 0.0, [Gb[1][c3][hh] for c3 in range(3) for hh in range(2)])
            P.dma(SP, wst[0:64, :, :], swkv[l, si].rearrange("h v k -> v h k"), writes=[wst.b])
            for c3 in range(3):
                for hh in range(2):
                    hb = 64 * hh
                    mm(psum[2][hb:hb + 64, 256 + hh * 64:256 + (hh + 1) * 64], wst[0:64, 2 * c3 + hh, :], C["identf"][0:64, 0:64],
                       True, True, [wst.b, C["identf"].b], [psum[2].b])
                    cp(DVE, Gst[0][hb:hb + 64, c3, :], psum[2][hb:hb + 64, 256 + hh * 64:256 + (hh + 1) * 64], [psum[2].b], [Gb[0][c3][hh]])

        def uproc(Ut, bank, ct, last_blk):
            cp(ACT, Ut[:, 1:BW + 1], bank[:, 0:BW], [bank.b], [Ut.b])
            cp(ACT, Ut[:, 0:1], ucarry[:, ct:ct + 1], [ucb[ct]], [Ut.b])
            cp(ACT, ucarry[:, ct:ct + 1], Ut[:, BW:BW + 1], [Ut.b], [ucb[ct]])
            if last_blk:
                with nc.allow_non_contiguous_dma("shift state store"):
                    P.dma(POOL, O["sh"][l, si, ct * 128:(ct + 1) * 128].rearrange("(p o) -> p o", o=1), Ut[:, BW:BW + 1], reads=[Ut.b])
            act(Dt[:, :], Ut[:, 0:BW], AF.Copy, [Ut.b, mu_t.b], [Dt.b], scale=mu_t[:, l, ct:ct + 1])
            stt(Ut[:, 1:BW + 1], Ut[:, 1:BW + 1], omm_t[:, l, ct:ct + 1], Dt[:, :], ALU.mult, ALU.add, [Dt.b, omm_t.b, Ut.b], [Ut.b])

        ABANKS = (2, 4)

        def prep_A(blk, c3):
            last_blk = blk == cx.NBLK - 1
            if c3 == 0:
                P.tag = "R.lora"
                w9 = load_w(cx, wsl, 9)
                bank = next_bank(*ABANKS)
                project(cx, w9, 128, blk, bank)
                uproc(Ul, bank, 9, last_blk)
                act(tw[0:64, :], Ul[0:64, 1:BW + 1], AF.Tanh, [Ul.b], [tw.b])
                cp(DVE, tw[64:128, :], Ul[64:128, 1:BW + 1], [Ul.b], [tw.b])
                yield
            P.tag = "R.prepA"
            for j, ct in enumerate((c3, 3 + c3, 6 + c3)):
                ws = load_w(cx, wsl, ct)
                bank = next_bank(*ABANKS)
                project(cx, ws, 128, blk, bank)
                uproc(U[j], bank, ct, last_blk)
                yield
            cs = slice(c3 * 128, (c3 + 1) * 128)
            bank = next_bank(*ABANKS)
            mm(bank[:, 0:BW], lw[0:64, l, cs], tw[0:64, :], True, True, [lw.b, tw.b], [bank.b])
            act(f["ld"][:, :], bank[:, 0:BW], AF.Sigmoid, [bank.b, w0_t.b], [f["ld"].b], bias=w0_t[:, l, c3:c3 + 1])
            yield
            bank = next_bank(*ABANKS)
            mm(bank[:, 0:BW], lw[64:128, l, cs], tw[64:128, :], True, True, [lw.b, tw.b], [bank.b])
            act(f["av"][:, :], bank[:, 0:BW], AF.Sigmoid, [bank.b, a0_t.b], [f["av"].b], bias=a0_t[:, l, c3:c3 + 1])
            act(f["ld"][:, :], f["ld"][:, :], AF.Copy, [f["ld"].b], [f["ld"].b], scale=DEC_SCALE)
            yield
            P.op(DVE, lambda e: e.tensor_tensor_scan(out=f["lgc"][:, :], data0=C["chunkmask"][:, 0:BW], data1=f["ld"][:, :],
                                                     initial=0.0, op0=ALU.mult, op1=ALU.add),
                 reads=[C["chunkmask"].b, f["ld"].b], writes=[f["lgc"].b])
            tt(DVE, f["lgx"][:, :], f["lgc"][:, :], f["ld"][:, :], ALU.subtract, [f["lgc"].b, f["ld"].b], [f["lgx"].b])
            yield
            act(f["epos"][:, :], f["lgc"][:, :], AF.Exp, [f["lgc"].b], [f["epos"].b])
            act(f["ex"][:, :], f["lgx"][:, :], AF.Exp, [f["lgx"].b], [f["ex"].b])
            act(f["eneg"][:, :], f["lgc"][:, :], AF.Exp, [f["lgc"].b], [f["eneg"].b], scale=-1.0)
            yield
            lg3 = f["lgc"][:, :].rearrange("p (c n) -> p c n", n=64)
            tt(DVE, f["esfx"][:, :].rearrange("p (c n) -> p c n", n=64), lg3[:, :, 63:64].to_broadcast([128, BW // 64, 64]), lg3,
               ALU.subtract, [f["lgc"].b], [f["esfx"].b])
            act(f["esfx"][:, :], f["esfx"][:, :], AF.Exp, [f["esfx"].b], [f["esfx"].b])
            yield

        def advance(g, n):
            if g is None:
                return
            tag0 = P.tag
            for _ in range(n):
                try:
                    next(g)
                except StopIteration:
                    break
            P.tag = tag0

        def drain(g):
            advance(g, 10 ** 6)

        its = [(blk_, c3_) for blk_ in range(cx.NBLK) for c3_ in range(3)]
        drain(prep_A(0, 0))
        for blk in range(cx.NBLK):
            for c3 in range(3):
                idx_it = blk * 3 + c3
                nxtA = prep_A(*its[idx_it + 1]) if idx_it + 1 < len(its) else None
                P.tag = "R.prep"
                ws = load_w(cx, wsl, FT_GA + c3)
                bank = next_bank()
                project(cx, ws, 128, blk, bank)
                act(gaT[:, :], bank[:, 0:BW], AF.Silu, [bank.b], [gaT.b])
                r_, k_, v_ = U[0][:, 1:BW + 1], U[1][:, 1:BW + 1], U[2][:, 1:BW + 1]
                rb, kb_, vb_ = U[0].b, U[1].b, U[2].b
                act(b["kk2"][:, :], k_, AF.Square, [kb_, kk_t.b], [b["kk2"].b], scale=kk_t[:, l, c3:c3 + 1])
                bank = next_bank()
                mm(bank[:, 0:BW], bonesb[:, :], b["kk2"][:, :], True, True, [bonesb.b, b["kk2"].b], [bank.b])
                act(f["tmp"][:, :], bank[:, 0:BW], AF.Ln, [bank.b], [f["tmp"].b], bias=1e-12)
                act(f["tmp"][:, :], f["tmp"][:, :], AF.Exp, [f["tmp"].b], [f["tmp"].b], scale=-0.5)
                stt(f["kkn"][:, :], k_, kk_t[:, l, c3:c3 + 1], f["tmp"][:, :], ALU.mult, ALU.mult, [kb_, kk_t.b, f["tmp"].b], [f["kkn"].b])
                ts(DVE, f["k2"][:, :], f["av"][:, :], ka_t[:, l, c3:c3 + 1], omka_t[:, l, c3:c3 + 1], ALU.mult, ALU.add,
                   [f["av"].b, ka_t.b, omka_t.b], [f["k2"].b])
                tt(DVE, f["k2"][:, :], f["k2"][:, :], k_, ALU.mult, [f["k2"].b, kb_], [f["k2"].b])
                tt(DVE, f["bb"][:, :], f["kkn"][:, :], f["av"][:, :], ALU.mult, [f["kkn"].b, f["av"].b], [f["bb"].b])
                stt(b["rkb"][:, :], r_, rk_t[:, l, c3:c3 + 1], f["k2"][:, :], ALU.mult, ALU.mult, [rb, rk_t.b, f["k2"].b], [b["rkb"].b])
                bank = next_bank()
                mm(bank[:, 0:BW], bonesb[:, :], b["rkb"][:, :], True, True, [bonesb.b, b["rkb"].b], [bank.b])
                tt(DVE, f["bonus"][:, :], bank[:, 0:BW], v_, ALU.mult, [bank.b, vb_], [f["bonus"].b])
                tt(DVE, f["Rt"][:, :], r_, f["epos"][:, :], ALU.mult, [rb, f["epos"].b], [f["Rt"].b])
                cp(ACT, b["Rtb"][:, :], f["Rt"][:, :], [f["Rt"].b], [b["Rtb"].b])
                tt(DVE, b["KKt"][:, :], f["kkn"][:, :], f["ex"][:, :], ALU.mult, [f["kkn"].b, f["ex"].b], [b["KKt"].b])
                tt(DVE, b["Kh"][:, :], f["k2"][:, :], f["eneg"][:, :], ALU.mult, [f["k2"].b, f["eneg"].b], [b["Kh"].b])
                tt(DVE, b["Bh"][:, :], f["bb"][:, :], f["eneg"][:, :], ALU.mult, [f["bb"].b, f["eneg"].b], [b["Bh"].b])
                tt(DVE, b["Kg"][:, :], f["k2"][:, :], f["esfx"][:, :], ALU.mult, [f["k2"].b, f["esfx"].b], [b["Kg"].b])
                tt(POOL, b["Bg"][:, :], f["bb"][:, :], f["esfx"][:, :], ALU.mult, [f["bb"].b, f["esfx"].b], [b["Bg"].b])
                cp(ACT, b["Vbf"][:, :], v_, [vb_], [b["Vbf"].b])

                v3 = lambda ap, c=128, n=TT: ap.rearrange("p (a c) -> p a c", c=c)[:, :, 0:n]
                P.tag = "R.S0"
                for tl in range(NT):
                    tc = slice(tl * TT, (tl + 1) * TT)
                    bk = psum[6 + tl % 2]
                    tb = pbf(6 + tl % 2)
                    for j, nm in enumerate(("KKt", "Kg", "Bg", "Vbf")):
                        tr(tb[0:TT, j * 128:(j + 1) * 128], b[nm][:, tc], identb[:, :], [b[nm].b, identb.b], [bk.b])
                    cp(DVE if tl % 2 else ACT, TOK[0:TT, tl, :, :], tb[0:TT, 0:512].rearrange("p (a c) -> p a c", c=128), [bk.b], [TOK.b])
                P.tag = "R.S1"
                for tl in range(NT):
                    tc = slice(tl * TT, (tl + 1) * TT)
                    for hh in range(2):
                        hs = slice(64 * hh, 64 * hh + 64)
                        bA = psum[2 * (tl % 2) + hh]
                        mm(bA[0:TT, 0:TT], b["KKt"][hs, tc], b["Bh"][hs, tc], True, True, [b["KKt"].b, b["Bh"].b], [bA.b])
                        mm(bA[0:TT, 128:128 + TT], b["Bh"][hs, tc], b["KKt"][hs, tc], True, True, [b["KKt"].b, b["Bh"].b], [bA.b])
                        mm(bA[0:TT, 256:256 + TT], b["Kh"][hs, tc], b["KKt"][hs, tc], True, True, [b["KKt"].b, b["Kh"].b], [bA.b])
                        mm(bA[0:TT, 384:384 + TT], b["Kh"][hs, tc], b["Rtb"][hs, tc], True, True, [b["Rtb"].b, b["Kh"].b], [bA.b])
                        mm(psum[4 + hh][0:TT, tl * 128:tl * 128 + TT], b["Bh"][hs, tc], b["Rtb"][hs, tc], True, True,
                           [b["Rtb"].b, b["Bh"].b], [psum[4 + hh].b])
                    for hh in range(2):
                        u = 2 * tl + hh
                        bA = psum[2 * (tl % 2) + hh]
                        if hh == 0 or tl % 2 == 1:
                            tt(DVE, v3(MK[0:TT, u, :]), v3(bA[0:TT, :]), v3(C["rwmask"][0:TT, :]), ALU.mult, [bA.b, C["rwmask"].b], [MKb[u]])
                        else:
                            raw = MKraw[tl % 2]
                            cp(ACT, v3(raw[0:TT, :]), v3(bA[0:TT, :]), [bA.b], [raw.b])
                            tt(POOL, v3(MK[0:TT, u, :]), v3(raw[0:TT, :]), v3(C["rwmask"][0:TT, :]), ALU.mult, [raw.b, C["rwmask"].b], [MKb[u]])
                for hh in range(2):
                    tt(DVE, ArbT[0:TT, hh, 0:NT, 0:TT], v3(psum[4 + hh][0:TT, :])[:, 0:NT, :], v3(C["iumask4"][0:TT, :])[:, 0:NT, :], ALU.mult,
                       [psum[4 + hh].b, C["iumask4"].b], [ArbT.b])
                mkall = sorted(set(MKb), key=id)
                tt(POOL, Pt[0][0:TT, 0:NU, 0:TT], MK[0:TT, 0:NU, 128:128 + TT], ident8[0:TT, 0:NU, 0:TT], ALU.add,
                   mkall + [ident8.b], [Ptb[0][0], Ptb[0][1]])
                P.tag = "R.S2"
                Mcur = [(MK[0:TT, u, 0:TT], MK[0:TT, u, 128:128 + TT], MKb[u]) for u in range(NU)]
                for m in range(1, 6):
                    par = m % 2
                    for u in range(NU):
                        bk = psum[u // 2]
                        off = 256 * (u % 2)
                        Mp, Mtp, mb = Mcur[u]
                        mm(bk[0:TT, off:off + TT], Mtp, Mp, True, True, [mb], [bk.b])
                        mm(bk[0:TT, off + 128:off + 128 + TT], Mp, Mtp, True, True, [mb], [bk.b])
                    for g in range(NU // 2):
                        dst = MM[par][0:TT, 2 * g:2 * g + 2, :].rearrange("p u (a c) -> p (u a) c", c=128)[:, :, 0:TT]
                        cp(DVE if g == 3 else ACT, dst, v3(psum[g][0:TT, :]), [psum[g].b], [MMb[par][g]])
                        for u in (2 * g, 2 * g + 1):
                            Mcur[u] = (MM[par][0:TT, u, 0:TT], MM[par][0:TT, u, 128:128 + TT], MMb[par][g])
                    for u in range(NU):
                        pbk = psum[4 + u // 4]
                        po = (u % 4) * 128
                        Pp = Pt[1 - par]
                        ppb = Ptb[1 - par][u // 4]
                        mm(pbk[0:TT, po:po + TT], Mcur[u][0], Pp[0:TT, u, 0:TT], True, True, [Mcur[u][2], ppb], [pbk.b])
                    for g in range((NU + 3) // 4):
                        nu_ = min(4, NU - 4 * g)
                        tt(DVE, Pt[par][0:TT, 4 * g:4 * g + nu_, 0:TT], v3(psum[4 + g][0:TT, :])[:, 0:nu_, :],
                           Pt[1 - par][0:TT, 4 * g:4 * g + nu_, 0:TT], ALU.add, [psum[4 + g].b, Ptb[1 - par][g]], [Ptb[par][g]])
                    advance(nxtA, ADV2)
                PtF, PtFb = Pt[1], Ptb[1]
                P.tag = "R.S3-6"
                for u in range(NU):
                    tl, hh = u // 2, u % 2
                    hs = slice(64 * hh, 64 * hh + 64)
                    mm(psum[6][0:TT, u * 64:(u + 1) * 64], MK[0:TT, u, 256:256 + TT], TOK[0:TT, tl, 3, hs], True, True, [MKb[u], TOK.b], [psum[6].b])
                cp(ACT, W1b[0:TT, 0:NU, :], psum[6][0:TT, 0:NU * 64].rearrange("p (u c) -> p u c", c=64), [psum[6].b], [W1b.b])
                for u in range(NU):
                    tl, hh = u // 2, u % 2
                    hs = slice(64 * hh, 64 * hh + 64)
                    bk = psum[u // 4]
                    uo = (u % 4) * 128
                    mm(bk[0:TT, uo:uo + 64], PtF[0:TT, u, 0:TT], W1b[0:TT, u, :], True, True, [PtFb[u // 4], W1b.b], [bk.b])
                    mm(bk[0:TT, uo + 64:uo + 128], PtF[0:TT, u, 0:TT], TOK[0:TT, tl, 0, hs], True, True, [PtFb[u // 4], TOK.b], [bk.b])
                for g in range((NU + 3) // 4):
                    nu_ = min(4, NU - 4 * g)
                    tt(DVE, UW[0:TT, 4 * g:4 * g + nu_, :], v3(psum[g][0:TT, :], n=128)[:, 0:nu_, :], v3(C["signs4"][0:TT, :], n=128)[:, 0:nu_, :],
                       ALU.mult, [psum[g].b, C["signs4"].b], [UW.b])
                for cc in range(NCH):
                    ts(POOL, UWm[cc][0:TT, 0:NU, :], UW[0:TT, 0:NU, :], C["cind"][0:TT, cc:cc + 1], None, ALU.mult, None,
                       [UW.b, C["cind"].b], [UWm[cc].b])
                    ts(POOL, Vm[cc][0:TT, 0:NT, :], TOK[0:TT, 0:NT, 3, :], C["cind"][0:TT, cc:cc + 1], None, ALU.mult, None,
                       [TOK.b, C["cind"].b], [Vm[cc].b])
                for u in range(NU):
                    tl, hh = u // 2, u % 2
                    hs = slice(64 * hh, 64 * hh + 64)
                    mm(psum[2][hs, tl * 128:tl * 128 + TT], UW[0:TT, u, 64:128], ArbT[0:TT, hh, tl, 0:TT], True, True, [UW.b, ArbT.b], [psum[2].b])
                vb = lambda ap: ap.rearrange("p (t c) -> p t c", c=TT)
                tt(DVE, vb(QcT[:, 0:BW]), vb(f["Rt"][:, 0:BW]), v3(psum[2][:, :])[:, 0:NT, :], ALU.subtract, [f["Rt"].b, psum[2].b], [QcT.b])
                for u in range(NU):
                    tl, hh = u // 2, u % 2
                    hs = slice(64 * hh, 64 * hh + 64)
                    mm(psum[3][hs, tl * 128:tl * 128 + TT], TOK[0:TT, tl, 3, hs], MK[0:TT, u, 384:384 + TT], True, False, [TOK.b, MKb[u]], [psum[3].b])
                    mm(psum[3][hs, tl * 128:tl * 128 + TT], UW[0:TT, u, 0:64], ArbT[0:TT, hh, tl, 0:TT], False, True, [UW.b, ArbT.b], [psum[3].b])
                cp(ACT, vb(Y0sb[:, 0:BW]), v3(psum[3][:, :])[:, 0:NT, :], [psum[3].b], [Y0sb.b])
                P.tag = "R.S7ab"
                McT = McTt[:, :].rearrange("p (c k) -> p c k", k=64) if BW == 512 else McTt[:, 0:64].rearrange("p (c k) -> p c k", k=64)
                for u in range(NU):
                    tl, hh = u // 2, u % 2
                    hs = slice(64 * hh, 64 * hh + 64)
                    for cc in range(NCH):
                        ch = tl * NCH + cc
                        mm(psum[6][hs, ch * 64:(ch + 1) * 64], UWm[cc][0:TT, u, 64:128], TOK[0:TT, tl, 2, hs], True, True,
                           [UWm[cc].b, TOK.b], [psum[6].b])
                for ch in range(NCHK):
                    gcol = ch * 64 + 63
                    stt(McT[:, ch, :], C["ident2"][:, :], f["epos"][:, gcol:gcol + 1], psum[6][:, ch * 64:(ch + 1) * 64],
                        ALU.mult, ALU.subtract, [C["ident2"].b, f["epos"].b, psum[6].b], [McTt.b])
                for u in range(NU):
                    tl, hh = u // 2, u % 2
                    hs = slice(64 * hh, 64 * hh + 64)
                    for cc in range(NCH):
                        ch = tl * NCH + cc
                        mm(psum[7][hs, ch * 64:(ch + 1) * 64], TOK[0:TT, tl, 1, hs], Vm[cc][0:TT, tl, hs], True, False, [TOK.b, Vm[cc].b], [psum[7].b])
                        mm(psum[7][hs, ch * 64:(ch + 1) * 64], TOK[0:TT, tl, 2, hs], UWm[cc][0:TT, u, 0:64], False, True, [TOK.b, UWm[cc].b], [psum[7].b])
                cp(DVE, D1sb[:, 0:NCHK * 64], psum[7][:, 0:NCHK * 64], [psum[7].b], [D1sb.b])
                P.tag = "R.S7c"
                for ch in range(NCHK):
                    cs_ = slice(ch * 64, (ch + 1) * 64)
                    for hh in range(2):
                        hs = slice(64 * hh, 64 * hh + 64)
                        gi = gidx[c3][hh]
                        Gc, Gn = Gst[gi], Gst[1 - gi]
                        Gcb, Gnb = Gb[gi][c3][hh], Gb[1 - gi][c3][hh]
                        mm(psum[4 + hh][hs, cs_], Gc[hs, c3, :], QcT[hs, cs_], True, True, [Gcb, QcT.b], [psum[4 + hh].b])
                        mm(psum[hh][hs, cs_], McT[hs, ch, :], Gc[hs, c3, :], True, True, [McTt.b, Gcb], [psum[hh].b])
                        tt(DVE, Gn[hs, c3, :], psum[hh][hs, cs_], D1sb[hs, cs_], ALU.add, [psum[hh].b, D1sb.b], [Gnb])
                        gidx[c3][hh] = 1 - gi
                    advance(nxtA, int(os.environ.get("ADV", "0")))
                for hh in range(2):
                    hs = slice(64 * hh, 64 * hh + 64)
                    tt(DVE, f["Y"][hs, 0:BW], psum[4 + hh][hs, 0:BW], Y0sb[hs, 0:BW], ALU.add, [psum[4 + hh].b, Y0sb.b], [f["Y"].b])
                P.tag = "R.post"
                act(b["Ybf"][:, :], f["Y"][:, :], AF.Copy, [f["Y"].b], [b["Ybf"].b])
                act(b["Ysq"][:, :], f["Y"][:, :], AF.Square, [f["Y"].b], [b["Ysq"].b])
                bm = next_bank()
                mm(bm[:, 0:BW], bones64[:, :], b["Ybf"][:, :], True, True, [bones64.b, b["Ybf"].b], [bm.b])
                cp(ACT, f["tmp"][:, :], bm[:, 0:BW], [bm.b], [f["tmp"].b])
                bq = next_bank()
                mm(bq[:, 0:BW], bones64[:, :], b["Ysq"][:, :], True, True, [bones64.b, b["Ysq"].b], [bq.b])
                act(f["lgx"][:, :], bm[:, 0:BW], AF.Square, [bm.b], [f["lgx"].b])
                tt(DVE, f["lgx"][:, :], bq[:, 0:BW], f["lgx"][:, :], ALU.subtract, [bq.b, f["lgx"].b], [f["lgx"].b])
                ts(DVE, f["lgx"][:, :], f["lgx"][:, :], 0.0, None, ALU.max, None, [f["lgx"].b], [f["lgx"].b])
                act(f["lgx"][:, :], f["lgx"][:, :], AF.Ln, [f["lgx"].b], [f["lgx"].b], bias=GN_EPS)
                act(f["lgx"][:, :], f["lgx"][:, :], AF.Exp, [f["lgx"].b], [f["lgx"].b], scale=-0.5)
                tt(DVE, f["Y"][:, :], f["Y"][:, :], f["tmp"][:, :], ALU.subtract, [f["Y"].b, f["tmp"].b], [f["Y"].b])
                tt(DVE, f["Y"][:, :], f["Y"][:, :], f["lgx"][:, :], ALU.mult, [f["Y"].b, f["lgx"].b], [f["Y"].b])
                ts(DVE, f["Y"][:, :], f["Y"][:, :], lg_t[:, l, c3:c3 + 1], lb_t[:, l, c3:c3 + 1], ALU.mult, ALU.add,
                   [f["Y"].b, lg_t.b, lb_t.b], [f["Y"].b])
                tt(DVE, f["Y"][:, :], f["Y"][:, :], f["bonus"][:, :], ALU.add, [f["Y"].b, f["bonus"].b], [f["Y"].b])
                tt(POOL, oT[:, c3, blk * BW:(blk + 1) * BW], f["Y"][:, :], gaT[:, :], ALU.mult, [f["Y"].b, gaT.b], [oTb[c3]])
                drain(nxtA)

        for c3 in range(3):
            for hh in range(2):
                hb = 64 * hh
                hs = slice(hb, hb + 64)
                gi = gidx[c3][hh]
                h = 2 * c3 + hh
                fb_ = psum[3] if hh == 0 else psum[2]
                tr(fb_[0:64, 128:192], Gst[gi][hs, c3, :], C["identf"][hs, hs], [Gb[gi][c3][hh], C["identf"].b], [fb_.b])
                cp(DVE, wst[0:64, h, :], fb_[0:64, 128:192], [fb_.b], [wst.b])
        P.dma(POOL, O["wkv"][l, si].rearrange("h v k -> v h k"), wst[0:64, :, :], reads=[wst.b])

    for kind, n in (("p", NP), ("s", NS)):
        for si in range(n):
            T = T_P if kind == "p" else T_S
            past = 0 if kind == "p" else PAST
            cx = SimpleNamespace(kind=kind, si=si, T=T, past=past, NK=past + T, TT=min(128, T), NTT=T // min(128, T),
                                 BW=min(512, T), NBLK=T // min(512, T), NKT=(past + T + 127) // 128, PKT=past // 128,
                                 xsrc=(xp if kind == "p" else xs_), yidx=(si if kind == "p" else NPm + si),
                                 O={k[1:]: v for k, v in outs.items() if k[0] == kind})
            for l in range(NL):
                cx.l = l
                if l == 0:
                    phase_X(cx)
                if "t" not in PHASES:
                    phase_T(cx)
                if "R" in PHASES:
                    phase_R(cx)
                if "F" in PHASES:
                    phase_F(cx)
                if "S" in PHASES:
                    phase_S(cx)
                if dbg and "oT" in dbg_out and l == dbg.get("_layer", 0) and T == dbg["oT"][2] and si == 0:
                    P.dma(POOL, dbg_out["oT"], oT[:, :, 0:T], reads=oTb)
                if "e" not in PHASES:
                    phase_E(cx)
    P.finish()
    global LASTP
    LASTP = P
    return nc


_NC_CACHE = {}


def kernel(**inputs):
    n = 8
    NP, NS = 32 // n, 32 // n
    key = (NP, NS)
    consts = host_consts()
    in_maps = []
    f32 = lambda a: np.ascontiguousarray(np.asarray(a, dtype=np.float32))
    for c in range(n):
        ps, ss = slice(c * NP, (c + 1) * NP), slice(c * NS, (c + 1) * NS)
        m = {
            "x_prompt": f32(inputs["x_prompt"][ps]),
            "x_sample": f32(inputs["x_sample"][ss]),
            "cache_fox_k": f32(np.asarray(inputs["cache_fox_k"])[:, ss].reshape(NL, NS, PAST, 384)),
            "cache_fox_v": f32(np.asarray(inputs["cache_fox_v"])[:, ss].reshape(NL, NS, PAST, 384)),
            "cache_fox_logf": f32(np.asarray(inputs["cache_fox_logf"])[:, ss]),
            "cache_sb_k": f32(np.asarray(inputs["cache_sb_k"])[:, ss].reshape(NL, NS, PAST, 256)),
            "cache_sb_v": f32(np.asarray(inputs["cache_sb_v"])[:, ss].reshape(NL, NS, PAST, 256)),
            "state_wkv": f32(np.asarray(inputs["state_wkv"])[:, ss]),
            "state_shift": f32(np.asarray(inputs["state_shift"])[:, ss].reshape(NL, NS, SHIFT_W)),
            "r_k": f32(np.asarray(inputs["r_k"]).reshape(NL, 384)),
        }
        for k in ("w_in", "mu_shift", "w0_decay", "w_decay", "a0", "w_aaa", "k_k", "k_a", "lnx_g", "lnx_b",
                  "fox_fb", "w_out", "ln_g", "ln_b"):
            m[k] = f32(inputs[k])
        for k, v in consts.items():
            m["c_" + k] = v
        in_maps.append(m)
    nc = build_program(NP, NS)
    res = run_bass_kernel_spmd(nc, in_maps, core_ids=list(range(n)))
    R = res.results

    def cat(name, axis, shape=None):
        a = np.concatenate([np.asarray(r[name], dtype=np.float32) for r in R], axis=axis)
        return a.reshape(shape) if shape is not None else a
    B = 32
    out = (
        cat("p_y", 0), cat("s_y", 0),
        cat("p_fox_k", 1, (NL, B, T_P, 6, 64)), cat("p_fox_v", 1, (NL, B, T_P, 6, 64)), cat("p_fox_logf", 1),
        cat("p_sb_k", 1, (NL, B, T_P, 4, 64)), cat("p_sb_v", 1, (NL, B, T_P, 4, 64)),
        cat("p_wkv", 1), cat("p_shift", 1, (NL, B, 1, SHIFT_W)),
        cat("s_fox_k", 1, (NL, B, T_S, 6, 64)), cat("s_fox_v", 1, (NL, B, T_S, 6, 64)), cat("s_fox_logf", 1),
        cat("s_sb_k", 1, (NL, B, T_S, 4, 64)), cat("s_sb_v", 1, (NL, B, T_S, 4, 64)),
        cat("s_wkv", 1), cat("s_shift", 1, (NL, B, 1, SHIFT_W)),
    )
    return out
```

```python
import contextlib
import os
import numpy as np
import concourse.bass as bass
import concourse.mybir as mybir
from concourse.bass_utils import run_bass_kernel_spmd

F32 = mybir.dt.float32
BF16 = mybir.dt.bfloat16
ALU = mybir.AluOpType
AF = mybir.ActivationFunctionType

PE, ACT, DVE, POOL, SP = "tensor", "scalar", "vector", "gpsimd", "sync"
ENGS = (PE, ACT, DVE, POOL, SP)
SEM_WRAP = 30000
ANNOTATE = bool(os.environ.get("ANNOTATE"))
HEAT_S = int(os.environ.get("HEAT_S", "1"))
HEAT_F = int(os.environ.get("HEAT_F", "0"))
HEAT_R = int(os.environ.get("HEAT_R", "0"))
ADV2 = int(os.environ.get("ADV2", "0"))

D = 1024
T_P = 2048
T_S = 64
PAST = 2048
NL = 2
INC = 4230
SHIFT_W = 1280
ALPHA = (2 * NL) ** 0.25
GN_EPS = 64e-5
LN_EPS = 1e-5
NEG = -30000.0
DEC_SCALE = -float(np.exp(-0.5))

FT = ([(128 * i, 128) for i in range(10)] +
      [(1280 + 128 * i, 128) for i in range(3)] +
      [(1664 + 128 * i, 128) for i in range(3)] +
      [(2048 + 128 * i, 128) for i in range(3)] +
      [(2822 + 128 * i, 128) for i in range(3)] +
      [(3206 + 128 * i, 128) for i in range(2)] +
      [(3462 + 128 * i, 128) for i in range(2)] +
      [(3974 + 128 * i, 128) for i in range(2)] +
      [(2816, 6)])
FT_GA, FT_QB, FT_KB, FT_GB, FT_QC, FT_KC, FT_GC, FT_FB = 10, 13, 16, 19, 22, 24, 26, 28
NFT = len(FT)
TG = [(2048, 384), (2432, 390), (3462, 512)]


class Buf:
    __slots__ = ("name", "w", "r", "excl")

    def __init__(self, name, excl=False):
        self.name = name
        self.w = None
        self.r = []
        self.excl = excl


class Prog:
    def __init__(self, nc, n_dma_sems=64):
        self.nc = nc
        self.q = {e: [] for e in ENGS}
        self.nsem = 0
        self.cur = {}
        self.cnt = {}
        self.allsems = {e: [] for e in ENGS}
        for e in ENGS:
            if e != SP:
                self.cur[e] = self._newsem()
                self.cnt[e] = 0
                self.allsems[e].append(self.cur[e])
        self.dma_pool = {SP: [self._newsem() for _ in range(20)], POOL: [self._newsem() for _ in range(12)]}
        self.dma_cnt = {s: 0 for e in self.dma_pool for s in self.dma_pool[e]}
        self.dma_next = {e: 0 for e in self.dma_pool}
        self.waited = {e: {} for e in ENGS}
        self.bar = {e: [] for e in ENGS}
        self.final = {}
        self.tag = None

    def _newsem(self):
        k = self.nsem
        self.nsem += 1
        return k

    def _need(self, eng, ev, waits):
        if ev is None:
            return
        k, v = ev[0], ev[1]
        if self.waited[eng].get(k, 0) >= v:
            return
        self.waited[eng][k] = v
        waits.append((k, v))

    def barrier(self):
        evs = []
        for e in ENGS:
            if e != SP and self.cnt[e]:
                evs.append((self.cur[e], self.cnt[e], e, False))
        for s, c in self.dma_cnt.items():
            if c:
                evs.append((s, 16 * c, None, True))
        for e in ENGS:
            self.bar[e] = self.bar[e] + evs

    def op(self, eng, fn, reads=(), writes=(), is_dma=False):
        waits = []
        if self.bar[eng]:
            for ev in self.bar[eng]:
                self._need(eng, ev, waits)
            self.bar[eng] = []
        for b in reads:
            self._need(eng, b.w, waits)
            if b.excl:
                for ev in b.r:
                    if ev[2] != eng:
                        self._need(eng, ev, waits)
        for b in writes:
            w = b.w
            if w is not None and not (w[2] == eng and not w[3] and not is_dma and eng != POOL):
                self._need(eng, w, waits)
            for ev in b.r:
                if ev[2] == eng and not is_dma and not ev[3] and eng != POOL:
                    continue
                self._need(eng, ev, waits)
        if is_dma:
            pool = self.dma_pool[eng]
            s = pool[self.dma_next[eng]]
            self.dma_next[eng] = (self.dma_next[eng] + 1) % len(pool)
            prev = self.dma_cnt[s]
            if prev:
                self._need(eng, (s, 16 * prev, None, True), waits)
            self.dma_cnt[s] = prev + 1
            ev = (s, 16 * (prev + 1), eng, True)
            inc = (s, 16)
        else:
            if self.cnt[eng] >= SEM_WRAP:
                self.final[self.cur[eng]] = self.cnt[eng]
                self.cur[eng] = self._newsem()
                self.cnt[eng] = 0
            self.cnt[eng] += 1
            ev = (self.cur[eng], self.cnt[eng], eng, False)
            inc = (self.cur[eng], 1)
        m = {}
        for k, v in waits:
            m[k] = max(m.get(k, 0), v)
        self.q[eng].append((list(m.items()), fn, inc, self.tag, ev[1]))
        for b in reads:
            b.r.append(ev)
            if len(b.r) > 64:
                b.r = b.r[-64:] if False else b.r
        for b in writes:
            b.w = ev
            b.r = []
        return ev

    def dma(self, eng, out, in_, reads=(), writes=(), **kw):
        def fn(e, out=out, in_=in_, kw=kw):
            return e.dma_start(out=out, in_=in_, **kw)
        return self.op(eng, fn, reads, writes, is_dma=True)

    def finish(self):
        nc = self.nc
        fw = dict(self.final)
        for e in ENGS:
            if e != SP and self.cnt[e]:
                fw[self.cur[e]] = self.cnt[e]
        for s, c in self.dma_cnt.items():
            if c:
                fw[s] = 16 * c
        dma_keys = set(self.dma_cnt)
        marked = {}
        for e_ in ENGS:
            for item in self.q[e_]:
                for k, v in item[0]:
                    if k not in dma_keys:
                        marked.setdefault(k, set()).add(v)
        for k, v in fw.items():
            if k not in dma_keys:
                marked.setdefault(k, set()).add(v)
        remap = {k: {v: i + 1 for i, v in enumerate(sorted(vs))} for k, vs in marked.items()}
        fw = {k: (v if k in dma_keys else remap[k][v]) for k, v in fw.items()}
        with contextlib.ExitStack() as es:
            sems = [es.enter_context(nc.semaphore(f"s{i}")) for i in range(self.nsem)]
            es.enter_context(nc.allow_non_contiguous_dma("small strided parameter / state transfers"))
            block = es.enter_context(nc.Block())

            def emit(engname):
                def body(e):
                    for waits, fn, inc, tag, pv in self.q[engname]:
                        for k, v in waits:
                            e.wait_ge(sems[k], v if k in dma_keys else remap[k][v])
                        ins = fn(e)
                        if inc[0] in dma_keys or pv in marked.get(inc[0], ()):
                            ins.then_inc(sems[inc[0]], inc[1])
                        if tag and ANNOTATE:
                            ins.annotate(tag)
                    if engname == SP:
                        for k, v in fw.items():
                            e.wait_ge(sems[k], v)
                return body
            block.tensor(emit(PE))
            block.scalar(emit(ACT))
            block.vector(emit(DVE))
            block.gpsimd(emit(POOL))
            block.sync(emit(SP))


class Tl:
    def __init__(self, h, name):
        self.h = h
        self.b = Buf(name)

    def __getitem__(self, k):
        return self.h[k]


class Mem:
    def __init__(self, nc, base, limit):
        self.nc, self.top, self.limit, self.n = nc, base, limit, 0

    def alloc(self, name, shape, dt):
        sz = int(np.prod(shape[1:])) * (4 if dt == F32 else 2)
        sz = (sz + 31) // 32 * 32
        self.n += 1
        h = self.nc.alloc_sbuf_tensor_at(f"{name}_{self.n}", list(shape), dt, offset=self.top)
        self.top += sz
        assert self.top <= self.limit, f"SBUF overflow at {name}: {self.top}"
        return Tl(h, name)


def host_consts():
    c = {}
    c["identf"] = np.eye(128, dtype=np.float32)
    bo = np.zeros((128, 128), np.float32)
    bo[:64, :64] = 1
    bo[64:, 64:] = 1
    c["bones"] = bo
    cm = np.ones((128, 512), np.float32)
    cm[:, ::64] = 0
    c["chunkmask"] = cm
    i = np.arange(128)[:, None]
    j = np.arange(128)[None, :]
    same = (i // 64) == (j // 64)
    sl = (same & (j < i)).astype(np.float32)
    su = (same & (i < j)).astype(np.float32)
    iu = (same & (i <= j)).astype(np.float32)
    c["rwmask"] = np.concatenate([-sl, -su, su, iu], axis=1)
    c["iumask"] = iu
    sg = np.ones((128, 128), np.float32)
    sg[:, :64] = -1
    c["signs"] = sg
    kl = np.arange(128)[:, None]
    cc = np.arange(896)[None, :]
    c["negle"] = np.where(kl <= cc - 384, 0.0, NEG).astype(np.float32)
    c["neglt"] = np.where(kl < cc - 384, 0.0, NEG).astype(np.float32)
    c["trineg"] = np.where(i >= j, -1.0, 0.0).astype(np.float32)
    c["iumask4"] = np.tile(iu, (1, 4))
    c["signs4"] = np.tile(sg, (1, 4))
    c["ident2"] = np.concatenate([np.eye(64, dtype=np.float32)] * 2, axis=0)
    ci = np.zeros((128, 2), np.float32)
    ci[:64, 0] = 1
    ci[64:, 1] = 1
    c["cind"] = ci
    return c


CONST_SHAPES = dict(identf=(128, 128), bones=(128, 128), chunkmask=(128, 512), rwmask=(128, 512),
                    iumask=(128, 128), signs=(128, 128), negle=(128, 896), neglt=(128, 896),
                    trineg=(128, 128), cind=(128, 2), iumask4=(128, 512), signs4=(128, 512),
                    ident2=(128, 64))


def build_program(NP, NS, dbg=None):
    nc = bass.Bass("TRN2", target_bir_lowering=False)
    P = Prog(nc)

    def din(name, shape):
        return nc.dram_tensor(name, list(shape), F32, kind="ExternalInput").ap()

    def dout(name, shape):
        return nc.dram_tensor(name, list(shape), F32, kind="ExternalOutput").ap()

    NPm, NSm = max(NP, 1), max(NS, 1)
    xp = din("x_prompt", (NPm, T_P, D))
    xs_ = din("x_sample", (NSm, T_S, D))
    cfk = din("cache_fox_k", (NL, NSm, PAST, 384))
    cfv = din("cache_fox_v", (NL, NSm, PAST, 384))
    cfl = din("cache_fox_logf", (NL, NSm, PAST, 6))
    csk = din("cache_sb_k", (NL, NSm, PAST, 256))
    csv = din("cache_sb_v", (NL, NSm, PAST, 256))
    swkv = din("state_wkv", (NL, NSm, 6, 64, 64))
    sshift = din("state_shift", (NL, NSm, SHIFT_W))
    w_in = din("w_in", (NL, D, INC))
    mu_shift = din("mu_shift", (NL, SHIFT_W))
    w0_decay = din("w0_decay", (NL, 384))
    w_decay = din("w_decay", (NL, 64, 384))
    a0 = din("a0", (NL, 384))
    w_aaa = din("w_aaa", (NL, 64, 384))
    k_k = din("k_k", (NL, 384))
    k_a = din("k_a", (NL, 384))
    r_k = din("r_k", (NL, 384))
    lnx_g = din("lnx_g", (NL, 384))
    lnx_b = din("lnx_b", (NL, 384))
    fox_fb = din("fox_fb", (NL, 6))
    w_out = din("w_out", (NL, D, D))
    ln_g = din("ln_g", (NL, D))
    ln_b = din("ln_b", (NL, D))
    cdr = {k: din("c_" + k, s) for k, s in CONST_SHAPES.items()}

    outs = {}
    for pre, n, t in (("p", NPm, T_P), ("s", NSm, T_S)):
        outs[pre + "y"] = dout(pre + "_y", (n, t, D))
        outs[pre + "fk"] = dout(pre + "_fox_k", (NL, n, t, 384))
        outs[pre + "fv"] = dout(pre + "_fox_v", (NL, n, t, 384))
        outs[pre + "fl"] = dout(pre + "_fox_logf", (NL, n, t, 6))
        outs[pre + "sk"] = dout(pre + "_sb_k", (NL, n, t, 256))
        outs[pre + "sv"] = dout(pre + "_sb_v", (NL, n, t, 256))
        outs[pre + "wkv"] = dout(pre + "_wkv", (NL, n, 6, 64, 64))
        outs[pre + "sh"] = dout(pre + "_shift", (NL, n, SHIFT_W))
    dbg_out = {}
    if dbg:
        for k, s in dbg.items():
            if not k.startswith("_"):
                dbg_out[k] = dout("dbg_" + k, s)

    WF = nc.dram_tensor("WF", [NL, NFT, 128, 8, 128], BF16).ap()
    WT = nc.dram_tensor("WT", [NL, 3, 128, 8, 512], BF16).ap()
    WO = nc.dram_tensor("WO", [NL, 128, 8, 1024], BF16).ap()
    Y0 = nc.dram_tensor("Y0", [NPm + NSm, T_P, D], F32).ap()
    bWF = [[Buf(f"WF{l}_{t}") for t in range(NFT)] for l in range(NL)]
    bWT = [[Buf(f"WT{l}_{g}") for g in range(3)] for l in range(NL)]
    bWO = [Buf(f"WO{l}") for l in range(NL)]
    bY0 = [[Buf(f"Y0_{i}_{t}") for t in range(16)] for i in range(NPm + NSm)]

    for l in range(NL):
        wv = w_in[l].rearrange("(c p) n -> p c n", p=128)
        for g, (s0, n) in enumerate(TG):
            P.dma(POOL, WT[l, g, :, :, 0:n], wv[:, :, s0:s0 + n], writes=[bWT[l][g]])
        use_order = [9, 0, 3, 6, 10, 1, 4, 7, 11, 2, 5, 8, 12, FT_FB] + list(range(13, 28))
        for t in use_order:
            s0, n = FT[t]
            P.dma(POOL, WF[l, t, :, :, 0:n], wv[:, :, s0:s0 + n], writes=[bWF[l][t]])
        wov = w_out[l].rearrange("(c p) n -> p c n", p=128)
        for hfi in range(2):
            P.dma(POOL, WO[l, :, :, hfi * 512:(hfi + 1) * 512], wov[:, :, hfi * 512:(hfi + 1) * 512],
                  writes=[bWO[l]])

    mem = Mem(nc, 16640, 228000)
    psum = [Tl(nc.alloc_psum_tensor(f"pb{i}", [128, 512], F32), f"pb{i}") for i in range(8)]
    for p_ in psum:
        p_.b.excl = True

    def pbf(i):
        return psum[i].h.ap().bitcast(BF16)

    C = {}
    for k, s in CONST_SHAPES.items():
        C[k] = mem.alloc("c_" + k, s, F32)
        P.dma(SP, C[k][:], cdr[k], writes=[C[k].b])
    identb = mem.alloc("identb", (128, 128), BF16)
    bonesb = mem.alloc("bonesb", (128, 128), BF16)
    bones64 = mem.alloc("bones64", (128, 128), BF16)
    ones64 = mem.alloc("ones64", (128, 64), BF16)
    onesneg = mem.alloc("onesneg", (128, 128), BF16)
    trinegb = mem.alloc("trinegb", (128, 128), BF16)
    negleb = mem.alloc("negleb", (128, 896), BF16)
    negltb = mem.alloc("negltb", (128, 896), BF16)
    onecol = mem.alloc("onecol", (128, 1), F32)
    ident8 = mem.alloc("ident8", (128, 8, 128), BF16)
    for u_ in range(8):
        P.op(DVE, lambda e, u_=u_: e.tensor_copy(out=ident8[:, u_, :], in_=C["identf"][:]), reads=[C["identf"].b], writes=[ident8.b])
    P.op(DVE, lambda e: e.tensor_copy(out=identb[:], in_=C["identf"][:]), reads=[C["identf"].b], writes=[identb.b])
    P.op(DVE, lambda e: e.tensor_copy(out=bonesb[:], in_=C["bones"][:]), reads=[C["bones"].b], writes=[bonesb.b])
    P.op(DVE, lambda e: e.tensor_scalar(out=bones64[:], in0=C["bones"][:], scalar1=1.0 / 64, scalar2=None, op0=ALU.mult),
         reads=[C["bones"].b], writes=[bones64.b])
    P.op(DVE, lambda e: e.memset(ones64[:], 1.0), writes=[ones64.b])
    P.op(DVE, lambda e: e.memset(onesneg[:], -1.0), writes=[onesneg.b])
    P.op(DVE, lambda e: e.memset(onecol[:], 1.0), writes=[onecol.b])
    P.op(DVE, lambda e: e.tensor_copy(out=trinegb[:], in_=C["trineg"][:]), reads=[C["trineg"].b], writes=[trinegb.b])
    P.op(DVE, lambda e: e.tensor_copy(out=negleb[:], in_=C["negle"][:]), reads=[C["negle"].b], writes=[negleb.b])
    P.op(DVE, lambda e: e.tensor_copy(out=negltb[:], in_=C["neglt"][:]), reads=[C["neglt"].b], writes=[negltb.b])

    def load_cols(name, src, ntile):
        t = mem.alloc(name, (128, NL, ntile), F32)
        with nc.allow_non_contiguous_dma("small param transpose load"):
            for l in range(NL):
                P.dma(SP, t[:, l, :], src[l].rearrange("(t p) -> p t", p=128), writes=[t.b])
        return t
    mu_t = load_cols("mu", mu_shift, 10)
    omm_t = mem.alloc("omm", (128, NL, 10), F32)
    P.op(DVE, lambda e: e.tensor_scalar(out=omm_t[:], in0=mu_t[:], scalar1=-1.0, scalar2=1.0, op0=ALU.mult, op1=ALU.add),
         reads=[mu_t.b], writes=[omm_t.b])
    w0_t = load_cols("w0", w0_decay, 3)
    a0_t = load_cols("a0", a0, 3)
    kk_t = load_cols("kk", k_k, 3)
    ka_t = load_cols("ka", k_a, 3)
    rk_t = load_cols("rk", r_k, 3)
    lg_t = load_cols("lxg", lnx_g, 3)
    lb_t = load_cols("lxb", lnx_b, 3)
    omka_t = mem.alloc("omka", (128, NL, 3), F32)
    P.op(DVE, lambda e: e.tensor_scalar(out=omka_t[:], in0=ka_t[:], scalar1=-1.0, scalar2=1.0, op0=ALU.mult, op1=ALU.add),
         reads=[ka_t.b], writes=[omka_t.b])
    nfb_t = mem.alloc("nfb", (128, NL), F32)
    with nc.allow_non_contiguous_dma("small param transpose load"):
        P.dma(SP, nfb_t[0:6, :], fox_fb.rearrange("l h -> h l"), writes=[nfb_t.b])
    P.op(DVE, lambda e: e.tensor_scalar(out=nfb_t[0:6, :], in0=nfb_t[0:6, :], scalar1=-1.0, scalar2=None, op0=ALU.mult),
         reads=[nfb_t.b], writes=[nfb_t.b])
    fbb = mem.alloc("fbb", (128, NL, 6), F32)
    P.dma(SP, fbb[:].rearrange("p l h -> p (l h)"), fox_fb.rearrange("l h -> (l h)").partition_broadcast(128), writes=[fbb.b])
    lwf = mem.alloc("lwf", (128, NL, 384), F32)
    lw = mem.alloc("lw", (128, NL, 384), BF16)
    for l in range(NL):
        P.dma(SP, lwf[0:64, l, :], w_decay[l], writes=[lwf.b])
        P.dma(SP, lwf[64:128, l, :], w_aaa[l], writes=[lwf.b])
    P.op(DVE, lambda e: e.tensor_copy(out=lw[:], in_=lwf[:]), reads=[lwf.b], writes=[lw.b])

    xT = mem.alloc("xT", (128, 8, T_P), BF16)
    oT = mem.alloc("oT", (128, 8, T_P), BF16)
    oTb = [Buf(f"oT{c}") for c in range(8)]
    Vb = mem.alloc("Vb", (128, 17, 384), BF16)
    Vc = mem.alloc("Vc", (128, 17, 256), BF16)
    Gst = [mem.alloc(f"G{i}", (128, 3, 64), F32) for i in range(2)]
    Gb = [[[Buf(f"G{i}_{c3}_{hh}") for hh in range(2)] for c3 in range(3)] for i in range(2)]
    ucarry = mem.alloc("ucarry", (128, 10), F32)
    ucb = [Buf(f"uc{ct}") for ct in range(10)]
    phase_base = mem.top
    if dbg:
        for c in range(8):
            P.op(POOL, lambda e, c=c: e.memset(oT[:, c, :], 0.0), writes=[oTb[c]])

    bank_rr = [0]

    def next_bank(lo=0, hi=2):
        i = lo + bank_rr[0] % (hi - lo)
        bank_rr[0] += 1
        return psum[i]

    wslot_rr = [0]

    from types import SimpleNamespace
    PHASES = set(dbg["_phases"]) if (dbg and "_phases" in dbg) else set("RFS")

    def dump(name, ap, reads):
        if name in dbg_out:
            P.dma(POOL, dbg_out[name], ap, reads=reads)

    def act(out, in_, func, R, W, bias=None, scale=None):
        kw = {}
        if bias is not None:
            kw["bias"] = bias
        if scale is not None:
            kw["scale"] = scale
        P.op(ACT, lambda e: e.activation(out=out, in_=in_, func=func, **kw), reads=R, writes=W)

    def tt(eng, out, in0, in1, op, R, W):
        P.op(eng, lambda e: e.tensor_tensor(out=out, in0=in0, in1=in1, op=op), reads=R, writes=W)

    def ts(eng, out, in0, s1, s2, op0, op1, R, W):
        if op1 is None and eng == POOL and op0 == ALU.mult:
            op1, s2 = ALU.mult, 1.0
        if op1 is None:
            P.op(eng, lambda e: e.tensor_scalar(out=out, in0=in0, scalar1=s1, scalar2=None, op0=op0), reads=R, writes=W)
        else:
            P.op(eng, lambda e: e.tensor_scalar(out=out, in0=in0, scalar1=s1, scalar2=s2, op0=op0, op1=op1), reads=R, writes=W)

    def stt(out, in0, scalar, in1, op0, op1, R, W):
        P.op(DVE, lambda e: e.scalar_tensor_tensor(out=out, in0=in0, scalar=scalar, in1=in1, op0=op0, op1=op1), reads=R, writes=W)

    def cp(eng, out, in_, R, W):
        if eng == ACT:
            P.op(ACT, lambda e: e.activation(out=out, in_=in_, func=AF.Copy), reads=R, writes=W)
        else:
            P.op(eng, lambda e: e.tensor_copy(out=out, in_=in_), reads=R, writes=W)

    heat = {"n": 0, "bank": None}

    def mm(out, lhsT, rhs, start, stop, R, W):
        P.op(PE, lambda e: e.matmul(out, lhsT=lhsT, rhs=rhs, start=start, stop=stop), reads=R, writes=W)
        if heat["n"] and stop:
            hb_ = heat["bank"]
            for _ in range(heat["n"]):
                P.op(PE, lambda e: e.matmul(hb_[:, 0:512], lhsT=identb[:, :], rhs=negltb[:, 0:512], start=True, stop=True),
                     reads=[identb.b, negltb.b], writes=[hb_.b])

    def tr(out, in_, ident, R, W):
        P.op(PE, lambda e: e.transpose(out=out, in_=in_, identity=ident), reads=R, writes=W)

    def memset(eng, ap, val, W):
        P.op(eng, lambda e: e.memset(ap, val), writes=W)

    def load_w(cx, slots, t):
        ws = slots[wslot_rr[0] % len(slots)]
        wslot_rr[0] += 1
        n = FT[t][1]
        P.dma(SP, ws[:, :, 0:n], WF[cx.l, t, :, :, 0:n], reads=[bWF[cx.l][t]], writes=[ws.b])
        return ws

    def project(cx, ws, ncol, blk, bank):
        cols = slice(blk * cx.BW, (blk + 1) * cx.BW)
        for c in range(8):
            mm(bank[0:ncol, 0:cx.BW], ws[:, c, 0:ncol], xT[:, c, cols], c == 0, c == 7, [ws.b, xT.b], [bank.b])

    def to_xT(cx, src, tt_, banks):
        TT = cx.TT
        for c in range(8):
            tr(banks[c // 4][:, (c % 4) * TT:(c % 4 + 1) * TT], src[:TT, c * 128:(c + 1) * 128],
               C["identf"][:TT, :TT], [src.b, C["identf"].b], [banks[c // 4].b])
        for j in range(2):
            s_ = banks[j][:, 0:4 * TT].rearrange("p (c t) -> p c t", t=TT)
            d_ = xT[:, 4 * j:4 * j + 4, tt_ * TT:(tt_ + 1) * TT]
            cp(ACT if j == 0 else DVE, d_, s_, [banks[j].b], [xT.b])

    def phase_X(cx):
        P.tag = "X"
        mem.top = phase_base
        P.barrier()
        xin = [mem.alloc(f"xin{i}", (128, D), F32) for i in range(2)]
        for tt_ in range(cx.NTT):
            sl = xin[tt_ % 2]
            P.dma(SP, sl[:cx.TT, :], cx.xsrc[cx.si, tt_ * cx.TT:(tt_ + 1) * cx.TT, :], writes=[sl.b])
            bk = (psum[0], psum[1]) if tt_ % 2 == 0 else (psum[2], psum[3])
            to_xT(cx, sl, tt_, bk)

    def phase_T(cx):
        P.tag = "T"
        l, si, TT, O = cx.l, cx.si, cx.TT, cx.O
        mem.top = phase_base
        P.barrier()
        wt = [mem.alloc(f"wt{g}", (128, 8, 512), BF16) for g in range(3)]
        stage = [mem.alloc(f"stg{i}", (128, 1288), F32) for i in range(2)]
        for g in range(3):
            P.dma(SP, wt[g][:, :, 0:TG[g][1]], WT[l, g, :, :, 0:TG[g][1]], reads=[bWT[l][g]], writes=[wt[g].b])
        if cx.kind == "s":
            P.dma(POOL, Vb[:, 0:16, :], cfv[l, si].rearrange("(t p) c -> p t c", p=128), writes=[Vb.b])
            P.dma(POOL, Vc[:, 0:16, :], csv[l, si].rearrange("(t p) c -> p t c", p=128), writes=[Vc.b])
        for tt_ in range(min(cx.NTT, int(os.environ.get("DBG_NTT", "99")))):
            st = stage[tt_ % 2]
            rows = slice(tt_ * TT, (tt_ + 1) * TT)
            bks = [psum[3 * (tt_ % 2) + g] for g in range(3)]
            for g, (s0, n) in enumerate(TG):
                for c in range(8):
                    mm(bks[g][:TT, 0:n], xT[:, c, rows], wt[g][:, c, 0:n], c == 0, c == 7, [xT.b, wt[g].b], [bks[g].b])
            cp(ACT, st[:TT, 0:384], bks[0][:TT, 0:384], [bks[0].b], [st.b])
            cp(DVE, st[:TT, 384:768], bks[1][:TT, 0:384], [bks[1].b], [st.b])
            cp(ACT, st[:TT, 768:1280], bks[2][:TT, 0:512], [bks[2].b], [st.b])
            tt(DVE, st[:TT, 1280:1286], bks[1][:TT, 384:390], fbb[:TT, l, :], ALU.add, [bks[1].b, fbb.b], [st.b])
            act(st[:TT, 1280:1286], st[:TT, 1280:1286], AF.Exp, [st.b], [st.b], scale=-1.0)
            act(st[:TT, 1280:1286], st[:TT, 1280:1286], AF.Ln, [st.b], [st.b], bias=1.0)
            ts(DVE, st[:TT, 1280:1286], st[:TT, 1280:1286], -1.0, None, ALU.mult, None, [st.b], [st.b])
            P.dma(SP, O["fl"][l, si, rows, :], st[:TT, 1280:1286], reads=[st.b])
            DBGT = int(os.environ.get("DBG_T", "9"))
            if DBGT in (2, 4, 9):
                cp(DVE, Vb[:TT, cx.PKT + tt_, :], bks[1][:TT, 0:384], [bks[1].b], [Vb.b])
            if DBGT in (2, 5, 9):
                cp(DVE, Vc[:TT, cx.PKT + tt_, :], st[:TT, 1024:1280], [st.b], [Vc.b])
            if DBGT != 9:
                continue
            P.dma(SP, O["fk"][l, si, rows, :], st[:TT, 0:384], reads=[st.b])
            P.dma(SP, O["fv"][l, si, rows, :], st[:TT, 384:768], reads=[st.b])
            P.dma(SP, O["sk"][l, si, rows, :], st[:TT, 768:1024], reads=[st.b])
            P.dma(SP, O["sv"][l, si, rows, :], st[:TT, 1024:1280], reads=[st.b])

    def phase_F(cx):
        P.tag = "F.pre"
        l, si, T, NK, BW, QW, past, O = cx.l, cx.si, cx.T, cx.NK, cx.BW, cx.BW, cx.past, cx.O
        mem.top = phase_base
        P.barrier()
        wsl = [mem.alloc(f"wf{i}", (128, 8, 128), BF16) for i in range(4)]
        Qa = [mem.alloc(f"Qa{i}", (128, T), BF16) for i in range(2)]
        Ka = [mem.alloc(f"Ka{i}", (128, NK), BF16) for i in range(2)]
        gate = mem.alloc("gate", (128, T), BF16)
        Aa = mem.alloc("Aa", (128, NK), F32)
        Sa = mem.alloc("Sa", (128, NK), F32)
        SPL = mem.alloc("SPL", (128, 3, NK), BF16)
        pts = [mem.alloc(f"pt{i}", (128, QW), BF16) for i in range(4)]
        rD = [mem.alloc(f"rD{i}", (128, QW), F32) for i in range(2)]
        o1 = [mem.alloc(f"o1{i}", (128, QW), F32) for i in range(2)]
        kst = [mem.alloc(f"kst{i}", (128, 16, 128), F32) for i in range(2)] if cx.kind == "s" else None

        def load_kcache(hp_):
            k_ = kst[hp_ % 2]
            P.dma(SP, k_[:], cfk[l, si].rearrange("(t p) c -> p t c", p=128)[:, :, hp_ * 128:(hp_ + 1) * 128], writes=[k_.b])
        if cx.kind == "s":
            load_kcache(0)
        Vaug = mem.alloc("Vaug", (128, 17, 2, 128), BF16)
        memset(POOL, Vaug[:, :, :, :], 1.0, [Vaug.b])

        def pre_chain():
            P.tag = "F.pre"
            wfb = load_w(cx, wsl, FT_FB)
            if cx.kind == "s":
                lst = mem.alloc("lst", (128, 16, 6), F32)
                P.dma(SP, lst[:, :, :], cfl[l, si].rearrange("(t p) h -> p t h", p=128), writes=[lst.b])
                for t4 in range(4):
                    bank = next_bank()
                    for t in range(4):
                        tr(bank[0:6, t * 128:(t + 1) * 128], lst[:, 4 * t4 + t, :], C["identf"][:, :], [lst.b, C["identf"].b], [bank.b])
                    cp(DVE if t4 % 2 else ACT, Aa[0:6, t4 * 512:(t4 + 1) * 512], bank[0:6, 0:512], [bank.b], [Aa.b])
            for blk in range(cx.NBLK):
                bank = next_bank()
                project(cx, wfb, 6, blk, bank)
                cols = slice(past + blk * BW, past + (blk + 1) * BW)
                act(Aa[0:6, cols], bank[0:6, 0:BW], AF.Exp, [bank.b, nfb_t.b], [Aa.b], bias=nfb_t[0:6, l:l + 1], scale=-1.0)
                act(Aa[0:6, cols], Aa[0:6, cols], AF.Ln, [Aa.b], [Aa.b], bias=1.0)
                ts(DVE, Aa[0:6, cols], Aa[0:6, cols], -1.0, None, ALU.mult, None, [Aa.b], [Aa.b])
            P.op(DVE, lambda e: e.tensor_tensor_scan(out=Sa[0:6, 0:NK], data0=onecol[0:6, 0:1].to_broadcast([6, NK]),
                                                     data1=Aa[0:6, 0:NK], initial=0.0, op0=ALU.mult, op1=ALU.subtract),
                 reads=[Aa.b, onecol.b], writes=[Sa.b])
            cp(DVE, SPL[0:6, 0, :], Sa[0:6, 0:NK], [Sa.b], [SPL.b])
            tt(DVE, Aa[0:6, 0:NK], Sa[0:6, 0:NK], SPL[0:6, 0, :], ALU.subtract, [Sa.b, SPL.b], [Aa.b])
            cp(DVE, SPL[0:6, 1, :], Aa[0:6, 0:NK], [Aa.b], [SPL.b])
            tt(DVE, Aa[0:6, 0:NK], Aa[0:6, 0:NK], SPL[0:6, 1, :], ALU.subtract, [Aa.b, SPL.b], [Aa.b])
            cp(DVE, SPL[0:6, 2, :], Aa[0:6, 0:NK], [Aa.b], [SPL.b])


        DBGF = int(os.environ.get("DBG_F", "9"))
        if DBGF < 1:
            return
        for hp in range(3):
            P.tag = "F.proj"
            wq = load_w(cx, wsl, FT_QB + hp)
            wk = load_w(cx, wsl, FT_KB + hp)
            wg = load_w(cx, wsl, FT_GB + hp)
            if cx.kind == "s":
                kcur = kst[hp % 2]
                for hh in range(2):
                    for t4 in range(4):
                        bank = next_bank()
                        for t in range(4):
                            tr(bank[0:64, t * 128:(t + 1) * 128], kcur[:, 4 * t4 + t, hh * 64:(hh + 1) * 64], C["identf"][:, :],
                               [kcur.b, C["identf"].b], [bank.b])
                        cp(DVE if t4 % 2 else ACT, Ka[hh][0:64, t4 * 512:(t4 + 1) * 512], bank[0:64, 0:512], [bank.b], [Ka[hh].b])
                if hp + 1 < 3:
                    load_kcache(hp + 1)
            for blk in range(cx.NBLK):
                cols = slice(blk * BW, (blk + 1) * BW)
                kcols = slice(past + blk * BW, past + (blk + 1) * BW)
                bq = next_bank()
                project(cx, wq, 128, blk, bq)
                act(Qa[0][0:64, cols], bq[0:64, 0:BW], AF.Copy, [bq.b], [Qa[0].b], scale=0.125)
                ts(DVE, Qa[1][0:64, cols], bq[64:128, 0:BW], 0.125, None, ALU.mult, None, [bq.b], [Qa[1].b])
                bk_ = next_bank()
                project(cx, wk, 128, blk, bk_)
                cp(ACT, Ka[0][0:64, kcols], bk_[0:64, 0:BW], [bk_.b], [Ka[0].b])
                cp(DVE, Ka[1][0:64, kcols], bk_[64:128, 0:BW], [bk_.b], [Ka[1].b])
                bg = next_bank()
                project(cx, wg, 128, blk, bg)
                act(gate[:, cols], bg[:, 0:BW], AF.Silu, [bg.b], [gate.b])
            if hp == 0:
                pre_chain()
            P.tag = "F.proj"
            for hh in range(2):
                h = 2 * hp + hh
                memset(POOL, Qa[hh][64:70, 0:T], 1.0, [Qa[hh].b])
                memset(POOL, Ka[hh][64:70, 0:NK], -1.0, [Ka[hh].b])
                for r in range(3):
                    P.dma(SP, Qa[hh][64 + r:65 + r, 0:T], SPL[h:h + 1, r, past:NK], reads=[SPL.b], writes=[Qa[hh].b])
                    P.dma(SP, Ka[hh][67 + r:68 + r, 0:NK], SPL[h:h + 1, r, 0:NK], reads=[SPL.b], writes=[Ka[hh].b])
            P.tag = "F.attn"
            nfull = NK // 128
            for hh_ in range(2):
                vcols = slice((2 * hp + hh_) * 64, (2 * hp + hh_ + 1) * 64)
                acols = slice(64 * hh_, 64 * hh_ + 64)
                cp(POOL, Vaug[:, 0:nfull, hh_, acols], Vb[:, 0:nfull, vcols], [Vb.b], [Vaug.b])
                if NK % 128:
                    cp(POOL, Vaug[0:NK % 128, nfull, hh_, acols], Vb[0:NK % 128, nfull, vcols], [Vb.b], [Vaug.b])
            heat["n"], heat["bank"] = HEAT_F, psum[0]
            for hh in range(2):
                if DBGF < 2:
                    break
                h = 2 * hp + hh
                ob = 64 * hh
                db = 64 - ob
                for qt in range(T // QW):
                    q0 = qt * QW
                    qlo = past + q0
                    qhi = qlo + QW - 1
                    kts = [kt for kt in range(cx.NKT) if kt * 128 <= qhi]
                    Ob = psum[4 + (2 * hh + qt) % 4]

                    def stage2(kt, kn, pt, first, last):
                        mm(Ob[:, 0:QW], Vaug[0:kn, kt, hh, :], pt[0:kn, 0:QW], first, last, [Vaug.b, pt.b], [Ob.b])
                    pend = None
                    for i, kt in enumerate(kts):
                        kn = min(128, NK - kt * 128)
                        Sb = psum[i % 4]
                        partial = kt * 128 + kn - 1 > qlo
                        mm(Sb[0:kn, 0:QW], Ka[hh][0:70, kt * 128:kt * 128 + kn], Qa[hh][0:70, q0:q0 + QW], True, not partial,
                           [Ka[hh].b, Qa[hh].b], [Sb.b])
                        if partial:
                            d = kt * 128 - qlo
                            mm(Sb[0:kn, 0:QW], identb[0:kn, 0:kn], negleb[0:kn, 384 - d:384 - d + QW], False, True,
                               [identb.b, negleb.b], [Sb.b])
                        pt = pts[i % 4]
                        act(pt[0:kn, 0:QW], Sb[0:kn, 0:QW], AF.Exp, [Sb.b], [pt.b])
                        if pend:
                            stage2(*pend)
                        pend = (kt, kn, pt, i == 0, i == len(kts) - 1)
                    stage2(*pend)
                    rd, oo = rD[qt % 2], o1[qt % 2]
                    P.op(DVE, lambda e, o=rd[ob:ob + 64, 0:QW], i_=Ob[db:db + 64, 0:QW]: e.reciprocal(out=o, in_=i_),
                         reads=[Ob.b], writes=[rd.b])
                    tt(DVE, oo[ob:ob + 64, 0:QW], Ob[ob:ob + 64, 0:QW], rd[ob:ob + 64, 0:QW], ALU.mult, [Ob.b, rd.b], [oo.b])
                    tt(POOL, oT[ob:ob + 64, 3 + h // 2, q0:q0 + QW], oo[ob:ob + 64, 0:QW], gate[ob:ob + 64, q0:q0 + QW], ALU.mult,
                       [oo.b, gate.b], [oTb[3 + h // 2]])
            heat["n"] = 0

    def phase_S(cx):
        P.tag = "S.pre"
        l, si, T, NK, BW, QW, past, O = cx.l, cx.si, cx.T, cx.NK, cx.BW, cx.BW, cx.past, cx.O
        mem.top = phase_base
        P.barrier()
        wsl = [mem.alloc(f"wf{i}", (128, 8, 128), BF16) for i in range(4)]
        Qc = [mem.alloc(f"Qc{i}", (128, T), BF16) for i in range(2)]
        Kc = [mem.alloc(f"Kc{i}", (128, NK), BF16) for i in range(2)]
        gate = mem.alloc("gate", (128, T), BF16)
        Ef = [mem.alloc(f"Ef{i}", (128, QW), F32) for i in range(3)]
        Xe = [mem.alloc(f"Xe{i}", (128, QW), BF16) for i in range(3)]
        spb = [mem.alloc(f"spb{i}", (128, QW), BF16) for i in range(3)]
        Lsb = [mem.alloc(f"Lsb{i}", (128, QW), BF16) for i in range(3)]
        Wt = [mem.alloc(f"Wt{i}", (128, QW), BF16) for i in range(3)]
        Lsum = [mem.alloc(f"Lsum{i}", (128, QW), F32) for i in range(2)]
        kst = [mem.alloc(f"kst{i}", (128, 16, 128), F32) for i in range(2)] if cx.kind == "s" else None

        def load_kcache(sp_):
            k_ = kst[sp_ % 2]
            P.dma(SP, k_[:], csk[l, si].rearrange("(t p) c -> p t c", p=128)[:, :, sp_ * 128:(sp_ + 1) * 128], writes=[k_.b])
        if cx.kind == "s":
            load_kcache(0)
            load_kcache(1)
        for sp in range(2):
            P.tag = "S.proj"
            wq = load_w(cx, wsl, FT_QC + sp)
            wk = load_w(cx, wsl, FT_KC + sp)
            wg = load_w(cx, wsl, FT_GC + sp)
            if cx.kind == "s":
                kcur = kst[sp % 2]
                for hh in range(2):
                    for t4 in range(4):
                        bank = next_bank()
                        for t in range(4):
                            tr(bank[0:64, t * 128:(t + 1) * 128], kcur[:, 4 * t4 + t, hh * 64:(hh + 1) * 64], C["identf"][:, :],
                               [kcur.b, C["identf"].b], [bank.b])
                        cp(DVE if t4 % 2 else ACT, Kc[hh][0:64, t4 * 512:(t4 + 1) * 512], bank[0:64, 0:512], [bank.b], [Kc[hh].b])
            for blk in range(cx.NBLK):
                cols = slice(blk * BW, (blk + 1) * BW)
                kcols = slice(past + blk * BW, past + (blk + 1) * BW)
                bq = next_bank()
                project(cx, wq, 128, blk, bq)
                act(Qc[0][0:64, cols], bq[0:64, 0:BW], AF.Copy, [bq.b], [Qc[0].b], scale=0.125)
                ts(DVE, Qc[1][0:64, cols], bq[64:128, 0:BW], 0.125, None, ALU.mult, None, [bq.b], [Qc[1].b])
                bk_ = next_bank()
                project(cx, wk, 128, blk, bk_)
                cp(ACT, Kc[0][0:64, kcols], bk_[0:64, 0:BW], [bk_.b], [Kc[0].b])
                cp(DVE, Kc[1][0:64, kcols], bk_[64:128, 0:BW], [bk_.b], [Kc[1].b])
                bg = next_bank()
                project(cx, wg, 128, blk, bg)
                act(gate[:, cols], bg[:, 0:BW], AF.Silu, [bg.b], [gate.b])
            P.tag = "S.attn"
            heat["n"], heat["bank"] = HEAT_S, psum[0]
            for hh in range(2):
                hc = 2 * sp + hh
                ob = 64 * hh
                for qt in range(T // QW):
                    q0 = qt * QW
                    qlo = past + q0
                    qhi = qlo + QW - 1
                    kts = list(reversed([kt for kt in range(cx.NKT) if kt * 128 <= qhi]))
                    n = len(kts)
                    Ob = psum[6 + qt % 2]
                    memset(POOL, Lsum[0][:, :], 0.0, [Lsum[0].b])
                    memset(POOL, Lsum[1][:, :], 0.0, [Lsum[1].b])

                    def info(i):
                        kt = kts[i]
                        return kt, min(128, NK - kt * 128), psum[2 + i % 2]

                    def stage1(i):
                        kt, kn, Zb = info(i)
                        partial = kt * 128 + kn - 1 >= qlo
                        mm(Zb[0:kn, 0:QW], Kc[hh][0:64, kt * 128:kt * 128 + kn], Qc[hh][0:64, q0:q0 + QW], True, not partial,
                           [Kc[hh].b, Qc[hh].b], [Zb.b])
                        if partial:
                            d = kt * 128 - qlo
                            mm(Zb[0:kn, 0:QW], identb[0:kn, 0:kn], negltb[0:kn, 384 - d:384 - d + QW], False, True,
                               [identb.b, negltb.b], [Zb.b])
                        j = i % 3
                        act(Ef[j][0:kn, :], Zb[0:kn, 0:QW], AF.Exp, [Zb.b], [Ef[j].b])
                        act(spb[j][0:kn, :], Ef[j][0:kn, :], AF.Ln, [Ef[j].b], [spb[j].b], bias=1.0)
                        La, Lb = Lsum[i % 2], Lsum[(i + 1) % 2]
                        if i + 1 < n:
                            jn = (i + 1) % 3
                            if kn < 128:
                                memset(POOL, Lsb[jn][kn:128, :], 0.0, [Lsb[jn].b])
                            tt(DVE, Lsb[jn][0:kn, :], La[0:kn, :], spb[j][0:kn, :], ALU.add, [La.b, spb[j].b], [Lsb[jn].b])
                            tt(DVE, Lb[0:kn, :], La[0:kn, :], spb[j][0:kn, :], ALU.add, [La.b, spb[j].b], [Lb.b])

                    def stage2(i):
                        kt, kn, _ = info(i)
                        Zb = psum[4 + i % 2]
                        j = i % 3
                        mm(Zb[0:kn, 0:QW], trinegb[0:kn, 0:kn], spb[j][0:kn, :], True, i == 0, [trinegb.b, spb[j].b], [Zb.b])
                        if i > 0:
                            mm(Zb[0:kn, 0:QW], onesneg[:, 0:kn], Lsb[j][:, :], False, True, [onesneg.b, Lsb[j].b], [Zb.b])
                        act(Xe[j][0:kn, :], Zb[0:kn, 0:QW], AF.Exp, [Zb.b], [Xe[j].b])
                        tt(DVE, Wt[j][0:kn, :], Ef[j][0:kn, :], Xe[j][0:kn, :], ALU.mult, [Ef[j].b, Xe[j].b], [Wt[j].b])

                    def stage3(i):
                        kt, kn, Zb = info(i)
                        j = i % 3
                        mm(Ob[ob:ob + 64, 0:QW], Vc[0:kn, kt, hc * 64:(hc + 1) * 64], Wt[j][0:kn, :], i == 0, i == n - 1,
                           [Vc.b, Wt[j].b], [Ob.b])
                    for s_ in range(n + 2):
                        if s_ < n:
                            stage1(s_)
                        if 0 <= s_ - 1 < n:
                            stage2(s_ - 1)
                        if 0 <= s_ - 2 < n:
                            stage3(s_ - 2)
                    tt(DVE, oT[ob:ob + 64, 6 + hc // 2, q0:q0 + QW], Ob[ob:ob + 64, 0:QW], gate[ob:ob + 64, q0:q0 + QW], ALU.mult,
                       [Ob.b, gate.b], [oTb[6 + hc // 2]])
            heat["n"] = 0

    def phase_E(cx):
        P.tag = "E"
        l, si, TT, O = cx.l, cx.si, cx.TT, cx.O
        mem.top = phase_base
        P.barrier()
        wo = mem.alloc("wo", (128, 8, 1024), BF16)
        lng = mem.alloc("lng", (128, D), F32)
        lnb = mem.alloc("lnb", (128, D), F32)
        xres = [mem.alloc(f"xres{i}", (128, D), F32) for i in range(3)]
        Rr = [mem.alloc(f"Rr{i}", (128, D), F32) for i in range(2)]
        yv = [mem.alloc(f"yv{i}", (128, D), F32) for i in range(3)]
        st = mem.alloc("bnst", (128, 12), F32)
        mv = mem.alloc("bnmv", (128, 2), F32)
        rs = mem.alloc("bnrs", (128, 1), F32)
        nb = mem.alloc("bnnb", (128, 1), F32)
        P.dma(SP, wo[:], WO[l], reads=[bWO[l]], writes=[wo.b])
        P.dma(SP, lng[:], ln_g[l].partition_broadcast(128), writes=[lng.b])
        P.dma(SP, lnb[:], ln_b[l].partition_broadcast(128), writes=[lnb.b])
        st2 = [st, mem.alloc("bnst2", (128, 12), F32)]

        def front(tt_):
            rows = slice(tt_ * TT, (tt_ + 1) * TT)
            xr, R_, st_ = xres[tt_ % 3], Rr[tt_ % 2], st2[tt_ % 2]
            if l == 0:
                P.dma(SP, xr[:TT, :], cx.xsrc[si, rows, :], writes=[xr.b])
            else:
                P.dma(SP, xr[:TT, :], Y0[cx.yidx, rows, :], reads=[bY0[cx.yidx][tt_]], writes=[xr.b])
            bA, bB = psum[2 * (tt_ % 2)], psum[2 * (tt_ % 2) + 1]
            for c in range(8):
                mm(bA[:TT, 0:512], oT[:, c, rows], wo[:, c, 0:512], c == 0, c == 7, [oTb[c], wo.b], [bA.b])
            for c in range(8):
                mm(bB[:TT, 0:512], oT[:, c, rows], wo[:, c, 512:1024], c == 0, c == 7, [oTb[c], wo.b], [bB.b])

        def frontB(tt_):
            xr, R_, st_ = xres[tt_ % 3], Rr[tt_ % 2], st2[tt_ % 2]
            bA, bB = psum[2 * (tt_ % 2)], psum[2 * (tt_ % 2) + 1]
            stt(R_[:TT, 0:512], xr[:TT, 0:512], ALPHA, bA[:TT, 0:512], ALU.mult, ALU.add, [xr.b, bA.b], [R_.b])
            stt(R_[:TT, 512:1024], xr[:TT, 512:1024], ALPHA, bB[:TT, 0:512], ALU.mult, ALU.add, [xr.b, bB.b], [R_.b])
            P.op(DVE, lambda e, o=st_[:TT, 0:6], i_=R_[:TT, 0:512]: e.bn_stats(out=o, in_=i_), reads=[R_.b], writes=[st_.b])
            P.op(DVE, lambda e, o=st_[:TT, 6:12], i_=R_[:TT, 512:1024]: e.bn_stats(out=o, in_=i_), reads=[R_.b], writes=[st_.b])

        def back(tt_):
            rows = slice(tt_ * TT, (tt_ + 1) * TT)
            R_, y_, st_ = Rr[tt_ % 2], yv[tt_ % 3], st2[tt_ % 2]
            P.op(DVE, lambda e, o=mv[:TT, 0:2], i_=st_[:TT, 0:12]: e.bn_aggr(out=o, in_=i_), reads=[st_.b], writes=[mv.b])
            act(rs[:TT, :], mv[:TT, 1:2], AF.Ln, [mv.b], [rs.b], bias=LN_EPS)
            act(rs[:TT, :], rs[:TT, :], AF.Exp, [rs.b], [rs.b], scale=-0.5)
            stt(nb[:TT, :], mv[:TT, 0:1], -1.0, rs[:TT, 0:1], ALU.mult, ALU.mult, [mv.b, rs.b], [nb.b])
            act(y_[:TT, :], R_[:TT, :], AF.Identity, [R_.b, rs.b, nb.b], [y_.b], bias=nb[:TT, 0:1], scale=rs[:TT, 0:1])
            tt(DVE, y_[:TT, :], y_[:TT, :], lng[:TT, :], ALU.mult, [y_.b, lng.b], [y_.b])
            tt(POOL, y_[:TT, :], y_[:TT, :], lnb[:TT, :], ALU.add, [y_.b, lnb.b], [y_.b])
            if l == 0:
                P.dma(POOL, Y0[cx.yidx, rows, :], y_[:TT, :], reads=[y_.b], writes=[bY0[cx.yidx][tt_]])
                bk = (psum[4], psum[5]) if tt_ % 2 == 0 else (psum[6], psum[7])
                to_xT(cx, y_, tt_, bk)
            else:
                P.dma(POOL, O["y"][si, rows, :], y_[:TT, :], reads=[y_.b])

        front(0)
        frontB(0)
        if cx.NTT > 1:
            front(1)
        for tt_ in range(cx.NTT):
            back(tt_)
            if tt_ + 1 < cx.NTT:
                frontB(tt_ + 1)
            if tt_ + 2 < cx.NTT:
                front(tt_ + 2)

    def phase_R(cx):
        P.tag = "R.init"
        l, si, T, BW, TT, past, O = cx.l, cx.si, cx.T, cx.BW, cx.TT, cx.past, cx.O
        NCH = TT // 64
        mem.top = phase_base
        P.barrier()
        NT = BW // TT
        NU = 2 * NT
        NCHK = NT * NCH
        wsl = [mem.alloc(f"wf{i}", (128, 8, 128), BF16) for i in range(3)]
        U = [mem.alloc(f"U{i}", (128, BW + 1), F32) for i in range(3)]
        Ul = mem.alloc("Ul", (128, BW + 1), F32)
        Dt = mem.alloc("Dt", (128, BW), F32)
        tw = mem.alloc("tw", (128, BW), BF16)
        gaT = mem.alloc("gaT", (128, BW), BF16)
        fnames = "lgc lgx ex eneg ld av tmp esfx kk kkn k2 bb epos Rt bonus Y".split()
        f = {}
        foff = {}
        for n_ in fnames:
            foff[n_] = mem.top
            f[n_] = mem.alloc(n_, (128, BW), F32)
        b = {n: mem.alloc(n, (128, BW), BF16) for n in "kk2 Rtb KKt Kh Bh Kg Bg Vbf rkb Ybf Ysq".split()}
        def alias(name, shape, dt, off):
            mem.n += 1
            return nc.alloc_sbuf_tensor_at(f"{name}_{mem.n}", list(shape), dt, offset=off)
        if BW == 512:
            MK = alias("MK", (128, 8, 512), BF16, foff["lgc"])
            MKb = [f[("lgc", "lgx", "ex", "eneg")[u // 2]].b for u in range(8)]
            MM = [alias("MM0", (128, 8, 256), BF16, foff["ld"]), alias("MM1", (128, 8, 256), BF16, foff["tmp"])]
            MMb = [[f[("ld", "av")[g // 2]].b for g in range(4)], [f[("tmp", "esfx")[g // 2]].b for g in range(4)]]
        else:
            MKt = mem.alloc("MK", (128, 8, 512), BF16)
            MK, MKb = MKt.h, [MKt.b] * 8
            MMt = [mem.alloc(f"MM{i}", (128, 8, 256), BF16) for i in range(2)]
            MM, MMb = [t_.h for t_ in MMt], [[t_.b] * 4 for t_ in MMt]
        QcT, McTt, D1sb, Y0sb = f["kk"], f["kkn"], f["k2"], f["bb"]
        TOK = mem.alloc("TOK", (128, 4, 4, 128), BF16)
        Pt = [mem.alloc(f"Pt{i}", (128, 8, 128), BF16) for i in range(2)]
        Ptb = [[Buf(f"Ptb{i}{g}") for g in range(2)] for i in range(2)]
        ArbT = mem.alloc("ArbT", (128, 2, 4, 128), BF16)
        MKraw = [mem.alloc(f"MKraw{i}", (128, 512), BF16) for i in range(2)]
        W1b = mem.alloc("W1b", (128, 8, 64), BF16)
        UW = mem.alloc("UW", (128, 8, 128), BF16)
        UWm = [mem.alloc(f"UWm{i}", (128, 8, 128), BF16) for i in range(2)]
        Vm = [mem.alloc(f"Vm{i}", (128, 4, 128), BF16) for i in range(2)]
        wst = mem.alloc("wst", (128, 6, 64), F32)
        gidx = [[0, 0] for _ in range(3)]

        if cx.kind == "p":
            memset(POOL, ucarry[:, :], 0.0, ucb)
            memset(POOL, Gst[0][:, :, :], 0.0, [Gb[0][c3][hh] for c3 in range(3) for hh in range(2)])
            memset(POOL, Gst[1][:, :, :], 0.0, [Gb[1][c3][hh] for c3 in range(3) for hh in range(2)])
        else:
            with nc.allow_non_contiguous_dma("state_shift transpose load"):
                P.dma(SP, ucarry[:, :], sshift[l, si].rearrange("(t p) -> p t", p=128), writes=ucb)
            memset(POOL, Gst[1][:, :, :], 0.0, [Gb[1][c3][hh] for c3 in range(3) for hh in range(2)])
            P.dma(SP, wst[0:64, :, :], swkv[l, si].rearrange("h v k -> v h k"), writes=[wst.b])
            for c3 in range(3):
                for hh in range(2):
                    hb = 64 * hh
                    mm(psum[2][hb:hb + 64, 256 + hh * 64:256 + (hh + 1) * 64], wst[0:64, 2 * c3 + hh, :], C["identf"][0:64, 0:64],
                       True, True, [wst.b, C["identf"].b], [psum[2].b])
                    cp(DVE, Gst[0][hb:hb + 64, c3, :], psum[2][hb:hb + 64, 256 + hh * 64:256 + (hh + 1) * 64], [psum[2].b], [Gb[0][c3][hh]])

        def uproc(Ut, bank, ct, last_blk):
            cp(ACT, Ut[:, 1:BW + 1], bank[:, 0:BW], [bank.b], [Ut.b])
            cp(ACT, Ut[:, 0:1], ucarry[:, ct:ct + 1], [ucb[ct]], [Ut.b])
            cp(ACT, ucarry[:, ct:ct + 1], Ut[:, BW:BW + 1], [Ut.b], [ucb[ct]])
            if last_blk:
                with nc.allow_non_contiguous_dma("shift state store"):
                    P.dma(POOL, O["sh"][l, si, ct * 128:(ct + 1) * 128].rearrange("(p o) -> p o", o=1), Ut[:, BW:BW + 1], reads=[Ut.b])
            act(Dt[:, :], Ut[:, 0:BW], AF.Copy, [Ut.b, mu_t.b], [Dt.b], scale=mu_t[:, l, ct:ct + 1])
            stt(Ut[:, 1:BW + 1], Ut[:, 1:BW + 1], omm_t[:, l, ct:ct + 1], Dt[:, :], ALU.mult, ALU.add, [Dt.b, omm_t.b, Ut.b], [Ut.b])

        ABANKS = (2, 4)

        def prep_A(blk, c3):
            last_blk = blk == cx.NBLK - 1
            if c3 == 0:
                P.tag = "R.lora"
                w9 = load_w(cx, wsl, 9)
                bank = next_bank(*ABANKS)
                project(cx, w9, 128, blk, bank)
                uproc(Ul, bank, 9, last_blk)
                act(tw[0:64, :], Ul[0:64, 1:BW + 1], AF.Tanh, [Ul.b], [tw.b])
                cp(DVE, tw[64:128, :], Ul[64:128, 1:BW + 1], [Ul.b], [tw.b])
                yield
            P.tag = "R.prepA"
            for j, ct in enumerate((c3, 3 + c3, 6 + c3)):
                ws = load_w(cx, wsl, ct)
                bank = next_bank(*ABANKS)
                project(cx, ws, 128, blk, bank)
                uproc(U[j], bank, ct, last_blk)
                yield
            cs = slice(c3 * 128, (c3 + 1) * 128)
            bank = next_bank(*ABANKS)
            mm(bank[:, 0:BW], lw[0:64, l, cs], tw[0:64, :], True, True, [lw.b, tw.b], [bank.b])
            act(f["ld"][:, :], bank[:, 0:BW], AF.Sigmoid, [bank.b, w0_t.b], [f["ld"].b], bias=w0_t[:, l, c3:c3 + 1])
            yield
            bank = next_bank(*ABANKS)
            mm(bank[:, 0:BW], lw[64:128, l, cs], tw[64:128, :], True, True, [lw.b, tw.b], [bank.b])
            act(f["av"][:, :], bank[:, 0:BW], AF.Sigmoid, [bank.b, a0_t.b], [f["av"].b], bias=a0_t[:, l, c3:c3 + 1])
            act(f["ld"][:, :], f["ld"][:, :], AF.Copy, [f["ld"].b], [f["ld"].b], scale=DEC_SCALE)
            yield
            P.op(DVE, lambda e: e.tensor_tensor_scan(out=f["lgc"][:, :], data0=C["chunkmask"][:, 0:BW], data1=f["ld"][:, :],
                                                     initial=0.0, op0=ALU.mult, op1=ALU.add),
                 reads=[C["chunkmask"].b, f["ld"].b], writes=[f["lgc"].b])
            tt(DVE, f["lgx"][:, :], f["lgc"][:, :], f["ld"][:, :], ALU.subtract, [f["lgc"].b, f["ld"].b], [f["lgx"].b])
            yield
            act(f["epos"][:, :], f["lgc"][:, :], AF.Exp, [f["lgc"].b], [f["epos"].b])
            act(f["ex"][:, :], f["lgx"][:, :], AF.Exp, [f["lgx"].b], [f["ex"].b])
            act(f["eneg"][:, :], f["lgc"][:, :], AF.Exp, [f["lgc"].b], [f["eneg"].b], scale=-1.0)
            yield
            lg3 = f["lgc"][:, :].rearrange("p (c n) -> p c n", n=64)
            tt(DVE, f["esfx"][:, :].rearrange("p (c n) -> p c n", n=64), lg3[:, :, 63:64].to_broadcast([128, BW // 64, 64]), lg3,
               ALU.subtract, [f["lgc"].b], [f["esfx"].b])
            act(f["esfx"][:, :], f["esfx"][:, :], AF.Exp, [f["esfx"].b], [f["esfx"].b])
            yield

        def advance(g, n):
            if g is None:
                return
            tag0 = P.tag
            for _ in range(n):
                try:
                    next(g)
                except StopIteration:
                    break
            P.tag = tag0

        def drain(g):
            advance(g, 10 ** 6)

        its = [(blk_, c3_) for blk_ in range(cx.NBLK) for c3_ in range(3)]
        drain(prep_A(0, 0))
        for blk in range(cx.NBLK):
            for c3 in range(3):
                idx_it = blk * 3 + c3
                nxtA = prep_A(*its[idx_it + 1]) if idx_it + 1 < len(its) else None
                P.tag = "R.prep"
                ws = load_w(cx, wsl, FT_GA + c3)
                bank = next_bank()
                project(cx, ws, 128, blk, bank)
                act(gaT[:, :], bank[:, 0:BW], AF.Silu, [bank.b], [gaT.b])
                r_, k_, v_ = U[0][:, 1:BW + 1], U[1][:, 1:BW + 1], U[2][:, 1:BW + 1]
                rb, kb_, vb_ = U[0].b, U[1].b, U[2].b
                act(b["kk2"][:, :], k_, AF.Square, [kb_, kk_t.b], [b["kk2"].b], scale=kk_t[:, l, c3:c3 + 1])
                bank = next_bank()
                mm(bank[:, 0:BW], bonesb[:, :], b["kk2"][:, :], True, True, [bonesb.b, b["kk2"].b], [bank.b])
                act(f["tmp"][:, :], bank[:, 0:BW], AF.Ln, [bank.b], [f["tmp"].b], bias=1e-12)
                act(f["tmp"][:, :], f["tmp"][:, :], AF.Exp, [f["tmp"].b], [f["tmp"].b], scale=-0.5)
                stt(f["kkn"][:, :], k_, kk_t[:, l, c3:c3 + 1], f["tmp"][:, :], ALU.mult, ALU.mult, [kb_, kk_t.b, f["tmp"].b], [f["kkn"].b])
                ts(DVE, f["k2"][:, :], f["av"][:, :], ka_t[:, l, c3:c3 + 1], omka_t[:, l, c3:c3 + 1], ALU.mult, ALU.add,
                   [f["av"].b, ka_t.b, omka_t.b], [f["k2"].b])
                tt(DVE, f["k2"][:, :], f["k2"][:, :], k_, ALU.mult, [f["k2"].b, kb_], [f["k2"].b])
                tt(DVE, f["bb"][:, :], f["kkn"][:, :], f["av"][:, :], ALU.mult, [f["kkn"].b, f["av"].b], [f["bb"].b])
                stt(b["rkb"][:, :], r_, rk_t[:, l, c3:c3 + 1], f["k2"][:, :], ALU.mult, ALU.mult, [rb, rk_t.b, f["k2"].b], [b["rkb"].b])
                bank = next_bank()
                mm(bank[:, 0:BW], bonesb[:, :], b["rkb"][:, :], True, True, [bonesb.b, b["rkb"].b], [bank.b])
                tt(DVE, f["bonus"][:, :], bank[:, 0:BW], v_, ALU.mult, [bank.b, vb_], [f["bonus"].b])
                tt(DVE, f["Rt"][:, :], r_, f["epos"][:, :], ALU.mult, [rb, f["epos"].b], [f["Rt"].b])
                cp(ACT, b["Rtb"][:, :], f["Rt"][:, :], [f["Rt"].b], [b["Rtb"].b])
                tt(DVE, b["KKt"][:, :], f["kkn"][:, :], f["ex"][:, :], ALU.mult, [f["kkn"].b, f["ex"].b], [b["KKt"].b])
                tt(DVE, b["Kh"][:, :], f["k2"][:, :], f["eneg"][:, :], ALU.mult, [f["k2"].b, f["eneg"].b], [b["Kh"].b])
                tt(DVE, b["Bh"][:, :], f["bb"][:, :], f["eneg"][:, :], ALU.mult, [f["bb"].b, f["eneg"].b], [b["Bh"].b])
                tt(DVE, b["Kg"][:, :], f["k2"][:, :], f["esfx"][:, :], ALU.mult, [f["k2"].b, f["esfx"].b], [b["Kg"].b])
                tt(POOL, b["Bg"][:, :], f["bb"][:, :], f["esfx"][:, :], ALU.mult, [f["bb"].b, f["esfx"].b], [b["Bg"].b])
                cp(ACT, b["Vbf"][:, :], v_, [vb_], [b["Vbf"].b])

                v3 = lambda ap, c=128, n=TT: ap.rearrange("p (a c) -> p a c", c=c)[:, :, 0:n]
                P.tag = "R.S0"
                for tl in range(NT):
                    tc = slice(tl * TT, (tl + 1) * TT)
                    bk = psum[6 + tl % 2]
                    tb = pbf(6 + tl % 2)
                    for j, nm in enumerate(("KKt", "Kg", "Bg", "Vbf")):
                        tr(tb[0:TT, j * 128:(j + 1) * 128], b[nm][:, tc], identb[:, :], [b[nm].b, identb.b], [bk.b])
                    cp(DVE if tl % 2 else ACT, TOK[0:TT, tl, :, :], tb[0:TT, 0:512].rearrange("p (a c) -> p a c", c=128), [bk.b], [TOK.b])
                P.tag = "R.S1"
                for tl in range(NT):
                    tc = slice(tl * TT, (tl + 1) * TT)
                    for hh in range(2):
                        hs = slice(64 * hh, 64 * hh + 64)
                        bA = psum[2 * (tl % 2) + hh]
                        mm(bA[0:TT, 0:TT], b["KKt"][hs, tc], b["Bh"][hs, tc], True, True, [b["KKt"].b, b["Bh"].b], [bA.b])
                        mm(bA[0:TT, 128:128 + TT], b["Bh"][hs, tc], b["KKt"][hs, tc], True, True, [b["KKt"].b, b["Bh"].b], [bA.b])
                        mm(bA[0:TT, 256:256 + TT], b["Kh"][hs, tc], b["KKt"][hs, tc], True, True, [b["KKt"].b, b["Kh"].b], [bA.b])
                        mm(bA[0:TT, 384:384 + TT], b["Kh"][hs, tc], b["Rtb"][hs, tc], True, True, [b["Rtb"].b, b["Kh"].b], [bA.b])
                        mm(psum[4 + hh][0:TT, tl * 128:tl * 128 + TT], b["Bh"][hs, tc], b["Rtb"][hs, tc], True, True,
                           [b["Rtb"].b, b["Bh"].b], [psum[4 + hh].b])
                    for hh in range(2):
                        u = 2 * tl + hh
                        bA = psum[2 * (tl % 2) + hh]
                        if hh == 0 or tl % 2 == 1:
                            tt(DVE, v3(MK[0:TT, u, :]), v3(bA[0:TT, :]), v3(C["rwmask"][0:TT, :]), ALU.mult, [bA.b, C["rwmask"].b], [MKb[u]])
                        else:
                            raw = MKraw[tl % 2]
                            cp(ACT, v3(raw[0:TT, :]), v3(bA[0:TT, :]), [bA.b], [raw.b])
                            tt(POOL, v3(MK[0:TT, u, :]), v3(raw[0:TT, :]), v3(C["rwmask"][0:TT, :]), ALU.mult, [raw.b, C["rwmask"].b], [MKb[u]])
                for hh in range(2):
                    tt(DVE, ArbT[0:TT, hh, 0:NT, 0:TT], v3(psum[4 + hh][0:TT, :])[:, 0:NT, :], v3(C["iumask4"][0:TT, :])[:, 0:NT, :], ALU.mult,
                       [psum[4 + hh].b, C["iumask4"].b], [ArbT.b])
                mkall = sorted(set(MKb), key=id)
                tt(POOL, Pt[0][0:TT, 0:NU, 0:TT], MK[0:TT, 0:NU, 128:128 + TT], ident8[0:TT, 0:NU, 0:TT], ALU.add,
                   mkall + [ident8.b], [Ptb[0][0], Ptb[0][1]])
                P.tag = "R.S2"
                Mcur = [(MK[0:TT, u, 0:TT], MK[0:TT, u, 128:128 + TT], MKb[u]) for u in range(NU)]
                for m in range(1, 6):
                    par = m % 2
                    for u in range(NU):
                        bk = psum[u // 2]
                        off = 256 * (u % 2)
                        Mp, Mtp, mb = Mcur[u]
                        mm(bk[0:TT, off:off + TT], Mtp, Mp, True, True, [mb], [bk.b])
                        mm(bk[0:TT, off + 128:off + 128 + TT], Mp, Mtp, True, True, [mb], [bk.b])
                    for g in range(NU // 2):
                        dst = MM[par][0:TT, 2 * g:2 * g + 2, :].rearrange("p u (a c) -> p (u a) c", c=128)[:, :, 0:TT]
                        cp(DVE if g == 3 else ACT, dst, v3(psum[g][0:TT, :]), [psum[g].b], [MMb[par][g]])
                        for u in (2 * g, 2 * g + 1):
                            Mcur[u] = (MM[par][0:TT, u, 0:TT], MM[par][0:TT, u, 128:128 + TT], MMb[par][g])
                    for u in range(NU):
                        pbk = psum[4 + u // 4]
                        po = (u % 4) * 128
                        Pp = Pt[1 - par]
                        ppb = Ptb[1 - par][u // 4]
                        mm(pbk[0:TT, po:po + TT], Mcur[u][0], Pp[0:TT, u, 0:TT], True, True, [Mcur[u][2], ppb], [pbk.b])
                    for g in range((NU + 3) // 4):
                        nu_ = min(4, NU - 4 * g)
                        tt(DVE, Pt[par][0:TT, 4 * g:4 * g + nu_, 0:TT], v3(psum[4 + g][0:TT, :])[:, 0:nu_, :],
                           Pt[1 - par][0:TT, 4 * g:4 * g + nu_, 0:TT], ALU.add, [psum[4 + g].b, Ptb[1 - par][g]], [Ptb[par][g]])
                    advance(nxtA, ADV2)
                PtF, PtFb = Pt[1], Ptb[1]
                P.tag = "R.S3-6"
                for u in range(NU):
                    tl, hh = u // 2, u % 2
                    hs = slice(64 * hh, 64 * hh + 64)
                    mm(psum[6][0:TT, u * 64:(u + 1) * 64], MK[0:TT, u, 256:256 + TT], TOK[0:TT, tl, 3, hs], True, True, [MKb[u], TOK.b], [psum[6].b])
                cp(ACT, W1b[0:TT, 0:NU, :], psum[6][0:TT, 0:NU * 64].rearrange("p (u c) -> p u c", c=64), [psum[6].b], [W1b.b])
                for u in range(NU):
                    tl, hh = u // 2, u % 2
                    hs = slice(64 * hh, 64 * hh + 64)
                    bk = psum[u // 4]
                    uo = (u % 4) * 128
                    mm(bk[0:TT, uo:uo + 64], PtF[0:TT, u, 0:TT], W1b[0:TT, u, :], True, True, [PtFb[u // 4], W1b.b], [bk.b])
                    mm(bk[0:TT, uo + 64:uo + 128], PtF[0:TT, u, 0:TT], TOK[0:TT, tl, 0, hs], True, True, [PtFb[u // 4], TOK.b], [bk.b])
                for g in range((NU + 3) // 4):
                    nu_ = min(4, NU - 4 * g)
                    tt(DVE, UW[0:TT, 4 * g:4 * g + nu_, :], v3(psum[g][0:TT, :], n=128)[:, 0:nu_, :], v3(C["signs4"][0:TT, :], n=128)[:, 0:nu_, :],
                       ALU.mult, [psum[g].b, C["signs4"].b], [UW.b])
                for cc in range(NCH):
                    ts(POOL, UWm[cc][0:TT, 0:NU, :], UW[0:TT, 0:NU, :], C["cind"][0:TT, cc:cc + 1], None, ALU.mult, None,
                       [UW.b, C["cind"].b], [UWm[cc].b])
                    ts(POOL, Vm[cc][0:TT, 0:NT, :], TOK[0:TT, 0:NT, 3, :], C["cind"][0:TT, cc:cc + 1], None, ALU.mult, None,
                       [TOK.b, C["cind"].b], [Vm[cc].b])
                for u in range(NU):
                    tl, hh = u // 2, u % 2
                    hs = slice(64 * hh, 64 * hh + 64)
                    mm(psum[2][hs, tl * 128:tl * 128 + TT], UW[0:TT, u, 64:128], ArbT[0:TT, hh, tl, 0:TT], True, True, [UW.b, ArbT.b], [psum[2].b])
                vb = lambda ap: ap.rearrange("p (t c) -> p t c", c=TT)
                tt(DVE, vb(QcT[:, 0:BW]), vb(f["Rt"][:, 0:BW]), v3(psum[2][:, :])[:, 0:NT, :], ALU.subtract, [f["Rt"].b, psum[2].b], [QcT.b])
                for u in range(NU):
                    tl, hh = u // 2, u % 2
                    hs = slice(64 * hh, 64 * hh + 64)
                    mm(psum[3][hs, tl * 128:tl * 128 + TT], TOK[0:TT, tl, 3, hs], MK[0:TT, u, 384:384 + TT], True, False, [TOK.b, MKb[u]], [psum[3].b])
                    mm(psum[3][hs, tl * 128:tl * 128 + TT], UW[0:TT, u, 0:64], ArbT[0:TT, hh, tl, 0:TT], False, True, [UW.b, ArbT.b], [psum[3].b])
                cp(ACT, vb(Y0sb[:, 0:BW]), v3(psum[3][:, :])[:, 0:NT, :], [psum[3].b], [Y0sb.b])
                P.tag = "R.S7ab"
                McT = McTt[:, :].rearrange("p (c k) -> p c k", k=64) if BW == 512 else McTt[:, 0:64].rearrange("p (c k) -> p c k", k=64)
                for u in range(NU):
                    tl, hh = u // 2, u % 2
                    hs = slice(64 * hh, 64 * hh + 64)
                    for cc in range(NCH):
                        ch = tl * NCH + cc
                        mm(psum[6][hs, ch * 64:(ch + 1) * 64], UWm[cc][0:TT, u, 64:128], TOK[0:TT, tl, 2, hs], True, True,
                           [UWm[cc].b, TOK.b], [psum[6].b])
                for ch in range(NCHK):
                    gcol = ch * 64 + 63
                    stt(McT[:, ch, :], C["ident2"][:, :], f["epos"][:, gcol:gcol + 1], psum[6][:, ch * 64:(ch + 1) * 64],
                        ALU.mult, ALU.subtract, [C["ident2"].b, f["epos"].b, psum[6].b], [McTt.b])
                for u in range(NU):
                    tl, hh = u // 2, u % 2
                    hs = slice(64 * hh, 64 * hh + 64)
                    for cc in range(NCH):
                        ch = tl * NCH + cc
                        mm(psum[7][hs, ch * 64:(ch + 1) * 64], TOK[0:TT, tl, 1, hs], Vm[cc][0:TT, tl, hs], True, False, [TOK.b, Vm[cc].b], [psum[7].b])
                        mm(psum[7][hs, ch * 64:(ch + 1) * 64], TOK[0:TT, tl, 2, hs], UWm[cc][0:TT, u, 0:64], False, True, [TOK.b, UWm[cc].b], [psum[7].b])
                cp(DVE, D1sb[:, 0:NCHK * 64], psum[7][:, 0:NCHK * 64], [psum[7].b], [D1sb.b])
                P.tag = "R.S7c"
                for ch in range(NCHK):
                    cs_ = slice(ch * 64, (ch + 1) * 64)
                    for hh in range(2):
                        hs = slice(64 * hh, 64 * hh + 64)
                        gi = gidx[c3][hh]
                        Gc, Gn = Gst[gi], Gst[1 - gi]
                        Gcb, Gnb = Gb[gi][c3][hh], Gb[1 - gi][c3][hh]
                        mm(psum[4 + hh][hs, cs_], Gc[hs, c3, :], QcT[hs, cs_], True, True, [Gcb, QcT.b], [psum[4 + hh].b])
                        mm(psum[hh][hs, cs_], McT[hs, ch, :], Gc[hs, c3, :], True, True, [McTt.b, Gcb], [psum[hh].b])
                        tt(DVE, Gn[hs, c3, :], psum[hh][hs, cs_], D1sb[hs, cs_], ALU.add, [psum[hh].b, D1sb.b], [Gnb])
                        gidx[c3][hh] = 1 - gi
                    advance(nxtA, int(os.environ.get("ADV", "0")))
                for hh in range(2):
                    hs = slice(64 * hh, 64 * hh + 64)
                    tt(DVE, f["Y"][hs, 0:BW], psum[4 + hh][hs, 0:BW], Y0sb[hs, 0:BW], ALU.add, [psum[4 + hh].b, Y0sb.b], [f["Y"].b])
                P.tag = "R.post"
                act(b["Ybf"][:, :], f["Y"][:, :], AF.Copy, [f["Y"].b], [b["Ybf"].b])
                act(b["Ysq"][:, :], f["Y"][:, :], AF.Square, [f["Y"].b], [b["Ysq"].b])
                bm = next_bank()
                mm(bm[:, 0:BW], bones64[:, :], b["Ybf"][:, :], True, True, [bones64.b, b["Ybf"].b], [bm.b])
                cp(ACT, f["tmp"][:, :], bm[:, 0:BW], [bm.b], [f["tmp"].b])
                bq = next_bank()
                mm(bq[:, 0:BW], bones64[:, :], b["Ysq"][:, :], True, True, [bones64.b, b["Ysq"].b], [bq.b])
                act(f["lgx"][:, :], bm[:, 0:BW], AF.Square, [bm.b], [f["lgx"].b])
                tt(DVE, f["lgx"][:, :], bq[:, 0:BW], f["lgx"][:, :], ALU.subtract, [bq.b, f["lgx"].b], [f["lgx"].b])
                ts(DVE, f["lgx"][:, :], f["lgx"][:, :], 0.0, None, ALU.max, None, [f["lgx"].b], [f["lgx"].b])
                act(f["lgx"][:, :], f["lgx"][:, :], AF.Ln, [f["lgx"].b], [f["lgx"].b], bias=GN_EPS)
                act(f["lgx"][:, :], f["lgx"][:, :], AF.Exp, [f["lgx"].b], [f["lgx"].b], scale=-0.5)
                tt(DVE, f["Y"][:, :], f["Y"][:, :], f["tmp"][:, :], ALU.subtract, [f["Y"].b, f["tmp"].b], [f["Y"].b])
                tt(DVE, f["Y"][:, :], f["Y"][:, :], f["lgx"][:, :], ALU.mult, [f["Y"].b, f["lgx"].b], [f["Y"].b])
                ts(DVE, f["Y"][:, :], f["Y"][:, :], lg_t[:, l, c3:c3 + 1], lb_t[:, l, c3:c3 + 1], ALU.mult, ALU.add,
                   [f["Y"].b, lg_t.b, lb_t.b], [f["Y"].b])
                tt(DVE, f["Y"][:, :], f["Y"][:, :], f["bonus"][:, :], ALU.add, [f["Y"].b, f["bonus"].b], [f["Y"].b])
                tt(POOL, oT[:, c3, blk * BW:(blk + 1) * BW], f["Y"][:, :], gaT[:, :], ALU.mult, [f["Y"].b, gaT.b], [oTb[c3]])
                drain(nxtA)

        for c3 in range(3):
            for hh in range(2):
                hb = 64 * hh
                hs = slice(hb, hb + 64)
                gi = gidx[c3][hh]
                h = 2 * c3 + hh
                fb_ = psum[3] if hh == 0 else psum[2]
                tr(fb_[0:64, 128:192], Gst[gi][hs, c3, :], C["identf"][hs, hs], [Gb[gi][c3][hh], C["identf"].b], [fb_.b])
                cp(DVE, wst[0:64, h, :], fb_[0:64, 128:192], [fb_.b], [wst.b])
        P.dma(POOL, O["wkv"][l, si].rearrange("h v k -> v h k"), wst[0:64, :, :], reads=[wst.b])

    for kind, n in (("p", NP), ("s", NS)):
        for si in range(n):
            T = T_P if kind == "p" else T_S
            past = 0 if kind == "p" else PAST
            cx = SimpleNamespace(kind=kind, si=si, T=T, past=past, NK=past + T, TT=min(128, T), NTT=T // min(128, T),
                                 BW=min(512, T), NBLK=T // min(512, T), NKT=(past + T + 127) // 128, PKT=past // 128,
                                 xsrc=(xp if kind == "p" else xs_), yidx=(si if kind == "p" else NPm + si),
                                 O={k[1:]: v for k, v in outs.items() if k[0] == kind})
            for l in range(NL):
                cx.l = l
                if l == 0:
                    phase_X(cx)
                if "t" not in PHASES:
                    phase_T(cx)
                if "R" in PHASES:
                    phase_R(cx)
                if "F" in PHASES:
                    phase_F(cx)
                if "S" in PHASES:
                    phase_S(cx)
                if dbg and "oT" in dbg_out and l == dbg.get("_layer", 0) and T == dbg["oT"][2] and si == 0:
                    P.dma(POOL, dbg_out["oT"], oT[:, :, 0:T], reads=oTb)
                if "e" not in PHASES:
                    phase_E(cx)
    P.finish()
    global LASTP
    LASTP = P
    return nc


_NC_CACHE = {}


def kernel(**inputs):
    n = 8
    NP, NS = 32 // n, 32 // n
    key = (NP, NS)
    consts = host_consts()
    in_maps = []
    f32 = lambda a: np.ascontiguousarray(np.asarray(a, dtype=np.float32))
    for c in range(n):
        ps, ss = slice(c * NP, (c + 1) * NP), slice(c * NS, (c + 1) * NS)
        m = {
            "x_prompt": f32(inputs["x_prompt"][ps]),
            "x_sample": f32(inputs["x_sample"][ss]),
            "cache_fox_k": f32(np.asarray(inputs["cache_fox_k"])[:, ss].reshape(NL, NS, PAST, 384)),
            "cache_fox_v": f32(np.asarray(inputs["cache_fox_v"])[:, ss].reshape(NL, NS, PAST, 384)),
            "cache_fox_logf": f32(np.asarray(inputs["cache_fox_logf"])[:, ss]),
            "cache_sb_k": f32(np.asarray(inputs["cache_sb_k"])[:, ss].reshape(NL, NS, PAST, 256)),
            "cache_sb_v": f32(np.asarray(inputs["cache_sb_v"])[:, ss].reshape(NL, NS, PAST, 256)),
            "state_wkv": f32(np.asarray(inputs["state_wkv"])[:, ss]),
            "state_shift": f32(np.asarray(inputs["state_shift"])[:, ss].reshape(NL, NS, SHIFT_W)),
            "r_k": f32(np.asarray(inputs["r_k"]).reshape(NL, 384)),
        }
        for k in ("w_in", "mu_shift", "w0_decay", "w_decay", "a0", "w_aaa", "k_k", "k_a", "lnx_g", "lnx_b",
                  "fox_fb", "w_out", "ln_g", "ln_b"):
            m[k] = f32(inputs[k])
        for k, v in consts.items():
            m["c_" + k] = v
        in_maps.append(m)
    nc = build_program(NP, NS)
    res = run_bass_kernel_spmd(nc, in_maps, core_ids=list(range(n)))
    R = res.results

    def cat(name, axis, shape=None):
        a = np.concatenate([np.asarray(r[name], dtype=np.float32) for r in R], axis=axis)
        return a.reshape(shape) if shape is not None else a
    B = 32
    out = (
        cat("p_y", 0), cat("s_y", 0),
        cat("p_fox_k", 1, (NL, B, T_P, 6, 64)), cat("p_fox_v", 1, (NL, B, T_P, 6, 64)), cat("p_fox_logf", 1),
        cat("p_sb_k", 1, (NL, B, T_P, 4, 64)), cat("p_sb_v", 1, (NL, B, T_P, 4, 64)),
        cat("p_wkv", 1), cat("p_shift", 1, (NL, B, 1, SHIFT_W)),
        cat("s_fox_k", 1, (NL, B, T_S, 6, 64)), cat("s_fox_v", 1, (NL, B, T_S, 6, 64)), cat("s_fox_logf", 1),
        cat("s_sb_k", 1, (NL, B, T_S, 4, 64)), cat("s_sb_v", 1, (NL, B, T_S, 4, 64)),
        cat("s_wkv", 1), cat("s_shift", 1, (NL, B, 1, SHIFT_W)),
    )
    return out
```
